# Optimizing a Trainium2 kernel written in Bass

```python
import math
import jax, jax.numpy as jnp
from jax import lax
import numpy as np

D_MODEL = 1024
BATCH = 8
SEQ = 2048
DEPTH = 2

GRID_W = 64
CTX_LEN = 256
EPS = 1e-6
A_HEADS = 4
A_HEAD_DIM = 64
A_WIDTH = 2 * A_HEADS * A_HEAD_DIM
ROPE_THETA = 10000.0
ATTN_Q_BLOCK = 128
B_HEADS = 4
B_KEY_DIM = 128
B_VAL_DIM = 128
B_KEY_TOT = B_HEADS * B_KEY_DIM
B_WIDTH = B_HEADS * B_VAL_DIM
HGRN_CHUNK = 64
C_GROUPS = 4
C_GROUP_DIM = 128
C_WIDTH = C_GROUPS * C_GROUP_DIM
C_CHUNK = 128
D_WIDTH = 512
CONV_W = 3

EVEN_SPLITS = (A_WIDTH, A_WIDTH, A_WIDTH, A_WIDTH, B_KEY_TOT, B_WIDTH, B_KEY_TOT, B_KEY_TOT, B_WIDTH)
ODD_SPLITS = (C_WIDTH, C_WIDTH, C_WIDTH, D_WIDTH, D_WIDTH, D_WIDTH, D_WIDTH)
EVEN_IN = sum(EVEN_SPLITS)
ODD_IN = sum(ODD_SPLITS)
N_EVEN = (DEPTH + 1) // 2
N_ODD = DEPTH // 2

kernel_name = "hybrid_diffattn_hgrn2_gmlp_shortconv_prefix"


def rms_norm(t, gain):
    tf = t.astype(jnp.float32)
    y = tf * lax.rsqrt(jnp.mean(tf * tf, axis=-1, keepdims=True) + EPS)
    return (y * gain.astype(jnp.float32)).astype(t.dtype)


def split_cols(p, sizes):
    return jnp.split(p, np.cumsum(sizes)[:-1].tolist(), axis=-1)


def adaln(cond, w, b):
    m = jnp.matmul(jax.nn.silu(cond), w) + b
    return jnp.split(m, 3, axis=-1)


def modulate(t, gain, shift, scale):
    return rms_norm(t, gain) * (1 + scale) + shift


def axial_rope(rows):
    row = jnp.repeat(jnp.arange(rows, dtype=jnp.float32), GRID_W)
    col = jnp.tile(jnp.arange(GRID_W, dtype=jnp.float32), rows)
    n_freq = A_HEAD_DIM // 4
    inv = ROPE_THETA ** (-jnp.arange(n_freq, dtype=jnp.float32) / n_freq)
    ang = jnp.concatenate([row[:, None] * inv, col[:, None] * inv], axis=-1)
    return jnp.cos(ang), jnp.sin(ang)


def apply_rope(t, cos, sin):
    t1, t2 = jnp.split(t, 2, axis=-1)
    cs = cos[None, :, None, :].astype(t.dtype)
    sn = sin[None, :, None, :].astype(t.dtype)
    return jnp.concatenate([t1 * cs - t2 * sn, t1 * sn + t2 * cs], axis=-1)


def attn_heads(t, gain, rope=None):
    bn, n, _ = t.shape
    t = rms_norm(t.reshape(bn, n, 2 * A_HEADS, A_HEAD_DIM), gain)
    if rope is not None:
        t = apply_rope(t, rope[0], rope[1])
    return t.reshape(bn, n, A_HEADS, 2, A_HEAD_DIM)


def diff_softmax_mix(q, k, v, lam):
    s = jnp.einsum('bqhmd,bkhmd->bhmqk', q, k, preferred_element_type=jnp.float32)
    p = jax.nn.softmax(s, axis=-1)
    w = p[:, :, 0] - lam * p[:, :, 1]
    return jnp.einsum('bhqk,bkhe->bqhe', w.astype(v.dtype), v)


def hgrn2_scan(q, k, v, log_f, s0):
    bn, n, h, _ = q.shape
    L = HGRN_CHUNK
    nc = n // L

    def to_chunks(t):
        return t.reshape(bn, nc, L, h, t.shape[-1]).transpose(1, 0, 3, 2, 4)

    mask = jnp.tril(jnp.ones((L, L), dtype=bool))[:, :, None]

    def step(S, inp):
        qb, kb, vb, gb = inp
        G = jnp.cumsum(gb, axis=2)
        diff = G[:, :, :, None, :] - G[:, :, None, :, :]
        decay = jnp.exp(jnp.where(mask, diff, -jnp.inf))
        A = jnp.einsum('bhtk,bhsk,bhtsk->bhts', qb, kb, decay)
        o = jnp.einsum('bhts,bhsv->bhtv', A, vb) + jnp.einsum('bhtk,bhkv->bhtv', qb * jnp.exp(G), S)
        G_last = G[:, :, -1:, :]
        S_new = jnp.exp(G_last[:, :, 0, :])[..., None] * S + jnp.einsum('bhsk,bhsv->bhkv', kb * jnp.exp(G_last - G), vb)
        return S_new, o

    S_fin, oc = lax.scan(step, s0, (to_chunks(q), to_chunks(k), to_chunks(v), to_chunks(log_f)))
    return oc.transpose(1, 0, 3, 2, 4).reshape(bn, n, h, v.shape[-1]), S_fin


def hgrn_gates(f_raw, lb):
    f = lb + (1.0 - lb) * jax.nn.sigmoid(f_raw.astype(jnp.float32))
    return 1.0 - f, jnp.log(f)


def hgrn2_bidir(q, i_val, f_fwd, f_bwd, lb, s0_fwd, s0_bwd):
    bn, n, _ = q.shape
    heads = lambda t, d: t.astype(jnp.float32).reshape(bn, n, B_HEADS, d)
    flip = lambda t: jnp.flip(t, axis=1)
    qh = heads(q, B_KEY_DIM)
    vh = heads(i_val, B_VAL_DIM)
    k_f, g_f = hgrn_gates(f_fwd, lb[0])
    k_b, g_b = hgrn_gates(f_bwd, lb[1])
    o_f, S_f = hgrn2_scan(qh, heads(k_f, B_KEY_DIM), vh, heads(g_f, B_KEY_DIM), s0_fwd)
    o_b, S_b = hgrn2_scan(flip(qh), flip(heads(k_b, B_KEY_DIM)), flip(vh), flip(heads(g_b, B_KEY_DIM)), s0_bwd)
    return o_f + flip(o_b), S_f, S_b


def short_conv(t, w):
    return lax.conv_general_dilated(t, w[:, None, :].astype(t.dtype), window_strides=(1,),
                                    padding=((CONV_W // 2, CONV_W // 2),),
                                    dimension_numbers=('NWC', 'WIO', 'NWC'),
                                    feature_group_count=t.shape[-1])


def even_layer(x, ctx, c, c_ctx, layer_idx, need_ctx_out, rope, norm_g, ada_w, ada_b,
               w_in, w_out, qk_gain, lam_p, subln_g, lb, hgrn_g):
    shift, scale, gate = adaln(c[:, None, :], ada_w, ada_b)
    shift_c, scale_c, gate_c = adaln(c_ctx, ada_w, ada_b)
    bn, n, _ = x.shape
    pl = split_cols(jnp.matmul(modulate(x, norm_g, shift, scale), w_in), EVEN_SPLITS)
    pc = split_cols(jnp.matmul(modulate(ctx, norm_g, shift_c, scale_c), w_in), EVEN_SPLITS)
    q_scale = A_HEAD_DIM ** -0.5

    lam_init = 0.8 - 0.6 * math.exp(-0.3 * layer_idx)
    lp = lam_p.astype(jnp.float32)
    lam = jnp.exp(jnp.sum(lp[0] * lp[1])) - jnp.exp(jnp.sum(lp[2] * lp[3])) + lam_init
    k_c = attn_heads(pc[1], qk_gain[1])
    v_c = pc[2].reshape(bn, ctx.shape[1], A_HEADS, 2 * A_HEAD_DIM)
    q_l = attn_heads(pl[0], qk_gain[0], rope) * q_scale
    k_l = attn_heads(pl[1], qk_gain[1], rope)
    v_l = pl[2].reshape(bn, n, A_HEADS, 2 * A_HEAD_DIM)
    k_all = jnp.concatenate([k_c, k_l], axis=1)
    v_all = jnp.concatenate([v_c, v_l], axis=1)
    qb = q_l.reshape(bn, n // ATTN_Q_BLOCK, ATTN_Q_BLOCK, A_HEADS, 2, A_HEAD_DIM).transpose(1, 0, 2, 3, 4, 5)
    o_al = lax.map(lambda qq: diff_softmax_mix(qq, k_all, v_all, lam), qb)
    o_al = o_al.transpose(1, 0, 2, 3, 4).reshape(bn, n, A_HEADS, 2 * A_HEAD_DIM)

    def attn_out(o):
        return (rms_norm(o, subln_g) * (1.0 - lam_init)).reshape(o.shape[0], o.shape[1], A_WIDTH)

    s0 = jnp.zeros((bn, B_HEADS, B_KEY_DIM, B_VAL_DIM), jnp.float32)
    o_bc, S_f, S_b = hgrn2_bidir(pc[4], pc[5], pc[6], pc[7], lb, s0, s0)
    o_bl, _, _ = hgrn2_bidir(pl[4], pl[5], pl[6], pl[7], lb, S_f, S_b)

    def hgrn_out(o):
        return rms_norm(o, hgrn_g).reshape(o.shape[0], o.shape[1], B_WIDTH).astype(x.dtype)

    y = jnp.concatenate([attn_out(o_al) * jax.nn.silu(pl[3]), hgrn_out(o_bl) * jax.nn.silu(pl[8])], axis=-1)
    x_new = x + gate * jnp.matmul(y, w_out)
    if need_ctx_out:
        q_c = attn_heads(pc[0], qk_gain[0]) * q_scale
        o_ac = diff_softmax_mix(q_c, k_c, v_c, lam)
        yc = jnp.concatenate([attn_out(o_ac) * jax.nn.silu(pc[3]), hgrn_out(o_bc) * jax.nn.silu(pc[8])], axis=-1)
        ctx = ctx + gate_c * jnp.matmul(yc, w_out)
    return x_new, ctx


def odd_mix(p, v_g, w_s, b_s, conv_w, w_out):
    u, v, g_c, b_gate, c_gate, h_d, g_d = p
    bn, n, _ = u.shape
    u = jax.nn.gelu(u, approximate=False)
    v = rms_norm(jax.nn.gelu(v, approximate=False), v_g)
    vc = v.reshape(bn, n // C_CHUNK, C_CHUNK, C_GROUPS, C_GROUP_DIM)
    s = jnp.einsum('gts,bcsgd->bctgd', w_s, vc) + b_s.T[:, :, None]
    o_c = u * s.reshape(bn, n, C_WIDTH) * jax.nn.silu(g_c)
    o_d = b_gate * short_conv(c_gate * h_d, conv_w) * jax.nn.silu(g_d)
    return jnp.matmul(jnp.concatenate([o_c, o_d], axis=-1), w_out)


def odd_layer(x, ctx, c, c_ctx, need_ctx_out, norm_g, ada_w, ada_b, w_in, w_out, v_g, w_s, b_s, conv_w):
    shift, scale, gate = adaln(c[:, None, :], ada_w, ada_b)
    p = split_cols(jnp.matmul(modulate(x, norm_g, shift, scale), w_in), ODD_SPLITS)
    x_new = x + gate * odd_mix(p, v_g, w_s, b_s, conv_w, w_out)
    if need_ctx_out:
        shift_c, scale_c, gate_c = adaln(c_ctx, ada_w, ada_b)
        pc = split_cols(jnp.matmul(modulate(ctx, norm_g, shift_c, scale_c), w_in), ODD_SPLITS)
        ctx = ctx + gate_c * odd_mix(pc, v_g, w_s, b_s, conv_w, w_out)
    return x_new, ctx


def setup_inputs(seed: int = 0) -> dict:
    key = jax.random.key(seed)
    ks = jax.random.split(key, 20)
    nrm = lambda k, shape, s: jax.random.normal(k, shape, jnp.float32) * s
    return {
        "x": nrm(ks[0], (BATCH, SEQ, D_MODEL), 1.0),
        "c": nrm(ks[1], (BATCH, D_MODEL), 1.0),
        "ctx": nrm(ks[2], (BATCH, CTX_LEN, D_MODEL), 1.0),
        "c_ctx": nrm(ks[3], (D_MODEL,), 1.0),
        "norm_gain": 1.0 + nrm(ks[4], (DEPTH, D_MODEL), 0.02),
        "ada_w": nrm(ks[5], (DEPTH, D_MODEL, 3 * D_MODEL), D_MODEL ** -0.5),
        "ada_b": nrm(ks[6], (DEPTH, 3 * D_MODEL), 0.02),
        "even_w_in": nrm(ks[7], (N_EVEN, D_MODEL, EVEN_IN), D_MODEL ** -0.5),
        "even_w_out": nrm(ks[8], (N_EVEN, A_WIDTH + B_WIDTH, D_MODEL), (A_WIDTH + B_WIDTH) ** -0.5),
        "attn_qk_gain": 1.0 + nrm(ks[9], (N_EVEN, 2, A_HEAD_DIM), 0.02),
        "attn_lambda": nrm(ks[10], (N_EVEN, 4, A_HEAD_DIM), 0.1),
        "attn_subln_gain": 1.0 + nrm(ks[11], (N_EVEN, 2 * A_HEAD_DIM), 0.02),
        "hgrn_lb_logits": nrm(ks[12], (2, N_EVEN + 1, B_KEY_TOT), 0.5),
        "hgrn_norm_gain": 1.0 + nrm(ks[13], (N_EVEN, B_VAL_DIM), 0.02),
        "odd_w_in": nrm(ks[14], (N_ODD, D_MODEL, ODD_IN), D_MODEL ** -0.5),
        "odd_w_out": nrm(ks[15], (N_ODD, C_WIDTH + D_WIDTH, D_MODEL), (C_WIDTH + D_WIDTH) ** -0.5),
        "gmlp_v_gain": 1.0 + nrm(ks[16], (N_ODD, C_WIDTH), 0.02),
        "gmlp_w_s": nrm(ks[17], (N_ODD, C_GROUPS, C_CHUNK, C_CHUNK), C_CHUNK ** -0.5),
        "gmlp_b_s": 1.0 + nrm(ks[18], (N_ODD, C_GROUPS, C_CHUNK), 0.02),
        "conv_w": nrm(ks[19], (N_ODD, CONV_W, D_WIDTH), CONV_W ** -0.5),
    }


def reference(x, c, ctx, c_ctx, norm_gain, ada_w, ada_b, even_w_in, even_w_out, attn_qk_gain,
              attn_lambda, attn_subln_gain, hgrn_lb_logits, hgrn_norm_gain, odd_w_in, odd_w_out,
              gmlp_v_gain, gmlp_w_s, gmlp_b_s, conv_w):
    n = x.shape[1]
    ROWS = n // GRID_W
    rope = axial_rope(ROWS)
    lb_all = jnp.cumsum(jax.nn.softmax(hgrn_lb_logits.astype(jnp.float32), axis=1), axis=1)
    for i in range(DEPTH):
        need_ctx_out = any(j % 2 == 0 for j in range(i + 1, DEPTH))
        if i % 2 == 0:
            e = i // 2
            x, ctx = even_layer(x, ctx, c, c_ctx, i, need_ctx_out, rope, norm_gain[i], ada_w[i], ada_b[i],
                                even_w_in[e], even_w_out[e], attn_qk_gain[e], attn_lambda[e],
                                attn_subln_gain[e], lb_all[:, e], hgrn_norm_gain[e])
        else:
            o = i // 2
            x, ctx = odd_layer(x, ctx, c, c_ctx, need_ctx_out, norm_gain[i], ada_w[i], ada_b[i],
                               odd_w_in[o], odd_w_out[o], gmlp_v_gain[o], gmlp_w_s[o], gmlp_b_s[o], conv_w[o])
    return x
```

```python
import numpy as np
import concourse.bass as bass
import concourse.mybir as mybir
from concourse.bass_utils import run_bass_kernel_spmd
from concourse.alu_op_type import AluOpType as ALU

F32 = mybir.dt.float32
BF16 = mybir.dt.bfloat16
AF = mybir.ActivationFunctionType

D = 1024
SEQ = 2048
CTX = 256
NT = SEQ + CTX
EPS = 1e-6
LAM_INIT = 0.8 - 0.6 * 1.0


class Dep:
    __slots__ = ("name", "w", "r", "dsem", "dcnt", "excl")

    def __init__(self, name, excl=False):
        self.name = name
        self.excl = excl
        self.w = None
        self.r = []
        self.dsem = None
        self.dcnt = 0


class Sched:
    ENG = ("pe", "act", "dve", "pool", "sp")

    def __init__(self, nc):
        self.nc = nc
        self.engs = {"pe": nc.tensor, "act": nc.scalar, "dve": nc.vector, "pool": nc.gpsimd, "sp": nc.sync}
        self.cnt = {e: 0 for e in self.ENG}
        self.sem = {}
        self.waited = {e: {} for e in self.ENG}
        self.dsems = []
        self.nops = 0
        for e in self.ENG:
            self.sem[e] = nc.alloc_semaphore("s_" + e)

    def _waits(self, eng, reads, writes):
        need = {}

        def add(p):
            if p is None:
                return
            s, v = p
            k = id(s)
            if k not in need or need[k][1] < v:
                need[k] = (s, v)

        for d in reads:
            add(d.w)
        for d in writes:
            add(d.w)
            for p in d.r:
                add(p)
        wd = self.waited[eng]
        own = self.sem[eng]
        engine = self.engs[eng]
        for k, (s, v) in need.items():
            if s is own and (eng == "pe" or v > self.cnt[eng]):
                continue
            if wd.get(k, 0) >= v:
                continue
            wd[k] = v
            engine.wait_ge(s, v)

    def op(self, eng, fn, reads=(), writes=(), inc=True):
        ex = [d for d in reads if d.excl]
        if ex:
            reads = [d for d in reads if not d.excl]
            writes = list(writes) + ex
        self._waits(eng, reads, writes)
        val = self.cnt[eng] + 1
        ins = fn()
        self.nops += 1
        if inc:
            self.cnt[eng] = val
            ins.then_inc(self.sem[eng], 1)
        tok = (self.sem[eng], val)
        for d in reads:
            d.r.append(tok)
            if len(d.r) > 64:
                d.r = self._compact(d.r)
        for d in writes:
            d.w = tok
            d.r = []

    @staticmethod
    def _compact(lst):
        best = {}
        for s, v in lst:
            k = id(s)
            if k not in best or best[k][1] < v:
                best[k] = (s, v)
        return list(best.values())

    def dma(self, eng, fn, reads=(), writes=(), semdep=None):
        self._waits(eng, reads, writes)
        d0 = semdep
        if d0.dsem is None:
            d0.dsem = self.nc.alloc_semaphore("d%d_%s" % (len(self.dsems), d0.name))
            self.dsems.append(d0)
        d0.dcnt += 16
        ins = fn()
        ins.then_inc(d0.dsem, 16)
        tok = (d0.dsem, d0.dcnt)
        for d in reads:
            d.r.append(tok)
        for d in writes:
            d.w = tok
            d.r = []

    def wait_all(self, eng, deps):
        self._waits(eng, (), deps)

    def barrier(self):
        for e in self.ENG:
            engine = self.engs[e]
            wd = self.waited[e]
            for f in self.ENG:
                if f == e or self.cnt[f] == 0:
                    continue
                s = self.sem[f]
                if wd.get(id(s), 0) >= self.cnt[f]:
                    continue
                wd[id(s)] = self.cnt[f]
                engine.wait_ge(s, self.cnt[f])
            for d0 in self.dsems:
                if wd.get(id(d0.dsem), 0) >= d0.dcnt:
                    continue
                wd[id(d0.dsem)] = d0.dcnt
                engine.wait_ge(d0.dsem, d0.dcnt)


def skew(items, newest_first=False):
    nst = max(len(it) for it in items)
    for t in range(len(items) + nst - 1):
        for s_ in (range(nst) if newest_first else reversed(range(nst))):
            i = t - s_
            if 0 <= i < len(items) and s_ < len(items[i]):
                items[i][s_]()


class Arena:
    def __init__(self, ap_f32):
        self.ap = ap_f32
        self.top = 0
        self.n = ap_f32.shape[1]

    @staticmethod
    def _view(v, shape):
        if len(shape) == 1:
            return v
        names = ["d%d" % i for i in range(len(shape))]
        pat = "p (" + " ".join(names) + ") -> p " + " ".join(names)
        kw = {names[i]: int(shape[i]) for i in range(1, len(shape))}
        return v.rearrange(pat, **kw)

    def f32(self, *shape):
        n = int(np.prod(shape))
        off = self.top
        self.top += n
        assert self.top <= self.n, ("arena overflow", self.top, self.n)
        return self._view(self.ap[:, off:off + n], shape)

    def bf16(self, *shape):
        n = int(np.prod(shape))
        nw = (n + 1) // 2
        off = self.top
        self.top += nw
        assert self.top <= self.n, ("arena overflow", self.top, self.n)
        return self._view(self.ap[:, off:off + nw].bitcast(BF16)[:, 0:n], shape)


def build(debug=False):
    nc = bass.Bass("TRN2", target_bir_lowering=False)

    def din(name, shape, dt=F32):
        return nc.dram_tensor(name, list(shape), dt, kind="ExternalInput").ap()

    x_d = din("x", [SEQ, D])
    ctx_d = din("ctx", [CTX, D])
    cvec_d = din("cvec", [2, D])
    ng_d = din("norm_gain", [2, D])
    adaw_d = din("ada_w", [2, D, 3 * D])
    adab_d = din("ada_b", [2, 3 * D])
    ewin_d = din("even_w_in", [D, 4608])
    ewout_d = din("even_w_out", [D, D])
    qkg_d = din("qk_gain", [2, 64])
    lam_d = din("attn_lambda", [256])
    subln_d = din("subln", [128])
    lbl_d = din("lb_logits", [2, 2, 512])
    hg_d = din("hgrn_g", [128])
    owin_d = din("odd_w_in", [D, 3584])
    owout_d = din("odd_w_out", [D, D])
    vg_d = din("v_gain", [512])
    ws_d = din("w_s", [4, 128, 128])
    bs_d = din("b_s", [4, 128])
    cw_d = din("conv_w", [3, 512])
    identF_d = din("identF", [128, 128])
    perm_d = din("perm", [128, 128])
    bones_d = din("bones", [128, 128])
    cos_d = din("cosT", [128, SEQ])
    sin_d = din("sinT", [128, SEQ])
    tri_d = din("tri", [64, 2, 64])
    smask_d = din("smask", [128, 512])
    out_d = nc.dram_tensor("out", [SEQ, D], F32, kind="ExternalOutput").ap()
    dbg = {}
    if debug:
        dbg["hT0"] = nc.dram_tensor("dbg_hT0", [128, 8, NT], BF16, kind="ExternalOutput").ap()
        dbg["yT0"] = nc.dram_tensor("dbg_yT0", [128, 8, SEQ], BF16, kind="ExternalOutput").ap()
        dbg["xnew"] = nc.dram_tensor("dbg_xnew", [128, 16, D], F32, kind="ExternalOutput").ap()
        dbg["yT1"] = nc.dram_tensor("dbg_yT1", [128, 8, SEQ], BF16, kind="ExternalOutput").ap()
        dbg["mods"] = nc.dram_tensor("dbg_mods", [128, 2, 3, 8, 2], F32, kind="ExternalOutput").ap()

    S = Sched(nc)
    E = nc
    NW = (nc.sbuf_bytes_remaining - 2048) // 4
    arena_t = nc.alloc_sbuf_tensor("arena", [128, NW], F32).ap()
    AR = Arena(arena_t)
    psum = nc.alloc_psum_tensor("psum", [128, 8, 512], F32).ap()
    PB = [Dep("pb%d" % i, excl=True) for i in range(8)]

    def bank(i):
        return psum[:, i, :]

    def bank_bf(i):
        return psum[:, i, :].bitcast(BF16)

    hT = AR.bf16(8, NT)
    yT = AR.bf16(8, SEQ)
    identF = AR.f32(128)
    identB = AR.bf16(128)
    zerosB = AR.bf16(512)
    gate_bc = [AR.f32(D), AR.f32(D)]
    modsc = AR.f32(2, 3, 8, 2)
    small = AR.f32(64)
    P_TOP = AR.top
    d_hT = [Dep("hT%d" % i) for i in range(18)]
    d_yT = [[Dep("yT%d_%d" % (k, i)) for i in range(16)] for k in range(8)]
    d_const = Dep("const")
    d_mods = Dep("mods")
    d_gate = [Dep("gate0"), Dep("gate1")]
    d_small = Dep("small")

    sp, act, dve, pool, pe = nc.sync, nc.scalar, nc.vector, nc.gpsimd, nc.tensor

    S.dma("sp", lambda: sp.dma_start(out=identF, in_=identF_d), writes=[d_const], semdep=d_const)
    S.op("dve", lambda: dve.tensor_copy(out=identB, in_=identF), reads=[d_const], writes=[d_const])
    S.op("pool", lambda: pool.memset(zerosB, 0.0), writes=[d_const])

    AR.top = P_TOP
    cvT = AR.f32(8, 2)
    csT = AR.bf16(8, 2)
    Rrow = AR.f32(3 * D)
    adab = AR.f32(3 * D)
    gT = AR.f32(2, 8)
    onesr = AR.f32(128)
    wada = [AR.f32(3 * D) for _ in range(3)]
    wadab = [AR.bf16(3 * D), AR.bf16(3 * D)]
    d_cv, d_R, d_adab, d_gT, d_ones = Dep("cv"), Dep("R"), Dep("adab"), Dep("gT"), Dep("ones")
    d_wada = [[Dep("wada%d_%d" % (i, j)) for j in range(3)] for i in range(3)]
    d_wadab = [[Dep("wadab%d_%d" % (i, j)) for j in range(4)] for i in range(2)]

    for r in range(2):
        S.dma("sp", lambda r=r: sp.dma_start(out=cvT[:, :, r], in_=cvec_d[r].rearrange("(k p) -> p k", p=128), allow_slow_non_contiguous=True), writes=[d_cv], semdep=d_cv)
        S.dma("sp", lambda r=r: sp.dma_start(out=gT[:, r, :], in_=ng_d[r].rearrange("(k p) -> p k", p=128), allow_slow_non_contiguous=True), writes=[d_gT], semdep=d_gT)
    S.op("act", lambda: act.activation(out=csT, in_=cvT, func=AF.Silu), reads=[d_cv], writes=[d_cv])
    S.op("pool", lambda: pool.memset(onesr, 1.0), writes=[d_ones])
    wi = 0
    for l in range(2):
        S.dma("sp", lambda l=l: sp.dma_start(out=adab[0:2, :], in_=adab_d[l].partition_broadcast(2)), writes=[d_adab], semdep=d_adab)
        for k in range(8):
            slot = wi % 3
            bs_ = wi % 2
            wi += 1
            for hh in range(3):
                qn_ = ("sp", "pool", "act")[hh]
                qe_ = (sp, pool, act)[hh]
                S.dma(qn_, lambda l=l, k=k, slot=slot, hh=hh, qe_=qe_: qe_.dma_start(
                    out=wada[slot][:, hh * 1024:(hh + 1) * 1024], in_=adaw_d[l][k * 128:(k + 1) * 128, hh * 1024:(hh + 1) * 1024]),
                    writes=[d_wada[slot][hh]], semdep=d_wada[slot][hh])
            S.op("dve", lambda slot=slot, bs_=bs_: dve.tensor_copy(out=wadab[bs_][:, 0:1024], in_=wada[slot][:, 0:1024]), reads=[d_wada[slot][0]], writes=[d_wadab[bs_][0]])
            S.op("act", lambda slot=slot, bs_=bs_: act.copy(out=wadab[bs_][:, 1024:1536], in_=wada[slot][:, 1024:1536]), reads=[d_wada[slot][1]], writes=[d_wadab[bs_][1]])
            S.op("act", lambda slot=slot, bs_=bs_: act.copy(out=wadab[bs_][:, 1536:2560], in_=wada[slot][:, 1536:2560]), reads=[d_wada[slot][1], d_wada[slot][2]], writes=[d_wadab[bs_][2]])
            S.op("pool", lambda slot=slot, bs_=bs_: pool.tensor_copy(out=wadab[bs_][:, 2560:3072], in_=wada[slot][:, 2560:3072]), reads=[d_wada[slot][2]], writes=[d_wadab[bs_][3]])
            for n in range(6):
                part = (0, 0, 1, 2, 2, 3)[n]
                S.op("pe", lambda k=k, n=n, bs_=bs_: pe.matmul(bank(n)[0:2, :], lhsT=csT[:, k, :], rhs=wadab[bs_][:, n * 512:(n + 1) * 512], start=(k == 0), stop=(k == 7)),
                     reads=[d_cv, d_wadab[bs_][part]], writes=[PB[n]])
        for n in range(6):
            S.op("dve", lambda n=n: dve.tensor_tensor(out=Rrow[0:2, n * 512:(n + 1) * 512], in0=bank(n)[0:2, :], in1=adab[0:2, n * 512:(n + 1) * 512], op=ALU.add),
                 reads=[PB[n], d_adab], writes=[d_R])
        pt = bank(6)[:, 0:32].rearrange("p (j r) -> p j r", r=2)
        for j in range(16):
            S.op("pe", lambda j=j: pe.transpose(out=pt[:, j, :], in_=Rrow[0:2, j * 128:(j + 1) * 128], identity=identF[0:2, 0:2]),
                 reads=[d_R, d_const], writes=[PB[6]], inc=(j == 15))
        S.op("dve", lambda l=l: dve.tensor_copy(out=modsc[:, l, 0, :, :], in_=pt[:, 0:8, :]), reads=[PB[6]], writes=[d_mods])
        S.op("dve", lambda l=l: dve.tensor_copy(out=modsc[:, l, 2, :, :], in_=pt[:, 8:16, :]), reads=[PB[6]], writes=[d_mods])
        for r in range(2):
            S.op("dve", lambda l=l, r=r: dve.scalar_tensor_tensor(out=modsc[:, l, 1, :, r], in0=modsc[:, l, 2, :, r], scalar=1.0, in1=gT[:, l, :], op0=ALU.add, op1=ALU.mult),
                 reads=[d_mods, d_gT], writes=[d_mods])
        for n in range(2):
            S.op("pe", lambda n=n: pe.matmul(bank(2 + n), lhsT=onesr[0:1, :], rhs=Rrow[0:1, 2048 + n * 512:2048 + (n + 1) * 512], start=True, stop=True),
                 reads=[d_R, d_ones], writes=[PB[2 + n]])
            S.op("act", lambda n=n, l=l: act.copy(out=gate_bc[l][:, n * 512:(n + 1) * 512], in_=bank(2 + n)), reads=[PB[2 + n]], writes=[d_gate[l]])
    if debug:
        S.dma("sp", lambda: sp.dma_start(out=dbg["mods"], in_=modsc), reads=[d_mods], writes=[], semdep=d_mods)
    S.barrier()

    def norm_phase(l, ntiles, src_fn, top):
        AR.top = top
        nxt = 4 if l == 0 else 0
        xt = [AR.f32(D) for _ in range(nxt)]
        xn = [AR.f32(D), AR.f32(D)]
        junk = AR.bf16(D)
        stat = AR.f32(3, 18)
        d_xt = [Dep("xt%d" % i_) for i_ in range(nxt)]
        d_xn = [Dep("xn0"), Dep("xn1")]
        d_junk, d_stat = Dep("junk"), [Dep("stat%d" % i) for i in range(18)]

        def item(i):
            s = i % 2
            src, is_ctx, sdep = src_fn(i)
            b0 = 4 + 2 * (i % 2)
            pt = psum[:, b0:b0 + 2, :].rearrange("p b (j c) -> p (b j) c", c=128)
            r = 1 if is_ctx else 0
            if sdep is None:
                xin, xdep = xt[i % 4], d_xt[i % 4]
            else:
                xin, xdep = src, sdep

            def s0():
                if sdep is None:
                    S.dma("sp", lambda: sp.dma_start(out=xt[i % 4], in_=src), writes=[d_xt[i % 4]], semdep=d_xt[i % 4])

            def s1():
                S.op("act", lambda: act.activation(out=junk, in_=xin, func=AF.Square, accum_out=stat[:, 0, i:i + 1]), reads=[xdep], writes=[d_junk, d_stat[i]])
                S.op("act", lambda: act.activation(out=stat[:, 1, i:i + 1], in_=stat[:, 0, i:i + 1], func=AF.Sqrt, scale=1.0 / D, bias=EPS), reads=[d_stat[i]], writes=[d_stat[i]])
                S.op("dve", lambda: dve.reciprocal(out=stat[:, 2, i:i + 1], in_=stat[:, 1, i:i + 1]), reads=[d_stat[i]], writes=[d_stat[i]])

            def s2():
                S.op("dve", lambda: dve.tensor_scalar(out=xn[s], in0=xin, scalar1=stat[:, 2, i:i + 1], scalar2=None, op0=ALU.mult), reads=[xdep, d_stat[i]], writes=[d_xn[s]])
                for k in range(8):
                    S.op("pe", lambda k=k: pe.transpose(out=pt[:, k, :], in_=xn[s][:, k * 128:(k + 1) * 128], identity=identF),
                         reads=[d_xn[s], d_const], writes=[PB[b0 + k // 4]], inc=(k == 7))
                for _ in range(6):
                    S.op("pe", lambda: pe.matmul(bank(3), lhsT=zerosB[:, 0:128], rhs=zerosB, start=True, stop=True, skip_group_check=True), reads=[d_const], writes=[PB[3]], inc=False)

            def s3():
                for k in range(8):
                    if k < 4:
                        S.op("act", lambda k=k: act.activation(out=hT[:, k, i * 128:(i + 1) * 128], in_=pt[:, k, :], func=AF.Identity,
                                                               scale=modsc[:, l, 1, k, r:r + 1], bias=modsc[:, l, 0, k, r:r + 1]),
                             reads=[PB[b0 + k // 4], d_mods], writes=[d_hT[i]])
                    else:
                        S.op("dve", lambda k=k: dve.tensor_scalar(out=hT[:, k, i * 128:(i + 1) * 128], in0=pt[:, k, :],
                                                                  scalar1=modsc[:, l, 1, k, r:r + 1], scalar2=modsc[:, l, 0, k, r:r + 1], op0=ALU.mult, op1=ALU.add),
                             reads=[PB[b0 + k // 4], d_mods], writes=[d_hT[i]])
            return [s0, s1, s2, s3]

        skew([item(i) for i in range(ntiles)], newest_first=True)

    def src0(i):
        if i < 2:
            return ctx_d[i * 128:(i + 1) * 128, :], True, None
        return x_d[(i - 2) * 128:(i - 1) * 128, :], False, None


    AR.top = P_TOP
    cosT = AR.f32(SEQ)
    sinT = AR.f32(SEQ)
    permS = AR.f32(128)
    bonesS = AR.f32(128)
    gqk = AR.f32(2)
    lamb = AR.f32(256)
    lamt = AR.f32(8)
    gS = AR.f32(128)
    epsT = AR.f32(1)
    mhA = AR.f32(1)
    wA = [AR.bf16(4, 8, 128), AR.bf16(4, 8, 128)]
    qT = [AR.bf16(SEQ), AR.bf16(SEQ)]
    kT0 = [AR.bf16(NT), AR.bf16(NT)]
    kT1 = [AR.bf16(NT), AR.bf16(NT)]
    vaug = [AR.bf16(18, 130), AR.bf16(18, 130)]
    gateA = [AR.bf16(16, 128), AR.bf16(16, 128)]
    t_sq = [AR.f32(512), AR.f32(512)]
    t_qg = [AR.f32(512), AR.f32(512)]
    t_rs = [AR.f32(512), AR.f32(512)]
    t_a = [AR.f32(512), AR.f32(512)]
    t_b = [AR.f32(512), AR.f32(512)]
    t_gs = AR.f32(512)
    Et = [AR.bf16(512) for _ in range(4)]
    ep_f = AR.f32(16, 128)
    ep_s = AR.f32(8, 8)
    ep_y = AR.bf16(8, 128)
    uc = [0]
    d_A = Dep("Aconst")
    d_Ar = Dep("Arope")
    d_wA = [Dep("wA0"), Dep("wA1")]
    d_qT = [[Dep("qT%d_%d" % (p_, i)) for i in range(4)] for p_ in range(2)]
    d_kT = [[Dep("kT%d_%d" % (p_, i)) for i in range(5)] for p_ in range(2)]
    d_v = [[Dep("vaug%d_%d" % (p_, i)) for i in range(5)] for p_ in range(2)]
    d_gA = [[Dep("gateA%d_%d" % (p_, i)) for i in range(4)] for p_ in range(2)]
    d_tsq = [Dep("tsq0"), Dep("tsq1")]
    d_tqg = [Dep("tqg0"), Dep("tqg1")]
    d_trs = [Dep("trs0"), Dep("trs1")]
    d_ta = [Dep("ta0"), Dep("ta1")]
    d_tb = [Dep("tb0"), Dep("tb1")]
    d_tgs = Dep("tgs")
    d_E = [Dep("E%d" % i) for i in range(4)]
    d_ep = [Dep("ep%d" % i) for i in range(8)]

    S.dma("sp", lambda: sp.dma_start(out=cosT, in_=cos_d), writes=[d_Ar], semdep=d_Ar)
    S.dma("sp", lambda: sp.dma_start(out=sinT, in_=sin_d), writes=[d_Ar], semdep=d_Ar)
    S.dma("sp", lambda: sp.dma_start(out=permS, in_=perm_d), writes=[d_Ar], semdep=d_Ar)
    S.dma("sp", lambda: sp.dma_start(out=bonesS, in_=bones_d), writes=[d_Ar], semdep=d_Ar)
    for m in range(2):
        S.dma("sp", lambda m=m: sp.dma_start(out=gqk[m * 64:(m + 1) * 64, :], in_=qkg_d.rearrange("r d -> d r"), allow_slow_non_contiguous=True), writes=[d_A], semdep=d_A)
    S.dma("sp", lambda: sp.dma_start(out=lamb, in_=lam_d.partition_broadcast(128)), writes=[d_A], semdep=d_A)
    S.dma("sp", lambda: sp.dma_start(out=gS, in_=subln_d.partition_broadcast(128)), writes=[d_A], semdep=d_A)
    S.op("dve", lambda: dve.tensor_scalar(out=gqk[:, 0:1], in0=gqk[:, 0:1], scalar1=0.125, scalar2=None, op0=ALU.mult), reads=[d_A], writes=[d_A])
    S.op("dve", lambda: dve.tensor_scalar(out=gS, in0=gS, scalar1=1.0 - LAM_INIT, scalar2=None, op0=ALU.mult), reads=[d_A], writes=[d_A])
    S.op("dve", lambda: dve.tensor_tensor(out=lamb[:, 0:64], in0=lamb[:, 0:64], in1=lamb[:, 64:128], op=ALU.mult), reads=[d_A], writes=[d_A])
    S.op("dve", lambda: dve.tensor_tensor(out=lamb[:, 128:192], in0=lamb[:, 128:192], in1=lamb[:, 192:256], op=ALU.mult), reads=[d_A], writes=[d_A])
    S.op("dve", lambda: dve.tensor_reduce(out=lamt[:, 0:1], in_=lamb[:, 0:64], op=ALU.add, axis=mybir.AxisListType.X), reads=[d_A], writes=[d_A])
    S.op("dve", lambda: dve.tensor_reduce(out=lamt[:, 1:2], in_=lamb[:, 128:192], op=ALU.add, axis=mybir.AxisListType.X), reads=[d_A], writes=[d_A])
    S.op("act", lambda: act.activation(out=lamt[:, 2:4], in_=lamt[:, 0:2], func=AF.Exp), reads=[d_A], writes=[d_A])
    S.op("dve", lambda: dve.scalar_tensor_tensor(out=lamt[:, 4:5], in0=lamt[:, 3:4], scalar=-LAM_INIT, in1=lamt[:, 2:3], op0=ALU.add, op1=ALU.subtract), reads=[d_A], writes=[d_A])
    S.op("pool", lambda: pool.memset(epsT, EPS), writes=[d_A])
    S.op("pool", lambda: pool.memset(mhA, -0.5), writes=[d_A])
    for p_ in range(2):
        S.op("pool", lambda p_=p_: pool.memset(kT0[p_], 0.0), writes=d_kT[p_])
        S.op("pool", lambda p_=p_: pool.memset(kT1[p_], 0.0), writes=d_kT[p_])
        S.op("pool", lambda p_=p_: pool.memset(vaug[p_][:, :, 128:130], 1.0), writes=d_v[p_])

    PROJ_B = [3, 4, 5, 6, 7]
    pr = [0]

    def nextbank():
        b = PROJ_B[pr[0] % len(PROJ_B)]
        pr[0] += 1
        return b

    blk = [0]

    def qk_item(h, which, tok0, ntok, rope, dst_fn, ddst):
        s = blk[0] % 2
        blk[0] += 1
        slot = h % 2
        st = {}

        def m1():
            bp = nextbank()
            st["bp"] = bp
            for k in range(8):
                S.op("pe", lambda k=k: pe.matmul(bank(bp)[:, 0:ntok], lhsT=wA[slot][:, which, k, :], rhs=hT[:, k, tok0:tok0 + ntok], start=(k == 0), stop=(k == 7)),
                     reads=[d_wA[slot]] + d_hT[tok0 // 128:(tok0 + ntok + 127) // 128], writes=[PB[bp]], inc=(k == 7))

        def m2():
            bp = st["bp"]
            S.op("dve", lambda: dve.tensor_copy(out=t_b[s][:, 0:ntok], in_=bank(bp)[:, 0:ntok]), reads=[PB[bp]], writes=[d_tb[s]])
            S.op("dve", lambda: dve.tensor_tensor(out=t_sq[s][:, 0:ntok], in0=t_b[s][:, 0:ntok], in1=t_b[s][:, 0:ntok], op=ALU.mult), reads=[d_tb[s]], writes=[d_tsq[s]])
            S.op("dve", lambda: dve.tensor_scalar(out=t_qg[s][:, 0:ntok], in0=t_b[s][:, 0:ntok], scalar1=gqk[:, which:which + 1], scalar2=None, op0=ALU.mult),
                 reads=[d_tb[s], d_A], writes=[d_tqg[s]])

        def m3():
            bs = nextbank()
            st["bs"] = bs
            S.op("pe", lambda: pe.matmul(bank(bs)[:, 0:ntok], lhsT=bonesS, rhs=t_sq[s][:, 0:ntok], start=True, stop=True), reads=[d_tsq[s], d_Ar], writes=[PB[bs]])
            if rope:
                br = nextbank()
                st["br"] = br
                S.op("pe", lambda: pe.matmul(bank(br)[:, 0:ntok], lhsT=permS, rhs=t_qg[s][:, 0:ntok], start=True, stop=True), reads=[d_tqg[s], d_Ar], writes=[PB[br]])

        def m4():
            bs = st["bs"]
            S.op("act", lambda: act.activation(out=t_rs[s][:, 0:ntok], in_=bank(bs)[:, 0:ntok], func=AF.Ln, scale=1.0 / 64, bias=epsT[:, 0:1]), reads=[PB[bs], d_A], writes=[d_trs[s]])
            S.op("act", lambda: act.activation(out=t_rs[s][:, 0:ntok], in_=t_rs[s][:, 0:ntok], func=AF.Exp, scale=-0.5), reads=[d_trs[s]], writes=[d_trs[s]])
            if rope:
                br = st["br"]
                p0 = tok0 - CTX
                S.op("pool", lambda: pool.tensor_tensor(out=t_a[s][:, 0:ntok], in0=t_qg[s][:, 0:ntok], in1=cosT[:, p0:p0 + ntok], op=ALU.mult), reads=[d_tqg[s], d_Ar], writes=[d_ta[s]])
                S.op("dve", lambda: dve.tensor_tensor(out=t_b[s][:, 0:ntok], in0=bank(br)[:, 0:ntok], in1=sinT[:, p0:p0 + ntok], op=ALU.mult), reads=[PB[br], d_Ar, d_tsq[s], d_tqg[s]], writes=[d_tb[s]])

        def m5():
            if rope:
                S.op("dve", lambda: dve.tensor_tensor(out=t_a[s][:, 0:ntok], in0=t_a[s][:, 0:ntok], in1=t_b[s][:, 0:ntok], op=ALU.add), reads=[d_ta[s], d_tb[s]], writes=[d_ta[s]])
                srcv, sdep = t_a[s], d_ta[s]
            else:
                srcv, sdep = t_qg[s], d_tqg[s]
            for (dst, p_lo, p_hi) in dst_fn():
                eng_, ee_ = ("dve", dve)
                S.op(eng_, lambda dst=dst, p_lo=p_lo, p_hi=p_hi: ee_.tensor_tensor(out=dst, in0=srcv[p_lo:p_hi, 0:ntok], in1=t_rs[s][p_lo:p_hi, 0:ntok], op=ALU.mult),
                     reads=[sdep, d_trs[s]], writes=[ddst])
        return [m1, m2, m3, m4, m5]

    def proj_tasks(h, interleave=False):
        hp = h % 2
        slot = h % 2
        items = []
        for i in range(4):
            items.append(qk_item(h, 0, CTX + i * 512, 512, True, lambda i=i: [(qT[hp][:, i * 512:(i + 1) * 512], 0, 128)], d_qT[hp][i]))
        items.append(qk_item(h, 1, 0, 256, False, lambda: [(kT0[hp][0:64, 0:256], 0, 64), (kT1[hp][64:128, 0:256], 64, 128)], d_kT[hp][0]))
        for i in range(4):
            t0 = CTX + i * 512
            items.append(qk_item(h, 1, t0, 512, True, lambda t0=t0: [(kT0[hp][0:64, t0:t0 + 512], 0, 64), (kT1[hp][64:128, t0:t0 + 512], 64, 128)], d_kT[hp][1 + i]))
        tasks = []
        if interleave:
            tasks.extend(items[0][0:3])
            for n_ in range(1, len(items)):
                a_, b_ = items[n_], items[n_ - 1]
                tasks.extend([a_[0], b_[3], a_[1], b_[4], a_[2]])
            tasks.extend(items[-1][3:5])
        else:
            for it in items:
                tasks.extend(it)

        def v_task(grp):
            st = {}

            def a():
                tiles = list(range(grp * 4, min(grp * 4 + 4, 18)))
                bp = nextbank()
                st["bp"] = bp
                pv = bank(bp).rearrange("p (j c) -> p j c", c=128)
                for jj, ti in enumerate(tiles):
                    for k in range(8):
                        S.op("pe", lambda k=k, jj=jj, ti=ti: pe.matmul(pv[:, jj, :], lhsT=hT[:, k, ti * 128:(ti + 1) * 128], rhs=wA[slot][:, 2, k, :], start=(k == 0), stop=(k == 7)),
                             reads=[d_wA[slot], d_hT[ti]], writes=[PB[bp]], inc=(k == 7 and jj == len(tiles) - 1))

            def b():
                bp = st["bp"]
                pv = bank(bp).rearrange("p (j c) -> p j c", c=128)
                n = len(list(range(grp * 4, min(grp * 4 + 4, 18))))
                S.op("dve", lambda: dve.tensor_copy(out=vaug[hp][:, grp * 4:grp * 4 + n, 0:128], in_=pv[:, 0:n, :]), reads=[PB[bp]], writes=[d_v[hp][grp]])
            return [a, b]

        def g_task(grp):
            st = {}

            def a():
                bp = nextbank()
                st["bp"] = bp
                pv = bank(bp).rearrange("p (j c) -> p j c", c=128)
                for jj in range(4):
                    ti = 2 + grp * 4 + jj
                    for k in range(8):
                        S.op("pe", lambda k=k, jj=jj, ti=ti: pe.matmul(pv[:, jj, :], lhsT=hT[:, k, ti * 128:(ti + 1) * 128], rhs=wA[slot][:, 3, k, :], start=(k == 0), stop=(k == 7)),
                             reads=[d_wA[slot], d_hT[ti]], writes=[PB[bp]], inc=(k == 7 and jj == 3))

            def b():
                bp = st["bp"]
                S.op("act", lambda: act.activation(out=t_gs, in_=bank(bp), func=AF.Exp, scale=-1.0), reads=[PB[bp]], writes=[d_tgs])
                S.op("dve", lambda: dve.tensor_scalar(out=t_gs, in0=t_gs, scalar1=1.0, scalar2=1e30, op0=ALU.add, op1=ALU.min), reads=[], writes=[d_tgs])

            def c():
                bp = st["bp"]
                pv = bank(bp).rearrange("p (j c) -> p j c", c=128)
                S.op("act", lambda: act.activation(out=t_gs, in_=t_gs, func=AF.Ln), reads=[], writes=[d_tgs])
                S.op("act", lambda: act.activation(out=t_gs, in_=t_gs, func=AF.Exp, scale=-1.0), reads=[], writes=[d_tgs])
                S.op("dve", lambda: dve.tensor_tensor(out=gateA[hp][:, grp * 4:grp * 4 + 4, :], in0=pv, in1=t_gs.rearrange("p (j c) -> p j c", c=128), op=ALU.mult), reads=[PB[bp], d_tgs], writes=[d_gA[hp][grp]])
            return [a, b, c]

        for grp in range(5):
            tasks.extend(v_task(grp))
        for grp in range(4):
            tasks.extend(g_task(grp))
        return tasks

    def load_wA(h):
        slot = h % 2
        for g in range(4):
            S.dma("pool", lambda g=g: pool.dma_start(
                out=wA[slot][:, g, :, :], in_=ewin_d.rearrange("(k p) n -> p k n", p=128)[:, :, g * 512 + h * 128:g * 512 + (h + 1) * 128]),
                writes=[d_wA[slot]], semdep=d_wA[slot])

    _save_top = AR.top
    AR.top = P_TOP
    lbl = AR.f32(2, 2, 4)
    lbv = AR.f32(3, 2, 4)
    tri = AR.f32(2, 64)
    smask = AR.f32(512)
    hgT = AR.f32(1)
    epsB = AR.f32(1)
    oneB = AR.f32(1)
    mhB = AR.f32(1)
    wB = AR.bf16(5, 8, 128)
    assert AR.top <= P_TOP + 2 * SEQ
    AR.top = _save_top
    d_B = Dep("Bconst")
    d_wB = Dep("wB")

    def load_wB(h, extra=()):
        for g in range(5):
            S.dma("pool", lambda g=g: pool.dma_start(
                out=wB[:, g, :, :], in_=ewin_d.rearrange("(k p) n -> p k n", p=128)[:, :, 2048 + g * 512 + h * 128:2048 + g * 512 + (h + 1) * 128]),
                writes=[d_wB] + list(extra), semdep=d_wB)

    def b_prefetch():
        ex = [d_Ar]
        for dd_ in range(2):
            for ee_ in range(2):
                S.dma("sp", lambda dd_=dd_, ee_=ee_: sp.dma_start(out=lbl[:, dd_, ee_, :], in_=lbl_d[dd_, ee_].rearrange("(h p) -> p h", p=128), allow_slow_non_contiguous=True), writes=[d_B] + ex, semdep=d_B)
        S.dma("sp", lambda: sp.dma_start(out=tri[0:64, :, :], in_=tri_d), writes=[d_B] + ex, semdep=d_B)
        S.dma("sp", lambda: sp.dma_start(out=smask, in_=smask_d), writes=[d_B] + ex, semdep=d_B)
        S.dma("sp", lambda: sp.dma_start(out=hgT, in_=hg_d.rearrange("(p o) -> p o", o=1), allow_slow_non_contiguous=True), writes=[d_B] + ex, semdep=d_B)
        load_wB(0, extra=ex)

    load_wA(0)
    norm_phase(0, 18, src0, NW - 6900)
    if debug:
        S.dma("sp", lambda: sp.dma_start(out=dbg["hT0"], in_=hT), reads=d_hT, writes=[], semdep=d_hT[0])
    S.barrier()
    for f in proj_tasks(0, interleave=True):
        f()
    deferred = []
    PROJ_B[:] = [6, 7]
    for h in range(4):
        hp = h % 2
        if h + 1 < 4:
            load_wA(h + 1)
            tasks = proj_tasks(h + 1)
        else:
            tasks = []

        def acc(m, t):
            sidx = m * 4 + t
            return psum[:, sidx // 3, (sidx % 3) * 130:(sidx % 3) * 130 + 130]

        def emit_score(i, u):
            j, m = u // 2, u % 2
            sb = 3 + (uc[0] + u) % 3
            kTm = kT0[hp] if m == 0 else kT1[hp]
            kd = d_kT[hp][0] if j < 2 else d_kT[hp][1 + (j - 2) // 4]
            S.op("pe", lambda: pe.matmul(bank(sb), lhsT=kTm[:, j * 128:(j + 1) * 128], rhs=qT[hp][:, i * 512:(i + 1) * 512], start=True, stop=True),
                 reads=[kd, d_qT[hp][i]], writes=[PB[sb]])

        def emit_exp_pv(i, u):
            j, m = u // 2, u % 2
            sb = 3 + (uc[0] + u) % 3
            es = (uc[0] + u) % 4
            S.op("act", lambda: act.activation(out=Et[es], in_=bank(sb), func=AF.Exp), reads=[PB[sb]], writes=[d_E[es]])
            for t in range(4):
                S.op("pe", lambda t=t: pe.matmul(acc(m, t)[:, 0:129], lhsT=Et[es][:, t * 128:(t + 1) * 128], rhs=vaug[hp][:, j, 0:129], start=False, stop=(j == 17), skip_group_check=True),
                     reads=[d_E[es], d_v[hp][j // 4]], writes=[PB[(m * 4 + t) // 3]], inc=(t == 3))

        def epilogue1(i):
            sl = i % 2
            for t in range(4):
                a0, a1 = acc(0, t), acc(1, t)
                b0_, b1_ = PB[t // 3], PB[(4 + t) // 3]
                es_ = ep_s[:, sl * 4 + t, :]
                f0, f1 = ep_f[:, sl * 8 + 2 * t, :], ep_f[:, sl * 8 + 2 * t + 1, :]
                dd = d_ep[sl * 4 + t]
                S.op("dve", lambda: dve.reciprocal(out=es_[:, 0:1], in_=a0[:, 128:129]), reads=[b0_], writes=[dd])
                S.op("dve", lambda: dve.reciprocal(out=es_[:, 1:2], in_=a1[:, 128:129]), reads=[b1_], writes=[dd])
                S.op("dve", lambda: dve.tensor_tensor(out=es_[:, 1:2], in0=es_[:, 1:2], in1=lamt[:, 4:5], op=ALU.mult), reads=[d_A], writes=[dd])
                S.op("dve", lambda: dve.tensor_scalar(out=f0, in0=a1[:, 0:128], scalar1=es_[:, 1:2], scalar2=None, op0=ALU.mult), reads=[b1_], writes=[dd])
                S.op("dve", lambda: dve.scalar_tensor_tensor(out=f1, in0=a0[:, 0:128], scalar=es_[:, 0:1], in1=f0, op0=ALU.mult, op1=ALU.add), reads=[b0_], writes=[dd])

        def epilogue1b(i, hp=hp):
            sl = i % 2
            for t in range(4):
                qt = i * 4 + t
                es_ = ep_s[:, sl * 4 + t, :]
                f0, f1 = ep_f[:, sl * 8 + 2 * t, :], ep_f[:, sl * 8 + 2 * t + 1, :]
                dd = d_ep[sl * 4 + t]
                S.op("dve", lambda: dve.scalar_tensor_tensor(out=f0, in0=f1, scalar=1.0, in1=f1, op0=ALU.mult, op1=ALU.mult, accum_out=es_[:, 2:3]), reads=[dd], writes=[dd])
                S.op("dve", lambda: dve.tensor_scalar(out=es_[:, 3:4], in0=es_[:, 2:3], scalar1=1.0 / 128, scalar2=EPS, op0=ALU.mult, op1=ALU.add), reads=[dd], writes=[dd])
                S.op("pool", lambda: pool.tensor_tensor(out=es_[:, 4:5], in0=es_[:, 3:4], in1=mhA, op=ALU.pow), reads=[dd, d_A], writes=[dd])
                S.op("dve", lambda: dve.scalar_tensor_tensor(out=f0, in0=f1, scalar=es_[:, 4:5], in1=gS, op0=ALU.mult, op1=ALU.mult), reads=[dd, d_A], writes=[dd])
                S.op("pool", lambda: pool.tensor_tensor(out=ep_y[:, sl * 4 + t, :], in0=f0, in1=gateA[hp][:, qt, :], op=ALU.mult), reads=[dd, d_gA[hp][qt // 4]], writes=[dd])

        def epilogue2(i, h=h):
            sl = i % 2
            bt = nextbank()
            pyt = bank_bf(bt)[:, 0:512].rearrange("p (t c) -> p t c", c=128)
            for t in range(4):
                S.op("pe", lambda t=t: pe.transpose(out=pyt[:, t, :], in_=ep_y[:, sl * 4 + t, :], identity=identB), reads=[d_ep[sl * 4 + t], d_const], writes=[PB[bt]], inc=(t == 3))
            S.op("dve", lambda: dve.tensor_copy(out=yT[:, h, i * 512:(i + 1) * 512], in_=bank_bf(bt)[:, 0:512]), reads=[PB[bt]], writes=d_yT[h][i * 4:i * 4 + 4])

        def zero_acc():
            for b in range(3):
                S.op("pe", lambda b=b: pe.matmul(bank(b), lhsT=zerosB[:, 0:128], rhs=zerosB, start=True, stop=True, skip_group_check=True), reads=[d_const], writes=[PB[b]])

        for i in range(4):
            if i == 0:
                zero_acc()
                for u0 in range(3):
                    emit_score(0, u0)
            for u in range(36):
                emit_exp_pv(i, u)
                if u + 3 < 36:
                    emit_score(i, u + 3)
                for dq in list(deferred):
                    dq[0] -= 1
                    if dq[0] <= 0:
                        deferred.remove(dq)
                        dq[1]()
                if tasks and u % 2 == 1:
                    tasks.pop(0)()
            uc[0] += 36
            if i + 1 < 4:
                for u0 in range(3):
                    emit_score(i + 1, u0)
            epilogue1(i)
            deferred.append([4, (lambda e=epilogue1b, i=i: e(i))])
            deferred.append([12, (lambda e=epilogue2, i=i: e(i))])
            if i + 1 < 4:
                zero_acc()
        while tasks:
            tasks.pop(0)()
        if h == 2:
            b_prefetch()
    for dq in deferred:
        dq[1]()
    PROJ_B[:] = [0, 1, 2, 3, 4, 5, 6, 7]
    S.barrier()

    AR.top = P_TOP
    lbl = AR.f32(2, 2, 4)
    lbv = AR.f32(3, 2, 4)
    tri = AR.f32(2, 64)
    smask = AR.f32(512)
    hgT = AR.f32(1)
    epsB = AR.f32(1)
    oneB = AR.f32(1)
    mhB = AR.f32(1)
    wB = AR.bf16(5, 8, 128)
    qg = [AR.bf16(NT), AR.bf16(NT)]
    kg = [AR.bf16(NT), AR.bf16(NT)]
    _off_kgt0 = AR.top
    kgtok = [AR.bf16(36, 128), AR.bf16(36, 128)]
    vtok = AR.bf16(36, 128)
    gtok = AR.bf16(32, 128)
    e1 = [AR.f32(36), AR.f32(36)]
    e2 = [AR.f32(36), AR.f32(36)]
    eL = [AR.f32(36), AR.f32(36)]
    Sring = [AR.f32(8, 128), AR.f32(8, 128)]
    Smid = [AR.bf16(32, 128), AR.bf16(32, 128)]
    Asb = AR.bf16(4, 2, 64)
    _off_t = AR.top
    bt_sig = [AR.f32(512), AR.f32(512)]
    bt_g = [AR.f32(512), AR.f32(512)]
    bt_kk = [AR.f32(512), AR.f32(512)]
    bt_G = [AR.f32(512), AR.f32(512)]
    kg2 = [bt_kk[0][:, 0:256].bitcast(BF16), bt_kk[1][:, 0:256].bitcast(BF16)]
    ot = [bt_sig[0].rearrange("p (j c) -> p j c", c=128), bt_sig[1].rearrange("p (j c) -> p j c", c=128)]
    bt_s8 = [AR.f32(2, 8), AR.f32(2, 8)]
    stage = [AR.bf16(512), AR.bf16(512)]
    ojunk = AR.f32(128)
    hb_ss = AR.f32(3, 32)
    hb_y = AR._view(AR.ap[:, _off_kgt0:_off_kgt0 + 2048].bitcast(BF16), (32, 128))
    d_qg = [[Dep("qg%d_%d" % (d, b)) for b in range(5)] for d in range(2)]
    d_kg = [[Dep("kg%d_%d" % (d, b)) for b in range(5)] for d in range(2)]
    d_kgt = [[Dep("kgt%d_%d" % (d, b)) for b in range(5)] for d in range(2)]
    d_vt = [Dep("vt%d" % i) for i in range(5)]
    d_gt = [Dep("gt%d" % i) for i in range(5)]
    d_e = [[Dep("e%d_%d" % (d, b)) for b in range(5)] for d in range(2)]
    d_Sr = [[Dep("Sr%d_%d" % (d, i)) for i in range(2)] for d in range(2)]
    d_Sm = [[Dep("Sm%d_%d" % (d, i)) for i in range(8)] for d in range(2)]
    d_As = Dep("As")
    d_bt = [{k: Dep("bt%d_%s" % (p_, k)) for k in ("sig", "g", "kk", "G", "s8")} for p_ in range(2)]
    d_stage = [Dep("stage0"), Dep("stage1")]
    d_ot = [Dep("ot0"), Dep("ot1")]
    d_hb = Dep("hb")
    d_hby = [Dep("hby%d" % i) for i in range(4)]

    S.op("pool", lambda: pool.memset(epsB, EPS), writes=[d_B])
    S.op("pool", lambda: pool.memset(oneB, 1.0), writes=[d_B])
    S.op("pool", lambda: pool.memset(mhB, -0.5), writes=[d_B])
    S.op("dve", lambda: dve.tensor_tensor(out=lbv[:, 1, :, :], in0=lbl[:, :, 1, :], in1=lbl[:, :, 0, :], op=ALU.subtract), reads=[d_B], writes=[d_B])
    S.op("act", lambda: act.activation(out=lbv[:, 0, :, :], in_=lbv[:, 1, :, :], func=AF.Exp), reads=[d_B], writes=[d_B])
    S.op("dve", lambda: dve.tensor_scalar(out=lbv[:, 0, :, :], in0=lbv[:, 0, :, :], scalar1=1.0, scalar2=None, op0=ALU.add), reads=[d_B], writes=[d_B])
    S.op("dve", lambda: dve.reciprocal(out=lbv[:, 0, :, :], in_=lbv[:, 0, :, :]), reads=[d_B], writes=[d_B])
    S.op("dve", lambda: dve.tensor_scalar(out=lbv[:, 1, :, :], in0=lbv[:, 0, :, :], scalar1=-1.0, scalar2=1.0, op0=ALU.mult, op1=ALU.add), reads=[d_B], writes=[d_B])
    S.op("dve", lambda: dve.tensor_scalar(out=lbv[:, 2, :, :], in0=lbv[:, 0, :, :], scalar1=-1.0, scalar2=None, op0=ALU.add), reads=[d_B], writes=[d_B])

    BLKS = [(0, 512), (512, 512), (1024, 512), (1536, 512), (2048, 256)]
    nb8 = [0]

    def nextbank8():
        b = nb8[0] % 7
        nb8[0] += 1
        return b

    d_junkps = Dep("junkps")

    def pe_keepwarm(n):
        for _ in range(n):
            S.op("pe", lambda: pe.matmul(bank(7), lhsT=zerosB[:, 0:128], rhs=zerosB, start=True, stop=True, skip_group_check=True), reads=[d_const], writes=[PB[7]], inc=False)

    bd = [0]
    for h in range(4):
        def vg_item(which, grp, bi, t0, nt, sp_):
            nch = nt // 64
            c0 = t0 // 64
            hdeps = d_hT[t0 // 128:(t0 + nt) // 128]
            st = {}

            def s1():
                bv = nextbank8()
                for k in range(8):
                    S.op("pe", lambda k=k: pe.matmul(bank(bv)[:, 0:nt], lhsT=wB[:, grp, k, :], rhs=hT[:, k, t0:t0 + nt], start=(k == 0), stop=(k == 7)),
                         reads=[d_wB] + hdeps, writes=[PB[bv]], inc=(k == 7))
                if which == 0:
                    S.op("act", lambda: act.copy(out=stage[sp_][:, 0:nt], in_=bank(bv)[:, 0:nt]), reads=[PB[bv]], writes=[d_stage[sp_]])
                else:
                    S.op("act", lambda: act.activation(out=stage[sp_][:, 0:nt], in_=bank(bv)[:, 0:nt], func=AF.Silu), reads=[PB[bv]], writes=[d_stage[sp_]])

            def s2():
                bt_ = nextbank8()
                pk = bank_bf(bt_).rearrange("p (j c) -> p j c", c=128)
                for cc in range(nch):
                    S.op("pe", lambda cc=cc: pe.transpose(out=pk[0:64, cc, :], in_=stage[sp_][:, cc * 64:(cc + 1) * 64], identity=identB),
                         reads=[d_stage[sp_], d_const], writes=[PB[bt_]], inc=(cc == nch - 1))
                if which == 0:
                    S.op("dve", lambda: dve.tensor_copy(out=vtok[0:64, c0:c0 + nch, :], in_=pk[0:64, 0:nch, :]), reads=[PB[bt_]], writes=[d_vt[bi]])
                else:
                    lo = 4 if bi == 0 else 0
                    S.op("dve", lambda: dve.tensor_copy(out=gtok[0:64, c0 + lo - 4:c0 + nch - 4, :], in_=pk[0:64, lo:nch, :]), reads=[PB[bt_]], writes=[d_gt[bi]])
            return [s1, s2]

        items = []
        for which, grp in ((0, 1), (1, 4)):
            for bi, (t0, nt) in enumerate(BLKS):
                items.append(vg_item(which, grp, bi, t0, nt, bd[0] % 2))
                bd[0] += 1
        skew(items, newest_first=True)

        qbank = {}

        def gd_item(bi, t0, nt, d, p_):
            nch = nt // 64
            c0 = t0 // 64
            hdeps = d_hT[t0 // 128:(t0 + nt) // 128]
            T_sig, T_g, T_kk, T_G, T_s8, T_kg2 = bt_sig[p_], bt_g[p_], bt_kk[p_], bt_G[p_], bt_s8[p_], kg2[p_]
            D_ = d_bt[p_]
            G3 = T_G[:, 0:nt].rearrange("p (c l) -> p c l", l=64)
            eR = e1[d] if d == 0 else e2[d]
            eD = e2[d] if d == 0 else e1[d]
            sa = 1.0 if d == 0 else -1.0

            def s1():
                if d == 0:
                    bq = nextbank8()
                    qbank[bi] = bq
                    for k in range(8):
                        S.op("pe", lambda k=k: pe.matmul(bank(bq)[:, 0:nt], lhsT=wB[:, 0, k, :], rhs=hT[:, k, t0:t0 + nt], start=(k == 0), stop=(k == 7)),
                             reads=[d_wB] + hdeps, writes=[PB[bq]], inc=(k == 7))
                bf = nextbank8()
                for k in range(8):
                    S.op("pe", lambda k=k: pe.matmul(bank(bf)[:, 0:nt], lhsT=wB[:, 2 + d, k, :], rhs=hT[:, k, t0:t0 + nt], start=(k == 0), stop=(k == 7)),
                         reads=[d_wB] + hdeps, writes=[PB[bf]], inc=(k == 7))
                pe_keepwarm(8)
                S.op("act", lambda: act.activation(out=T_sig[:, 0:nt], in_=bank(bf)[:, 0:nt], func=AF.Exp, scale=-1.0), reads=[PB[bf]], writes=[D_["sig"]])
                S.op("act", lambda: act.activation(out=T_sig[:, 0:nt], in_=T_sig[:, 0:nt], func=AF.Ln, bias=oneB[:, 0:1]), reads=[d_B], writes=[D_["sig"]])
                S.op("act", lambda: act.activation(out=T_sig[:, 0:nt], in_=T_sig[:, 0:nt], func=AF.Exp, scale=-1.0), reads=[], writes=[D_["sig"]])
                S.op("act", lambda: act.activation(out=T_g[:, 0:nt], in_=T_sig[:, 0:nt], func=AF.Ln, scale=lbv[:, 1, d, h:h + 1], bias=lbv[:, 0, d, h:h + 1]),
                     reads=[D_["sig"], d_B], writes=[D_["g"]])
                S.op("dve", lambda: dve.tensor_scalar(out=T_kk[:, 0:nt], in0=T_sig[:, 0:nt], scalar1=lbv[:, 2, d, h:h + 1], scalar2=lbv[:, 1, d, h:h + 1], op0=ALU.mult, op1=ALU.add),
                     reads=[D_["sig"], d_B], writes=[D_["kk"]])
                S.op("dve", lambda: dve.tensor_tensor_scan(out=T_G[:, 0:nt], data0=smask[:, 0:nt], data1=T_g[:, 0:nt], initial=0.0, op0=ALU.mult, op1=ALU.add),
                     reads=[D_["g"], d_B], writes=[D_["G"]])

            def s2():
                bq = qbank[bi]
                S.op("dve", lambda: dve.tensor_copy(out=T_s8[:, 0, 0:nch].unsqueeze(2), in_=G3[:, :, 31:32]), reads=[D_["G"]], writes=[D_["s8"]])
                S.op("dve", lambda: dve.tensor_tensor(out=T_s8[:, 1, 0:nch].unsqueeze(2), in0=G3[:, :, 63:64], in1=G3[:, :, 31:32], op=ALU.subtract), reads=[D_["G"]], writes=[D_["s8"]])
                S.op("act", lambda: act.activation(out=eR[:, c0:c0 + nch], in_=T_s8[:, 0, 0:nch], func=AF.Exp), reads=[D_["s8"]], writes=[d_e[d][bi]])
                S.op("act", lambda: act.activation(out=eL[d][:, c0:c0 + nch].unsqueeze(2), in_=G3[:, :, 63:64], func=AF.Exp), reads=[D_["G"]], writes=[d_e[d][bi]])
                S.op("act", lambda: act.activation(out=eD[:, c0:c0 + nch], in_=T_s8[:, 1, 0:nch], func=AF.Exp), reads=[D_["s8"]], writes=[d_e[d][bi]])
                S.op("dve", lambda: dve.tensor_tensor(out=G3, in0=G3, in1=T_s8[:, 0, 0:nch].unsqueeze(2).to_broadcast([128, nch, 64]), op=ALU.subtract), reads=[D_["s8"]], writes=[D_["G"]])
                if d == 1:
                    S.op("dve", lambda: dve.tensor_tensor(out=T_G[:, 0:nt], in0=T_G[:, 0:nt], in1=T_g[:, 0:nt], op=ALU.subtract), reads=[D_["g"]], writes=[D_["G"]])
                S.op("act", lambda: act.activation(out=T_sig[:, 0:nt], in_=T_G[:, 0:nt], func=AF.Exp, scale=sa), reads=[D_["G"]], writes=[D_["sig"]])
                S.op("act", lambda: act.activation(out=T_g[:, 0:nt], in_=T_G[:, 0:nt], func=AF.Exp, scale=-sa), reads=[D_["G"]], writes=[D_["g"]])
                S.op("dve", lambda: dve.tensor_tensor(out=qg[d][:, t0:t0 + nt], in0=bank(bq)[:, 0:nt], in1=T_sig[:, 0:nt], op=ALU.mult), reads=[PB[bq], D_["sig"]], writes=[d_qg[d][bi]])
                S.op("pool", lambda: pool.tensor_tensor(out=T_g[:, 0:nt], in0=T_kk[:, 0:nt], in1=T_g[:, 0:nt], op=ALU.mult), reads=[D_["kk"]], writes=[D_["g"]])
                S.op("act", lambda: act.copy(out=kg[d][:, t0:t0 + nt], in_=T_g[:, 0:nt]), reads=[D_["g"]], writes=[d_kg[d][bi]])
                S.op("dve", lambda: dve.tensor_tensor(out=T_kg2[:, 0:nt].rearrange("p (c l) -> p c l", l=64), in0=T_g[:, 0:nt].rearrange("p (c l) -> p c l", l=64),
                                                      in1=e2[d][:, c0:c0 + nch].unsqueeze(2).to_broadcast([128, nch, 64]), op=ALU.mult),
                     reads=[D_["g"], d_e[d][bi]], writes=[D_["kk"]])

            def s3():
                bt_ = nextbank8()
                pk = bank_bf(bt_).rearrange("p (j c) -> p j c", c=128)
                for cc in range(nch):
                    S.op("pe", lambda cc=cc: pe.transpose(out=pk[0:64, cc, :], in_=T_kg2[:, cc * 64:(cc + 1) * 64], identity=identB),
                         reads=[D_["kk"], d_const], writes=[PB[bt_]], inc=(cc == nch - 1))
                S.op("act", lambda: act.copy(out=kgtok[d][0:64, c0:c0 + nch, :], in_=pk[0:64, 0:nch, :]), reads=[PB[bt_]], writes=[d_kgt[d][bi]])
            return [s1, s2, s3]

        items = []
        for bi, (t0, nt) in enumerate(BLKS):
            for d in range(2):
                items.append(gd_item(bi, t0, nt, d, bd[0] % 2))
                bd[0] += 1
        skew(items)
        def slot_of(c):
            return c % 8

        for d in range(2):
            first = 0 if d == 0 else 3
            S.op("pool", lambda d=d, first=first: pool.memset(Sring[d][:, slot_of(first), :], 0.0), writes=[d_Sr[d][slot_of(first) // 4]])
        order = [list(range(36)), [3, 2, 1, 0] + list(range(35, 3, -1))]
        for g in range(9):
            for d in range(2):
                bu = (0, 1)[g % 2] if d == 0 else (2, 3)[g % 2]
                pu = bank(bu).rearrange("p (j c) -> p j c", c=128)
                cs = order[d][g * 4:g * 4 + 4]
                for jj, c in enumerate(cs):
                    S.op("pe", lambda jj=jj, c=c: pe.matmul(pu[:, jj, :], lhsT=kgtok[d][0:64, c, :], rhs=vtok[0:64, c, :], start=True, stop=True),
                         reads=[d_kgt[d][c // 8], d_vt[c // 8]], writes=[PB[bu]], inc=(jj == 3))
                for jj, c in enumerate(cs):
                    bi = c // 8
                    if d == 0:
                        cn = c + 1
                    else:
                        cn = 35 if c == 0 else c - 1
                    if (d == 0 and c == 35) or (d == 1 and c == 4):
                        continue
                    sp_, sn_ = slot_of(c), slot_of(cn)
                    S.op("dve", lambda c=c, sp_=sp_, sn_=sn_, jj=jj: dve.scalar_tensor_tensor(out=Sring[d][:, sn_, :], in0=Sring[d][:, sp_, :], scalar=eL[d][:, c:c + 1], in1=pu[:, jj, :], op0=ALU.mult, op1=ALU.add),
                         reads=[d_Sr[d][sp_ // 4], d_e[d][bi], PB[bu]], writes=[d_Sr[d][sn_ // 4]])
                if d == 0:
                    c0 = g * 4
                else:
                    c0 = (36 - g * 4) if g >= 1 else None
                    if c0 is not None and c0 > 32:
                        c0 = None
                if c0 is not None and c0 >= 4 and c0 + 3 <= 35:
                    sl0 = slot_of(c0)
                    S.op("pool", lambda c0=c0, sl0=sl0: pool.tensor_tensor(out=Smid[d][:, c0 - 4:c0, :], in0=Sring[d][:, sl0:sl0 + 4, :],
                                                                         in1=e1[d][:, c0:c0 + 4].unsqueeze(2).to_broadcast([128, 4, 128]), op=ALU.mult),
                         reads=[d_Sr[d][sl0 // 4], d_e[d][c0 // 8]], writes=[d_Sm[d][(c0 - 4) // 4]])
        S.op("pool", lambda: pool.tensor_tensor(out=Smid[1][:, 0:4, :], in0=Sring[1][:, 4:8, :], in1=e1[1][:, 4:8].unsqueeze(2).to_broadcast([128, 4, 128]), op=ALU.mult),
             reads=[d_Sr[1][1], d_e[1][0]], writes=[d_Sm[1][0]])
        if h < 3:
            load_wB(h + 1)
        tri4 = tri[0:64, :, :].unsqueeze(1).to_broadcast([64, 4, 2, 64])

        def o_s1(g):
            ba, bo = 4 + g % 2, 6 + g % 2
            pa = bank(ba)[0:64, :].rearrange("p (j d c) -> p j d c", d=2, c=64)
            po = bank(bo)[0:64, :].rearrange("p (j c) -> p j c", c=128)
            os_ = g % 2
            for jj in range(4):
                c = 4 + g * 4 + jj
                bi = c // 8
                for d in range(2):
                    S.op("pe", lambda jj=jj, c=c, d=d: pe.matmul(pa[:, jj, d, :], lhsT=kg[d][:, c * 64:(c + 1) * 64], rhs=qg[d][:, c * 64:(c + 1) * 64], start=True, stop=True, skip_group_check=True),
                         reads=[d_kg[d][bi], d_qg[d][bi]], writes=[PB[ba]], inc=(jj == 3 and d == 1))
            S.op("dve", lambda: dve.tensor_tensor(out=Asb[0:64, :, :, :], in0=pa, in1=tri4, op=ALU.mult), reads=[PB[ba], d_B], writes=[d_As])

        def o_s1b(g):
            ba, bo = 4 + g % 2, 6 + g % 2
            po = bank(bo)[0:64, :].rearrange("p (j c) -> p j c", c=128)
            os_ = g % 2
            first = True
            for jj in range(4):
                c = 4 + g * 4 + jj
                for d in range(2):
                    S.op("pe", lambda jj=jj, c=c, d=d, first=first: pe.matmul(po[:, jj, :], lhsT=Asb[0:64, jj, d, :], rhs=vtok[0:64, c, :], start=first, stop=False, skip_group_check=True),
                         reads=[d_As, d_vt[c // 8]], writes=[PB[bo]], inc=False)
                    first = False
            for jj in range(4):
                c = 4 + g * 4 + jj
                bi = c // 8
                for d in range(2):
                    last = (jj == 3 and d == 1)
                    S.op("pe", lambda jj=jj, c=c, d=d, last=last: pe.matmul(po[:, jj, :], lhsT=qg[d][:, c * 64:(c + 1) * 64], rhs=Smid[d][:, c - 4, :], start=False, stop=last, skip_group_check=True),
                         reads=[d_qg[d][bi], d_Sm[d][(c - 4) // 4]], writes=[PB[bo]], inc=last)
            S.op("act", lambda: act.copy(out=ot[os_][0:64, :, :], in_=po), reads=[PB[bo]], writes=[d_ot[os_], d_bt[os_]["sig"]])

        def o_s2(g):
            os_ = g % 2
            od_ = [d_ot[os_], d_bt[os_]["sig"]]
            for jj in range(4):
                cl = g * 4 + jj
                S.op("dve", lambda jj=jj, cl=cl: dve.scalar_tensor_tensor(out=ojunk[0:64, :], in0=ot[os_][0:64, jj, :], scalar=1.0, in1=ot[os_][0:64, jj, :], op0=ALU.mult, op1=ALU.mult, accum_out=hb_ss[0:64, 0, cl:cl + 1]),
                     reads=od_, writes=[d_hb])
            S.op("dve", lambda: dve.tensor_scalar(out=hb_ss[0:64, 1, g * 4:g * 4 + 4], in0=hb_ss[0:64, 0, g * 4:g * 4 + 4], scalar1=1.0 / 128, scalar2=EPS, op0=ALU.mult, op1=ALU.add), reads=[d_hb], writes=[d_hb])
            S.op("pool", lambda: pool.tensor_tensor(out=hb_ss[0:64, 2, g * 4:g * 4 + 4], in0=hb_ss[0:64, 1, g * 4:g * 4 + 4], in1=mhB[0:64, 0:1].to_broadcast([64, 4]), op=ALU.pow), reads=[d_hb, d_B], writes=[d_hb])
            for jj in range(4):
                cl = g * 4 + jj
                S.op("dve", lambda jj=jj, cl=cl: dve.scalar_tensor_tensor(out=hb_y[0:64, cl, :], in0=ot[os_][0:64, jj, :], scalar=hb_ss[0:64, 2, cl:cl + 1], in1=gtok[0:64, cl, :], op0=ALU.mult, op1=ALU.mult),
                     reads=od_ + [d_hb, d_gt[(cl + 4) // 8]], writes=[d_hby[g // 2]] + d_kgt[0])

        def o_s2b(g):
            if g % 2 == 1:
                grp = g // 2
                bt_ = grp % 4
                py = bank_bf(bt_)[:, 0:512].rearrange("p (j c) -> p j c", c=64)
                for jj in range(8):
                    c = grp * 8 + jj
                    S.op("pe", lambda jj=jj, c=c: pe.transpose(out=py[:, jj, :], in_=hb_y[0:64, c, :], identity=identB[0:64, 0:64]), reads=[d_hby[grp], d_const] + d_kgt[0], writes=[PB[bt_]], inc=(jj == 7))
                S.op("act", lambda: act.activation(out=yT[:, 4 + h, grp * 512:(grp + 1) * 512], in_=bank_bf(bt_)[:, 0:512], func=AF.Copy, scale=hgT[:, 0:1]), reads=[PB[bt_], d_B], writes=d_yT[4 + h][grp * 4:grp * 4 + 4])

        for g in range(10):
            if g < 8:
                o_s1(g)
            if 1 <= g <= 8:
                o_s2(g - 1)
            if g < 8:
                o_s1b(g)
            if 2 <= g <= 9:
                o_s2b(g - 2)
    if debug:
        S.dma("sp", lambda: sp.dma_start(out=dbg["yT0"], in_=yT), reads=[x for r in d_yT for x in r], writes=[], semdep=d_yT[0][0])
    S.barrier()

    X_OFF = P_TOP
    AR.top = X_OFF
    xnew = AR.f32(16, D)
    T2 = AR.top
    d_xnew = [Dep("xnew%d" % i) for i in range(16)]

    def out_phase(l, w_d, pre=None):
        AR.top = T2
        if pre is None:
            wo = AR.bf16(8, D)
            stg = [AR.f32(D) for _ in range(4)]
        xt = [AR.f32(D), AR.f32(D)]
        gtmp = [AR.f32(512), AR.f32(512)]
        d_gtmp = [Dep("gtmp0"), Dep("gtmp1")]
        d_wo, d_stg, d_xt = [Dep("wo%d" % k_) for k_ in range(8)], [Dep("stg%d" % k_) for k_ in range(4)], [Dep("oxt0"), Dep("oxt1")]
        def ld(k):
            S.dma("sp", lambda: sp.dma_start(out=stg[k % 4], in_=w_d[k * 128:(k + 1) * 128, :]), writes=[d_stg[k % 4]], semdep=d_stg[k % 4])

        if pre is None:
            for k in range(4):
                ld(k)
            for k in range(8):
                s = k % 4
                S.op("dve", lambda k=k, s=s: dve.tensor_tensor(out=wo[:, k, :], in0=stg[s], in1=gate_bc[l], op=ALU.mult), reads=[d_stg[s], d_gate[l]], writes=[d_wo[k]])
                if k + 4 < 8:
                    ld(k + 4)
        else:
            wo, d_wo = pre
        for i in range(16):
            s = i % 2
            if l == 0:
                S.dma("sp", lambda i=i, s=s: sp.dma_start(out=xt[s], in_=x_d[i * 128:(i + 1) * 128, :]), writes=[d_xt[s]], semdep=d_xt[s])
            for n in range(2):
                bp = nextbank()
                for k in range(8):
                    S.op("pe", lambda k=k, n=n, i=i: pe.matmul(bank(bp), lhsT=yT[:, k, i * 128:(i + 1) * 128], rhs=wo[:, k, n * 512:(n + 1) * 512], start=(k == 0), stop=(k == 7)),
                         reads=[d_wo[k], d_yT[k][i]], writes=[PB[bp]], inc=(k == 7))
                if l == 0:
                    S.op("dve", lambda n=n, i=i, s=s: dve.tensor_tensor(out=xnew[:, i, n * 512:(n + 1) * 512], in0=bank(bp), in1=xt[s][:, n * 512:(n + 1) * 512], op=ALU.add),
                         reads=[PB[bp], d_xt[s]], writes=[d_xnew[i]])
                elif pre is None:
                    S.op("dve", lambda n=n, i=i, s=s: dve.tensor_tensor(out=xt[s][:, n * 512:(n + 1) * 512], in0=bank(bp), in1=xnew[:, i, n * 512:(n + 1) * 512], op=ALU.add),
                         reads=[PB[bp], d_xnew[i]], writes=[d_xt[s]])
                else:
                    S.op("dve", lambda n=n: dve.tensor_tensor(out=gtmp[n], in0=bank(bp), in1=gate_bc[l][:, n * 512:(n + 1) * 512], op=ALU.mult),
                         reads=[PB[bp], d_gate[l]], writes=[d_gtmp[n]])
                    S.op("pool", lambda n=n, i=i, s=s: pool.tensor_tensor(out=xt[s][:, n * 512:(n + 1) * 512], in0=gtmp[n], in1=xnew[:, i, n * 512:(n + 1) * 512], op=ALU.add),
                         reads=[d_gtmp[n], d_xnew[i]], writes=[d_xt[s]])
            if l == 1:
                S.dma("sp", lambda i=i, s=s: sp.dma_start(out=out_d[i * 128:(i + 1) * 128, :], in_=xt[s]), reads=[d_xt[s]], writes=[], semdep=d_xt[s])
        return d_xt

    out_phase(0, ewout_d)
    if debug:
        S.dma("sp", lambda: sp.dma_start(out=dbg["xnew"], in_=xnew), reads=d_xnew, writes=[], semdep=d_xnew[0])
    S.barrier()

    def src1(i):
        return xnew[:, i, :], False, d_xnew[i]

    AR.top = T2
    wC = AR.bf16(3, 8, 512)
    L1_TMP = AR.top
    AR.top = L1_TMP + 6000
    wD = [AR.bf16(4, 8, 128), AR.bf16(4, 8, 128)]
    L1_END = AR.top
    d_wC = Dep("wC")
    dd_w = [Dep("wD0"), Dep("wD1")]

    def load_wD(j):
        for g in range(4):
            S.dma("pool", lambda g=g: pool.dma_start(
                out=wD[j % 2][:, g, :, :], in_=owin_d.rearrange("(k p) n -> p k n", p=128)[:, :, 1536 + g * 512 + j * 128:1536 + g * 512 + (j + 1) * 128]),
                writes=[dd_w[j % 2]], semdep=dd_w[j % 2])

    for g in range(3):
        for kk in range(2):
            S.dma("pool", lambda g=g, kk=kk: pool.dma_start(out=wC[:, g, kk * 4:(kk + 1) * 4, :], in_=owin_d.rearrange("(k p) n -> p k n", p=128)[:, kk * 4:(kk + 1) * 4, g * 512:(g + 1) * 512]),
                  writes=[d_wC], semdep=d_wC)
    load_wD(0)
    load_wD(1)
    norm_phase(1, 16, src1, L1_TMP)
    S.barrier()

    AR.top = L1_TMP
    wsF = AR.f32(4, 128)
    wsT = AR.bf16(4, 128)
    bsT = AR.f32(4)
    vgS = AR.f32(512)
    mhalf = AR.f32(1)
    c_gu = [AR.f32(512), AR.f32(512)]
    c_sg = [AR.f32(512), AR.f32(512)]
    c_gv = [AR.f32(512), AR.f32(512)]
    c_vn = [AR.bf16(512), AR.bf16(512)]
    c_y = [AR.bf16(512), AR.bf16(512)]
    c_junk = AR.f32(512)
    c_st = AR.f32(16, 4)
    assert AR.top <= L1_TMP + 6000, AR.top - L1_TMP
    d_C = Dep("Cconst")
    d_c = [{k: Dep("c%d_%s" % (p_, k)) for k in ("gu", "sg", "gv", "vn", "y")} for p_ in range(2)]
    d_cj, d_cst = Dep("c_junk"), [Dep("c_st%d" % i) for i in range(16)]
    S.dma("sp", lambda: sp.dma_start(out=wsF, in_=ws_d.rearrange("g t s -> t g s")), writes=[d_C], semdep=d_C)
    S.dma("sp", lambda: sp.dma_start(out=bsT, in_=bs_d.rearrange("g t -> t g"), allow_slow_non_contiguous=True), writes=[d_C], semdep=d_C)
    S.dma("sp", lambda: sp.dma_start(out=vgS, in_=vg_d.partition_broadcast(128)), writes=[d_C], semdep=d_C)
    S.op("pool", lambda: pool.memset(mhalf, -0.5), writes=[d_C])
    bw = nextbank8()
    pw = bank(bw).rearrange("p (g c) -> p g c", c=128)
    for g in range(4):
        S.op("pe", lambda g=g: pe.transpose(out=pw[:, g, :], in_=wsF[:, g, :], identity=identF), reads=[d_C, d_const], writes=[PB[bw]], inc=(g == 3))
    S.op("dve", lambda: dve.tensor_copy(out=wsT, in_=pw), reads=[PB[bw]], writes=[d_C])

    def c_item(i):
        p_ = i % 2
        Dc = d_c[p_]
        gu, sg, gv, vn, yy = c_gu[p_], c_sg[p_], c_gv[p_], c_vn[p_], c_y[p_]
        st = {}

        def s1():
            bu, bv, bg = nextbank8(), nextbank8(), nextbank8()
            st["bg"] = bg
            for g, bb in ((0, bu), (1, bv), (2, bg)):
                for k in range(8):
                    S.op("pe", lambda k=k, g=g, bb=bb: pe.matmul(bank(bb), lhsT=hT[:, k, i * 128:(i + 1) * 128], rhs=wC[:, g, k, :], start=(k == 0), stop=(k == 7)),
                         reads=[d_wC, d_hT[i]], writes=[PB[bb]], inc=(k == 7))
            S.op("act", lambda: act.activation(out=gu, in_=bank(bu), func=AF.Gelu), reads=[PB[bu]], writes=[Dc["gu"]])
            S.op("act", lambda: act.activation(out=gv, in_=bank(bv), func=AF.Gelu), reads=[PB[bv]], writes=[Dc["gv"]])
            S.op("act", lambda: act.activation(out=sg, in_=bank(bg), func=AF.Tanh, scale=0.5), reads=[PB[bg]], writes=[Dc["sg"]])
            S.op("dve", lambda: dve.tensor_scalar(out=sg, in0=sg, scalar1=0.5, scalar2=0.5, op0=ALU.mult, op1=ALU.add), reads=[], writes=[Dc["sg"]])
            S.op("dve", lambda: dve.tensor_tensor(out=sg, in0=bank(bg), in1=sg, op=ALU.mult), reads=[PB[bg]], writes=[Dc["sg"]])

        def s2():
            S.op("pool", lambda: pool.tensor_tensor(out=gu, in0=gu, in1=sg, op=ALU.mult), reads=[Dc["sg"]], writes=[Dc["gu"]])
            S.op("dve", lambda: dve.scalar_tensor_tensor(out=c_junk, in0=gv, scalar=1.0, in1=gv, op0=ALU.mult, op1=ALU.mult, accum_out=c_st[:, i, 0:1]), reads=[Dc["gv"]], writes=[d_cj, d_cst[i]])
            S.op("dve", lambda: dve.tensor_scalar(out=c_st[:, i, 1:2], in0=c_st[:, i, 0:1], scalar1=1.0 / 512, scalar2=EPS, op0=ALU.mult, op1=ALU.add), reads=[], writes=[d_cst[i]])
            S.op("pool", lambda: pool.tensor_tensor(out=c_st[:, i, 2:3], in0=c_st[:, i, 1:2], in1=mhalf, op=ALU.pow), reads=[d_C], writes=[d_cst[i]])
            S.op("dve", lambda: dve.scalar_tensor_tensor(out=vn, in0=gv, scalar=c_st[:, i, 2:3], in1=vgS, op0=ALU.mult, op1=ALU.mult), reads=[Dc["gv"], d_cst[i], d_C], writes=[Dc["vn"]])

        def s3():
            bs_ = nextbank8()
            ps = bank(bs_).rearrange("p (g c) -> p g c", c=128)
            for g in range(4):
                S.op("pe", lambda g=g: pe.matmul(ps[:, g, :], lhsT=wsT[:, g, :], rhs=vn[:, g * 128:(g + 1) * 128], start=True, stop=True), reads=[d_C, Dc["vn"]], writes=[PB[bs_]], inc=(g == 3))
            for g in range(4):
                S.op("dve", lambda g=g: dve.scalar_tensor_tensor(out=yy[:, g * 128:(g + 1) * 128], in0=ps[:, g, :], scalar=bsT[:, g:g + 1], in1=gu[:, g * 128:(g + 1) * 128], op0=ALU.add, op1=ALU.mult),
                     reads=[PB[bs_], d_C, Dc["gu"]], writes=[Dc["y"]])

        def s3b():
            bt_ = nextbank8()
            py = bank_bf(bt_)[:, 0:512].rearrange("p (g c) -> p g c", c=128)
            for g in range(4):
                S.op("pe", lambda g=g: pe.transpose(out=py[:, g, :], in_=yy[:, g * 128:(g + 1) * 128], identity=identB), reads=[Dc["y"], d_const], writes=[PB[bt_]], inc=(g == 3))
            S.op("act", lambda: act.copy(out=yT[:, 0:4, i * 128:(i + 1) * 128], in_=py), reads=[PB[bt_]], writes=[d_yT[g][i] for g in range(4)])
        return [s1, s2, s3, s3b]

    c_items = [c_item(i) for i in range(16)]
    for t in range(16 + 2):
        if 0 <= t - 2 < 16:
            c_items[t - 2][2]()
        if 0 <= t - 1 < 16:
            c_items[t - 1][1]()
        if t < 16:
            c_items[t][0]()
        if 0 <= t - 2 < 16:
            c_items[t - 2][3]()
    S.barrier()

    AR.top = T2
    cwT = AR.f32(4, 3)
    zb = AR.f32(SEQ + 2)
    bsg = AR.f32(SEQ)
    cvt = AR.f32(SEQ)
    d_csb = [AR.f32(512), AR.f32(512)]
    _sgd = AR.f32(512)
    d_sgd = [_sgd, _sgd]
    wo1 = AR.bf16(8, D)
    assert AR.top <= L1_TMP + 6000, (AR.top, L1_TMP)
    dd_c = Dep("cw")
    dd_z = [Dep("z%d" % b_) for b_ in range(4)]
    dd_bsg = [Dep("bsg%d" % b_) for b_ in range(4)]
    dd_cv = [Dep("cvt%d" % b_) for b_ in range(4)]
    _dsgd = Dep("sgd")
    dd_csb, dd_sgd = [Dep("csb0"), Dep("csb1")], [_dsgd, _dsgd]
    d_wo1 = [Dep("wo1_%d" % k_) for k_ in range(8)]
    for kk_ in range(2):
        S.dma("pool", lambda kk_=kk_: pool.dma_start(out=wo1[:, kk_ * 4:(kk_ + 1) * 4, :], in_=owout_d.rearrange("(k p) n -> p k n", p=128)[:, kk_ * 4:(kk_ + 1) * 4, :]),
              writes=d_wo1[kk_ * 4:(kk_ + 1) * 4], semdep=d_wo1[kk_ * 4])
    for w_ in range(3):
        S.dma("sp", lambda w_=w_: sp.dma_start(out=cwT[:, :, w_], in_=cw_d[w_].rearrange("(j p) -> p j", p=128), allow_slow_non_contiguous=True), writes=[dd_c], semdep=dd_c)
    S.op("pool", lambda: pool.memset(zb, 0.0), writes=dd_z)
    for j in range(4):
        slot = j % 2

        def d_item(b, p_):
            def s1():
                b1, b2 = nextbank8(), nextbank8()
                for g, bk in ((1, b1), (2, b2)):
                    for k in range(8):
                        S.op("pe", lambda k=k, g=g, bk=bk: pe.matmul(bank(bk), lhsT=wD[slot][:, g, k, :], rhs=hT[:, k, b * 512:(b + 1) * 512], start=(k == 0), stop=(k == 7)),
                             reads=[dd_w[slot]] + d_hT[b * 4:b * 4 + 4], writes=[PB[bk]], inc=(k == 7))
                S.op("act", lambda: act.copy(out=d_csb[p_], in_=bank(b1)), reads=[PB[b1]], writes=[dd_csb[p_]])
                S.op("dve", lambda: dve.tensor_tensor(out=zb[:, 1 + b * 512:1 + (b + 1) * 512], in0=bank(b2), in1=d_csb[p_], op=ALU.mult), reads=[PB[b2], dd_csb[p_]], writes=[dd_z[b]])

            def s2():
                b3, b4 = nextbank8(), nextbank8()
                for g, bk in ((3, b3), (0, b4)):
                    for k in range(8):
                        S.op("pe", lambda k=k, g=g, bk=bk: pe.matmul(bank(bk), lhsT=wD[slot][:, g, k, :], rhs=hT[:, k, b * 512:(b + 1) * 512], start=(k == 0), stop=(k == 7)),
                             reads=[dd_w[slot]] + d_hT[b * 4:b * 4 + 4], writes=[PB[bk]], inc=(k == 7))
                S.op("act", lambda: act.activation(out=d_sgd[p_], in_=bank(b3), func=AF.Silu), reads=[PB[b3]], writes=[dd_sgd[p_]])
                S.op("dve", lambda: dve.tensor_tensor(out=bsg[:, b * 512:(b + 1) * 512], in0=bank(b4), in1=d_sgd[p_], op=ALU.mult), reads=[PB[b4], dd_sgd[p_]], writes=[dd_bsg[b]])
            return [s1, s2]

        d_items = [d_item(b, b % 2) for b in range(4)]

        def conv_blk(b, j=j):
            lo, hi = b * 512, (b + 1) * 512
            zdeps = dd_z[max(b - 1, 0):min(b + 2, 4)]
            S.op("act", lambda: act.activation(out=cvt[:, lo:hi], in_=zb[:, lo:hi], func=AF.Copy, scale=cwT[:, j, 0:1]), reads=zdeps + [dd_c], writes=[dd_cv[b]])
            S.op("dve", lambda: dve.scalar_tensor_tensor(out=cvt[:, lo:hi], in0=zb[:, lo + 1:hi + 1], scalar=cwT[:, j, 1:2], in1=cvt[:, lo:hi], op0=ALU.mult, op1=ALU.add), reads=zdeps + [dd_c], writes=[dd_cv[b]])
            S.op("dve", lambda: dve.scalar_tensor_tensor(out=cvt[:, lo:hi], in0=zb[:, lo + 2:hi + 2], scalar=cwT[:, j, 2:3], in1=cvt[:, lo:hi], op0=ALU.mult, op1=ALU.add), reads=zdeps + [dd_c], writes=[dd_cv[b]])
            S.op("pool", lambda: pool.tensor_tensor(out=yT[:, 4 + j, lo:hi], in0=cvt[:, lo:hi], in1=bsg[:, lo:hi], op=ALU.mult), reads=[dd_cv[b], dd_bsg[b]], writes=d_yT[4 + j][b * 4:b * 4 + 4])

        for t in range(5):
            if t >= 1:
                d_items[t - 1][1]()
            if t < 4:
                d_items[t][0]()
            if t == 4 and j + 2 < 4:
                load_wD(j + 2)
            if t >= 1:
                conv_blk(t - 1)
    if debug:
        S.dma("sp", lambda: sp.dma_start(out=dbg["yT1"], in_=yT), reads=[x for r in d_yT for x in r], writes=[], semdep=d_yT[0][0])
    S.barrier()

    d_fin = out_phase(1, owout_d, pre=(wo1, d_wo1))
    S.barrier()
    return nc


_CONST = {}


def _consts():
    if _CONST:
        return _CONST
    f32 = np.float32
    ident = np.eye(128, dtype=f32)
    perm = np.zeros((128, 128), f32)
    for m in range(128):
        partner = m + 32 if (m % 64) < 32 else m - 32
        perm[partner, m] = 1.0
    bones = np.zeros((128, 128), f32)
    bones[0:64, 0:64] = 1.0
    bones[64:128, 64:128] = 1.0
    rows = SEQ // 64
    row = np.repeat(np.arange(rows, dtype=f32), 64)
    col = np.tile(np.arange(64, dtype=f32), rows)
    n_freq = 16
    inv = (f32(10000.0) ** (-np.arange(n_freq, dtype=f32) / f32(n_freq))).astype(f32)
    ang = np.concatenate([row[:, None] * inv, col[:, None] * inv], axis=-1).astype(f32)
    cos = np.cos(ang).astype(f32).T
    sin = np.sin(ang).astype(f32).T
    cosT = np.concatenate([cos, cos, cos, cos], axis=0)
    sinT = np.concatenate([-sin, sin, -sin, sin], axis=0)
    tri = np.zeros((64, 2, 64), f32)
    s_idx = np.arange(64)[:, None]
    t_idx = np.arange(64)[None, :]
    tri[:, 0, :] = (s_idx <= t_idx)
    tri[:, 1, :] = (s_idx >= t_idx)
    smask = np.ones((128, 512), f32)
    smask[:, ::64] = 0.0
    _CONST.update(identF=ident, perm=perm, bones=bones, cosT=np.ascontiguousarray(cosT), sinT=np.ascontiguousarray(sinT), tri=tri, smask=smask)
    return _CONST


def make_in_maps(x, c, ctx, c_ctx, norm_gain, ada_w, ada_b, even_w_in, even_w_out, attn_qk_gain,
                 attn_lambda, attn_subln_gain, hgrn_lb_logits, hgrn_norm_gain, odd_w_in, odd_w_out,
                 gmlp_v_gain, gmlp_w_s, gmlp_b_s, conv_w):
    f = lambda a: np.ascontiguousarray(np.asarray(a, dtype=np.float32))
    shared = dict(
        norm_gain=f(norm_gain), ada_w=f(ada_w), ada_b=f(ada_b), even_w_in=f(even_w_in)[0], even_w_out=f(even_w_out)[0],
        qk_gain=f(attn_qk_gain)[0], attn_lambda=f(attn_lambda)[0].reshape(256), subln=f(attn_subln_gain)[0],
        lb_logits=f(hgrn_lb_logits), hgrn_g=f(hgrn_norm_gain)[0], odd_w_in=f(odd_w_in)[0], odd_w_out=f(odd_w_out)[0],
        v_gain=f(gmlp_v_gain)[0], w_s=f(gmlp_w_s)[0], b_s=f(gmlp_b_s)[0], conv_w=f(conv_w)[0])
    shared.update(_consts())
    x = f(x)
    c = f(c)
    ctx = f(ctx)
    c_ctx = f(c_ctx)
    maps = []
    for b in range(8):
        m = dict(shared)
        m["x"] = x[b]
        m["ctx"] = ctx[b]
        m["cvec"] = np.ascontiguousarray(np.stack([c[b], c_ctx], axis=0))
        maps.append(m)
    return maps


def kernel(**inputs):
    maps = make_in_maps(**inputs)
    nc = build(debug=False)
    res = run_bass_kernel_spmd(nc, maps, core_ids=list(range(8)))
    return np.stack([np.asarray(r["out"], dtype=np.float32) for r in res.results], axis=0)
```

```python
import numpy as np
import concourse.bass as bass
import concourse.mybir as mybir
from concourse.bass_utils import run_bass_kernel_spmd
from concourse.alu_op_type import AluOpType as ALU

F32 = mybir.dt.float32
BF16 = mybir.dt.bfloat16
AF = mybir.ActivationFunctionType

D = 1024
SEQ = 2048
CTX = 256
NT = SEQ + CTX
EPS = 1e-6
LAM_INIT = 0.8 - 0.6 * 1.0


class Dep:
    __slots__ = ("name", "w", "r", "dsem", "dcnt", "excl")

    def __init__(self, name, excl=False):
        self.name = name
        self.excl = excl
        self.w = None
        self.r = []
        self.dsem = None
        self.dcnt = 0


class Sched:
    ENG = ("pe", "act", "dve", "pool", "sp")

    def __init__(self, nc):
        self.nc = nc
        self.engs = {"pe": nc.tensor, "act": nc.scalar, "dve": nc.vector, "pool": nc.gpsimd, "sp": nc.sync}
        self.cnt = {e: 0 for e in self.ENG}
        self.sem = {}
        self.waited = {e: {} for e in self.ENG}
        self.dsems = []
        self.nops = 0
        for e in self.ENG:
            self.sem[e] = nc.alloc_semaphore("s_" + e)

    def _waits(self, eng, reads, writes):
        need = {}

        def add(p):
            if p is None:
                return
            s, v = p
            k = id(s)
            if k not in need or need[k][1] < v:
                need[k] = (s, v)

        for d in reads:
            add(d.w)
        for d in writes:
            add(d.w)
            for p in d.r:
                add(p)
        wd = self.waited[eng]
        own = self.sem[eng]
        engine = self.engs[eng]
        for k, (s, v) in need.items():
            if s is own and (eng == "pe" or v > self.cnt[eng]):
                continue
            if wd.get(k, 0) >= v:
                continue
            wd[k] = v
            engine.wait_ge(s, v)

    def op(self, eng, fn, reads=(), writes=(), inc=True):
        ex = [d for d in reads if d.excl]
        if ex:
            reads = [d for d in reads if not d.excl]
            writes = list(writes) + ex
        self._waits(eng, reads, writes)
        val = self.cnt[eng] + 1
        ins = fn()
        self.nops += 1
        if inc:
            self.cnt[eng] = val
            ins.then_inc(self.sem[eng], 1)
        tok = (self.sem[eng], val)
        for d in reads:
            d.r.append(tok)
            if len(d.r) > 64:
                d.r = self._compact(d.r)
        for d in writes:
            d.w = tok
            d.r = []

    @staticmethod
    def _compact(lst):
        best = {}
        for s, v in lst:
            k = id(s)
            if k not in best or best[k][1] < v:
                best[k] = (s, v)
        return list(best.values())

    def dma(self, eng, fn, reads=(), writes=(), semdep=None):
        self._waits(eng, reads, writes)
        d0 = semdep
        if d0.dsem is None:
            d0.dsem = self.nc.alloc_semaphore("d%d_%s" % (len(self.dsems), d0.name))
            self.dsems.append(d0)
        d0.dcnt += 16
        ins = fn()
        ins.then_inc(d0.dsem, 16)
        tok = (d0.dsem, d0.dcnt)
        for d in reads:
            d.r.append(tok)
        for d in writes:
            d.w = tok
            d.r = []

    def wait_all(self, eng, deps):
        self._waits(eng, (), deps)

    def barrier(self):
        for e in self.ENG:
            engine = self.engs[e]
            wd = self.waited[e]
            for f in self.ENG:
                if f == e or self.cnt[f] == 0:
                    continue
                s = self.sem[f]
                if wd.get(id(s), 0) >= self.cnt[f]:
                    continue
                wd[id(s)] = self.cnt[f]
                engine.wait_ge(s, self.cnt[f])
            for d0 in self.dsems:
                if wd.get(id(d0.dsem), 0) >= d0.dcnt:
                    continue
                wd[id(d0.dsem)] = d0.dcnt
                engine.wait_ge(d0.dsem, d0.dcnt)


def skew(items, newest_first=False):
    nst = max(len(it) for it in items)
    for t in range(len(items) + nst - 1):
        for s_ in (range(nst) if newest_first else reversed(range(nst))):
            i = t - s_
            if 0 <= i < len(items) and s_ < len(items[i]):
                items[i][s_]()


class Arena:
    def __init__(self, ap_f32):
        self.ap = ap_f32
        self.top = 0
        self.n = ap_f32.shape[1]

    @staticmethod
    def _view(v, shape):
        if len(shape) == 1:
            return v
        names = ["d%d" % i for i in range(len(shape))]
        pat = "p (" + " ".join(names) + ") -> p " + " ".join(names)
        kw = {names[i]: int(shape[i]) for i in range(1, len(shape))}
        return v.rearrange(pat, **kw)

    def f32(self, *shape):
        n = int(np.prod(shape))
        off = self.top
        self.top += n
        assert self.top <= self.n, ("arena overflow", self.top, self.n)
        return self._view(self.ap[:, off:off + n], shape)

    def bf16(self, *shape):
        n = int(np.prod(shape))
        nw = (n + 1) // 2
        off = self.top
        self.top += nw
        assert self.top <= self.n, ("arena overflow", self.top, self.n)
        return self._view(self.ap[:, off:off + nw].bitcast(BF16)[:, 0:n], shape)


def build(debug=False):
    nc = bass.Bass("TRN2", target_bir_lowering=False)

    def din(name, shape, dt=F32):
        return nc.dram_tensor(name, list(shape), dt, kind="ExternalInput").ap()

    x_d = din("x", [SEQ, D])
    ctx_d = din("ctx", [CTX, D])
    cvec_d = din("cvec", [2, D])
    ng_d = din("norm_gain", [2, D])
    adaw_d = din("ada_w", [2, D, 3 * D])
    adab_d = din("ada_b", [2, 3 * D])
    ewin_d = din("even_w_in", [D, 4608])
    ewout_d = din("even_w_out", [D, D])
    qkg_d = din("qk_gain", [2, 64])
    lam_d = din("attn_lambda", [256])
    subln_d = din("subln", [128])
    lbl_d = din("lb_logits", [2, 2, 512])
    hg_d = din("hgrn_g", [128])
    owin_d = din("odd_w_in", [D, 3584])
    owout_d = din("odd_w_out", [D, D])
    vg_d = din("v_gain", [512])
    ws_d = din("w_s", [4, 128, 128])
    bs_d = din("b_s", [4, 128])
    cw_d = din("conv_w", [3, 512])
    identF_d = din("identF", [128, 128])
    perm_d = din("perm", [128, 128])
    bones_d = din("bones", [128, 128])
    cos_d = din("cosT", [128, SEQ])
    sin_d = din("sinT", [128, SEQ])
    tri_d = din("tri", [64, 2, 64])
    smask_d = din("smask", [128, 512])
    out_d = nc.dram_tensor("out", [SEQ, D], F32, kind="ExternalOutput").ap()
    dbg = {}
    if debug:
        dbg["hT0"] = nc.dram_tensor("dbg_hT0", [128, 8, NT], BF16, kind="ExternalOutput").ap()
        dbg["yT0"] = nc.dram_tensor("dbg_yT0", [128, 8, SEQ], BF16, kind="ExternalOutput").ap()
        dbg["xnew"] = nc.dram_tensor("dbg_xnew", [128, 16, D], F32, kind="ExternalOutput").ap()
        dbg["yT1"] = nc.dram_tensor("dbg_yT1", [128, 8, SEQ], BF16, kind="ExternalOutput").ap()
        dbg["mods"] = nc.dram_tensor("dbg_mods", [128, 2, 3, 8, 2], F32, kind="ExternalOutput").ap()

    S = Sched(nc)
    E = nc
    NW = (nc.sbuf_bytes_remaining - 2048) // 4
    arena_t = nc.alloc_sbuf_tensor("arena", [128, NW], F32).ap()
    AR = Arena(arena_t)
    psum = nc.alloc_psum_tensor("psum", [128, 8, 512], F32).ap()
    PB = [Dep("pb%d" % i, excl=True) for i in range(8)]

    def bank(i):
        return psum[:, i, :]

    def bank_bf(i):
        return psum[:, i, :].bitcast(BF16)

    hT = AR.bf16(8, NT)
    yT = AR.bf16(8, SEQ)
    identF = AR.f32(128)
    identB = AR.bf16(128)
    zerosB = AR.bf16(512)
    gate_bc = [AR.f32(D), AR.f32(D)]
    modsc = AR.f32(2, 3, 8, 2)
    small = AR.f32(64)
    P_TOP = AR.top
    d_hT = [Dep("hT%d" % i) for i in range(18)]
    d_yT = [[Dep("yT%d_%d" % (k, i)) for i in range(16)] for k in range(8)]
    d_const = Dep("const")
    d_mods = Dep("mods")
    d_gate = [Dep("gate0"), Dep("gate1")]
    d_small = Dep("small")

    sp, act, dve, pool, pe = nc.sync, nc.scalar, nc.vector, nc.gpsimd, nc.tensor

    S.dma("sp", lambda: sp.dma_start(out=identF, in_=identF_d), writes=[d_const], semdep=d_const)
    S.op("dve", lambda: dve.tensor_copy(out=identB, in_=identF), reads=[d_const], writes=[d_const])
    S.op("pool", lambda: pool.memset(zerosB, 0.0), writes=[d_const])

    AR.top = P_TOP
    cvT = AR.f32(8, 2)
    csT = AR.bf16(8, 2)
    Rrow = AR.f32(3 * D)
    adab = AR.f32(3 * D)
    gT = AR.f32(2, 8)
    onesr = AR.f32(128)
    wada = [AR.f32(3 * D) for _ in range(3)]
    wadab = [AR.bf16(3 * D), AR.bf16(3 * D)]
    d_cv, d_R, d_adab, d_gT, d_ones = Dep("cv"), Dep("R"), Dep("adab"), Dep("gT"), Dep("ones")
    d_wada = [[Dep("wada%d_%d" % (i, j)) for j in range(3)] for i in range(3)]
    d_wadab = [[Dep("wadab%d_%d" % (i, j)) for j in range(4)] for i in range(2)]

    for r in range(2):
        S.dma("sp", lambda r=r: sp.dma_start(out=cvT[:, :, r], in_=cvec_d[r].rearrange("(k p) -> p k", p=128), allow_slow_non_contiguous=True), writes=[d_cv], semdep=d_cv)
        S.dma("sp", lambda r=r: sp.dma_start(out=gT[:, r, :], in_=ng_d[r].rearrange("(k p) -> p k", p=128), allow_slow_non_contiguous=True), writes=[d_gT], semdep=d_gT)
    S.op("act", lambda: act.activation(out=csT, in_=cvT, func=AF.Silu), reads=[d_cv], writes=[d_cv])
    S.op("pool", lambda: pool.memset(onesr, 1.0), writes=[d_ones])
    wi = 0
    for l in range(2):
        S.dma("sp", lambda l=l: sp.dma_start(out=adab[0:2, :], in_=adab_d[l].partition_broadcast(2)), writes=[d_adab], semdep=d_adab)
        for k in range(8):
            slot = wi % 3
            bs_ = wi % 2
            wi += 1
            for hh in range(3):
                qn_ = ("sp", "pool", "act")[hh]
                qe_ = (sp, pool, act)[hh]
                S.dma(qn_, lambda l=l, k=k, slot=slot, hh=hh, qe_=qe_: qe_.dma_start(
                    out=wada[slot][:, hh * 1024:(hh + 1) * 1024], in_=adaw_d[l][k * 128:(k + 1) * 128, hh * 1024:(hh + 1) * 1024]),
                    writes=[d_wada[slot][hh]], semdep=d_wada[slot][hh])
            S.op("dve", lambda slot=slot, bs_=bs_: dve.tensor_copy(out=wadab[bs_][:, 0:1024], in_=wada[slot][:, 0:1024]), reads=[d_wada[slot][0]], writes=[d_wadab[bs_][0]])
            S.op("act", lambda slot=slot, bs_=bs_: act.copy(out=wadab[bs_][:, 1024:1536], in_=wada[slot][:, 1024:1536]), reads=[d_wada[slot][1]], writes=[d_wadab[bs_][1]])
            S.op("act", lambda slot=slot, bs_=bs_: act.copy(out=wadab[bs_][:, 1536:2560], in_=wada[slot][:, 1536:2560]), reads=[d_wada[slot][1], d_wada[slot][2]], writes=[d_wadab[bs_][2]])
            S.op("pool", lambda slot=slot, bs_=bs_: pool.tensor_copy(out=wadab[bs_][:, 2560:3072], in_=wada[slot][:, 2560:3072]), reads=[d_wada[slot][2]], writes=[d_wadab[bs_][3]])
            for n in range(6):
                part = (0, 0, 1, 2, 2, 3)[n]
                S.op("pe", lambda k=k, n=n, bs_=bs_: pe.matmul(bank(n)[0:2, :], lhsT=csT[:, k, :], rhs=wadab[bs_][:, n * 512:(n + 1) * 512], start=(k == 0), stop=(k == 7)),
                     reads=[d_cv, d_wadab[bs_][part]], writes=[PB[n]])
        for n in range(6):
            S.op("dve", lambda n=n: dve.tensor_tensor(out=Rrow[0:2, n * 512:(n + 1) * 512], in0=bank(n)[0:2, :], in1=adab[0:2, n * 512:(n + 1) * 512], op=ALU.add),
                 reads=[PB[n], d_adab], writes=[d_R])
        pt = bank(6)[:, 0:32].rearrange("p (j r) -> p j r", r=2)
        for j in range(16):
            S.op("pe", lambda j=j: pe.transpose(out=pt[:, j, :], in_=Rrow[0:2, j * 128:(j + 1) * 128], identity=identF[0:2, 0:2]),
                 reads=[d_R, d_const], writes=[PB[6]], inc=(j == 15))
        S.op("dve", lambda l=l: dve.tensor_copy(out=modsc[:, l, 0, :, :], in_=pt[:, 0:8, :]), reads=[PB[6]], writes=[d_mods])
        S.op("dve", lambda l=l: dve.tensor_copy(out=modsc[:, l, 2, :, :], in_=pt[:, 8:16, :]), reads=[PB[6]], writes=[d_mods])
        for r in range(2):
            S.op("dve", lambda l=l, r=r: dve.scalar_tensor_tensor(out=modsc[:, l, 1, :, r], in0=modsc[:, l, 2, :, r], scalar=1.0, in1=gT[:, l, :], op0=ALU.add, op1=ALU.mult),
                 reads=[d_mods, d_gT], writes=[d_mods])
        for n in range(2):
            S.op("pe", lambda n=n: pe.matmul(bank(2 + n), lhsT=onesr[0:1, :], rhs=Rrow[0:1, 2048 + n * 512:2048 + (n + 1) * 512], start=True, stop=True),
                 reads=[d_R, d_ones], writes=[PB[2 + n]])
            S.op("act", lambda n=n, l=l: act.copy(out=gate_bc[l][:, n * 512:(n + 1) * 512], in_=bank(2 + n)), reads=[PB[2 + n]], writes=[d_gate[l]])
    if debug:
        S.dma("sp", lambda: sp.dma_start(out=dbg["mods"], in_=modsc), reads=[d_mods], writes=[], semdep=d_mods)
    S.barrier()

    def norm_phase(l, ntiles, src_fn, top):
        AR.top = top
        nxt = 4 if l == 0 else 0
        xt = [AR.f32(D) for _ in range(nxt)]
        xn = [AR.f32(D), AR.f32(D)]
        junk = AR.bf16(D)
        stat = AR.f32(3, 18)
        d_xt = [Dep("xt%d" % i_) for i_ in range(nxt)]
        d_xn = [Dep("xn0"), Dep("xn1")]
        d_junk, d_stat = Dep("junk"), [Dep("stat%d" % i) for i in range(18)]

        def item(i):
            s = i % 2
            src, is_ctx, sdep = src_fn(i)
            b0 = 4 + 2 * (i % 2)
            pt = psum[:, b0:b0 + 2, :].rearrange("p b (j c) -> p (b j) c", c=128)
            r = 1 if is_ctx else 0
            if sdep is None:
                xin, xdep = xt[i % 4], d_xt[i % 4]
            else:
                xin, xdep = src, sdep

            def s0():
                if sdep is None:
                    S.dma("sp", lambda: sp.dma_start(out=xt[i % 4], in_=src), writes=[d_xt[i % 4]], semdep=d_xt[i % 4])

            def s1():
                S.op("act", lambda: act.activation(out=junk, in_=xin, func=AF.Square, accum_out=stat[:, 0, i:i + 1]), reads=[xdep], writes=[d_junk, d_stat[i]])
                S.op("act", lambda: act.activation(out=stat[:, 1, i:i + 1], in_=stat[:, 0, i:i + 1], func=AF.Sqrt, scale=1.0 / D, bias=EPS), reads=[d_stat[i]], writes=[d_stat[i]])
                S.op("dve", lambda: dve.reciprocal(out=stat[:, 2, i:i + 1], in_=stat[:, 1, i:i + 1]), reads=[d_stat[i]], writes=[d_stat[i]])

            def s2():
                S.op("dve", lambda: dve.tensor_scalar(out=xn[s], in0=xin, scalar1=stat[:, 2, i:i + 1], scalar2=None, op0=ALU.mult), reads=[xdep, d_stat[i]], writes=[d_xn[s]])
                for k in range(8):
                    S.op("pe", lambda k=k: pe.transpose(out=pt[:, k, :], in_=xn[s][:, k * 128:(k + 1) * 128], identity=identF),
                         reads=[d_xn[s], d_const], writes=[PB[b0 + k // 4]], inc=(k == 7))
                for _ in range(6):
                    S.op("pe", lambda: pe.matmul(bank(3), lhsT=zerosB[:, 0:128], rhs=zerosB, start=True, stop=True, skip_group_check=True), reads=[d_const], writes=[PB[3]], inc=False)

            def s3():
                for k in range(8):
                    if k < 4:
                        S.op("act", lambda k=k: act.activation(out=hT[:, k, i * 128:(i + 1) * 128], in_=pt[:, k, :], func=AF.Identity,
                                                               scale=modsc[:, l, 1, k, r:r + 1], bias=modsc[:, l, 0, k, r:r + 1]),
                             reads=[PB[b0 + k // 4], d_mods], writes=[d_hT[i]])
                    else:
                        S.op("dve", lambda k=k: dve.tensor_scalar(out=hT[:, k, i * 128:(i + 1) * 128], in0=pt[:, k, :],
                                                                  scalar1=modsc[:, l, 1, k, r:r + 1], scalar2=modsc[:, l, 0, k, r:r + 1], op0=ALU.mult, op1=ALU.add),
                             reads=[PB[b0 + k // 4], d_mods], writes=[d_hT[i]])
            return [s0, s1, s2, s3]

        skew([item(i) for i in range(ntiles)], newest_first=True)

    def src0(i):
        if i < 2:
            return ctx_d[i * 128:(i + 1) * 128, :], True, None
        return x_d[(i - 2) * 128:(i - 1) * 128, :], False, None


    AR.top = P_TOP
    cosT = AR.f32(SEQ)
    sinT = AR.f32(SEQ)
    permS = AR.f32(128)
    bonesS = AR.f32(128)
    gqk = AR.f32(2)
    lamb = AR.f32(256)
    lamt = AR.f32(8)
    gS = AR.f32(128)
    epsT = AR.f32(1)
    mhA = AR.f32(1)
    wA = [AR.bf16(4, 8, 128), AR.bf16(4, 8, 128)]
    qT = [AR.bf16(SEQ), AR.bf16(SEQ)]
    kT0 = [AR.bf16(NT), AR.bf16(NT)]
    kT1 = [AR.bf16(NT), AR.bf16(NT)]
    vaug = [AR.bf16(18, 130), AR.bf16(18, 130)]
    gateA = [AR.bf16(16, 128), AR.bf16(16, 128)]
    t_sq = [AR.f32(512), AR.f32(512)]
    t_qg = [AR.f32(512), AR.f32(512)]
    t_rs = [AR.f32(512), AR.f32(512)]
    t_a = [AR.f32(512), AR.f32(512)]
    t_b = [AR.f32(512), AR.f32(512)]
    t_gs = AR.f32(512)
    Et = [AR.bf16(512) for _ in range(4)]
    ep_f = AR.f32(16, 128)
    ep_s = AR.f32(8, 8)
    ep_y = AR.bf16(8, 128)
    uc = [0]
    d_A = Dep("Aconst")
    d_Ar = Dep("Arope")
    d_wA = [Dep("wA0"), Dep("wA1")]
    d_qT = [[Dep("qT%d_%d" % (p_, i)) for i in range(4)] for p_ in range(2)]
    d_kT = [[Dep("kT%d_%d" % (p_, i)) for i in range(5)] for p_ in range(2)]
    d_v = [[Dep("vaug%d_%d" % (p_, i)) for i in range(5)] for p_ in range(2)]
    d_gA = [[Dep("gateA%d_%d" % (p_, i)) for i in range(4)] for p_ in range(2)]
    d_tsq = [Dep("tsq0"), Dep("tsq1")]
    d_tqg = [Dep("tqg0"), Dep("tqg1")]
    d_trs = [Dep("trs0"), Dep("trs1")]
    d_ta = [Dep("ta0"), Dep("ta1")]
    d_tb = [Dep("tb0"), Dep("tb1")]
    d_tgs = Dep("tgs")
    d_E = [Dep("E%d" % i) for i in range(4)]
    d_ep = [Dep("ep%d" % i) for i in range(8)]

    S.dma("sp", lambda: sp.dma_start(out=cosT, in_=cos_d), writes=[d_Ar], semdep=d_Ar)
    S.dma("sp", lambda: sp.dma_start(out=sinT, in_=sin_d), writes=[d_Ar], semdep=d_Ar)
    S.dma("sp", lambda: sp.dma_start(out=permS, in_=perm_d), writes=[d_Ar], semdep=d_Ar)
    S.dma("sp", lambda: sp.dma_start(out=bonesS, in_=bones_d), writes=[d_Ar], semdep=d_Ar)
    for m in range(2):
        S.dma("sp", lambda m=m: sp.dma_start(out=gqk[m * 64:(m + 1) * 64, :], in_=qkg_d.rearrange("r d -> d r"), allow_slow_non_contiguous=True), writes=[d_A], semdep=d_A)
    S.dma("sp", lambda: sp.dma_start(out=lamb, in_=lam_d.partition_broadcast(128)), writes=[d_A], semdep=d_A)
    S.dma("sp", lambda: sp.dma_start(out=gS, in_=subln_d.partition_broadcast(128)), writes=[d_A], semdep=d_A)
    S.op("dve", lambda: dve.tensor_scalar(out=gqk[:, 0:1], in0=gqk[:, 0:1], scalar1=0.125, scalar2=None, op0=ALU.mult), reads=[d_A], writes=[d_A])
    S.op("dve", lambda: dve.tensor_scalar(out=gS, in0=gS, scalar1=1.0 - LAM_INIT, scalar2=None, op0=ALU.mult), reads=[d_A], writes=[d_A])
    S.op("dve", lambda: dve.tensor_tensor(out=lamb[:, 0:64], in0=lamb[:, 0:64], in1=lamb[:, 64:128], op=ALU.mult), reads=[d_A], writes=[d_A])
    S.op("dve", lambda: dve.tensor_tensor(out=lamb[:, 128:192], in0=lamb[:, 128:192], in1=lamb[:, 192:256], op=ALU.mult), reads=[d_A], writes=[d_A])
    S.op("dve", lambda: dve.tensor_reduce(out=lamt[:, 0:1], in_=lamb[:, 0:64], op=ALU.add, axis=mybir.AxisListType.X), reads=[d_A], writes=[d_A])
    S.op("dve", lambda: dve.tensor_reduce(out=lamt[:, 1:2], in_=lamb[:, 128:192], op=ALU.add, axis=mybir.AxisListType.X), reads=[d_A], writes=[d_A])
    S.op("act", lambda: act.activation(out=lamt[:, 2:4], in_=lamt[:, 0:2], func=AF.Exp), reads=[d_A], writes=[d_A])
    S.op("dve", lambda: dve.scalar_tensor_tensor(out=lamt[:, 4:5], in0=lamt[:, 3:4], scalar=-LAM_INIT, in1=lamt[:, 2:3], op0=ALU.add, op1=ALU.subtract), reads=[d_A], writes=[d_A])
    S.op("pool", lambda: pool.memset(epsT, EPS), writes=[d_A])
    S.op("pool", lambda: pool.memset(mhA, -0.5), writes=[d_A])
    for p_ in range(2):
        S.op("pool", lambda p_=p_: pool.memset(kT0[p_], 0.0), writes=d_kT[p_])
        S.op("pool", lambda p_=p_: pool.memset(kT1[p_], 0.0), writes=d_kT[p_])
        S.op("pool", lambda p_=p_: pool.memset(vaug[p_][:, :, 128:130], 1.0), writes=d_v[p_])

    PROJ_B = [3, 4, 5, 6, 7]
    pr = [0]

    def nextbank():
        b = PROJ_B[pr[0] % len(PROJ_B)]
        pr[0] += 1
        return b

    blk = [0]

    def qk_item(h, which, tok0, ntok, rope, dst_fn, ddst):
        s = blk[0] % 2
        blk[0] += 1
        slot = h % 2
        st = {}

        def m1():
            bp = nextbank()
            st["bp"] = bp
            for k in range(8):
                S.op("pe", lambda k=k: pe.matmul(bank(bp)[:, 0:ntok], lhsT=wA[slot][:, which, k, :], rhs=hT[:, k, tok0:tok0 + ntok], start=(k == 0), stop=(k == 7)),
                     reads=[d_wA[slot]] + d_hT[tok0 // 128:(tok0 + ntok + 127) // 128], writes=[PB[bp]], inc=(k == 7))

        def m2():
            bp = st["bp"]
            S.op("dve", lambda: dve.tensor_copy(out=t_b[s][:, 0:ntok], in_=bank(bp)[:, 0:ntok]), reads=[PB[bp]], writes=[d_tb[s]])
            S.op("dve", lambda: dve.tensor_tensor(out=t_sq[s][:, 0:ntok], in0=t_b[s][:, 0:ntok], in1=t_b[s][:, 0:ntok], op=ALU.mult), reads=[d_tb[s]], writes=[d_tsq[s]])
            S.op("dve", lambda: dve.tensor_scalar(out=t_qg[s][:, 0:ntok], in0=t_b[s][:, 0:ntok], scalar1=gqk[:, which:which + 1], scalar2=None, op0=ALU.mult),
                 reads=[d_tb[s], d_A], writes=[d_tqg[s]])

        def m3():
            bs = nextbank()
            st["bs"] = bs
            S.op("pe", lambda: pe.matmul(bank(bs)[:, 0:ntok], lhsT=bonesS, rhs=t_sq[s][:, 0:ntok], start=True, stop=True), reads=[d_tsq[s], d_Ar], writes=[PB[bs]])
            if rope:
                br = nextbank()
                st["br"] = br
                S.op("pe", lambda: pe.matmul(bank(br)[:, 0:ntok], lhsT=permS, rhs=t_qg[s][:, 0:ntok], start=True, stop=True), reads=[d_tqg[s], d_Ar], writes=[PB[br]])

        def m4():
            bs = st["bs"]
            S.op("act", lambda: act.activation(out=t_rs[s][:, 0:ntok], in_=bank(bs)[:, 0:ntok], func=AF.Ln, scale=1.0 / 64, bias=epsT[:, 0:1]), reads=[PB[bs], d_A], writes=[d_trs[s]])
            S.op("act", lambda: act.activation(out=t_rs[s][:, 0:ntok], in_=t_rs[s][:, 0:ntok], func=AF.Exp, scale=-0.5), reads=[d_trs[s]], writes=[d_trs[s]])
            if rope:
                br = st["br"]
                p0 = tok0 - CTX
                S.op("pool", lambda: pool.tensor_tensor(out=t_a[s][:, 0:ntok], in0=t_qg[s][:, 0:ntok], in1=cosT[:, p0:p0 + ntok], op=ALU.mult), reads=[d_tqg[s], d_Ar], writes=[d_ta[s]])
                S.op("dve", lambda: dve.tensor_tensor(out=t_b[s][:, 0:ntok], in0=bank(br)[:, 0:ntok], in1=sinT[:, p0:p0 + ntok], op=ALU.mult), reads=[PB[br], d_Ar, d_tsq[s], d_tqg[s]], writes=[d_tb[s]])

        def m5():
            if rope:
                S.op("dve", lambda: dve.tensor_tensor(out=t_a[s][:, 0:ntok], in0=t_a[s][:, 0:ntok], in1=t_b[s][:, 0:ntok], op=ALU.add), reads=[d_ta[s], d_tb[s]], writes=[d_ta[s]])
                srcv, sdep = t_a[s], d_ta[s]
            else:
                srcv, sdep = t_qg[s], d_tqg[s]
            for (dst, p_lo, p_hi) in dst_fn():
                eng_, ee_ = ("dve", dve)
                S.op(eng_, lambda dst=dst, p_lo=p_lo, p_hi=p_hi: ee_.tensor_tensor(out=dst, in0=srcv[p_lo:p_hi, 0:ntok], in1=t_rs[s][p_lo:p_hi, 0:ntok], op=ALU.mult),
                     reads=[sdep, d_trs[s]], writes=[ddst])
        return [m1, m2, m3, m4, m5]

    def proj_tasks(h, interleave=False):
        hp = h % 2
        slot = h % 2
        items = []
        for i in range(4):
            items.append(qk_item(h, 0, CTX + i * 512, 512, True, lambda i=i: [(qT[hp][:, i * 512:(i + 1) * 512], 0, 128)], d_qT[hp][i]))
        items.append(qk_item(h, 1, 0, 256, False, lambda: [(kT0[hp][0:64, 0:256], 0, 64), (kT1[hp][64:128, 0:256], 64, 128)], d_kT[hp][0]))
        for i in range(4):
            t0 = CTX + i * 512
            items.append(qk_item(h, 1, t0, 512, True, lambda t0=t0: [(kT0[hp][0:64, t0:t0 + 512], 0, 64), (kT1[hp][64:128, t0:t0 + 512], 64, 128)], d_kT[hp][1 + i]))
        tasks = []
        if interleave:
            tasks.extend(items[0][0:3])
            for n_ in range(1, len(items)):
                a_, b_ = items[n_], items[n_ - 1]
                tasks.extend([a_[0], b_[3], a_[1], b_[4], a_[2]])
            tasks.extend(items[-1][3:5])
        else:
            for it in items:
                tasks.extend(it)

        def v_task(grp):
            st = {}

            def a():
                tiles = list(range(grp * 4, min(grp * 4 + 4, 18)))
                bp = nextbank()
                st["bp"] = bp
                pv = bank(bp).rearrange("p (j c) -> p j c", c=128)
                for jj, ti in enumerate(tiles):
                    for k in range(8):
                        S.op("pe", lambda k=k, jj=jj, ti=ti: pe.matmul(pv[:, jj, :], lhsT=hT[:, k, ti * 128:(ti + 1) * 128], rhs=wA[slot][:, 2, k, :], start=(k == 0), stop=(k == 7)),
                             reads=[d_wA[slot], d_hT[ti]], writes=[PB[bp]], inc=(k == 7 and jj == len(tiles) - 1))

            def b():
                bp = st["bp"]
                pv = bank(bp).rearrange("p (j c) -> p j c", c=128)
                n = len(list(range(grp * 4, min(grp * 4 + 4, 18))))
                S.op("dve", lambda: dve.tensor_copy(out=vaug[hp][:, grp * 4:grp * 4 + n, 0:128], in_=pv[:, 0:n, :]), reads=[PB[bp]], writes=[d_v[hp][grp]])
            return [a, b]

        def g_task(grp):
            st = {}

            def a():
                bp = nextbank()
                st["bp"] = bp
                pv = bank(bp).rearrange("p (j c) -> p j c", c=128)
                for jj in range(4):
                    ti = 2 + grp * 4 + jj
                    for k in range(8):
                        S.op("pe", lambda k=k, jj=jj, ti=ti: pe.matmul(pv[:, jj, :], lhsT=hT[:, k, ti * 128:(ti + 1) * 128], rhs=wA[slot][:, 3, k, :], start=(k == 0), stop=(k == 7)),
                             reads=[d_wA[slot], d_hT[ti]], writes=[PB[bp]], inc=(k == 7 and jj == 3))

            def b():
                bp = st["bp"]
                S.op("act", lambda: act.activation(out=t_gs, in_=bank(bp), func=AF.Exp, scale=-1.0), reads=[PB[bp]], writes=[d_tgs])
                S.op("dve", lambda: dve.tensor_scalar(out=t_gs, in0=t_gs, scalar1=1.0, scalar2=1e30, op0=ALU.add, op1=ALU.min), reads=[], writes=[d_tgs])

            def c():
                bp = st["bp"]
                pv = bank(bp).rearrange("p (j c) -> p j c", c=128)
                S.op("act", lambda: act.activation(out=t_gs, in_=t_gs, func=AF.Ln), reads=[], writes=[d_tgs])
                S.op("act", lambda: act.activation(out=t_gs, in_=t_gs, func=AF.Exp, scale=-1.0), reads=[], writes=[d_tgs])
                S.op("dve", lambda: dve.tensor_tensor(out=gateA[hp][:, grp * 4:grp * 4 + 4, :], in0=pv, in1=t_gs.rearrange("p (j c) -> p j c", c=128), op=ALU.mult), reads=[PB[bp], d_tgs], writes=[d_gA[hp][grp]])
            return [a, b, c]

        for grp in range(5):
            tasks.extend(v_task(grp))
        for grp in range(4):
            tasks.extend(g_task(grp))
        return tasks

    def load_wA(h):
        slot = h % 2
        for g in range(4):
            S.dma("pool", lambda g=g: pool.dma_start(
                out=wA[slot][:, g, :, :], in_=ewin_d.rearrange("(k p) n -> p k n", p=128)[:, :, g * 512 + h * 128:g * 512 + (h + 1) * 128]),
                writes=[d_wA[slot]], semdep=d_wA[slot])

    _save_top = AR.top
    AR.top = P_TOP
    lbl = AR.f32(2, 2, 4)
    lbv = AR.f32(3, 2, 4)
    tri = AR.f32(2, 64)
    smask = AR.f32(512)
    hgT = AR.f32(1)
    epsB = AR.f32(1)
    oneB = AR.f32(1)
    mhB = AR.f32(1)
    wB = AR.bf16(5, 8, 128)
    assert AR.top <= P_TOP + 2 * SEQ
    AR.top = _save_top
    d_B = Dep("Bconst")
    d_wB = Dep("wB")

    def load_wB(h, extra=()):
        for g in range(5):
            S.dma("pool", lambda g=g: pool.dma_start(
                out=wB[:, g, :, :], in_=ewin_d.rearrange("(k p) n -> p k n", p=128)[:, :, 2048 + g * 512 + h * 128:2048 + g * 512 + (h + 1) * 128]),
                writes=[d_wB] + list(extra), semdep=d_wB)

    def b_prefetch():
        ex = [d_Ar]
        for dd_ in range(2):
            for ee_ in range(2):
                S.dma("sp", lambda dd_=dd_, ee_=ee_: sp.dma_start(out=lbl[:, dd_, ee_, :], in_=lbl_d[dd_, ee_].rearrange("(h p) -> p h", p=128), allow_slow_non_contiguous=True), writes=[d_B] + ex, semdep=d_B)
        S.dma("sp", lambda: sp.dma_start(out=tri[0:64, :, :], in_=tri_d), writes=[d_B] + ex, semdep=d_B)
        S.dma("sp", lambda: sp.dma_start(out=smask, in_=smask_d), writes=[d_B] + ex, semdep=d_B)
        S.dma("sp", lambda: sp.dma_start(out=hgT, in_=hg_d.rearrange("(p o) -> p o", o=1), allow_slow_non_contiguous=True), writes=[d_B] + ex, semdep=d_B)
        load_wB(0, extra=ex)

    load_wA(0)
    norm_phase(0, 18, src0, NW - 6900)
    if debug:
        S.dma("sp", lambda: sp.dma_start(out=dbg["hT0"], in_=hT), reads=d_hT, writes=[], semdep=d_hT[0])
    S.barrier()
    for f in proj_tasks(0, interleave=True):
        f()
    deferred = []
    PROJ_B[:] = [6, 7]
    for h in range(4):
        hp = h % 2
        if h + 1 < 4:
            load_wA(h + 1)
            tasks = proj_tasks(h + 1)
        else:
            tasks = []

        def acc(m, t):
            sidx = m * 4 + t
            return psum[:, sidx // 3, (sidx % 3) * 130:(sidx % 3) * 130 + 130]

        def emit_score(i, u):
            j, m = u // 2, u % 2
            sb = 3 + (uc[0] + u) % 3
            kTm = kT0[hp] if m == 0 else kT1[hp]
            kd = d_kT[hp][0] if j < 2 else d_kT[hp][1 + (j - 2) // 4]
            S.op("pe", lambda: pe.matmul(bank(sb), lhsT=kTm[:, j * 128:(j + 1) * 128], rhs=qT[hp][:, i * 512:(i + 1) * 512], start=True, stop=True),
                 reads=[kd, d_qT[hp][i]], writes=[PB[sb]])

        def emit_exp_pv(i, u):
            j, m = u // 2, u % 2
            sb = 3 + (uc[0] + u) % 3
            es = (uc[0] + u) % 4
            S.op("act", lambda: act.activation(out=Et[es], in_=bank(sb), func=AF.Exp), reads=[PB[sb]], writes=[d_E[es]])
            for t in range(4):
                S.op("pe", lambda t=t: pe.matmul(acc(m, t)[:, 0:129], lhsT=Et[es][:, t * 128:(t + 1) * 128], rhs=vaug[hp][:, j, 0:129], start=False, stop=(j == 17), skip_group_check=True),
                     reads=[d_E[es], d_v[hp][j // 4]], writes=[PB[(m * 4 + t) // 3]], inc=(t == 3))

        def epilogue1(i):
            sl = i % 2
            for t in range(4):
                a0, a1 = acc(0, t), acc(1, t)
                b0_, b1_ = PB[t // 3], PB[(4 + t) // 3]
                es_ = ep_s[:, sl * 4 + t, :]
                f0, f1 = ep_f[:, sl * 8 + 2 * t, :], ep_f[:, sl * 8 + 2 * t + 1, :]
                dd = d_ep[sl * 4 + t]
                S.op("dve", lambda: dve.reciprocal(out=es_[:, 0:1], in_=a0[:, 128:129]), reads=[b0_], writes=[dd])
                S.op("dve", lambda: dve.reciprocal(out=es_[:, 1:2], in_=a1[:, 128:129]), reads=[b1_], writes=[dd])
                S.op("dve", lambda: dve.tensor_tensor(out=es_[:, 1:2], in0=es_[:, 1:2], in1=lamt[:, 4:5], op=ALU.mult), reads=[d_A], writes=[dd])
                S.op("dve", lambda: dve.tensor_scalar(out=f0, in0=a1[:, 0:128], scalar1=es_[:, 1:2], scalar2=None, op0=ALU.mult), reads=[b1_], writes=[dd])
                S.op("dve", lambda: dve.scalar_tensor_tensor(out=f1, in0=a0[:, 0:128], scalar=es_[:, 0:1], in1=f0, op0=ALU.mult, op1=ALU.add), reads=[b0_], writes=[dd])

        def epilogue1b(i, hp=hp):
            sl = i % 2
            for t in range(4):
                qt = i * 4 + t
                es_ = ep_s[:, sl * 4 + t, :]
                f0, f1 = ep_f[:, sl * 8 + 2 * t, :], ep_f[:, sl * 8 + 2 * t + 1, :]
                dd = d_ep[sl * 4 + t]
                S.op("dve", lambda: dve.scalar_tensor_tensor(out=f0, in0=f1, scalar=1.0, in1=f1, op0=ALU.mult, op1=ALU.mult, accum_out=es_[:, 2:3]), reads=[dd], writes=[dd])
                S.op("dve", lambda: dve.tensor_scalar(out=es_[:, 3:4], in0=es_[:, 2:3], scalar1=1.0 / 128, scalar2=EPS, op0=ALU.mult, op1=ALU.add), reads=[dd], writes=[dd])
                S.op("pool", lambda: pool.tensor_tensor(out=es_[:, 4:5], in0=es_[:, 3:4], in1=mhA, op=ALU.pow), reads=[dd, d_A], writes=[dd])
                S.op("dve", lambda: dve.scalar_tensor_tensor(out=f0, in0=f1, scalar=es_[:, 4:5], in1=gS, op0=ALU.mult, op1=ALU.mult), reads=[dd, d_A], writes=[dd])
                S.op("pool", lambda: pool.tensor_tensor(out=ep_y[:, sl * 4 + t, :], in0=f0, in1=gateA[hp][:, qt, :], op=ALU.mult), reads=[dd, d_gA[hp][qt // 4]], writes=[dd])

        def epilogue2(i, h=h):
            sl = i % 2
            bt = nextbank()
            pyt = bank_bf(bt)[:, 0:512].rearrange("p (t c) -> p t c", c=128)
            for t in range(4):
                S.op("pe", lambda t=t: pe.transpose(out=pyt[:, t, :], in_=ep_y[:, sl * 4 + t, :], identity=identB), reads=[d_ep[sl * 4 + t], d_const], writes=[PB[bt]], inc=(t == 3))
            S.op("dve", lambda: dve.tensor_copy(out=yT[:, h, i * 512:(i + 1) * 512], in_=bank_bf(bt)[:, 0:512]), reads=[PB[bt]], writes=d_yT[h][i * 4:i * 4 + 4])

        def zero_acc():
            for b in range(3):
                S.op("pe", lambda b=b: pe.matmul(bank(b), lhsT=zerosB[:, 0:128], rhs=zerosB, start=True, stop=True, skip_group_check=True), reads=[d_const], writes=[PB[b]])

        for i in range(4):
            if i == 0:
                zero_acc()
                for u0 in range(3):
                    emit_score(0, u0)
            for u in range(36):
                emit_exp_pv(i, u)
                if u + 3 < 36:
                    emit_score(i, u + 3)
                for dq in list(deferred):
                    dq[0] -= 1
                    if dq[0] <= 0:
                        deferred.remove(dq)
                        dq[1]()
                if tasks and u % 2 == 1:
                    tasks.pop(0)()
            uc[0] += 36
            if i + 1 < 4:
                for u0 in range(3):
                    emit_score(i + 1, u0)
            epilogue1(i)
            deferred.append([4, (lambda e=epilogue1b, i=i: e(i))])
            deferred.append([12, (lambda e=epilogue2, i=i: e(i))])
            if i + 1 < 4:
                zero_acc()
        while tasks:
            tasks.pop(0)()
        if h == 2:
            b_prefetch()
    for dq in deferred:
        dq[1]()
    PROJ_B[:] = [0, 1, 2, 3, 4, 5, 6, 7]
    S.barrier()

    AR.top = P_TOP
    lbl = AR.f32(2, 2, 4)
    lbv = AR.f32(3, 2, 4)
    tri = AR.f32(2, 64)
    smask = AR.f32(512)
    hgT = AR.f32(1)
    epsB = AR.f32(1)
    oneB = AR.f32(1)
    mhB = AR.f32(1)
    wB = AR.bf16(5, 8, 128)
    qg = [AR.bf16(NT), AR.bf16(NT)]
    kg = [AR.bf16(NT), AR.bf16(NT)]
    _off_kgt0 = AR.top
    kgtok = [AR.bf16(36, 128), AR.bf16(36, 128)]
    vtok = AR.bf16(36, 128)
    gtok = AR.bf16(32, 128)
    e1 = [AR.f32(36), AR.f32(36)]
    e2 = [AR.f32(36), AR.f32(36)]
    eL = [AR.f32(36), AR.f32(36)]
    Sring = [AR.f32(8, 128), AR.f32(8, 128)]
    Smid = [AR.bf16(32, 128), AR.bf16(32, 128)]
    Asb = AR.bf16(4, 2, 64)
    _off_t = AR.top
    bt_sig = [AR.f32(512), AR.f32(512)]
    bt_g = [AR.f32(512), AR.f32(512)]
    bt_kk = [AR.f32(512), AR.f32(512)]
    bt_G = [AR.f32(512), AR.f32(512)]
    kg2 = [bt_kk[0][:, 0:256].bitcast(BF16), bt_kk[1][:, 0:256].bitcast(BF16)]
    ot = [bt_sig[0].rearrange("p (j c) -> p j c", c=128), bt_sig[1].rearrange("p (j c) -> p j c", c=128)]
    bt_s8 = [AR.f32(2, 8), AR.f32(2, 8)]
    stage = [AR.bf16(512), AR.bf16(512)]
    ojunk = AR.f32(128)
    hb_ss = AR.f32(3, 32)
    hb_y = AR._view(AR.ap[:, _off_kgt0:_off_kgt0 + 2048].bitcast(BF16), (32, 128))
    d_qg = [[Dep("qg%d_%d" % (d, b)) for b in range(5)] for d in range(2)]
    d_kg = [[Dep("kg%d_%d" % (d, b)) for b in range(5)] for d in range(2)]
    d_kgt = [[Dep("kgt%d_%d" % (d, b)) for b in range(5)] for d in range(2)]
    d_vt = [Dep("vt%d" % i) for i in range(5)]
    d_gt = [Dep("gt%d" % i) for i in range(5)]
    d_e = [[Dep("e%d_%d" % (d, b)) for b in range(5)] for d in range(2)]
    d_Sr = [[Dep("Sr%d_%d" % (d, i)) for i in range(2)] for d in range(2)]
    d_Sm = [[Dep("Sm%d_%d" % (d, i)) for i in range(8)] for d in range(2)]
    d_As = Dep("As")
    d_bt = [{k: Dep("bt%d_%s" % (p_, k)) for k in ("sig", "g", "kk", "G", "s8")} for p_ in range(2)]
    d_stage = [Dep("stage0"), Dep("stage1")]
    d_ot = [Dep("ot0"), Dep("ot1")]
    d_hb = Dep("hb")
    d_hby = [Dep("hby%d" % i) for i in range(4)]

    S.op("pool", lambda: pool.memset(epsB, EPS), writes=[d_B])
    S.op("pool", lambda: pool.memset(oneB, 1.0), writes=[d_B])
    S.op("pool", lambda: pool.memset(mhB, -0.5), writes=[d_B])
    S.op("dve", lambda: dve.tensor_tensor(out=lbv[:, 1, :, :], in0=lbl[:, :, 1, :], in1=lbl[:, :, 0, :], op=ALU.subtract), reads=[d_B], writes=[d_B])
    S.op("act", lambda: act.activation(out=lbv[:, 0, :, :], in_=lbv[:, 1, :, :], func=AF.Exp), reads=[d_B], writes=[d_B])
    S.op("dve", lambda: dve.tensor_scalar(out=lbv[:, 0, :, :], in0=lbv[:, 0, :, :], scalar1=1.0, scalar2=None, op0=ALU.add), reads=[d_B], writes=[d_B])
    S.op("dve", lambda: dve.reciprocal(out=lbv[:, 0, :, :], in_=lbv[:, 0, :, :]), reads=[d_B], writes=[d_B])
    S.op("dve", lambda: dve.tensor_scalar(out=lbv[:, 1, :, :], in0=lbv[:, 0, :, :], scalar1=-1.0, scalar2=1.0, op0=ALU.mult, op1=ALU.add), reads=[d_B], writes=[d_B])
    S.op("dve", lambda: dve.tensor_scalar(out=lbv[:, 2, :, :], in0=lbv[:, 0, :, :], scalar1=-1.0, scalar2=None, op0=ALU.add), reads=[d_B], writes=[d_B])

    BLKS = [(0, 512), (512, 512), (1024, 512), (1536, 512), (2048, 256)]
    nb8 = [0]

    def nextbank8():
        b = nb8[0] % 7
        nb8[0] += 1
        return b

    d_junkps = Dep("junkps")

    def pe_keepwarm(n):
        for _ in range(n):
            S.op("pe", lambda: pe.matmul(bank(7), lhsT=zerosB[:, 0:128], rhs=zerosB, start=True, stop=True, skip_group_check=True), reads=[d_const], writes=[PB[7]], inc=False)

    bd = [0]
    for h in range(4):
        def vg_item(which, grp, bi, t0, nt, sp_):
            nch = nt // 64
            c0 = t0 // 64
            hdeps = d_hT[t0 // 128:(t0 + nt) // 128]
            st = {}

            def s1():
                bv = nextbank8()
                for k in range(8):
                    S.op("pe", lambda k=k: pe.matmul(bank(bv)[:, 0:nt], lhsT=wB[:, grp, k, :], rhs=hT[:, k, t0:t0 + nt], start=(k == 0), stop=(k == 7)),
                         reads=[d_wB] + hdeps, writes=[PB[bv]], inc=(k == 7))
                if which == 0:
                    S.op("act", lambda: act.copy(out=stage[sp_][:, 0:nt], in_=bank(bv)[:, 0:nt]), reads=[PB[bv]], writes=[d_stage[sp_]])
                else:
                    S.op("act", lambda: act.activation(out=stage[sp_][:, 0:nt], in_=bank(bv)[:, 0:nt], func=AF.Silu), reads=[PB[bv]], writes=[d_stage[sp_]])

            def s2():
                bt_ = nextbank8()
                pk = bank_bf(bt_).rearrange("p (j c) -> p j c", c=128)
                for cc in range(nch):
                    S.op("pe", lambda cc=cc: pe.transpose(out=pk[0:64, cc, :], in_=stage[sp_][:, cc * 64:(cc + 1) * 64], identity=identB),
                         reads=[d_stage[sp_], d_const], writes=[PB[bt_]], inc=(cc == nch - 1))
                if which == 0:
                    S.op("dve", lambda: dve.tensor_copy(out=vtok[0:64, c0:c0 + nch, :], in_=pk[0:64, 0:nch, :]), reads=[PB[bt_]], writes=[d_vt[bi]])
                else:
                    lo = 4 if bi == 0 else 0
                    S.op("dve", lambda: dve.tensor_copy(out=gtok[0:64, c0 + lo - 4:c0 + nch - 4, :], in_=pk[0:64, lo:nch, :]), reads=[PB[bt_]], writes=[d_gt[bi]])
            return [s1, s2]

        items = []
        for which, grp in ((0, 1), (1, 4)):
            for bi, (t0, nt) in enumerate(BLKS):
                items.append(vg_item(which, grp, bi, t0, nt, bd[0] % 2))
                bd[0] += 1
        skew(items, newest_first=True)

        qbank = {}

        def gd_item(bi, t0, nt, d, p_):
            nch = nt // 64
            c0 = t0 // 64
            hdeps = d_hT[t0 // 128:(t0 + nt) // 128]
            T_sig, T_g, T_kk, T_G, T_s8, T_kg2 = bt_sig[p_], bt_g[p_], bt_kk[p_], bt_G[p_], bt_s8[p_], kg2[p_]
            D_ = d_bt[p_]
            G3 = T_G[:, 0:nt].rearrange("p (c l) -> p c l", l=64)
            eR = e1[d] if d == 0 else e2[d]
            eD = e2[d] if d == 0 else e1[d]
            sa = 1.0 if d == 0 else -1.0

            def s1():
                if d == 0:
                    bq = nextbank8()
                    qbank[bi] = bq
                    for k in range(8):
                        S.op("pe", lambda k=k: pe.matmul(bank(bq)[:, 0:nt], lhsT=wB[:, 0, k, :], rhs=hT[:, k, t0:t0 + nt], start=(k == 0), stop=(k == 7)),
                             reads=[d_wB] + hdeps, writes=[PB[bq]], inc=(k == 7))
                bf = nextbank8()
                for k in range(8):
                    S.op("pe", lambda k=k: pe.matmul(bank(bf)[:, 0:nt], lhsT=wB[:, 2 + d, k, :], rhs=hT[:, k, t0:t0 + nt], start=(k == 0), stop=(k == 7)),
                         reads=[d_wB] + hdeps, writes=[PB[bf]], inc=(k == 7))
                pe_keepwarm(8)
                S.op("act", lambda: act.activation(out=T_sig[:, 0:nt], in_=bank(bf)[:, 0:nt], func=AF.Exp, scale=-1.0), reads=[PB[bf]], writes=[D_["sig"]])
                S.op("act", lambda: act.activation(out=T_sig[:, 0:nt], in_=T_sig[:, 0:nt], func=AF.Ln, bias=oneB[:, 0:1]), reads=[d_B], writes=[D_["sig"]])
                S.op("act", lambda: act.activation(out=T_sig[:, 0:nt], in_=T_sig[:, 0:nt], func=AF.Exp, scale=-1.0), reads=[], writes=[D_["sig"]])
                S.op("act", lambda: act.activation(out=T_g[:, 0:nt], in_=T_sig[:, 0:nt], func=AF.Ln, scale=lbv[:, 1, d, h:h + 1], bias=lbv[:, 0, d, h:h + 1]),
                     reads=[D_["sig"], d_B], writes=[D_["g"]])
                S.op("dve", lambda: dve.tensor_scalar(out=T_kk[:, 0:nt], in0=T_sig[:, 0:nt], scalar1=lbv[:, 2, d, h:h + 1], scalar2=lbv[:, 1, d, h:h + 1], op0=ALU.mult, op1=ALU.add),
                     reads=[D_["sig"], d_B], writes=[D_["kk"]])
                S.op("dve", lambda: dve.tensor_tensor_scan(out=T_G[:, 0:nt], data0=smask[:, 0:nt], data1=T_g[:, 0:nt], initial=0.0, op0=ALU.mult, op1=ALU.add),
                     reads=[D_["g"], d_B], writes=[D_["G"]])

            def s2():
                bq = qbank[bi]
                S.op("dve", lambda: dve.tensor_copy(out=T_s8[:, 0, 0:nch].unsqueeze(2), in_=G3[:, :, 31:32]), reads=[D_["G"]], writes=[D_["s8"]])
                S.op("dve", lambda: dve.tensor_tensor(out=T_s8[:, 1, 0:nch].unsqueeze(2), in0=G3[:, :, 63:64], in1=G3[:, :, 31:32], op=ALU.subtract), reads=[D_["G"]], writes=[D_["s8"]])
                S.op("act", lambda: act.activation(out=eR[:, c0:c0 + nch], in_=T_s8[:, 0, 0:nch], func=AF.Exp), reads=[D_["s8"]], writes=[d_e[d][bi]])
                S.op("act", lambda: act.activation(out=eL[d][:, c0:c0 + nch].unsqueeze(2), in_=G3[:, :, 63:64], func=AF.Exp), reads=[D_["G"]], writes=[d_e[d][bi]])
                S.op("act", lambda: act.activation(out=eD[:, c0:c0 + nch], in_=T_s8[:, 1, 0:nch], func=AF.Exp), reads=[D_["s8"]], writes=[d_e[d][bi]])
                S.op("dve", lambda: dve.tensor_tensor(out=G3, in0=G3, in1=T_s8[:, 0, 0:nch].unsqueeze(2).to_broadcast([128, nch, 64]), op=ALU.subtract), reads=[D_["s8"]], writes=[D_["G"]])
                if d == 1:
                    S.op("dve", lambda: dve.tensor_tensor(out=T_G[:, 0:nt], in0=T_G[:, 0:nt], in1=T_g[:, 0:nt], op=ALU.subtract), reads=[D_["g"]], writes=[D_["G"]])
                S.op("act", lambda: act.activation(out=T_sig[:, 0:nt], in_=T_G[:, 0:nt], func=AF.Exp, scale=sa), reads=[D_["G"]], writes=[D_["sig"]])
                S.op("act", lambda: act.activation(out=T_g[:, 0:nt], in_=T_G[:, 0:nt], func=AF.Exp, scale=-sa), reads=[D_["G"]], writes=[D_["g"]])
                S.op("dve", lambda: dve.tensor_tensor(out=qg[d][:, t0:t0 + nt], in0=bank(bq)[:, 0:nt], in1=T_sig[:, 0:nt], op=ALU.mult), reads=[PB[bq], D_["sig"]], writes=[d_qg[d][bi]])
                S.op("pool", lambda: pool.tensor_tensor(out=T_g[:, 0:nt], in0=T_kk[:, 0:nt], in1=T_g[:, 0:nt], op=ALU.mult), reads=[D_["kk"]], writes=[D_["g"]])
                S.op("act", lambda: act.copy(out=kg[d][:, t0:t0 + nt], in_=T_g[:, 0:nt]), reads=[D_["g"]], writes=[d_kg[d][bi]])
                S.op("dve", lambda: dve.tensor_tensor(out=T_kg2[:, 0:nt].rearrange("p (c l) -> p c l", l=64), in0=T_g[:, 0:nt].rearrange("p (c l) -> p c l", l=64),
                                                      in1=e2[d][:, c0:c0 + nch].unsqueeze(2).to_broadcast([128, nch, 64]), op=ALU.mult),
                     reads=[D_["g"], d_e[d][bi]], writes=[D_["kk"]])

            def s3():
                for _ in range(8):
                    S.op("pe", lambda: pe.matmul(bank(7), lhsT=zerosB[:, 0:128], rhs=zerosB, start=True, stop=True, skip_group_check=True), reads=[d_const, D_["sig"]], writes=[PB[7]], inc=False)
                bt_ = nextbank8()
                pk = bank_bf(bt_).rearrange("p (j c) -> p j c", c=128)
                for cc in range(nch):
                    S.op("pe", lambda cc=cc: pe.transpose(out=pk[0:64, cc, :], in_=T_kg2[:, cc * 64:(cc + 1) * 64], identity=identB),
                         reads=[D_["kk"], d_const], writes=[PB[bt_]], inc=(cc == nch - 1))
                S.op("act", lambda: act.copy(out=kgtok[d][0:64, c0:c0 + nch, :], in_=pk[0:64, 0:nch, :]), reads=[PB[bt_]], writes=[d_kgt[d][bi]])
            return [s1, s2, s3]

        items = []
        for bi, (t0, nt) in enumerate(BLKS):
            for d in range(2):
                items.append(gd_item(bi, t0, nt, d, bd[0] % 2))
                bd[0] += 1
        skew(items)
        def slot_of(c):
            return c % 8

        for d in range(2):
            first = 0 if d == 0 else 3
            S.op("pool", lambda d=d, first=first: pool.memset(Sring[d][:, slot_of(first), :], 0.0), writes=[d_Sr[d][slot_of(first) // 4]])
        order = [list(range(36)), [3, 2, 1, 0] + list(range(35, 3, -1))]
        for g in range(9):
            for d in range(2):
                bu = (0, 1)[g % 2] if d == 0 else (2, 3)[g % 2]
                pu = bank(bu).rearrange("p (j c) -> p j c", c=128)
                cs = order[d][g * 4:g * 4 + 4]
                for jj, c in enumerate(cs):
                    S.op("pe", lambda jj=jj, c=c: pe.matmul(pu[:, jj, :], lhsT=kgtok[d][0:64, c, :], rhs=vtok[0:64, c, :], start=True, stop=True),
                         reads=[d_kgt[d][c // 8], d_vt[c // 8]], writes=[PB[bu]], inc=(jj == 3))
                for jj, c in enumerate(cs):
                    bi = c // 8
                    if d == 0:
                        cn = c + 1
                    else:
                        cn = 35 if c == 0 else c - 1
                    if (d == 0 and c == 35) or (d == 1 and c == 4):
                        continue
                    sp_, sn_ = slot_of(c), slot_of(cn)
                    S.op("dve", lambda c=c, sp_=sp_, sn_=sn_, jj=jj: dve.scalar_tensor_tensor(out=Sring[d][:, sn_, :], in0=Sring[d][:, sp_, :], scalar=eL[d][:, c:c + 1], in1=pu[:, jj, :], op0=ALU.mult, op1=ALU.add),
                         reads=[d_Sr[d][sp_ // 4], d_e[d][bi], PB[bu]], writes=[d_Sr[d][sn_ // 4]])
                if d == 0:
                    c0 = g * 4
                else:
                    c0 = (36 - g * 4) if g >= 1 else None
                    if c0 is not None and c0 > 32:
                        c0 = None
                if c0 is not None and c0 >= 4 and c0 + 3 <= 35:
                    sl0 = slot_of(c0)
                    S.op("pool", lambda c0=c0, sl0=sl0: pool.tensor_tensor(out=Smid[d][:, c0 - 4:c0, :], in0=Sring[d][:, sl0:sl0 + 4, :],
                                                                         in1=e1[d][:, c0:c0 + 4].unsqueeze(2).to_broadcast([128, 4, 128]), op=ALU.mult),
                         reads=[d_Sr[d][sl0 // 4], d_e[d][c0 // 8]], writes=[d_Sm[d][(c0 - 4) // 4]])
        S.op("pool", lambda: pool.tensor_tensor(out=Smid[1][:, 0:4, :], in0=Sring[1][:, 4:8, :], in1=e1[1][:, 4:8].unsqueeze(2).to_broadcast([128, 4, 128]), op=ALU.mult),
             reads=[d_Sr[1][1], d_e[1][0]], writes=[d_Sm[1][0]])
        if h < 3:
            load_wB(h + 1)
        tri4 = tri[0:64, :, :].unsqueeze(1).to_broadcast([64, 4, 2, 64])

        def o_s1(g):
            ba, bo = 4 + g % 2, 6 + g % 2
            pa = bank(ba)[0:64, :].rearrange("p (j d c) -> p j d c", d=2, c=64)
            po = bank(bo)[0:64, :].rearrange("p (j c) -> p j c", c=128)
            os_ = g % 2
            for jj in range(4):
                c = 4 + g * 4 + jj
                bi = c // 8
                for d in range(2):
                    S.op("pe", lambda jj=jj, c=c, d=d: pe.matmul(pa[:, jj, d, :], lhsT=kg[d][:, c * 64:(c + 1) * 64], rhs=qg[d][:, c * 64:(c + 1) * 64], start=True, stop=True, skip_group_check=True),
                         reads=[d_kg[d][bi], d_qg[d][bi]], writes=[PB[ba]], inc=(jj == 3 and d == 1))
            S.op("dve", lambda: dve.tensor_tensor(out=Asb[0:64, :, :, :], in0=pa, in1=tri4, op=ALU.mult), reads=[PB[ba], d_B], writes=[d_As])

        def o_s1b(g):
            ba, bo = 4 + g % 2, 6 + g % 2
            po = bank(bo)[0:64, :].rearrange("p (j c) -> p j c", c=128)
            os_ = g % 2
            first = True
            for jj in range(4):
                c = 4 + g * 4 + jj
                for d in range(2):
                    S.op("pe", lambda jj=jj, c=c, d=d, first=first: pe.matmul(po[:, jj, :], lhsT=Asb[0:64, jj, d, :], rhs=vtok[0:64, c, :], start=first, stop=False, skip_group_check=True),
                         reads=[d_As, d_vt[c // 8]], writes=[PB[bo]], inc=False)
                    first = False
            for jj in range(4):
                c = 4 + g * 4 + jj
                bi = c // 8
                for d in range(2):
                    last = (jj == 3 and d == 1)
                    S.op("pe", lambda jj=jj, c=c, d=d, last=last: pe.matmul(po[:, jj, :], lhsT=qg[d][:, c * 64:(c + 1) * 64], rhs=Smid[d][:, c - 4, :], start=False, stop=last, skip_group_check=True),
                         reads=[d_qg[d][bi], d_Sm[d][(c - 4) // 4]], writes=[PB[bo]], inc=last)
            S.op("act", lambda: act.copy(out=ot[os_][0:64, :, :], in_=po), reads=[PB[bo]], writes=[d_ot[os_], d_bt[os_]["sig"]])

        def o_s2(g):
            os_ = g % 2
            od_ = [d_ot[os_], d_bt[os_]["sig"]]
            for jj in range(4):
                cl = g * 4 + jj
                S.op("dve", lambda jj=jj, cl=cl: dve.scalar_tensor_tensor(out=ojunk[0:64, :], in0=ot[os_][0:64, jj, :], scalar=1.0, in1=ot[os_][0:64, jj, :], op0=ALU.mult, op1=ALU.mult, accum_out=hb_ss[0:64, 0, cl:cl + 1]),
                     reads=od_, writes=[d_hb])
            S.op("dve", lambda: dve.tensor_scalar(out=hb_ss[0:64, 1, g * 4:g * 4 + 4], in0=hb_ss[0:64, 0, g * 4:g * 4 + 4], scalar1=1.0 / 128, scalar2=EPS, op0=ALU.mult, op1=ALU.add), reads=[d_hb], writes=[d_hb])
            S.op("pool", lambda: pool.tensor_tensor(out=hb_ss[0:64, 2, g * 4:g * 4 + 4], in0=hb_ss[0:64, 1, g * 4:g * 4 + 4], in1=mhB[0:64, 0:1].to_broadcast([64, 4]), op=ALU.pow), reads=[d_hb, d_B], writes=[d_hb])
            for jj in range(4):
                cl = g * 4 + jj
                S.op("dve", lambda jj=jj, cl=cl: dve.scalar_tensor_tensor(out=hb_y[0:64, cl, :], in0=ot[os_][0:64, jj, :], scalar=hb_ss[0:64, 2, cl:cl + 1], in1=gtok[0:64, cl, :], op0=ALU.mult, op1=ALU.mult),
                     reads=od_ + [d_hb, d_gt[(cl + 4) // 8]], writes=[d_hby[g // 2]] + d_kgt[0])

        def o_s2b(g):
            if g % 2 == 1:
                grp = g // 2
                bt_ = grp % 4
                py = bank_bf(bt_)[:, 0:512].rearrange("p (j c) -> p j c", c=64)
                for jj in range(8):
                    c = grp * 8 + jj
                    S.op("pe", lambda jj=jj, c=c: pe.transpose(out=py[:, jj, :], in_=hb_y[0:64, c, :], identity=identB[0:64, 0:64]), reads=[d_hby[grp], d_const] + d_kgt[0], writes=[PB[bt_]], inc=(jj == 7))
                S.op("act", lambda: act.activation(out=yT[:, 4 + h, grp * 512:(grp + 1) * 512], in_=bank_bf(bt_)[:, 0:512], func=AF.Copy, scale=hgT[:, 0:1]), reads=[PB[bt_], d_B], writes=d_yT[4 + h][grp * 4:grp * 4 + 4])

        for g in range(10):
            if g < 8:
                o_s1(g)
            if 1 <= g <= 8:
                o_s2(g - 1)
            if g < 8:
                o_s1b(g)
            if 2 <= g <= 9:
                o_s2b(g - 2)
    if debug:
        S.dma("sp", lambda: sp.dma_start(out=dbg["yT0"], in_=yT), reads=[x for r in d_yT for x in r], writes=[], semdep=d_yT[0][0])
    S.barrier()

    X_OFF = P_TOP
    AR.top = X_OFF
    xnew = AR.f32(16, D)
    T2 = AR.top
    d_xnew = [Dep("xnew%d" % i) for i in range(16)]

    def out_phase(l, w_d, pre=None):
        AR.top = T2
        if pre is None:
            wo = AR.bf16(8, D)
            stg = [AR.f32(D) for _ in range(4)]
        xt = [AR.f32(D), AR.f32(D)]
        gtmp = [AR.f32(512), AR.f32(512)]
        d_gtmp = [Dep("gtmp0"), Dep("gtmp1")]
        d_wo, d_stg, d_xt = [Dep("wo%d" % k_) for k_ in range(8)], [Dep("stg%d" % k_) for k_ in range(4)], [Dep("oxt0"), Dep("oxt1")]
        def ld(k):
            S.dma("sp", lambda: sp.dma_start(out=stg[k % 4], in_=w_d[k * 128:(k + 1) * 128, :]), writes=[d_stg[k % 4]], semdep=d_stg[k % 4])

        if pre is None:
            for k in range(4):
                ld(k)
            for k in range(8):
                s = k % 4
                S.op("dve", lambda k=k, s=s: dve.tensor_tensor(out=wo[:, k, :], in0=stg[s], in1=gate_bc[l], op=ALU.mult), reads=[d_stg[s], d_gate[l]], writes=[d_wo[k]])
                if k + 4 < 8:
                    ld(k + 4)
        else:
            wo, d_wo = pre
        for i in range(16):
            s = i % 2
            if l == 0:
                S.dma("sp", lambda i=i, s=s: sp.dma_start(out=xt[s], in_=x_d[i * 128:(i + 1) * 128, :]), writes=[d_xt[s]], semdep=d_xt[s])
            for n in range(2):
                bp = nextbank()
                for k in range(8):
                    S.op("pe", lambda k=k, n=n, i=i: pe.matmul(bank(bp), lhsT=yT[:, k, i * 128:(i + 1) * 128], rhs=wo[:, k, n * 512:(n + 1) * 512], start=(k == 0), stop=(k == 7)),
                         reads=[d_wo[k], d_yT[k][i]], writes=[PB[bp]], inc=(k == 7))
                if l == 0:
                    S.op("dve", lambda n=n, i=i, s=s: dve.tensor_tensor(out=xnew[:, i, n * 512:(n + 1) * 512], in0=bank(bp), in1=xt[s][:, n * 512:(n + 1) * 512], op=ALU.add),
                         reads=[PB[bp], d_xt[s]], writes=[d_xnew[i]])
                elif pre is None:
                    S.op("dve", lambda n=n, i=i, s=s: dve.tensor_tensor(out=xt[s][:, n * 512:(n + 1) * 512], in0=bank(bp), in1=xnew[:, i, n * 512:(n + 1) * 512], op=ALU.add),
                         reads=[PB[bp], d_xnew[i]], writes=[d_xt[s]])
                else:
                    S.op("dve", lambda n=n: dve.tensor_tensor(out=gtmp[n], in0=bank(bp), in1=gate_bc[l][:, n * 512:(n + 1) * 512], op=ALU.mult),
                         reads=[PB[bp], d_gate[l]], writes=[d_gtmp[n]])
                    S.op("pool", lambda n=n, i=i, s=s: pool.tensor_tensor(out=xt[s][:, n * 512:(n + 1) * 512], in0=gtmp[n], in1=xnew[:, i, n * 512:(n + 1) * 512], op=ALU.add),
                         reads=[d_gtmp[n], d_xnew[i]], writes=[d_xt[s]])
            if l == 1:
                S.dma("sp", lambda i=i, s=s: sp.dma_start(out=out_d[i * 128:(i + 1) * 128, :], in_=xt[s]), reads=[d_xt[s]], writes=[], semdep=d_xt[s])
        return d_xt

    out_phase(0, ewout_d)
    if debug:
        S.dma("sp", lambda: sp.dma_start(out=dbg["xnew"], in_=xnew), reads=d_xnew, writes=[], semdep=d_xnew[0])
    S.barrier()

    def src1(i):
        return xnew[:, i, :], False, d_xnew[i]

    AR.top = T2
    wC = AR.bf16(3, 8, 512)
    L1_TMP = AR.top
    AR.top = L1_TMP + 6000
    wD = [AR.bf16(4, 8, 128), AR.bf16(4, 8, 128)]
    L1_END = AR.top
    d_wC = Dep("wC")
    dd_w = [Dep("wD0"), Dep("wD1")]

    def load_wD(j):
        for g in range(4):
            S.dma("pool", lambda g=g: pool.dma_start(
                out=wD[j % 2][:, g, :, :], in_=owin_d.rearrange("(k p) n -> p k n", p=128)[:, :, 1536 + g * 512 + j * 128:1536 + g * 512 + (j + 1) * 128]),
                writes=[dd_w[j % 2]], semdep=dd_w[j % 2])

    for g in range(3):
        for kk in range(2):
            S.dma("pool", lambda g=g, kk=kk: pool.dma_start(out=wC[:, g, kk * 4:(kk + 1) * 4, :], in_=owin_d.rearrange("(k p) n -> p k n", p=128)[:, kk * 4:(kk + 1) * 4, g * 512:(g + 1) * 512]),
                  writes=[d_wC], semdep=d_wC)
    load_wD(0)
    load_wD(1)
    norm_phase(1, 16, src1, L1_TMP)
    S.barrier()

    AR.top = L1_TMP
    wsF = AR.f32(4, 128)
    wsT = AR.bf16(4, 128)
    bsT = AR.f32(4)
    vgS = AR.f32(512)
    mhalf = AR.f32(1)
    c_gu = [AR.f32(512), AR.f32(512)]
    c_sg = [AR.f32(512), AR.f32(512)]
    c_gv = [AR.f32(512), AR.f32(512)]
    c_vn = [AR.bf16(512), AR.bf16(512)]
    c_y = [AR.bf16(512), AR.bf16(512)]
    c_junk = AR.f32(512)
    c_st = AR.f32(16, 4)
    assert AR.top <= L1_TMP + 6000, AR.top - L1_TMP
    d_C = Dep("Cconst")
    d_c = [{k: Dep("c%d_%s" % (p_, k)) for k in ("gu", "sg", "gv", "vn", "y")} for p_ in range(2)]
    d_cj, d_cst = Dep("c_junk"), [Dep("c_st%d" % i) for i in range(16)]
    S.dma("sp", lambda: sp.dma_start(out=wsF, in_=ws_d.rearrange("g t s -> t g s")), writes=[d_C], semdep=d_C)
    S.dma("sp", lambda: sp.dma_start(out=bsT, in_=bs_d.rearrange("g t -> t g"), allow_slow_non_contiguous=True), writes=[d_C], semdep=d_C)
    S.dma("sp", lambda: sp.dma_start(out=vgS, in_=vg_d.partition_broadcast(128)), writes=[d_C], semdep=d_C)
    S.op("pool", lambda: pool.memset(mhalf, -0.5), writes=[d_C])
    bw = nextbank8()
    pw = bank(bw).rearrange("p (g c) -> p g c", c=128)
    for g in range(4):
        S.op("pe", lambda g=g: pe.transpose(out=pw[:, g, :], in_=wsF[:, g, :], identity=identF), reads=[d_C, d_const], writes=[PB[bw]], inc=(g == 3))
    S.op("dve", lambda: dve.tensor_copy(out=wsT, in_=pw), reads=[PB[bw]], writes=[d_C])

    def c_item(i):
        p_ = i % 2
        Dc = d_c[p_]
        gu, sg, gv, vn, yy = c_gu[p_], c_sg[p_], c_gv[p_], c_vn[p_], c_y[p_]
        st = {}

        def s1():
            bu, bv, bg = nextbank8(), nextbank8(), nextbank8()
            st["bg"] = bg
            for g, bb in ((0, bu), (1, bv), (2, bg)):
                for k in range(8):
                    S.op("pe", lambda k=k, g=g, bb=bb: pe.matmul(bank(bb), lhsT=hT[:, k, i * 128:(i + 1) * 128], rhs=wC[:, g, k, :], start=(k == 0), stop=(k == 7)),
                         reads=[d_wC, d_hT[i]], writes=[PB[bb]], inc=(k == 7))
            S.op("act", lambda: act.activation(out=gu, in_=bank(bu), func=AF.Gelu), reads=[PB[bu]], writes=[Dc["gu"]])
            S.op("act", lambda: act.activation(out=gv, in_=bank(bv), func=AF.Gelu), reads=[PB[bv]], writes=[Dc["gv"]])
            S.op("act", lambda: act.activation(out=sg, in_=bank(bg), func=AF.Tanh, scale=0.5), reads=[PB[bg]], writes=[Dc["sg"]])
            S.op("dve", lambda: dve.tensor_scalar(out=sg, in0=sg, scalar1=0.5, scalar2=0.5, op0=ALU.mult, op1=ALU.add), reads=[], writes=[Dc["sg"]])
            S.op("dve", lambda: dve.tensor_tensor(out=sg, in0=bank(bg), in1=sg, op=ALU.mult), reads=[PB[bg]], writes=[Dc["sg"]])

        def s2():
            S.op("pool", lambda: pool.tensor_tensor(out=gu, in0=gu, in1=sg, op=ALU.mult), reads=[Dc["sg"]], writes=[Dc["gu"]])
            S.op("dve", lambda: dve.scalar_tensor_tensor(out=c_junk, in0=gv, scalar=1.0, in1=gv, op0=ALU.mult, op1=ALU.mult, accum_out=c_st[:, i, 0:1]), reads=[Dc["gv"]], writes=[d_cj, d_cst[i]])
            S.op("dve", lambda: dve.tensor_scalar(out=c_st[:, i, 1:2], in0=c_st[:, i, 0:1], scalar1=1.0 / 512, scalar2=EPS, op0=ALU.mult, op1=ALU.add), reads=[], writes=[d_cst[i]])
            S.op("pool", lambda: pool.tensor_tensor(out=c_st[:, i, 2:3], in0=c_st[:, i, 1:2], in1=mhalf, op=ALU.pow), reads=[d_C], writes=[d_cst[i]])
            S.op("dve", lambda: dve.scalar_tensor_tensor(out=vn, in0=gv, scalar=c_st[:, i, 2:3], in1=vgS, op0=ALU.mult, op1=ALU.mult), reads=[Dc["gv"], d_cst[i], d_C], writes=[Dc["vn"]])

        def s3():
            bs_ = nextbank8()
            ps = bank(bs_).rearrange("p (g c) -> p g c", c=128)
            for g in range(4):
                S.op("pe", lambda g=g: pe.matmul(ps[:, g, :], lhsT=wsT[:, g, :], rhs=vn[:, g * 128:(g + 1) * 128], start=True, stop=True), reads=[d_C, Dc["vn"]], writes=[PB[bs_]], inc=(g == 3))
            for g in range(4):
                S.op("dve", lambda g=g: dve.scalar_tensor_tensor(out=yy[:, g * 128:(g + 1) * 128], in0=ps[:, g, :], scalar=bsT[:, g:g + 1], in1=gu[:, g * 128:(g + 1) * 128], op0=ALU.add, op1=ALU.mult),
                     reads=[PB[bs_], d_C, Dc["gu"]], writes=[Dc["y"]])

        def s3b():
            bt_ = nextbank8()
            py = bank_bf(bt_)[:, 0:512].rearrange("p (g c) -> p g c", c=128)
            for g in range(4):
                S.op("pe", lambda g=g: pe.transpose(out=py[:, g, :], in_=yy[:, g * 128:(g + 1) * 128], identity=identB), reads=[Dc["y"], d_const], writes=[PB[bt_]], inc=(g == 3))
            S.op("act", lambda: act.copy(out=yT[:, 0:4, i * 128:(i + 1) * 128], in_=py), reads=[PB[bt_]], writes=[d_yT[g][i] for g in range(4)])
        return [s1, s2, s3, s3b]

    c_items = [c_item(i) for i in range(16)]
    for t in range(16 + 2):
        if 0 <= t - 2 < 16:
            c_items[t - 2][2]()
        if 0 <= t - 1 < 16:
            c_items[t - 1][1]()
        if t < 16:
            c_items[t][0]()
        if 0 <= t - 2 < 16:
            c_items[t - 2][3]()
    S.barrier()

    AR.top = T2
    cwT = AR.f32(4, 3)
    zb = AR.f32(SEQ + 2)
    bsg = AR.f32(SEQ)
    cvt = AR.f32(SEQ)
    d_csb = [AR.f32(512), AR.f32(512)]
    _sgd = AR.f32(512)
    d_sgd = [_sgd, _sgd]
    wo1 = AR.bf16(8, D)
    assert AR.top <= L1_TMP + 6000, (AR.top, L1_TMP)
    dd_c = Dep("cw")
    dd_z = [Dep("z%d" % b_) for b_ in range(4)]
    dd_bsg = [Dep("bsg%d" % b_) for b_ in range(4)]
    dd_cv = [Dep("cvt%d" % b_) for b_ in range(4)]
    _dsgd = Dep("sgd")
    dd_csb, dd_sgd = [Dep("csb0"), Dep("csb1")], [_dsgd, _dsgd]
    d_wo1 = [Dep("wo1_%d" % k_) for k_ in range(8)]
    for kk_ in range(2):
        S.dma("pool", lambda kk_=kk_: pool.dma_start(out=wo1[:, kk_ * 4:(kk_ + 1) * 4, :], in_=owout_d.rearrange("(k p) n -> p k n", p=128)[:, kk_ * 4:(kk_ + 1) * 4, :]),
              writes=d_wo1[kk_ * 4:(kk_ + 1) * 4], semdep=d_wo1[kk_ * 4])
    for w_ in range(3):
        S.dma("sp", lambda w_=w_: sp.dma_start(out=cwT[:, :, w_], in_=cw_d[w_].rearrange("(j p) -> p j", p=128), allow_slow_non_contiguous=True), writes=[dd_c], semdep=dd_c)
    S.op("pool", lambda: pool.memset(zb, 0.0), writes=dd_z)
    for j in range(4):
        slot = j % 2

        def d_item(b, p_):
            def s1():
                b1, b2 = nextbank8(), nextbank8()
                for g, bk in ((1, b1), (2, b2)):
                    for k in range(8):
                        S.op("pe", lambda k=k, g=g, bk=bk: pe.matmul(bank(bk), lhsT=wD[slot][:, g, k, :], rhs=hT[:, k, b * 512:(b + 1) * 512], start=(k == 0), stop=(k == 7)),
                             reads=[dd_w[slot]] + d_hT[b * 4:b * 4 + 4], writes=[PB[bk]], inc=(k == 7))
                S.op("act", lambda: act.copy(out=d_csb[p_], in_=bank(b1)), reads=[PB[b1]], writes=[dd_csb[p_]])
                S.op("dve", lambda: dve.tensor_tensor(out=zb[:, 1 + b * 512:1 + (b + 1) * 512], in0=bank(b2), in1=d_csb[p_], op=ALU.mult), reads=[PB[b2], dd_csb[p_]], writes=[dd_z[b]])

            def s2():
                b3, b4 = nextbank8(), nextbank8()
                for g, bk in ((3, b3), (0, b4)):
                    for k in range(8):
                        S.op("pe", lambda k=k, g=g, bk=bk: pe.matmul(bank(bk), lhsT=wD[slot][:, g, k, :], rhs=hT[:, k, b * 512:(b + 1) * 512], start=(k == 0), stop=(k == 7)),
                             reads=[dd_w[slot]] + d_hT[b * 4:b * 4 + 4], writes=[PB[bk]], inc=(k == 7))
                S.op("act", lambda: act.activation(out=d_sgd[p_], in_=bank(b3), func=AF.Silu), reads=[PB[b3]], writes=[dd_sgd[p_]])
                S.op("dve", lambda: dve.tensor_tensor(out=bsg[:, b * 512:(b + 1) * 512], in0=bank(b4), in1=d_sgd[p_], op=ALU.mult), reads=[PB[b4], dd_sgd[p_]], writes=[dd_bsg[b]])
            return [s1, s2]

        d_items = [d_item(b, b % 2) for b in range(4)]

        def conv_blk(b, j=j):
            lo, hi = b * 512, (b + 1) * 512
            zdeps = dd_z[max(b - 1, 0):min(b + 2, 4)]
            S.op("act", lambda: act.activation(out=cvt[:, lo:hi], in_=zb[:, lo:hi], func=AF.Copy, scale=cwT[:, j, 0:1]), reads=zdeps + [dd_c], writes=[dd_cv[b]])
            S.op("dve", lambda: dve.scalar_tensor_tensor(out=cvt[:, lo:hi], in0=zb[:, lo + 1:hi + 1], scalar=cwT[:, j, 1:2], in1=cvt[:, lo:hi], op0=ALU.mult, op1=ALU.add), reads=zdeps + [dd_c], writes=[dd_cv[b]])
            S.op("dve", lambda: dve.scalar_tensor_tensor(out=cvt[:, lo:hi], in0=zb[:, lo + 2:hi + 2], scalar=cwT[:, j, 2:3], in1=cvt[:, lo:hi], op0=ALU.mult, op1=ALU.add), reads=zdeps + [dd_c], writes=[dd_cv[b]])
            S.op("pool", lambda: pool.tensor_tensor(out=yT[:, 4 + j, lo:hi], in0=cvt[:, lo:hi], in1=bsg[:, lo:hi], op=ALU.mult), reads=[dd_cv[b], dd_bsg[b]], writes=d_yT[4 + j][b * 4:b * 4 + 4])

        for t in range(5):
            if t >= 1:
                d_items[t - 1][1]()
            if t < 4:
                d_items[t][0]()
            if t == 4 and j + 2 < 4:
                load_wD(j + 2)
            if t >= 1:
                conv_blk(t - 1)
    if debug:
        S.dma("sp", lambda: sp.dma_start(out=dbg["yT1"], in_=yT), reads=[x for r in d_yT for x in r], writes=[], semdep=d_yT[0][0])
    S.barrier()

    d_fin = out_phase(1, owout_d, pre=(wo1, d_wo1))
    S.barrier()
    return nc


_CONST = {}


def _consts():
    if _CONST:
        return _CONST
    f32 = np.float32
    ident = np.eye(128, dtype=f32)
    perm = np.zeros((128, 128), f32)
    for m in range(128):
        partner = m + 32 if (m % 64) < 32 else m - 32
        perm[partner, m] = 1.0
    bones = np.zeros((128, 128), f32)
    bones[0:64, 0:64] = 1.0
    bones[64:128, 64:128] = 1.0
    rows = SEQ // 64
    row = np.repeat(np.arange(rows, dtype=f32), 64)
    col = np.tile(np.arange(64, dtype=f32), rows)
    n_freq = 16
    inv = (f32(10000.0) ** (-np.arange(n_freq, dtype=f32) / f32(n_freq))).astype(f32)
    ang = np.concatenate([row[:, None] * inv, col[:, None] * inv], axis=-1).astype(f32)
    cos = np.cos(ang).astype(f32).T
    sin = np.sin(ang).astype(f32).T
    cosT = np.concatenate([cos, cos, cos, cos], axis=0)
    sinT = np.concatenate([-sin, sin, -sin, sin], axis=0)
    tri = np.zeros((64, 2, 64), f32)
    s_idx = np.arange(64)[:, None]
    t_idx = np.arange(64)[None, :]
    tri[:, 0, :] = (s_idx <= t_idx)
    tri[:, 1, :] = (s_idx >= t_idx)
    smask = np.ones((128, 512), f32)
    smask[:, ::64] = 0.0
    _CONST.update(identF=ident, perm=perm, bones=bones, cosT=np.ascontiguousarray(cosT), sinT=np.ascontiguousarray(sinT), tri=tri, smask=smask)
    return _CONST


def make_in_maps(x, c, ctx, c_ctx, norm_gain, ada_w, ada_b, even_w_in, even_w_out, attn_qk_gain,
                 attn_lambda, attn_subln_gain, hgrn_lb_logits, hgrn_norm_gain, odd_w_in, odd_w_out,
                 gmlp_v_gain, gmlp_w_s, gmlp_b_s, conv_w):
    f = lambda a: np.ascontiguousarray(np.asarray(a, dtype=np.float32))
    shared = dict(
        norm_gain=f(norm_gain), ada_w=f(ada_w), ada_b=f(ada_b), even_w_in=f(even_w_in)[0], even_w_out=f(even_w_out)[0],
        qk_gain=f(attn_qk_gain)[0], attn_lambda=f(attn_lambda)[0].reshape(256), subln=f(attn_subln_gain)[0],
        lb_logits=f(hgrn_lb_logits), hgrn_g=f(hgrn_norm_gain)[0], odd_w_in=f(odd_w_in)[0], odd_w_out=f(odd_w_out)[0],
        v_gain=f(gmlp_v_gain)[0], w_s=f(gmlp_w_s)[0], b_s=f(gmlp_b_s)[0], conv_w=f(conv_w)[0])
    shared.update(_consts())
    x = f(x)
    c = f(c)
    ctx = f(ctx)
    c_ctx = f(c_ctx)
    maps = []
    for b in range(8):
        m = dict(shared)
        m["x"] = x[b]
        m["ctx"] = ctx[b]
        m["cvec"] = np.ascontiguousarray(np.stack([c[b], c_ctx], axis=0))
        maps.append(m)
    return maps


def kernel(**inputs):
    maps = make_in_maps(**inputs)
    nc = build(debug=False)
    res = run_bass_kernel_spmd(nc, maps, core_ids=list(range(8)))
    return np.stack([np.asarray(r["out"], dtype=np.float32) for r in res.results], axis=0)
```

```python
import numpy as np
import concourse.bass as bass
import concourse.mybir as mybir
from concourse.bass_utils import run_bass_kernel_spmd
from concourse.alu_op_type import AluOpType as ALU

F32 = mybir.dt.float32
BF16 = mybir.dt.bfloat16
AF = mybir.ActivationFunctionType

D = 1024
SEQ = 2048
CTX = 256
NT = SEQ + CTX
EPS = 1e-6
LAM_INIT = 0.8 - 0.6 * 1.0


class Dep:
    __slots__ = ("name", "w", "r", "dsem", "dcnt", "excl")

    def __init__(self, name, excl=False):
        self.name = name
        self.excl = excl
        self.w = None
        self.r = []
        self.dsem = None
        self.dcnt = 0


class Sched:
    ENG = ("pe", "act", "dve", "pool", "sp")

    def __init__(self, nc):
        self.nc = nc
        self.engs = {"pe": nc.tensor, "act": nc.scalar, "dve": nc.vector, "pool": nc.gpsimd, "sp": nc.sync}
        self.cnt = {e: 0 for e in self.ENG}
        self.sem = {}
        self.waited = {e: {} for e in self.ENG}
        self.dsems = []
        self.nops = 0
        for e in self.ENG:
            self.sem[e] = nc.alloc_semaphore("s_" + e)

    def _waits(self, eng, reads, writes):
        need = {}

        def add(p):
            if p is None:
                return
            s, v = p
            k = id(s)
            if k not in need or need[k][1] < v:
                need[k] = (s, v)

        for d in reads:
            add(d.w)
        for d in writes:
            add(d.w)
            for p in d.r:
                add(p)
        wd = self.waited[eng]
        own = self.sem[eng]
        engine = self.engs[eng]
        for k, (s, v) in need.items():
            if s is own and (eng == "pe" or v > self.cnt[eng]):
                continue
            if wd.get(k, 0) >= v:
                continue
            wd[k] = v
            engine.wait_ge(s, v)

    def op(self, eng, fn, reads=(), writes=(), inc=True):
        ex = [d for d in reads if d.excl]
        if ex:
            reads = [d for d in reads if not d.excl]
            writes = list(writes) + ex
        self._waits(eng, reads, writes)
        val = self.cnt[eng] + 1
        ins = fn()
        self.nops += 1
        if inc:
            self.cnt[eng] = val
            ins.then_inc(self.sem[eng], 1)
        tok = (self.sem[eng], val)
        for d in reads:
            d.r.append(tok)
            if len(d.r) > 64:
                d.r = self._compact(d.r)
        for d in writes:
            d.w = tok
            d.r = []

    @staticmethod
    def _compact(lst):
        best = {}
        for s, v in lst:
            k = id(s)
            if k not in best or best[k][1] < v:
                best[k] = (s, v)
        return list(best.values())

    def dma(self, eng, fn, reads=(), writes=(), semdep=None):
        self._waits(eng, reads, writes)
        d0 = semdep
        if d0.dsem is None:
            d0.dsem = self.nc.alloc_semaphore("d%d_%s" % (len(self.dsems), d0.name))
            self.dsems.append(d0)
        d0.dcnt += 16
        ins = fn()
        ins.then_inc(d0.dsem, 16)
        tok = (d0.dsem, d0.dcnt)
        for d in reads:
            d.r.append(tok)
        for d in writes:
            d.w = tok
            d.r = []

    def wait_all(self, eng, deps):
        self._waits(eng, (), deps)

    def barrier(self):
        for e in self.ENG:
            engine = self.engs[e]
            wd = self.waited[e]
            for f in self.ENG:
                if f == e or self.cnt[f] == 0:
                    continue
                s = self.sem[f]
                if wd.get(id(s), 0) >= self.cnt[f]:
                    continue
                wd[id(s)] = self.cnt[f]
                engine.wait_ge(s, self.cnt[f])
            for d0 in self.dsems:
                if wd.get(id(d0.dsem), 0) >= d0.dcnt:
                    continue
                wd[id(d0.dsem)] = d0.dcnt
                engine.wait_ge(d0.dsem, d0.dcnt)


def skew(items, newest_first=False):
    nst = max(len(it) for it in items)
    for t in range(len(items) + nst - 1):
        for s_ in (range(nst) if newest_first else reversed(range(nst))):
            i = t - s_
            if 0 <= i < len(items) and s_ < len(items[i]):
                items[i][s_]()


class Arena:
    def __init__(self, ap_f32):
        self.ap = ap_f32
        self.top = 0
        self.n = ap_f32.shape[1]

    @staticmethod
    def _view(v, shape):
        if len(shape) == 1:
            return v
        names = ["d%d" % i for i in range(len(shape))]
        pat = "p (" + " ".join(names) + ") -> p " + " ".join(names)
        kw = {names[i]: int(shape[i]) for i in range(1, len(shape))}
        return v.rearrange(pat, **kw)

    def f32(self, *shape):
        n = int(np.prod(shape))
        off = self.top
        self.top += n
        assert self.top <= self.n, ("arena overflow", self.top, self.n)
        return self._view(self.ap[:, off:off + n], shape)

    def bf16(self, *shape):
        n = int(np.prod(shape))
        nw = (n + 1) // 2
        off = self.top
        self.top += nw
        assert self.top <= self.n, ("arena overflow", self.top, self.n)
        return self._view(self.ap[:, off:off + nw].bitcast(BF16)[:, 0:n], shape)


def build(debug=False):
    nc = bass.Bass("TRN2", target_bir_lowering=False)

    def din(name, shape, dt=F32):
        return nc.dram_tensor(name, list(shape), dt, kind="ExternalInput").ap()

    x_d = din("x", [SEQ, D])
    ctx_d = din("ctx", [CTX, D])
    cvec_d = din("cvec", [2, D])
    ng_d = din("norm_gain", [2, D])
    adaw_d = din("ada_w", [2, D, 3 * D])
    adab_d = din("ada_b", [2, 3 * D])
    ewin_d = din("even_w_in", [D, 4608])
    ewout_d = din("even_w_out", [D, D])
    qkg_d = din("qk_gain", [2, 64])
    lam_d = din("attn_lambda", [256])
    subln_d = din("subln", [128])
    lbl_d = din("lb_logits", [2, 2, 512])
    hg_d = din("hgrn_g", [128])
    owin_d = din("odd_w_in", [D, 3584])
    owout_d = din("odd_w_out", [D, D])
    vg_d = din("v_gain", [512])
    ws_d = din("w_s", [4, 128, 128])
    bs_d = din("b_s", [4, 128])
    cw_d = din("conv_w", [3, 512])
    identF_d = din("identF", [128, 128])
    perm_d = din("perm", [128, 128])
    bones_d = din("bones", [128, 128])
    cos_d = din("cosT", [128, SEQ])
    sin_d = din("sinT", [128, SEQ])
    tri_d = din("tri", [64, 2, 64])
    smask_d = din("smask", [128, 512])
    out_d = nc.dram_tensor("out", [SEQ, D], F32, kind="ExternalOutput").ap()
    dbg = {}
    if debug:
        dbg["hT0"] = nc.dram_tensor("dbg_hT0", [128, 8, NT], BF16, kind="ExternalOutput").ap()
        dbg["yT0"] = nc.dram_tensor("dbg_yT0", [128, 8, SEQ], BF16, kind="ExternalOutput").ap()
        dbg["xnew"] = nc.dram_tensor("dbg_xnew", [128, 16, D], F32, kind="ExternalOutput").ap()
        dbg["yT1"] = nc.dram_tensor("dbg_yT1", [128, 8, SEQ], BF16, kind="ExternalOutput").ap()
        dbg["mods"] = nc.dram_tensor("dbg_mods", [128, 2, 3, 8, 2], F32, kind="ExternalOutput").ap()

    S = Sched(nc)
    E = nc
    NW = (nc.sbuf_bytes_remaining - 2048) // 4
    arena_t = nc.alloc_sbuf_tensor("arena", [128, NW], F32).ap()
    AR = Arena(arena_t)
    psum = nc.alloc_psum_tensor("psum", [128, 8, 512], F32).ap()
    PB = [Dep("pb%d" % i, excl=True) for i in range(8)]

    def bank(i):
        return psum[:, i, :]

    def bank_bf(i):
        return psum[:, i, :].bitcast(BF16)

    hT = AR.bf16(8, NT)
    yT = AR.bf16(8, SEQ)
    identF = AR.f32(128)
    identB = AR.bf16(128)
    zerosB = AR.bf16(512)
    gate_bc = [AR.f32(D), AR.f32(D)]
    modsc = AR.f32(2, 3, 8, 2)
    small = AR.f32(64)
    P_TOP = AR.top
    d_hT = [Dep("hT%d" % i) for i in range(18)]
    d_yT = [[Dep("yT%d_%d" % (k, i)) for i in range(16)] for k in range(8)]
    d_const = Dep("const")
    d_mods = Dep("mods")
    d_gate = [Dep("gate0"), Dep("gate1")]
    d_small = Dep("small")

    sp, act, dve, pool, pe = nc.sync, nc.scalar, nc.vector, nc.gpsimd, nc.tensor

    S.dma("sp", lambda: sp.dma_start(out=identF, in_=identF_d), writes=[d_const], semdep=d_const)
    S.op("dve", lambda: dve.tensor_copy(out=identB, in_=identF), reads=[d_const], writes=[d_const])
    S.op("pool", lambda: pool.memset(zerosB, 0.0), writes=[d_const])

    AR.top = P_TOP
    cvT = AR.f32(8, 2)
    csT = AR.bf16(8, 2)
    Rrow = AR.f32(3 * D)
    adab = AR.f32(3 * D)
    gT = AR.f32(2, 8)
    onesr = AR.f32(128)
    wada = [AR.f32(3 * D) for _ in range(3)]
    wadab = [AR.bf16(3 * D), AR.bf16(3 * D)]
    d_cv, d_R, d_adab, d_gT, d_ones = Dep("cv"), Dep("R"), Dep("adab"), Dep("gT"), Dep("ones")
    d_wada = [[Dep("wada%d_%d" % (i, j)) for j in range(3)] for i in range(3)]
    d_wadab = [[Dep("wadab%d_%d" % (i, j)) for j in range(4)] for i in range(2)]

    for r in range(2):
        S.dma("sp", lambda r=r: sp.dma_start(out=cvT[:, :, r], in_=cvec_d[r].rearrange("(k p) -> p k", p=128), allow_slow_non_contiguous=True), writes=[d_cv], semdep=d_cv)
        S.dma("sp", lambda r=r: sp.dma_start(out=gT[:, r, :], in_=ng_d[r].rearrange("(k p) -> p k", p=128), allow_slow_non_contiguous=True), writes=[d_gT], semdep=d_gT)
    S.op("act", lambda: act.activation(out=csT, in_=cvT, func=AF.Silu), reads=[d_cv], writes=[d_cv])
    S.op("pool", lambda: pool.memset(onesr, 1.0), writes=[d_ones])
    wi = 0
    for l in range(2):
        S.dma("sp", lambda l=l: sp.dma_start(out=adab[0:2, :], in_=adab_d[l].partition_broadcast(2)), writes=[d_adab], semdep=d_adab)
        for k in range(8):
            slot = wi % 3
            bs_ = wi % 2
            wi += 1
            for hh in range(3):
                qn_ = ("sp", "pool", "act")[hh]
                qe_ = (sp, pool, act)[hh]
                S.dma(qn_, lambda l=l, k=k, slot=slot, hh=hh, qe_=qe_: qe_.dma_start(
                    out=wada[slot][:, hh * 1024:(hh + 1) * 1024], in_=adaw_d[l][k * 128:(k + 1) * 128, hh * 1024:(hh + 1) * 1024]),
                    writes=[d_wada[slot][hh]], semdep=d_wada[slot][hh])
            S.op("dve", lambda slot=slot, bs_=bs_: dve.tensor_copy(out=wadab[bs_][:, 0:1024], in_=wada[slot][:, 0:1024]), reads=[d_wada[slot][0]], writes=[d_wadab[bs_][0]])
            S.op("act", lambda slot=slot, bs_=bs_: act.copy(out=wadab[bs_][:, 1024:1536], in_=wada[slot][:, 1024:1536]), reads=[d_wada[slot][1]], writes=[d_wadab[bs_][1]])
            S.op("act", lambda slot=slot, bs_=bs_: act.copy(out=wadab[bs_][:, 1536:2560], in_=wada[slot][:, 1536:2560]), reads=[d_wada[slot][1], d_wada[slot][2]], writes=[d_wadab[bs_][2]])
            S.op("pool", lambda slot=slot, bs_=bs_: pool.tensor_copy(out=wadab[bs_][:, 2560:3072], in_=wada[slot][:, 2560:3072]), reads=[d_wada[slot][2]], writes=[d_wadab[bs_][3]])
            for n in range(6):
                part = (0, 0, 1, 2, 2, 3)[n]
                S.op("pe", lambda k=k, n=n, bs_=bs_: pe.matmul(bank(n)[0:2, :], lhsT=csT[:, k, :], rhs=wadab[bs_][:, n * 512:(n + 1) * 512], start=(k == 0), stop=(k == 7)),
                     reads=[d_cv, d_wadab[bs_][part]], writes=[PB[n]])
        for n in range(6):
            S.op("dve", lambda n=n: dve.tensor_tensor(out=Rrow[0:2, n * 512:(n + 1) * 512], in0=bank(n)[0:2, :], in1=adab[0:2, n * 512:(n + 1) * 512], op=ALU.add),
                 reads=[PB[n], d_adab], writes=[d_R])
        pt = bank(6)[:, 0:32].rearrange("p (j r) -> p j r", r=2)
        for j in range(16):
            S.op("pe", lambda j=j: pe.transpose(out=pt[:, j, :], in_=Rrow[0:2, j * 128:(j + 1) * 128], identity=identF[0:2, 0:2]),
                 reads=[d_R, d_const], writes=[PB[6]], inc=(j == 15))
        S.op("dve", lambda l=l: dve.tensor_copy(out=modsc[:, l, 0, :, :], in_=pt[:, 0:8, :]), reads=[PB[6]], writes=[d_mods])
        S.op("dve", lambda l=l: dve.tensor_copy(out=modsc[:, l, 2, :, :], in_=pt[:, 8:16, :]), reads=[PB[6]], writes=[d_mods])
        for r in range(2):
            S.op("dve", lambda l=l, r=r: dve.scalar_tensor_tensor(out=modsc[:, l, 1, :, r], in0=modsc[:, l, 2, :, r], scalar=1.0, in1=gT[:, l, :], op0=ALU.add, op1=ALU.mult),
                 reads=[d_mods, d_gT], writes=[d_mods])
        for n in range(2):
            S.op("pe", lambda n=n: pe.matmul(bank(2 + n), lhsT=onesr[0:1, :], rhs=Rrow[0:1, 2048 + n * 512:2048 + (n + 1) * 512], start=True, stop=True),
                 reads=[d_R, d_ones], writes=[PB[2 + n]])
            S.op("act", lambda n=n, l=l: act.copy(out=gate_bc[l][:, n * 512:(n + 1) * 512], in_=bank(2 + n)), reads=[PB[2 + n]], writes=[d_gate[l]])
    if debug:
        S.dma("sp", lambda: sp.dma_start(out=dbg["mods"], in_=modsc), reads=[d_mods], writes=[], semdep=d_mods)
    S.barrier()

    def norm_phase(l, ntiles, src_fn, top):
        AR.top = top
        nxt = 4 if l == 0 else 0
        xt = [AR.f32(D) for _ in range(nxt)]
        xn = [AR.f32(D), AR.f32(D)]
        junk = AR.bf16(D)
        stat = AR.f32(3, 18)
        d_xt = [Dep("xt%d" % i_) for i_ in range(nxt)]
        d_xn = [Dep("xn0"), Dep("xn1")]
        d_junk, d_stat = Dep("junk"), [Dep("stat%d" % i) for i in range(18)]

        def item(i):
            s = i % 2
            src, is_ctx, sdep = src_fn(i)
            b0 = 4 + 2 * (i % 2)
            pt = psum[:, b0:b0 + 2, :].rearrange("p b (j c) -> p (b j) c", c=128)
            r = 1 if is_ctx else 0
            if sdep is None:
                xin, xdep = xt[i % 4], d_xt[i % 4]
            else:
                xin, xdep = src, sdep

            def s0():
                if sdep is None:
                    S.dma("sp", lambda: sp.dma_start(out=xt[i % 4], in_=src), writes=[d_xt[i % 4]], semdep=d_xt[i % 4])

            def s1():
                S.op("act", lambda: act.activation(out=junk, in_=xin, func=AF.Square, accum_out=stat[:, 0, i:i + 1]), reads=[xdep], writes=[d_junk, d_stat[i]])
                S.op("act", lambda: act.activation(out=stat[:, 1, i:i + 1], in_=stat[:, 0, i:i + 1], func=AF.Sqrt, scale=1.0 / D, bias=EPS), reads=[d_stat[i]], writes=[d_stat[i]])
                S.op("dve", lambda: dve.reciprocal(out=stat[:, 2, i:i + 1], in_=stat[:, 1, i:i + 1]), reads=[d_stat[i]], writes=[d_stat[i]])

            def s2():
                S.op("dve", lambda: dve.tensor_scalar(out=xn[s], in0=xin, scalar1=stat[:, 2, i:i + 1], scalar2=None, op0=ALU.mult), reads=[xdep, d_stat[i]], writes=[d_xn[s]])
                for k in range(8):
                    S.op("pe", lambda k=k: pe.transpose(out=pt[:, k, :], in_=xn[s][:, k * 128:(k + 1) * 128], identity=identF),
                         reads=[d_xn[s], d_const], writes=[PB[b0 + k // 4]], inc=(k == 7))
                for _ in range(6):
                    S.op("pe", lambda: pe.matmul(bank(3), lhsT=zerosB[:, 0:128], rhs=zerosB, start=True, stop=True, skip_group_check=True), reads=[d_const], writes=[PB[3]], inc=False)

            def s3():
                for k in range(8):
                    if k < 4:
                        S.op("act", lambda k=k: act.activation(out=hT[:, k, i * 128:(i + 1) * 128], in_=pt[:, k, :], func=AF.Identity,
                                                               scale=modsc[:, l, 1, k, r:r + 1], bias=modsc[:, l, 0, k, r:r + 1]),
                             reads=[PB[b0 + k // 4], d_mods], writes=[d_hT[i]])
                    else:
                        S.op("dve", lambda k=k: dve.tensor_scalar(out=hT[:, k, i * 128:(i + 1) * 128], in0=pt[:, k, :],
                                                                  scalar1=modsc[:, l, 1, k, r:r + 1], scalar2=modsc[:, l, 0, k, r:r + 1], op0=ALU.mult, op1=ALU.add),
                             reads=[PB[b0 + k // 4], d_mods], writes=[d_hT[i]])
            return [s0, s1, s2, s3]

        skew([item(i) for i in range(ntiles)], newest_first=True)

    def src0(i):
        if i < 2:
            return ctx_d[i * 128:(i + 1) * 128, :], True, None
        return x_d[(i - 2) * 128:(i - 1) * 128, :], False, None


    AR.top = P_TOP
    cosT = AR.f32(SEQ)
    sinT = AR.f32(SEQ)
    permS = AR.f32(128)
    bonesS = AR.f32(128)
    gqk = AR.f32(2)
    lamb = AR.f32(256)
    lamt = AR.f32(8)
    gS = AR.f32(128)
    epsT = AR.f32(1)
    mhA = AR.f32(1)
    wA = [AR.bf16(4, 8, 128), AR.bf16(4, 8, 128)]
    qT = [AR.bf16(SEQ), AR.bf16(SEQ)]
    kT0 = [AR.bf16(NT), AR.bf16(NT)]
    kT1 = [AR.bf16(NT), AR.bf16(NT)]
    vaug = [AR.bf16(18, 130), AR.bf16(18, 130)]
    gateA = [AR.bf16(16, 128), AR.bf16(16, 128)]
    t_sq = [AR.f32(512), AR.f32(512)]
    t_qg = [AR.f32(512), AR.f32(512)]
    t_rs = [AR.f32(512), AR.f32(512)]
    t_a = [AR.f32(512), AR.f32(512)]
    t_b = [AR.f32(512), AR.f32(512)]
    t_gs = AR.f32(512)
    Et = [AR.bf16(512) for _ in range(4)]
    ep_f = AR.f32(16, 128)
    ep_s = AR.f32(8, 8)
    ep_y = AR.bf16(8, 128)
    uc = [0]
    d_A = Dep("Aconst")
    d_Ar = Dep("Arope")
    d_wA = [Dep("wA0"), Dep("wA1")]
    d_qT = [[Dep("qT%d_%d" % (p_, i)) for i in range(4)] for p_ in range(2)]
    d_kT = [[Dep("kT%d_%d" % (p_, i)) for i in range(5)] for p_ in range(2)]
    d_v = [[Dep("vaug%d_%d" % (p_, i)) for i in range(5)] for p_ in range(2)]
    d_gA = [[Dep("gateA%d_%d" % (p_, i)) for i in range(4)] for p_ in range(2)]
    d_tsq = [Dep("tsq0"), Dep("tsq1")]
    d_tqg = [Dep("tqg0"), Dep("tqg1")]
    d_trs = [Dep("trs0"), Dep("trs1")]
    d_ta = [Dep("ta0"), Dep("ta1")]
    d_tb = [Dep("tb0"), Dep("tb1")]
    d_tgs = Dep("tgs")
    d_E = [Dep("E%d" % i) for i in range(4)]
    d_ep = [Dep("ep%d" % i) for i in range(8)]

    S.dma("sp", lambda: sp.dma_start(out=cosT, in_=cos_d), writes=[d_Ar], semdep=d_Ar)
    S.dma("sp", lambda: sp.dma_start(out=sinT, in_=sin_d), writes=[d_Ar], semdep=d_Ar)
    S.dma("sp", lambda: sp.dma_start(out=permS, in_=perm_d), writes=[d_Ar], semdep=d_Ar)
    S.dma("sp", lambda: sp.dma_start(out=bonesS, in_=bones_d), writes=[d_Ar], semdep=d_Ar)
    for m in range(2):
        S.dma("sp", lambda m=m: sp.dma_start(out=gqk[m * 64:(m + 1) * 64, :], in_=qkg_d.rearrange("r d -> d r"), allow_slow_non_contiguous=True), writes=[d_A], semdep=d_A)
    S.dma("sp", lambda: sp.dma_start(out=lamb, in_=lam_d.partition_broadcast(128)), writes=[d_A], semdep=d_A)
    S.dma("sp", lambda: sp.dma_start(out=gS, in_=subln_d.partition_broadcast(128)), writes=[d_A], semdep=d_A)
    S.op("dve", lambda: dve.tensor_scalar(out=gqk[:, 0:1], in0=gqk[:, 0:1], scalar1=0.125, scalar2=None, op0=ALU.mult), reads=[d_A], writes=[d_A])
    S.op("dve", lambda: dve.tensor_scalar(out=gS, in0=gS, scalar1=1.0 - LAM_INIT, scalar2=None, op0=ALU.mult), reads=[d_A], writes=[d_A])
    S.op("dve", lambda: dve.tensor_tensor(out=lamb[:, 0:64], in0=lamb[:, 0:64], in1=lamb[:, 64:128], op=ALU.mult), reads=[d_A], writes=[d_A])
    S.op("dve", lambda: dve.tensor_tensor(out=lamb[:, 128:192], in0=lamb[:, 128:192], in1=lamb[:, 192:256], op=ALU.mult), reads=[d_A], writes=[d_A])
    S.op("dve", lambda: dve.tensor_reduce(out=lamt[:, 0:1], in_=lamb[:, 0:64], op=ALU.add, axis=mybir.AxisListType.X), reads=[d_A], writes=[d_A])
    S.op("dve", lambda: dve.tensor_reduce(out=lamt[:, 1:2], in_=lamb[:, 128:192], op=ALU.add, axis=mybir.AxisListType.X), reads=[d_A], writes=[d_A])
    S.op("act", lambda: act.activation(out=lamt[:, 2:4], in_=lamt[:, 0:2], func=AF.Exp), reads=[d_A], writes=[d_A])
    S.op("dve", lambda: dve.scalar_tensor_tensor(out=lamt[:, 4:5], in0=lamt[:, 3:4], scalar=-LAM_INIT, in1=lamt[:, 2:3], op0=ALU.add, op1=ALU.subtract), reads=[d_A], writes=[d_A])
    S.op("pool", lambda: pool.memset(epsT, EPS), writes=[d_A])
    S.op("pool", lambda: pool.memset(mhA, -0.5), writes=[d_A])
    for p_ in range(2):
        S.op("pool", lambda p_=p_: pool.memset(kT0[p_], 0.0), writes=d_kT[p_])
        S.op("pool", lambda p_=p_: pool.memset(kT1[p_], 0.0), writes=d_kT[p_])
        S.op("pool", lambda p_=p_: pool.memset(vaug[p_][:, :, 128:130], 1.0), writes=d_v[p_])

    PROJ_B = [3, 4, 5, 6, 7]
    pr = [0]

    def nextbank():
        b = PROJ_B[pr[0] % len(PROJ_B)]
        pr[0] += 1
        return b

    blk = [0]

    def qk_item(h, which, tok0, ntok, rope, dst_fn, ddst):
        s = blk[0] % 2
        blk[0] += 1
        slot = h % 2
        st = {}

        def m1():
            bp = nextbank()
            st["bp"] = bp
            for k in range(8):
                S.op("pe", lambda k=k: pe.matmul(bank(bp)[:, 0:ntok], lhsT=wA[slot][:, which, k, :], rhs=hT[:, k, tok0:tok0 + ntok], start=(k == 0), stop=(k == 7)),
                     reads=[d_wA[slot]] + d_hT[tok0 // 128:(tok0 + ntok + 127) // 128], writes=[PB[bp]], inc=(k == 7))

        def m2():
            bp = st["bp"]
            S.op("dve", lambda: dve.tensor_copy(out=t_b[s][:, 0:ntok], in_=bank(bp)[:, 0:ntok]), reads=[PB[bp]], writes=[d_tb[s]])
            S.op("dve", lambda: dve.tensor_tensor(out=t_sq[s][:, 0:ntok], in0=t_b[s][:, 0:ntok], in1=t_b[s][:, 0:ntok], op=ALU.mult), reads=[d_tb[s]], writes=[d_tsq[s]])
            S.op("dve", lambda: dve.tensor_scalar(out=t_qg[s][:, 0:ntok], in0=t_b[s][:, 0:ntok], scalar1=gqk[:, which:which + 1], scalar2=None, op0=ALU.mult),
                 reads=[d_tb[s], d_A], writes=[d_tqg[s]])

        def m3():
            bs = nextbank()
            st["bs"] = bs
            S.op("pe", lambda: pe.matmul(bank(bs)[:, 0:ntok], lhsT=bonesS, rhs=t_sq[s][:, 0:ntok], start=True, stop=True), reads=[d_tsq[s], d_Ar], writes=[PB[bs]])
            if rope:
                br = nextbank()
                st["br"] = br
                S.op("pe", lambda: pe.matmul(bank(br)[:, 0:ntok], lhsT=permS, rhs=t_qg[s][:, 0:ntok], start=True, stop=True), reads=[d_tqg[s], d_Ar], writes=[PB[br]])

        def m4():
            bs = st["bs"]
            S.op("act", lambda: act.activation(out=t_rs[s][:, 0:ntok], in_=bank(bs)[:, 0:ntok], func=AF.Ln, scale=1.0 / 64, bias=epsT[:, 0:1]), reads=[PB[bs], d_A], writes=[d_trs[s]])
            S.op("act", lambda: act.activation(out=t_rs[s][:, 0:ntok], in_=t_rs[s][:, 0:ntok], func=AF.Exp, scale=-0.5), reads=[d_trs[s]], writes=[d_trs[s]])
            if rope:
                br = st["br"]
                p0 = tok0 - CTX
                S.op("pool", lambda: pool.tensor_tensor(out=t_a[s][:, 0:ntok], in0=t_qg[s][:, 0:ntok], in1=cosT[:, p0:p0 + ntok], op=ALU.mult), reads=[d_tqg[s], d_Ar], writes=[d_ta[s]])
                S.op("dve", lambda: dve.tensor_tensor(out=t_b[s][:, 0:ntok], in0=bank(br)[:, 0:ntok], in1=sinT[:, p0:p0 + ntok], op=ALU.mult), reads=[PB[br], d_Ar, d_tsq[s], d_tqg[s]], writes=[d_tb[s]])

        def m5():
            if rope:
                S.op("dve", lambda: dve.tensor_tensor(out=t_a[s][:, 0:ntok], in0=t_a[s][:, 0:ntok], in1=t_b[s][:, 0:ntok], op=ALU.add), reads=[d_ta[s], d_tb[s]], writes=[d_ta[s]])
                srcv, sdep = t_a[s], d_ta[s]
            else:
                srcv, sdep = t_qg[s], d_tqg[s]
            for (dst, p_lo, p_hi) in dst_fn():
                eng_, ee_ = ("dve", dve)
                S.op(eng_, lambda dst=dst, p_lo=p_lo, p_hi=p_hi: ee_.tensor_tensor(out=dst, in0=srcv[p_lo:p_hi, 0:ntok], in1=t_rs[s][p_lo:p_hi, 0:ntok], op=ALU.mult),
                     reads=[sdep, d_trs[s]], writes=[ddst])
        return [m1, m2, m3, m4, m5]

    def proj_tasks(h, interleave=False):
        hp = h % 2
        slot = h % 2
        items = []
        for i in range(4):
            items.append(qk_item(h, 0, CTX + i * 512, 512, True, lambda i=i: [(qT[hp][:, i * 512:(i + 1) * 512], 0, 128)], d_qT[hp][i]))
        items.append(qk_item(h, 1, 0, 256, False, lambda: [(kT0[hp][0:64, 0:256], 0, 64), (kT1[hp][64:128, 0:256], 64, 128)], d_kT[hp][0]))
        for i in range(4):
            t0 = CTX + i * 512
            items.append(qk_item(h, 1, t0, 512, True, lambda t0=t0: [(kT0[hp][0:64, t0:t0 + 512], 0, 64), (kT1[hp][64:128, t0:t0 + 512], 64, 128)], d_kT[hp][1 + i]))
        tasks = []
        if interleave:
            tasks.extend(items[0][0:3])
            for n_ in range(1, len(items)):
                a_, b_ = items[n_], items[n_ - 1]
                tasks.extend([a_[0], b_[3], a_[1], b_[4], a_[2]])
            tasks.extend(items[-1][3:5])
        else:
            for it in items:
                tasks.extend(it)

        def v_task(grp):
            st = {}

            def a():
                tiles = list(range(grp * 4, min(grp * 4 + 4, 18)))
                bp = nextbank()
                st["bp"] = bp
                pv = bank(bp).rearrange("p (j c) -> p j c", c=128)
                for jj, ti in enumerate(tiles):
                    for k in range(8):
                        S.op("pe", lambda k=k, jj=jj, ti=ti: pe.matmul(pv[:, jj, :], lhsT=hT[:, k, ti * 128:(ti + 1) * 128], rhs=wA[slot][:, 2, k, :], start=(k == 0), stop=(k == 7)),
                             reads=[d_wA[slot], d_hT[ti]], writes=[PB[bp]], inc=(k == 7 and jj == len(tiles) - 1))

            def b():
                bp = st["bp"]
                pv = bank(bp).rearrange("p (j c) -> p j c", c=128)
                n = len(list(range(grp * 4, min(grp * 4 + 4, 18))))
                S.op("dve", lambda: dve.tensor_copy(out=vaug[hp][:, grp * 4:grp * 4 + n, 0:128], in_=pv[:, 0:n, :]), reads=[PB[bp]], writes=[d_v[hp][grp]])
            return [a, b]

        def g_task(grp):
            st = {}

            def a():
                bp = nextbank()
                st["bp"] = bp
                pv = bank(bp).rearrange("p (j c) -> p j c", c=128)
                for jj in range(4):
                    ti = 2 + grp * 4 + jj
                    for k in range(8):
                        S.op("pe", lambda k=k, jj=jj, ti=ti: pe.matmul(pv[:, jj, :], lhsT=hT[:, k, ti * 128:(ti + 1) * 128], rhs=wA[slot][:, 3, k, :], start=(k == 0), stop=(k == 7)),
                             reads=[d_wA[slot], d_hT[ti]], writes=[PB[bp]], inc=(k == 7 and jj == 3))

            def b():
                bp = st["bp"]
                S.op("act", lambda: act.activation(out=t_gs, in_=bank(bp), func=AF.Exp, scale=-1.0), reads=[PB[bp]], writes=[d_tgs])
                S.op("dve", lambda: dve.tensor_scalar(out=t_gs, in0=t_gs, scalar1=1.0, scalar2=1e30, op0=ALU.add, op1=ALU.min), reads=[], writes=[d_tgs])

            def c():
                bp = st["bp"]
                pv = bank(bp).rearrange("p (j c) -> p j c", c=128)
                S.op("act", lambda: act.activation(out=t_gs, in_=t_gs, func=AF.Ln), reads=[], writes=[d_tgs])
                S.op("act", lambda: act.activation(out=t_gs, in_=t_gs, func=AF.Exp, scale=-1.0), reads=[], writes=[d_tgs])
                S.op("dve", lambda: dve.tensor_tensor(out=gateA[hp][:, grp * 4:grp * 4 + 4, :], in0=pv, in1=t_gs.rearrange("p (j c) -> p j c", c=128), op=ALU.mult), reads=[PB[bp], d_tgs], writes=[d_gA[hp][grp]])
            return [a, b, c]

        for grp in range(5):
            tasks.extend(v_task(grp))
        for grp in range(4):
            tasks.extend(g_task(grp))
        return tasks

    def load_wA(h):
        slot = h % 2
        for g in range(4):
            S.dma("pool", lambda g=g: pool.dma_start(
                out=wA[slot][:, g, :, :], in_=ewin_d.rearrange("(k p) n -> p k n", p=128)[:, :, g * 512 + h * 128:g * 512 + (h + 1) * 128]),
                writes=[d_wA[slot]], semdep=d_wA[slot])

    _save_top = AR.top
    AR.top = P_TOP
    lbl = AR.f32(2, 2, 4)
    lbv = AR.f32(3, 2, 4)
    tri = AR.f32(2, 64)
    smask = AR.f32(512)
    hgT = AR.f32(1)
    epsB = AR.f32(1)
    oneB = AR.f32(1)
    mhB = AR.f32(1)
    wB = AR.bf16(5, 8, 128)
    assert AR.top <= P_TOP + 2 * SEQ
    AR.top = _save_top
    d_B = Dep("Bconst")
    d_wB = Dep("wB")

    def load_wB(h, extra=()):
        for g in range(5):
            S.dma("pool", lambda g=g: pool.dma_start(
                out=wB[:, g, :, :], in_=ewin_d.rearrange("(k p) n -> p k n", p=128)[:, :, 2048 + g * 512 + h * 128:2048 + g * 512 + (h + 1) * 128]),
                writes=[d_wB] + list(extra), semdep=d_wB)

    def b_prefetch():
        ex = [d_Ar]
        for dd_ in range(2):
            for ee_ in range(2):
                S.dma("sp", lambda dd_=dd_, ee_=ee_: sp.dma_start(out=lbl[:, dd_, ee_, :], in_=lbl_d[dd_, ee_].rearrange("(h p) -> p h", p=128), allow_slow_non_contiguous=True), writes=[d_B] + ex, semdep=d_B)
        S.dma("sp", lambda: sp.dma_start(out=tri[0:64, :, :], in_=tri_d), writes=[d_B] + ex, semdep=d_B)
        S.dma("sp", lambda: sp.dma_start(out=smask, in_=smask_d), writes=[d_B] + ex, semdep=d_B)
        S.dma("sp", lambda: sp.dma_start(out=hgT, in_=hg_d.rearrange("(p o) -> p o", o=1), allow_slow_non_contiguous=True), writes=[d_B] + ex, semdep=d_B)
        load_wB(0, extra=ex)

    load_wA(0)
    norm_phase(0, 18, src0, NW - 6900)
    if debug:
        S.dma("sp", lambda: sp.dma_start(out=dbg["hT0"], in_=hT), reads=d_hT, writes=[], semdep=d_hT[0])
    S.barrier()
    for f in proj_tasks(0, interleave=True):
        f()
    deferred = []
    PROJ_B[:] = [6, 7]
    for h in range(4):
        hp = h % 2
        if h + 1 < 4:
            load_wA(h + 1)
            tasks = proj_tasks(h + 1)
        else:
            tasks = []

        def acc(m, t):
            sidx = m * 4 + t
            return psum[:, sidx // 3, (sidx % 3) * 130:(sidx % 3) * 130 + 130]

        def emit_score(i, u):
            j, m = u // 2, u % 2
            sb = 3 + (uc[0] + u) % 3
            kTm = kT0[hp] if m == 0 else kT1[hp]
            kd = d_kT[hp][0] if j < 2 else d_kT[hp][1 + (j - 2) // 4]
            S.op("pe", lambda: pe.matmul(bank(sb), lhsT=kTm[:, j * 128:(j + 1) * 128], rhs=qT[hp][:, i * 512:(i + 1) * 512], start=True, stop=True),
                 reads=[kd, d_qT[hp][i]], writes=[PB[sb]])

        def emit_exp_pv(i, u):
            j, m = u // 2, u % 2
            sb = 3 + (uc[0] + u) % 3
            es = (uc[0] + u) % 4
            S.op("act", lambda: act.activation(out=Et[es], in_=bank(sb), func=AF.Exp), reads=[PB[sb]], writes=[d_E[es]])
            for t in range(4):
                S.op("pe", lambda t=t: pe.matmul(acc(m, t)[:, 0:129], lhsT=Et[es][:, t * 128:(t + 1) * 128], rhs=vaug[hp][:, j, 0:129], start=False, stop=(j == 17), skip_group_check=True),
                     reads=[d_E[es], d_v[hp][j // 4]], writes=[PB[(m * 4 + t) // 3]], inc=(t == 3))

        def epilogue1(i):
            sl = i % 2
            for t in range(4):
                a0, a1 = acc(0, t), acc(1, t)
                b0_, b1_ = PB[t // 3], PB[(4 + t) // 3]
                es_ = ep_s[:, sl * 4 + t, :]
                f0, f1 = ep_f[:, sl * 8 + 2 * t, :], ep_f[:, sl * 8 + 2 * t + 1, :]
                dd = d_ep[sl * 4 + t]
                S.op("dve", lambda: dve.reciprocal(out=es_[:, 0:1], in_=a0[:, 128:129]), reads=[b0_], writes=[dd])
                S.op("dve", lambda: dve.reciprocal(out=es_[:, 1:2], in_=a1[:, 128:129]), reads=[b1_], writes=[dd])
                S.op("dve", lambda: dve.tensor_tensor(out=es_[:, 1:2], in0=es_[:, 1:2], in1=lamt[:, 4:5], op=ALU.mult), reads=[d_A], writes=[dd])
                S.op("dve", lambda: dve.tensor_scalar(out=f0, in0=a1[:, 0:128], scalar1=es_[:, 1:2], scalar2=None, op0=ALU.mult), reads=[b1_], writes=[dd])
                S.op("dve", lambda: dve.scalar_tensor_tensor(out=f1, in0=a0[:, 0:128], scalar=es_[:, 0:1], in1=f0, op0=ALU.mult, op1=ALU.add), reads=[b0_], writes=[dd])

        def epilogue1b(i, hp=hp):
            sl = i % 2
            for t in range(4):
                qt = i * 4 + t
                es_ = ep_s[:, sl * 4 + t, :]
                f0, f1 = ep_f[:, sl * 8 + 2 * t, :], ep_f[:, sl * 8 + 2 * t + 1, :]
                dd = d_ep[sl * 4 + t]
                S.op("dve", lambda: dve.scalar_tensor_tensor(out=f0, in0=f1, scalar=1.0, in1=f1, op0=ALU.mult, op1=ALU.mult, accum_out=es_[:, 2:3]), reads=[dd], writes=[dd])
                S.op("dve", lambda: dve.tensor_scalar(out=es_[:, 3:4], in0=es_[:, 2:3], scalar1=1.0 / 128, scalar2=EPS, op0=ALU.mult, op1=ALU.add), reads=[dd], writes=[dd])
                S.op("pool", lambda: pool.tensor_tensor(out=es_[:, 4:5], in0=es_[:, 3:4], in1=mhA, op=ALU.pow), reads=[dd, d_A], writes=[dd])
                S.op("dve", lambda: dve.scalar_tensor_tensor(out=f0, in0=f1, scalar=es_[:, 4:5], in1=gS, op0=ALU.mult, op1=ALU.mult), reads=[dd, d_A], writes=[dd])
                S.op("pool", lambda: pool.tensor_tensor(out=ep_y[:, sl * 4 + t, :], in0=f0, in1=gateA[hp][:, qt, :], op=ALU.mult), reads=[dd, d_gA[hp][qt // 4]], writes=[dd])

        def epilogue2(i, h=h):
            sl = i % 2
            bt = nextbank()
            pyt = bank_bf(bt)[:, 0:512].rearrange("p (t c) -> p t c", c=128)
            for t in range(4):
                S.op("pe", lambda t=t: pe.transpose(out=pyt[:, t, :], in_=ep_y[:, sl * 4 + t, :], identity=identB), reads=[d_ep[sl * 4 + t], d_const], writes=[PB[bt]], inc=(t == 3))
            S.op("dve", lambda: dve.tensor_copy(out=yT[:, h, i * 512:(i + 1) * 512], in_=bank_bf(bt)[:, 0:512]), reads=[PB[bt]], writes=d_yT[h][i * 4:i * 4 + 4])

        def zero_acc():
            for b in range(3):
                S.op("pe", lambda b=b: pe.matmul(bank(b), lhsT=zerosB[:, 0:128], rhs=zerosB, start=True, stop=True, skip_group_check=True), reads=[d_const], writes=[PB[b]])

        for i in range(4):
            if i == 0:
                zero_acc()
                for u0 in range(3):
                    emit_score(0, u0)
            for u in range(36):
                emit_exp_pv(i, u)
                if u + 3 < 36:
                    emit_score(i, u + 3)
                for dq in list(deferred):
                    dq[0] -= 1
                    if dq[0] <= 0:
                        deferred.remove(dq)
                        dq[1]()
                if tasks and u % 2 == 1:
                    tasks.pop(0)()
            uc[0] += 36
            if i + 1 < 4:
                for u0 in range(3):
                    emit_score(i + 1, u0)
            epilogue1(i)
            deferred.append([4, (lambda e=epilogue1b, i=i: e(i))])
            deferred.append([12, (lambda e=epilogue2, i=i: e(i))])
            if i + 1 < 4:
                zero_acc()
        while tasks:
            tasks.pop(0)()
        if h == 2:
            b_prefetch()
    for dq in deferred:
        dq[1]()
    PROJ_B[:] = [0, 1, 2, 3, 4, 5, 6, 7]
    S.barrier()

    AR.top = P_TOP
    lbl = AR.f32(2, 2, 4)
    lbv = AR.f32(3, 2, 4)
    tri = AR.f32(2, 64)
    smask = AR.f32(512)
    hgT = AR.f32(1)
    epsB = AR.f32(1)
    oneB = AR.f32(1)
    mhB = AR.f32(1)
    wB = AR.bf16(5, 8, 128)
    qg = [AR.bf16(NT), AR.bf16(NT)]
    kg = [AR.bf16(NT), AR.bf16(NT)]
    _off_kgt0 = AR.top
    kgtok = [AR.bf16(36, 128), AR.bf16(36, 128)]
    vtok = AR.bf16(36, 128)
    gtok = AR.bf16(32, 128)
    e1 = [AR.f32(36), AR.f32(36)]
    e2 = [AR.f32(36), AR.f32(36)]
    eL = [AR.f32(36), AR.f32(36)]
    Sring = [AR.f32(8, 128), AR.f32(8, 128)]
    Smid = [AR.bf16(32, 128), AR.bf16(32, 128)]
    Asb = AR.bf16(4, 2, 64)
    _off_t = AR.top
    bt_sig = [AR.f32(512), AR.f32(512)]
    bt_g = [AR.f32(512), AR.f32(512)]
    bt_kk = [AR.f32(512), AR.f32(512)]
    bt_G = [AR.f32(512), AR.f32(512)]
    kg2 = [bt_kk[0][:, 0:256].bitcast(BF16), bt_kk[1][:, 0:256].bitcast(BF16)]
    ot = [bt_sig[0].rearrange("p (j c) -> p j c", c=128), bt_sig[1].rearrange("p (j c) -> p j c", c=128)]
    bt_s8 = [AR.f32(2, 8), AR.f32(2, 8)]
    stage = [AR.bf16(512), AR.bf16(512)]
    ojunk = AR.f32(128)
    hb_ss = AR.f32(3, 32)
    hb_y = AR._view(AR.ap[:, _off_kgt0:_off_kgt0 + 2048].bitcast(BF16), (32, 128))
    d_qg = [[Dep("qg%d_%d" % (d, b)) for b in range(5)] for d in range(2)]
    d_kg = [[Dep("kg%d_%d" % (d, b)) for b in range(5)] for d in range(2)]
    d_kgt = [[Dep("kgt%d_%d" % (d, b)) for b in range(5)] for d in range(2)]
    d_vt = [Dep("vt%d" % i) for i in range(5)]
    d_gt = [Dep("gt%d" % i) for i in range(5)]
    d_e = [[Dep("e%d_%d" % (d, b)) for b in range(5)] for d in range(2)]
    d_Sr = [[Dep("Sr%d_%d" % (d, i)) for i in range(2)] for d in range(2)]
    d_Sm = [[Dep("Sm%d_%d" % (d, i)) for i in range(8)] for d in range(2)]
    d_As = Dep("As")
    d_bt = [{k: Dep("bt%d_%s" % (p_, k)) for k in ("sig", "g", "kk", "G", "s8")} for p_ in range(2)]
    d_stage = [Dep("stage0"), Dep("stage1")]
    d_ot = [Dep("ot0"), Dep("ot1")]
    d_hb = Dep("hb")
    d_hby = [Dep("hby%d" % i) for i in range(4)]

    S.op("pool", lambda: pool.memset(epsB, EPS), writes=[d_B])
    S.op("pool", lambda: pool.memset(oneB, 1.0), writes=[d_B])
    S.op("pool", lambda: pool.memset(mhB, -0.5), writes=[d_B])
    S.op("dve", lambda: dve.tensor_tensor(out=lbv[:, 1, :, :], in0=lbl[:, :, 1, :], in1=lbl[:, :, 0, :], op=ALU.subtract), reads=[d_B], writes=[d_B])
    S.op("act", lambda: act.activation(out=lbv[:, 0, :, :], in_=lbv[:, 1, :, :], func=AF.Exp), reads=[d_B], writes=[d_B])
    S.op("dve", lambda: dve.tensor_scalar(out=lbv[:, 0, :, :], in0=lbv[:, 0, :, :], scalar1=1.0, scalar2=None, op0=ALU.add), reads=[d_B], writes=[d_B])
    S.op("dve", lambda: dve.reciprocal(out=lbv[:, 0, :, :], in_=lbv[:, 0, :, :]), reads=[d_B], writes=[d_B])
    S.op("dve", lambda: dve.tensor_scalar(out=lbv[:, 1, :, :], in0=lbv[:, 0, :, :], scalar1=-1.0, scalar2=1.0, op0=ALU.mult, op1=ALU.add), reads=[d_B], writes=[d_B])
    S.op("dve", lambda: dve.tensor_scalar(out=lbv[:, 2, :, :], in0=lbv[:, 0, :, :], scalar1=-1.0, scalar2=None, op0=ALU.add), reads=[d_B], writes=[d_B])

    BLKS = [(0, 512), (512, 512), (1024, 512), (1536, 512), (2048, 256)]
    nb8 = [0]

    def nextbank8():
        b = nb8[0] % 7
        nb8[0] += 1
        return b

    d_junkps = Dep("junkps")

    def pe_keepwarm(n):
        for _ in range(n):
            S.op("pe", lambda: pe.matmul(bank(7), lhsT=zerosB[:, 0:128], rhs=zerosB, start=True, stop=True, skip_group_check=True), reads=[d_const], writes=[PB[7]], inc=False)

    bd = [0]
    for h in range(4):
        def vg_item(which, grp, bi, t0, nt, sp_):
            nch = nt // 64
            c0 = t0 // 64
            hdeps = d_hT[t0 // 128:(t0 + nt) // 128]
            st = {}

            def s1():
                bv = nextbank8()
                for k in range(8):
                    S.op("pe", lambda k=k: pe.matmul(bank(bv)[:, 0:nt], lhsT=wB[:, grp, k, :], rhs=hT[:, k, t0:t0 + nt], start=(k == 0), stop=(k == 7)),
                         reads=[d_wB] + hdeps, writes=[PB[bv]], inc=(k == 7))
                if which == 0:
                    S.op("act", lambda: act.copy(out=stage[sp_][:, 0:nt], in_=bank(bv)[:, 0:nt]), reads=[PB[bv]], writes=[d_stage[sp_]])
                else:
                    S.op("act", lambda: act.activation(out=stage[sp_][:, 0:nt], in_=bank(bv)[:, 0:nt], func=AF.Silu), reads=[PB[bv]], writes=[d_stage[sp_]])

            def s2():
                bt_ = nextbank8()
                pk = bank_bf(bt_).rearrange("p (j c) -> p j c", c=128)
                for cc in range(nch):
                    S.op("pe", lambda cc=cc: pe.transpose(out=pk[0:64, cc, :], in_=stage[sp_][:, cc * 64:(cc + 1) * 64], identity=identB),
                         reads=[d_stage[sp_], d_const], writes=[PB[bt_]], inc=(cc == nch - 1))
                if which == 0:
                    S.op("dve", lambda: dve.tensor_copy(out=vtok[0:64, c0:c0 + nch, :], in_=pk[0:64, 0:nch, :]), reads=[PB[bt_]], writes=[d_vt[bi]])
                else:
                    lo = 4 if bi == 0 else 0
                    S.op("dve", lambda: dve.tensor_copy(out=gtok[0:64, c0 + lo - 4:c0 + nch - 4, :], in_=pk[0:64, lo:nch, :]), reads=[PB[bt_]], writes=[d_gt[bi]])
            return [s1, s2]

        items = []
        for which, grp in ((0, 1), (1, 4)):
            for bi, (t0, nt) in enumerate(BLKS):
                items.append(vg_item(which, grp, bi, t0, nt, bd[0] % 2))
                bd[0] += 1
        skew(items, newest_first=True)

        qbank = {}

        def gd_item(bi, t0, nt, d, p_):
            nch = nt // 64
            c0 = t0 // 64
            hdeps = d_hT[t0 // 128:(t0 + nt) // 128]
            T_sig, T_g, T_kk, T_G, T_s8, T_kg2 = bt_sig[p_], bt_g[p_], bt_kk[p_], bt_G[p_], bt_s8[p_], kg2[p_]
            D_ = d_bt[p_]
            G3 = T_G[:, 0:nt].rearrange("p (c l) -> p c l", l=64)
            eR = e1[d] if d == 0 else e2[d]
            eD = e2[d] if d == 0 else e1[d]
            sa = 1.0 if d == 0 else -1.0

            def s1():
                if d == 0:
                    bq = nextbank8()
                    qbank[bi] = bq
                    for k in range(8):
                        S.op("pe", lambda k=k: pe.matmul(bank(bq)[:, 0:nt], lhsT=wB[:, 0, k, :], rhs=hT[:, k, t0:t0 + nt], start=(k == 0), stop=(k == 7)),
                             reads=[d_wB] + hdeps, writes=[PB[bq]], inc=(k == 7))
                bf = nextbank8()
                for k in range(8):
                    S.op("pe", lambda k=k: pe.matmul(bank(bf)[:, 0:nt], lhsT=wB[:, 2 + d, k, :], rhs=hT[:, k, t0:t0 + nt], start=(k == 0), stop=(k == 7)),
                         reads=[d_wB] + hdeps, writes=[PB[bf]], inc=(k == 7))
                pe_keepwarm(8)
                S.op("act", lambda: act.activation(out=T_sig[:, 0:nt], in_=bank(bf)[:, 0:nt], func=AF.Exp, scale=-1.0), reads=[PB[bf]], writes=[D_["sig"]])
                S.op("act", lambda: act.activation(out=T_sig[:, 0:nt], in_=T_sig[:, 0:nt], func=AF.Ln, bias=oneB[:, 0:1]), reads=[d_B], writes=[D_["sig"]])
                S.op("act", lambda: act.activation(out=T_sig[:, 0:nt], in_=T_sig[:, 0:nt], func=AF.Exp, scale=-1.0), reads=[], writes=[D_["sig"]])
                S.op("act", lambda: act.activation(out=T_g[:, 0:nt], in_=T_sig[:, 0:nt], func=AF.Ln, scale=lbv[:, 1, d, h:h + 1], bias=lbv[:, 0, d, h:h + 1]),
                     reads=[D_["sig"], d_B], writes=[D_["g"]])
                S.op("dve", lambda: dve.tensor_scalar(out=T_kk[:, 0:nt], in0=T_sig[:, 0:nt], scalar1=lbv[:, 2, d, h:h + 1], scalar2=lbv[:, 1, d, h:h + 1], op0=ALU.mult, op1=ALU.add),
                     reads=[D_["sig"], d_B], writes=[D_["kk"]])
                S.op("dve", lambda: dve.tensor_tensor_scan(out=T_G[:, 0:nt], data0=smask[:, 0:nt], data1=T_g[:, 0:nt], initial=0.0, op0=ALU.mult, op1=ALU.add),
                     reads=[D_["g"], d_B], writes=[D_["G"]])

            def s2():
                bq = qbank[bi]
                S.op("dve", lambda: dve.tensor_copy(out=T_s8[:, 0, 0:nch].unsqueeze(2), in_=G3[:, :, 31:32]), reads=[D_["G"]], writes=[D_["s8"]])
                S.op("dve", lambda: dve.tensor_tensor(out=T_s8[:, 1, 0:nch].unsqueeze(2), in0=G3[:, :, 63:64], in1=G3[:, :, 31:32], op=ALU.subtract), reads=[D_["G"]], writes=[D_["s8"]])
                S.op("act", lambda: act.activation(out=eR[:, c0:c0 + nch], in_=T_s8[:, 0, 0:nch], func=AF.Exp), reads=[D_["s8"]], writes=[d_e[d][bi]])
                S.op("act", lambda: act.activation(out=eL[d][:, c0:c0 + nch].unsqueeze(2), in_=G3[:, :, 63:64], func=AF.Exp), reads=[D_["G"]], writes=[d_e[d][bi]])
                S.op("act", lambda: act.activation(out=eD[:, c0:c0 + nch], in_=T_s8[:, 1, 0:nch], func=AF.Exp), reads=[D_["s8"]], writes=[d_e[d][bi]])
                S.op("dve", lambda: dve.tensor_tensor(out=G3, in0=G3, in1=T_s8[:, 0, 0:nch].unsqueeze(2).to_broadcast([128, nch, 64]), op=ALU.subtract), reads=[D_["s8"]], writes=[D_["G"]])
                if d == 1:
                    S.op("dve", lambda: dve.tensor_tensor(out=T_G[:, 0:nt], in0=T_G[:, 0:nt], in1=T_g[:, 0:nt], op=ALU.subtract), reads=[D_["g"]], writes=[D_["G"]])
                S.op("act", lambda: act.activation(out=T_sig[:, 0:nt], in_=T_G[:, 0:nt], func=AF.Exp, scale=sa), reads=[D_["G"]], writes=[D_["sig"]])
                S.op("act", lambda: act.activation(out=T_g[:, 0:nt], in_=T_G[:, 0:nt], func=AF.Exp, scale=-sa), reads=[D_["G"]], writes=[D_["g"]])
                S.op("dve", lambda: dve.tensor_tensor(out=qg[d][:, t0:t0 + nt], in0=bank(bq)[:, 0:nt], in1=T_sig[:, 0:nt], op=ALU.mult), reads=[PB[bq], D_["sig"]], writes=[d_qg[d][bi]])
                S.op("pool", lambda: pool.tensor_tensor(out=T_g[:, 0:nt], in0=T_kk[:, 0:nt], in1=T_g[:, 0:nt], op=ALU.mult), reads=[D_["kk"]], writes=[D_["g"]])
                S.op("act", lambda: act.copy(out=kg[d][:, t0:t0 + nt], in_=T_g[:, 0:nt]), reads=[D_["g"]], writes=[d_kg[d][bi]])
                S.op("dve", lambda: dve.tensor_tensor(out=T_kg2[:, 0:nt].rearrange("p (c l) -> p c l", l=64), in0=T_g[:, 0:nt].rearrange("p (c l) -> p c l", l=64),
                                                      in1=e2[d][:, c0:c0 + nch].unsqueeze(2).to_broadcast([128, nch, 64]), op=ALU.mult),
                     reads=[D_["g"], d_e[d][bi]], writes=[D_["kk"]])

            def s3():
                bt_ = nextbank8()
                pk = bank_bf(bt_).rearrange("p (j c) -> p j c", c=128)
                for cc in range(nch):
                    S.op("pe", lambda cc=cc: pe.transpose(out=pk[0:64, cc, :], in_=T_kg2[:, cc * 64:(cc + 1) * 64], identity=identB),
                         reads=[D_["kk"], d_const], writes=[PB[bt_]], inc=(cc == nch - 1))
                S.op("act", lambda: act.copy(out=kgtok[d][0:64, c0:c0 + nch, :], in_=pk[0:64, 0:nch, :]), reads=[PB[bt_]], writes=[d_kgt[d][bi]])
            return [s1, s2, s3]

        items = []
        for bi, (t0, nt) in enumerate(BLKS):
            for d in range(2):
                items.append(gd_item(bi, t0, nt, d, bd[0] % 2))
                bd[0] += 1
        skew(items)
        def slot_of(c):
            return c % 8

        for d in range(2):
            first = 0 if d == 0 else 3
            S.op("pool", lambda d=d, first=first: pool.memset(Sring[d][:, slot_of(first), :], 0.0), writes=[d_Sr[d][slot_of(first) // 4]])
        order = [list(range(36)), [3, 2, 1, 0] + list(range(35, 3, -1))]
        for g in range(9):
            for d in range(2):
                bu = (0, 1)[g % 2] if d == 0 else (2, 3)[g % 2]
                pu = bank(bu).rearrange("p (j c) -> p j c", c=128)
                cs = order[d][g * 4:g * 4 + 4]
                for jj, c in enumerate(cs):
                    S.op("pe", lambda jj=jj, c=c: pe.matmul(pu[:, jj, :], lhsT=kgtok[d][0:64, c, :], rhs=vtok[0:64, c, :], start=True, stop=True),
                         reads=[d_kgt[d][c // 8], d_vt[c // 8]], writes=[PB[bu]], inc=(jj == 3))
                for jj, c in enumerate(cs):
                    bi = c // 8
                    if d == 0:
                        cn = c + 1
                    else:
                        cn = 35 if c == 0 else c - 1
                    if (d == 0 and c == 35) or (d == 1 and c == 4):
                        continue
                    sp_, sn_ = slot_of(c), slot_of(cn)
                    S.op("dve", lambda c=c, sp_=sp_, sn_=sn_, jj=jj: dve.scalar_tensor_tensor(out=Sring[d][:, sn_, :], in0=Sring[d][:, sp_, :], scalar=eL[d][:, c:c + 1], in1=pu[:, jj, :], op0=ALU.mult, op1=ALU.add),
                         reads=[d_Sr[d][sp_ // 4], d_e[d][bi], PB[bu]], writes=[d_Sr[d][sn_ // 4]])
                if d == 0:
                    c0 = g * 4
                else:
                    c0 = (36 - g * 4) if g >= 1 else None
                    if c0 is not None and c0 > 32:
                        c0 = None
                if c0 is not None and c0 >= 4 and c0 + 3 <= 35:
                    sl0 = slot_of(c0)
                    S.op("pool", lambda c0=c0, sl0=sl0: pool.tensor_tensor(out=Smid[d][:, c0 - 4:c0, :], in0=Sring[d][:, sl0:sl0 + 4, :],
                                                                         in1=e1[d][:, c0:c0 + 4].unsqueeze(2).to_broadcast([128, 4, 128]), op=ALU.mult),
                         reads=[d_Sr[d][sl0 // 4], d_e[d][c0 // 8]], writes=[d_Sm[d][(c0 - 4) // 4]])
        S.op("pool", lambda: pool.tensor_tensor(out=Smid[1][:, 0:4, :], in0=Sring[1][:, 4:8, :], in1=e1[1][:, 4:8].unsqueeze(2).to_broadcast([128, 4, 128]), op=ALU.mult),
             reads=[d_Sr[1][1], d_e[1][0]], writes=[d_Sm[1][0]])
        if h < 3:
            load_wB(h + 1)
        tri4 = tri[0:64, :, :].unsqueeze(1).to_broadcast([64, 4, 2, 64])

        def o_s1(g):
            ba, bo = 4 + g % 2, 6 + g % 2
            pa = bank(ba)[0:64, :].rearrange("p (j d c) -> p j d c", d=2, c=64)
            po = bank(bo)[0:64, :].rearrange("p (j c) -> p j c", c=128)
            os_ = g % 2
            for jj in range(4):
                c = 4 + g * 4 + jj
                bi = c // 8
                for d in range(2):
                    S.op("pe", lambda jj=jj, c=c, d=d: pe.matmul(pa[:, jj, d, :], lhsT=kg[d][:, c * 64:(c + 1) * 64], rhs=qg[d][:, c * 64:(c + 1) * 64], start=True, stop=True, skip_group_check=True),
                         reads=[d_kg[d][bi], d_qg[d][bi]], writes=[PB[ba]], inc=(jj == 3 and d == 1))
            S.op("dve", lambda: dve.tensor_tensor(out=Asb[0:64, :, :, :], in0=pa, in1=tri4, op=ALU.mult), reads=[PB[ba], d_B], writes=[d_As])

        def o_s1b(g):
            ba, bo = 4 + g % 2, 6 + g % 2
            po = bank(bo)[0:64, :].rearrange("p (j c) -> p j c", c=128)
            os_ = g % 2
            first = True
            for jj in range(4):
                c = 4 + g * 4 + jj
                for d in range(2):
                    S.op("pe", lambda jj=jj, c=c, d=d, first=first: pe.matmul(po[:, jj, :], lhsT=Asb[0:64, jj, d, :], rhs=vtok[0:64, c, :], start=first, stop=False, skip_group_check=True),
                         reads=[d_As, d_vt[c // 8]], writes=[PB[bo]], inc=False)
                    first = False
            for jj in range(4):
                c = 4 + g * 4 + jj
                bi = c // 8
                for d in range(2):
                    last = (jj == 3 and d == 1)
                    S.op("pe", lambda jj=jj, c=c, d=d, last=last: pe.matmul(po[:, jj, :], lhsT=qg[d][:, c * 64:(c + 1) * 64], rhs=Smid[d][:, c - 4, :], start=False, stop=last, skip_group_check=True),
                         reads=[d_qg[d][bi], d_Sm[d][(c - 4) // 4]], writes=[PB[bo]], inc=last)
            S.op("act", lambda: act.copy(out=ot[os_][0:64, :, :], in_=po), reads=[PB[bo]], writes=[d_ot[os_], d_bt[os_]["sig"]])

        def o_s2(g):
            os_ = g % 2
            od_ = [d_ot[os_], d_bt[os_]["sig"]]
            for jj in range(4):
                cl = g * 4 + jj
                S.op("dve", lambda jj=jj, cl=cl: dve.scalar_tensor_tensor(out=ojunk[0:64, :], in0=ot[os_][0:64, jj, :], scalar=1.0, in1=ot[os_][0:64, jj, :], op0=ALU.mult, op1=ALU.mult, accum_out=hb_ss[0:64, 0, cl:cl + 1]),
                     reads=od_, writes=[d_hb])
            S.op("dve", lambda: dve.tensor_scalar(out=hb_ss[0:64, 1, g * 4:g * 4 + 4], in0=hb_ss[0:64, 0, g * 4:g * 4 + 4], scalar1=1.0 / 128, scalar2=EPS, op0=ALU.mult, op1=ALU.add), reads=[d_hb], writes=[d_hb])
            S.op("pool", lambda: pool.tensor_tensor(out=hb_ss[0:64, 2, g * 4:g * 4 + 4], in0=hb_ss[0:64, 1, g * 4:g * 4 + 4], in1=mhB[0:64, 0:1].to_broadcast([64, 4]), op=ALU.pow), reads=[d_hb, d_B], writes=[d_hb])
            for jj in range(4):
                cl = g * 4 + jj
                S.op("dve", lambda jj=jj, cl=cl: dve.scalar_tensor_tensor(out=hb_y[0:64, cl, :], in0=ot[os_][0:64, jj, :], scalar=hb_ss[0:64, 2, cl:cl + 1], in1=gtok[0:64, cl, :], op0=ALU.mult, op1=ALU.mult),
                     reads=od_ + [d_hb, d_gt[(cl + 4) // 8]], writes=[d_hby[g // 2]] + d_kgt[0])

        def o_s2b(g):
            if g % 2 == 1:
                grp = g // 2
                bt_ = grp % 4
                py = bank_bf(bt_)[:, 0:512].rearrange("p (j c) -> p j c", c=64)
                for jj in range(8):
                    c = grp * 8 + jj
                    S.op("pe", lambda jj=jj, c=c: pe.transpose(out=py[:, jj, :], in_=hb_y[0:64, c, :], identity=identB[0:64, 0:64]), reads=[d_hby[grp], d_const] + d_kgt[0], writes=[PB[bt_]], inc=(jj == 7))
                S.op("act", lambda: act.activation(out=yT[:, 4 + h, grp * 512:(grp + 1) * 512], in_=bank_bf(bt_)[:, 0:512], func=AF.Copy, scale=hgT[:, 0:1]), reads=[PB[bt_], d_B], writes=d_yT[4 + h][grp * 4:grp * 4 + 4])

        for g in range(10):
            if g < 8:
                o_s1(g)
            if 1 <= g <= 8:
                o_s2(g - 1)
            if g < 8:
                o_s1b(g)
            if 2 <= g <= 9:
                o_s2b(g - 2)
    if debug:
        S.dma("sp", lambda: sp.dma_start(out=dbg["yT0"], in_=yT), reads=[x for r in d_yT for x in r], writes=[], semdep=d_yT[0][0])
    S.barrier()

    X_OFF = P_TOP
    AR.top = X_OFF
    xnew = AR.f32(16, D)
    T2 = AR.top
    d_xnew = [Dep("xnew%d" % i) for i in range(16)]

    def out_phase(l, w_d, pre=None):
        AR.top = T2
        if pre is None:
            wo = AR.bf16(8, D)
            stg = [AR.f32(D) for _ in range(4)]
        xt = [AR.f32(D), AR.f32(D)]
        gtmp = [AR.f32(512), AR.f32(512)]
        d_gtmp = [Dep("gtmp0"), Dep("gtmp1")]
        d_wo, d_stg, d_xt = [Dep("wo%d" % k_) for k_ in range(8)], [Dep("stg%d" % k_) for k_ in range(4)], [Dep("oxt0"), Dep("oxt1")]
        def ld(k):
            S.dma("sp", lambda: sp.dma_start(out=stg[k % 4], in_=w_d[k * 128:(k + 1) * 128, :]), writes=[d_stg[k % 4]], semdep=d_stg[k % 4])

        if pre is None:
            for k in range(4):
                ld(k)
            for k in range(8):
                s = k % 4
                S.op("dve", lambda k=k, s=s: dve.tensor_tensor(out=wo[:, k, :], in0=stg[s], in1=gate_bc[l], op=ALU.mult), reads=[d_stg[s], d_gate[l]], writes=[d_wo[k]])
                if k + 4 < 8:
                    ld(k + 4)
        else:
            wo, d_wo = pre
        for i in range(16):
            s = i % 2
            if l == 0:
                S.dma("sp", lambda i=i, s=s: sp.dma_start(out=xt[s], in_=x_d[i * 128:(i + 1) * 128, :]), writes=[d_xt[s]], semdep=d_xt[s])
            for n in range(2):
                bp = nextbank()
                for k in range(8):
                    S.op("pe", lambda k=k, n=n, i=i: pe.matmul(bank(bp), lhsT=yT[:, k, i * 128:(i + 1) * 128], rhs=wo[:, k, n * 512:(n + 1) * 512], start=(k == 0), stop=(k == 7)),
                         reads=[d_wo[k], d_yT[k][i]], writes=[PB[bp]], inc=(k == 7))
                if l == 0:
                    S.op("dve", lambda n=n, i=i, s=s: dve.tensor_tensor(out=xnew[:, i, n * 512:(n + 1) * 512], in0=bank(bp), in1=xt[s][:, n * 512:(n + 1) * 512], op=ALU.add),
                         reads=[PB[bp], d_xt[s]], writes=[d_xnew[i]])
                elif pre is None:
                    S.op("dve", lambda n=n, i=i, s=s: dve.tensor_tensor(out=xt[s][:, n * 512:(n + 1) * 512], in0=bank(bp), in1=xnew[:, i, n * 512:(n + 1) * 512], op=ALU.add),
                         reads=[PB[bp], d_xnew[i]], writes=[d_xt[s]])
                else:
                    S.op("dve", lambda n=n: dve.tensor_tensor(out=gtmp[n], in0=bank(bp), in1=gate_bc[l][:, n * 512:(n + 1) * 512], op=ALU.mult),
                         reads=[PB[bp], d_gate[l]], writes=[d_gtmp[n]])
                    S.op("pool", lambda n=n, i=i, s=s: pool.tensor_tensor(out=xt[s][:, n * 512:(n + 1) * 512], in0=gtmp[n], in1=xnew[:, i, n * 512:(n + 1) * 512], op=ALU.add),
                         reads=[d_gtmp[n], d_xnew[i]], writes=[d_xt[s]])
            if l == 1:
                S.dma("sp", lambda i=i, s=s: sp.dma_start(out=out_d[i * 128:(i + 1) * 128, :], in_=xt[s]), reads=[d_xt[s]], writes=[], semdep=d_xt[s])
        return d_xt

    out_phase(0, ewout_d)
    if debug:
        S.dma("sp", lambda: sp.dma_start(out=dbg["xnew"], in_=xnew), reads=d_xnew, writes=[], semdep=d_xnew[0])
    S.barrier()

    def src1(i):
        return xnew[:, i, :], False, d_xnew[i]

    AR.top = T2
    wC = AR.bf16(3, 8, 512)
    L1_TMP = AR.top
    AR.top = L1_TMP + 6000
    wD = [AR.bf16(4, 8, 128), AR.bf16(4, 8, 128)]
    L1_END = AR.top
    d_wC = Dep("wC")
    dd_w = [Dep("wD0"), Dep("wD1")]

    def load_wD(j):
        for g in range(4):
            S.dma("pool", lambda g=g: pool.dma_start(
                out=wD[j % 2][:, g, :, :], in_=owin_d.rearrange("(k p) n -> p k n", p=128)[:, :, 1536 + g * 512 + j * 128:1536 + g * 512 + (j + 1) * 128]),
                writes=[dd_w[j % 2]], semdep=dd_w[j % 2])

    for g in range(3):
        for kk in range(2):
            S.dma("pool", lambda g=g, kk=kk: pool.dma_start(out=wC[:, g, kk * 4:(kk + 1) * 4, :], in_=owin_d.rearrange("(k p) n -> p k n", p=128)[:, kk * 4:(kk + 1) * 4, g * 512:(g + 1) * 512]),
                  writes=[d_wC], semdep=d_wC)
    load_wD(0)
    load_wD(1)
    norm_phase(1, 16, src1, L1_TMP)
    S.barrier()

    AR.top = L1_TMP
    wsF = AR.f32(4, 128)
    wsT = AR.bf16(4, 128)
    bsT = AR.f32(4)
    vgS = AR.f32(512)
    mhalf = AR.f32(1)
    c_gu = [AR.f32(512), AR.f32(512)]
    c_sg = [AR.f32(512), AR.f32(512)]
    c_gv = [AR.f32(512), AR.f32(512)]
    c_vn = [AR.bf16(512), AR.bf16(512)]
    c_y = [AR.bf16(512), AR.bf16(512)]
    c_junk = AR.f32(512)
    c_st = AR.f32(16, 4)
    assert AR.top <= L1_TMP + 6000, AR.top - L1_TMP
    d_C = Dep("Cconst")
    d_c = [{k: Dep("c%d_%s" % (p_, k)) for k in ("gu", "sg", "gv", "vn", "y")} for p_ in range(2)]
    d_cj, d_cst = Dep("c_junk"), [Dep("c_st%d" % i) for i in range(16)]
    S.dma("sp", lambda: sp.dma_start(out=wsF, in_=ws_d.rearrange("g t s -> t g s")), writes=[d_C], semdep=d_C)
    S.dma("sp", lambda: sp.dma_start(out=bsT, in_=bs_d.rearrange("g t -> t g"), allow_slow_non_contiguous=True), writes=[d_C], semdep=d_C)
    S.dma("sp", lambda: sp.dma_start(out=vgS, in_=vg_d.partition_broadcast(128)), writes=[d_C], semdep=d_C)
    S.op("pool", lambda: pool.memset(mhalf, -0.5), writes=[d_C])
    bw = nextbank8()
    pw = bank(bw).rearrange("p (g c) -> p g c", c=128)
    for g in range(4):
        S.op("pe", lambda g=g: pe.transpose(out=pw[:, g, :], in_=wsF[:, g, :], identity=identF), reads=[d_C, d_const], writes=[PB[bw]], inc=(g == 3))
    S.op("dve", lambda: dve.tensor_scalar(out=wsT, in0=pw, scalar1=0.5, scalar2=None, op0=ALU.mult), reads=[PB[bw]], writes=[d_C])
    S.op("dve", lambda: dve.tensor_scalar(out=bsT, in0=bsT, scalar1=0.5, scalar2=None, op0=ALU.mult), reads=[d_C], writes=[d_C])

    def c_item(i):
        p_ = i % 2
        Dc = d_c[p_]
        gu, sg, gv, vn, yy = c_gu[p_], c_sg[p_], c_gv[p_], c_vn[p_], c_y[p_]
        st = {}

        def s1():
            bu, bv, bg = nextbank8(), nextbank8(), nextbank8()
            st["bg"] = bg
            for g, bb in ((0, bu), (1, bv), (2, bg)):
                for k in range(8):
                    S.op("pe", lambda k=k, g=g, bb=bb: pe.matmul(bank(bb), lhsT=hT[:, k, i * 128:(i + 1) * 128], rhs=wC[:, g, k, :], start=(k == 0), stop=(k == 7)),
                         reads=[d_wC, d_hT[i]], writes=[PB[bb]], inc=(k == 7))
            S.op("act", lambda: act.activation(out=gu, in_=bank(bu), func=AF.Gelu), reads=[PB[bu]], writes=[Dc["gu"]])
            S.op("act", lambda: act.activation(out=gv, in_=bank(bv), func=AF.Gelu), reads=[PB[bv]], writes=[Dc["gv"]])
            S.op("act", lambda: act.activation(out=sg, in_=bank(bg), func=AF.Tanh, scale=0.5), reads=[PB[bg]], writes=[Dc["sg"]])
            S.op("dve", lambda: dve.scalar_tensor_tensor(out=sg, in0=sg, scalar=1.0, in1=bank(bg), op0=ALU.add, op1=ALU.mult), reads=[PB[bg]], writes=[Dc["sg"]])

        def s2():
            S.op("pool", lambda: pool.tensor_tensor(out=gu, in0=gu, in1=sg, op=ALU.mult), reads=[Dc["sg"]], writes=[Dc["gu"]])
            S.op("dve", lambda: dve.scalar_tensor_tensor(out=c_junk, in0=gv, scalar=1.0, in1=gv, op0=ALU.mult, op1=ALU.mult, accum_out=c_st[:, i, 0:1]), reads=[Dc["gv"]], writes=[d_cj, d_cst[i]])
            S.op("dve", lambda: dve.tensor_scalar(out=c_st[:, i, 1:2], in0=c_st[:, i, 0:1], scalar1=1.0 / 512, scalar2=EPS, op0=ALU.mult, op1=ALU.add), reads=[], writes=[d_cst[i]])
            S.op("pool", lambda: pool.tensor_tensor(out=c_st[:, i, 2:3], in0=c_st[:, i, 1:2], in1=mhalf, op=ALU.pow), reads=[d_C], writes=[d_cst[i]])
            S.op("dve", lambda: dve.scalar_tensor_tensor(out=vn, in0=gv, scalar=c_st[:, i, 2:3], in1=vgS, op0=ALU.mult, op1=ALU.mult), reads=[Dc["gv"], d_cst[i], d_C], writes=[Dc["vn"]])

        def s3():
            bs_ = nextbank8()
            ps = bank(bs_).rearrange("p (g c) -> p g c", c=128)
            for g in range(4):
                S.op("pe", lambda g=g: pe.matmul(ps[:, g, :], lhsT=wsT[:, g, :], rhs=vn[:, g * 128:(g + 1) * 128], start=True, stop=True), reads=[d_C, Dc["vn"]], writes=[PB[bs_]], inc=(g == 3))
            for g in range(4):
                S.op("dve", lambda g=g: dve.scalar_tensor_tensor(out=yy[:, g * 128:(g + 1) * 128], in0=ps[:, g, :], scalar=bsT[:, g:g + 1], in1=gu[:, g * 128:(g + 1) * 128], op0=ALU.add, op1=ALU.mult),
                     reads=[PB[bs_], d_C, Dc["gu"]], writes=[Dc["y"]])

        def s3b():
            bt_ = nextbank8()
            py = bank_bf(bt_)[:, 0:512].rearrange("p (g c) -> p g c", c=128)
            for g in range(4):
                S.op("pe", lambda g=g: pe.transpose(out=py[:, g, :], in_=yy[:, g * 128:(g + 1) * 128], identity=identB), reads=[Dc["y"], d_const], writes=[PB[bt_]], inc=(g == 3))
            S.op("act", lambda: act.copy(out=yT[:, 0:4, i * 128:(i + 1) * 128], in_=py), reads=[PB[bt_]], writes=[d_yT[g][i] for g in range(4)])
        return [s1, s2, s3, s3b]

    c_items = [c_item(i) for i in range(16)]
    for t in range(16 + 2):
        if 0 <= t - 2 < 16:
            c_items[t - 2][2]()
        if 0 <= t - 1 < 16:
            c_items[t - 1][1]()
        if t < 16:
            c_items[t][0]()
        if 0 <= t - 2 < 16:
            c_items[t - 2][3]()
    S.barrier()

    AR.top = T2
    cwT = AR.f32(4, 3)
    zb = AR.f32(SEQ + 2)
    bsg = AR.f32(SEQ)
    cvt = AR.f32(SEQ)
    d_csb = [AR.f32(512), AR.f32(512)]
    _sgd = AR.f32(512)
    d_sgd = [_sgd, _sgd]
    wo1 = AR.bf16(8, D)
    assert AR.top <= L1_TMP + 6000, (AR.top, L1_TMP)
    dd_c = Dep("cw")
    dd_z = [Dep("z%d" % b_) for b_ in range(4)]
    dd_bsg = [Dep("bsg%d" % b_) for b_ in range(4)]
    dd_cv = [Dep("cvt%d" % b_) for b_ in range(4)]
    _dsgd = Dep("sgd")
    dd_csb, dd_sgd = [Dep("csb0"), Dep("csb1")], [_dsgd, _dsgd]
    d_wo1 = [Dep("wo1_%d" % k_) for k_ in range(8)]
    for kk_ in range(2):
        S.dma("pool", lambda kk_=kk_: pool.dma_start(out=wo1[:, kk_ * 4:(kk_ + 1) * 4, :], in_=owout_d.rearrange("(k p) n -> p k n", p=128)[:, kk_ * 4:(kk_ + 1) * 4, :]),
              writes=d_wo1[kk_ * 4:(kk_ + 1) * 4], semdep=d_wo1[kk_ * 4])
    for w_ in range(3):
        S.dma("sp", lambda w_=w_: sp.dma_start(out=cwT[:, :, w_], in_=cw_d[w_].rearrange("(j p) -> p j", p=128), allow_slow_non_contiguous=True), writes=[dd_c], semdep=dd_c)
    S.op("pool", lambda: pool.memset(zb, 0.0), writes=dd_z)
    for j in range(4):
        slot = j % 2

        def d_item(b, p_):
            def s1():
                b1, b2 = nextbank8(), nextbank8()
                for g, bk in ((1, b1), (2, b2)):
                    for k in range(8):
                        S.op("pe", lambda k=k, g=g, bk=bk: pe.matmul(bank(bk), lhsT=wD[slot][:, g, k, :], rhs=hT[:, k, b * 512:(b + 1) * 512], start=(k == 0), stop=(k == 7)),
                             reads=[dd_w[slot]] + d_hT[b * 4:b * 4 + 4], writes=[PB[bk]], inc=(k == 7))
                S.op("act", lambda: act.copy(out=d_csb[p_], in_=bank(b1)), reads=[PB[b1]], writes=[dd_csb[p_]])
                S.op("dve", lambda: dve.tensor_tensor(out=zb[:, 1 + b * 512:1 + (b + 1) * 512], in0=bank(b2), in1=d_csb[p_], op=ALU.mult), reads=[PB[b2], dd_csb[p_]], writes=[dd_z[b]])

            def s2():
                b3, b4 = nextbank8(), nextbank8()
                for g, bk in ((3, b3), (0, b4)):
                    for k in range(8):
                        S.op("pe", lambda k=k, g=g, bk=bk: pe.matmul(bank(bk), lhsT=wD[slot][:, g, k, :], rhs=hT[:, k, b * 512:(b + 1) * 512], start=(k == 0), stop=(k == 7)),
                             reads=[dd_w[slot]] + d_hT[b * 4:b * 4 + 4], writes=[PB[bk]], inc=(k == 7))
                S.op("act", lambda: act.activation(out=d_sgd[p_], in_=bank(b3), func=AF.Silu), reads=[PB[b3]], writes=[dd_sgd[p_]])
                S.op("dve", lambda: dve.tensor_tensor(out=bsg[:, b * 512:(b + 1) * 512], in0=bank(b4), in1=d_sgd[p_], op=ALU.mult), reads=[PB[b4], dd_sgd[p_]], writes=[dd_bsg[b]])
            return [s1, s2]

        d_items = [d_item(b, b % 2) for b in range(4)]

        def conv_blk(b, j=j):
            lo, hi = b * 512, (b + 1) * 512
            zdeps = dd_z[max(b - 1, 0):min(b + 2, 4)]
            S.op("act", lambda: act.activation(out=cvt[:, lo:hi], in_=zb[:, lo:hi], func=AF.Copy, scale=cwT[:, j, 0:1]), reads=zdeps + [dd_c], writes=[dd_cv[b]])
            S.op("dve", lambda: dve.scalar_tensor_tensor(out=cvt[:, lo:hi], in0=zb[:, lo + 1:hi + 1], scalar=cwT[:, j, 1:2], in1=cvt[:, lo:hi], op0=ALU.mult, op1=ALU.add), reads=zdeps + [dd_c], writes=[dd_cv[b]])
            S.op("dve", lambda: dve.scalar_tensor_tensor(out=cvt[:, lo:hi], in0=zb[:, lo + 2:hi + 2], scalar=cwT[:, j, 2:3], in1=cvt[:, lo:hi], op0=ALU.mult, op1=ALU.add), reads=zdeps + [dd_c], writes=[dd_cv[b]])
            S.op("pool", lambda: pool.tensor_tensor(out=yT[:, 4 + j, lo:hi], in0=cvt[:, lo:hi], in1=bsg[:, lo:hi], op=ALU.mult), reads=[dd_cv[b], dd_bsg[b]], writes=d_yT[4 + j][b * 4:b * 4 + 4])

        for t in range(5):
            if t >= 1:
                d_items[t - 1][1]()
            if t < 4:
                d_items[t][0]()
            if t == 4 and j + 2 < 4:
                load_wD(j + 2)
            if t >= 1:
                conv_blk(t - 1)
    if debug:
        S.dma("sp", lambda: sp.dma_start(out=dbg["yT1"], in_=yT), reads=[x for r in d_yT for x in r], writes=[], semdep=d_yT[0][0])
    S.barrier()

    d_fin = out_phase(1, owout_d, pre=(wo1, d_wo1))
    S.barrier()
    return nc


_CONST = {}


def _consts():
    if _CONST:
        return _CONST
    f32 = np.float32
    ident = np.eye(128, dtype=f32)
    perm = np.zeros((128, 128), f32)
    for m in range(128):
        partner = m + 32 if (m % 64) < 32 else m - 32
        perm[partner, m] = 1.0
    bones = np.zeros((128, 128), f32)
    bones[0:64, 0:64] = 1.0
    bones[64:128, 64:128] = 1.0
    rows = SEQ // 64
    row = np.repeat(np.arange(rows, dtype=f32), 64)
    col = np.tile(np.arange(64, dtype=f32), rows)
    n_freq = 16
    inv = (f32(10000.0) ** (-np.arange(n_freq, dtype=f32) / f32(n_freq))).astype(f32)
    ang = np.concatenate([row[:, None] * inv, col[:, None] * inv], axis=-1).astype(f32)
    cos = np.cos(ang).astype(f32).T
    sin = np.sin(ang).astype(f32).T
    cosT = np.concatenate([cos, cos, cos, cos], axis=0)
    sinT = np.concatenate([-sin, sin, -sin, sin], axis=0)
    tri = np.zeros((64, 2, 64), f32)
    s_idx = np.arange(64)[:, None]
    t_idx = np.arange(64)[None, :]
    tri[:, 0, :] = (s_idx <= t_idx)
    tri[:, 1, :] = (s_idx >= t_idx)
    smask = np.ones((128, 512), f32)
    smask[:, ::64] = 0.0
    _CONST.update(identF=ident, perm=perm, bones=bones, cosT=np.ascontiguousarray(cosT), sinT=np.ascontiguousarray(sinT), tri=tri, smask=smask)
    return _CONST


def make_in_maps(x, c, ctx, c_ctx, norm_gain, ada_w, ada_b, even_w_in, even_w_out, attn_qk_gain,
                 attn_lambda, attn_subln_gain, hgrn_lb_logits, hgrn_norm_gain, odd_w_in, odd_w_out,
                 gmlp_v_gain, gmlp_w_s, gmlp_b_s, conv_w):
    f = lambda a: np.ascontiguousarray(np.asarray(a, dtype=np.float32))
    shared = dict(
        norm_gain=f(norm_gain), ada_w=f(ada_w), ada_b=f(ada_b), even_w_in=f(even_w_in)[0], even_w_out=f(even_w_out)[0],
        qk_gain=f(attn_qk_gain)[0], attn_lambda=f(attn_lambda)[0].reshape(256), subln=f(attn_subln_gain)[0],
        lb_logits=f(hgrn_lb_logits), hgrn_g=f(hgrn_norm_gain)[0], odd_w_in=f(odd_w_in)[0], odd_w_out=f(odd_w_out)[0],
        v_gain=f(gmlp_v_gain)[0], w_s=f(gmlp_w_s)[0], b_s=f(gmlp_b_s)[0], conv_w=f(conv_w)[0])
    shared.update(_consts())
    x = f(x)
    c = f(c)
    ctx = f(ctx)
    c_ctx = f(c_ctx)
    maps = []
    for b in range(8):
        m = dict(shared)
        m["x"] = x[b]
        m["ctx"] = ctx[b]
        m["cvec"] = np.ascontiguousarray(np.stack([c[b], c_ctx], axis=0))
        maps.append(m)
    return maps


def kernel(**inputs):
    maps = make_in_maps(**inputs)
    nc = build(debug=False)
    res = run_bass_kernel_spmd(nc, maps, core_ids=list(range(8)))
    return np.stack([np.asarray(r["out"], dtype=np.float32) for r in res.results], axis=0)
```

```python
import numpy as np
import concourse.bass as bass
import concourse.mybir as mybir
from concourse.bass_utils import run_bass_kernel_spmd
from concourse.alu_op_type import AluOpType as ALU

F32 = mybir.dt.float32
BF16 = mybir.dt.bfloat16
AF = mybir.ActivationFunctionType

D = 1024
SEQ = 2048
CTX = 256
NT = SEQ + CTX
EPS = 1e-6
LAM_INIT = 0.8 - 0.6 * 1.0


class Dep:
    __slots__ = ("name", "w", "r", "dsem", "dcnt", "excl")

    def __init__(self, name, excl=False):
        self.name = name
        self.excl = excl
        self.w = None
        self.r = []
        self.dsem = None
        self.dcnt = 0


class Sched:
    ENG = ("pe", "act", "dve", "pool", "sp")

    def __init__(self, nc):
        self.nc = nc
        self.engs = {"pe": nc.tensor, "act": nc.scalar, "dve": nc.vector, "pool": nc.gpsimd, "sp": nc.sync}
        self.cnt = {e: 0 for e in self.ENG}
        self.sem = {}
        self.waited = {e: {} for e in self.ENG}
        self.dsems = []
        self.nops = 0
        for e in self.ENG:
            self.sem[e] = nc.alloc_semaphore("s_" + e)

    def _waits(self, eng, reads, writes):
        need = {}

        def add(p):
            if p is None:
                return
            s, v = p
            k = id(s)
            if k not in need or need[k][1] < v:
                need[k] = (s, v)

        for d in reads:
            add(d.w)
        for d in writes:
            add(d.w)
            for p in d.r:
                add(p)
        wd = self.waited[eng]
        own = self.sem[eng]
        engine = self.engs[eng]
        for k, (s, v) in need.items():
            if s is own and (eng == "pe" or v > self.cnt[eng]):
                continue
            if wd.get(k, 0) >= v:
                continue
            wd[k] = v
            engine.wait_ge(s, v)

    def op(self, eng, fn, reads=(), writes=(), inc=True):
        ex = [d for d in reads if d.excl]
        if ex:
            reads = [d for d in reads if not d.excl]
            writes = list(writes) + ex
        self._waits(eng, reads, writes)
        val = self.cnt[eng] + 1
        ins = fn()
        self.nops += 1
        if inc:
            self.cnt[eng] = val
            ins.then_inc(self.sem[eng], 1)
        tok = (self.sem[eng], val)
        for d in reads:
            d.r.append(tok)
            if len(d.r) > 64:
                d.r = self._compact(d.r)
        for d in writes:
            d.w = tok
            d.r = []

    @staticmethod
    def _compact(lst):
        best = {}
        for s, v in lst:
            k = id(s)
            if k not in best or best[k][1] < v:
                best[k] = (s, v)
        return list(best.values())

    def dma(self, eng, fn, reads=(), writes=(), semdep=None):
        self._waits(eng, reads, writes)
        d0 = semdep
        if d0.dsem is None:
            d0.dsem = self.nc.alloc_semaphore("d%d_%s" % (len(self.dsems), d0.name))
            self.dsems.append(d0)
        d0.dcnt += 16
        ins = fn()
        ins.then_inc(d0.dsem, 16)
        tok = (d0.dsem, d0.dcnt)
        for d in reads:
            d.r.append(tok)
        for d in writes:
            d.w = tok
            d.r = []

    def wait_all(self, eng, deps):
        self._waits(eng, (), deps)

    def barrier(self):
        for e in self.ENG:
            engine = self.engs[e]
            wd = self.waited[e]
            for f in self.ENG:
                if f == e or self.cnt[f] == 0:
                    continue
                s = self.sem[f]
                if wd.get(id(s), 0) >= self.cnt[f]:
                    continue
                wd[id(s)] = self.cnt[f]
                engine.wait_ge(s, self.cnt[f])
            for d0 in self.dsems:
                if wd.get(id(d0.dsem), 0) >= d0.dcnt:
                    continue
                wd[id(d0.dsem)] = d0.dcnt
                engine.wait_ge(d0.dsem, d0.dcnt)


def skew(items, newest_first=False):
    nst = max(len(it) for it in items)
    for t in range(len(items) + nst - 1):
        for s_ in (range(nst) if newest_first else reversed(range(nst))):
            i = t - s_
            if 0 <= i < len(items) and s_ < len(items[i]):
                items[i][s_]()


class Arena:
    def __init__(self, ap_f32):
        self.ap = ap_f32
        self.top = 0
        self.n = ap_f32.shape[1]

    @staticmethod
    def _view(v, shape):
        if len(shape) == 1:
            return v
        names = ["d%d" % i for i in range(len(shape))]
        pat = "p (" + " ".join(names) + ") -> p " + " ".join(names)
        kw = {names[i]: int(shape[i]) for i in range(1, len(shape))}
        return v.rearrange(pat, **kw)

    def f32(self, *shape):
        n = int(np.prod(shape))
        off = self.top
        self.top += n
        assert self.top <= self.n, ("arena overflow", self.top, self.n)
        return self._view(self.ap[:, off:off + n], shape)

    def bf16(self, *shape):
        n = int(np.prod(shape))
        nw = (n + 1) // 2
        off = self.top
        self.top += nw
        assert self.top <= self.n, ("arena overflow", self.top, self.n)
        return self._view(self.ap[:, off:off + nw].bitcast(BF16)[:, 0:n], shape)


def build(debug=False):
    nc = bass.Bass("TRN2", target_bir_lowering=False)

    def din(name, shape, dt=F32):
        return nc.dram_tensor(name, list(shape), dt, kind="ExternalInput").ap()

    x_d = din("x", [SEQ, D])
    ctx_d = din("ctx", [CTX, D])
    cvec_d = din("cvec", [2, D])
    ng_d = din("norm_gain", [2, D])
    adaw_d = din("ada_w", [2, D, 3 * D])
    adab_d = din("ada_b", [2, 3 * D])
    ewin_d = din("even_w_in", [D, 4608])
    ewout_d = din("even_w_out", [D, D])
    qkg_d = din("qk_gain", [2, 64])
    lam_d = din("attn_lambda", [256])
    subln_d = din("subln", [128])
    lbl_d = din("lb_logits", [2, 2, 512])
    hg_d = din("hgrn_g", [128])
    owin_d = din("odd_w_in", [D, 3584])
    owout_d = din("odd_w_out", [D, D])
    vg_d = din("v_gain", [512])
    ws_d = din("w_s", [4, 128, 128])
    bs_d = din("b_s", [4, 128])
    cw_d = din("conv_w", [3, 512])
    identF_d = din("identF", [128, 128])
    perm_d = din("perm", [128, 128])
    bones_d = din("bones", [128, 128])
    cos_d = din("cosT", [128, SEQ])
    sin_d = din("sinT", [128, SEQ])
    tri_d = din("tri", [64, 2, 64])
    smask_d = din("smask", [128, 512])
    out_d = nc.dram_tensor("out", [SEQ, D], F32, kind="ExternalOutput").ap()
    dbg = {}
    if debug:
        dbg["hT0"] = nc.dram_tensor("dbg_hT0", [128, 8, NT], BF16, kind="ExternalOutput").ap()
        dbg["yT0"] = nc.dram_tensor("dbg_yT0", [128, 8, SEQ], BF16, kind="ExternalOutput").ap()
        dbg["xnew"] = nc.dram_tensor("dbg_xnew", [128, 16, D], F32, kind="ExternalOutput").ap()
        dbg["yT1"] = nc.dram_tensor("dbg_yT1", [128, 8, SEQ], BF16, kind="ExternalOutput").ap()
        dbg["mods"] = nc.dram_tensor("dbg_mods", [128, 2, 3, 8, 2], F32, kind="ExternalOutput").ap()

    S = Sched(nc)
    E = nc
    NW = (nc.sbuf_bytes_remaining - 2048) // 4
    arena_t = nc.alloc_sbuf_tensor("arena", [128, NW], F32).ap()
    AR = Arena(arena_t)
    psum = nc.alloc_psum_tensor("psum", [128, 8, 512], F32).ap()
    PB = [Dep("pb%d" % i, excl=True) for i in range(8)]

    def bank(i):
        return psum[:, i, :]

    def bank_bf(i):
        return psum[:, i, :].bitcast(BF16)

    hT = AR.bf16(8, NT)
    yT = AR.bf16(8, SEQ)
    identF = AR.f32(128)
    identB = AR.bf16(128)
    zerosB = AR.bf16(512)
    gate_bc = [AR.f32(D), AR.f32(D)]
    modsc = AR.f32(2, 3, 8, 2)
    small = AR.f32(64)
    P_TOP = AR.top
    d_hT = [Dep("hT%d" % i) for i in range(18)]
    d_yT = [[Dep("yT%d_%d" % (k, i)) for i in range(16)] for k in range(8)]
    d_const = Dep("const")
    d_mods = Dep("mods")
    d_gate = [Dep("gate0"), Dep("gate1")]
    d_small = Dep("small")

    sp, act, dve, pool, pe = nc.sync, nc.scalar, nc.vector, nc.gpsimd, nc.tensor

    S.dma("sp", lambda: sp.dma_start(out=identF, in_=identF_d), writes=[d_const], semdep=d_const)
    S.op("dve", lambda: dve.tensor_copy(out=identB, in_=identF), reads=[d_const], writes=[d_const])
    S.op("pool", lambda: pool.memset(zerosB, 0.0), writes=[d_const])

    AR.top = P_TOP
    cvT = AR.f32(8, 2)
    csT = AR.bf16(8, 2)
    Rrow = AR.f32(3 * D)
    adab = AR.f32(3 * D)
    gT = AR.f32(2, 8)
    onesr = AR.f32(128)
    wada = [AR.f32(3 * D) for _ in range(3)]
    wadab = [AR.bf16(3 * D), AR.bf16(3 * D)]
    d_cv, d_R, d_adab, d_gT, d_ones = Dep("cv"), Dep("R"), Dep("adab"), Dep("gT"), Dep("ones")
    d_wada = [[Dep("wada%d_%d" % (i, j)) for j in range(3)] for i in range(3)]
    d_wadab = [[Dep("wadab%d_%d" % (i, j)) for j in range(4)] for i in range(2)]

    for r in range(2):
        S.dma("sp", lambda r=r: sp.dma_start(out=cvT[:, :, r], in_=cvec_d[r].rearrange("(k p) -> p k", p=128), allow_slow_non_contiguous=True), writes=[d_cv], semdep=d_cv)
        S.dma("sp", lambda r=r: sp.dma_start(out=gT[:, r, :], in_=ng_d[r].rearrange("(k p) -> p k", p=128), allow_slow_non_contiguous=True), writes=[d_gT], semdep=d_gT)
    S.op("act", lambda: act.activation(out=csT, in_=cvT, func=AF.Silu), reads=[d_cv], writes=[d_cv])
    S.op("pool", lambda: pool.memset(onesr, 1.0), writes=[d_ones])
    wi = 0
    for l in range(2):
        S.dma("sp", lambda l=l: sp.dma_start(out=adab[0:2, :], in_=adab_d[l].partition_broadcast(2)), writes=[d_adab], semdep=d_adab)
        for k in range(8):
            slot = wi % 3
            bs_ = wi % 2
            wi += 1
            for hh in range(3):
                qn_ = ("sp", "pool", "act")[hh]
                qe_ = (sp, pool, act)[hh]
                S.dma(qn_, lambda l=l, k=k, slot=slot, hh=hh, qe_=qe_: qe_.dma_start(
                    out=wada[slot][:, hh * 1024:(hh + 1) * 1024], in_=adaw_d[l][k * 128:(k + 1) * 128, hh * 1024:(hh + 1) * 1024]),
                    writes=[d_wada[slot][hh]], semdep=d_wada[slot][hh])
            S.op("dve", lambda slot=slot, bs_=bs_: dve.tensor_copy(out=wadab[bs_][:, 0:1024], in_=wada[slot][:, 0:1024]), reads=[d_wada[slot][0]], writes=[d_wadab[bs_][0]])
            S.op("act", lambda slot=slot, bs_=bs_: act.copy(out=wadab[bs_][:, 1024:1536], in_=wada[slot][:, 1024:1536]), reads=[d_wada[slot][1]], writes=[d_wadab[bs_][1]])
            S.op("act", lambda slot=slot, bs_=bs_: act.copy(out=wadab[bs_][:, 1536:2560], in_=wada[slot][:, 1536:2560]), reads=[d_wada[slot][1], d_wada[slot][2]], writes=[d_wadab[bs_][2]])
            S.op("pool", lambda slot=slot, bs_=bs_: pool.tensor_copy(out=wadab[bs_][:, 2560:3072], in_=wada[slot][:, 2560:3072]), reads=[d_wada[slot][2]], writes=[d_wadab[bs_][3]])
            for n in range(6):
                part = (0, 0, 1, 2, 2, 3)[n]
                S.op("pe", lambda k=k, n=n, bs_=bs_: pe.matmul(bank(n)[0:2, :], lhsT=csT[:, k, :], rhs=wadab[bs_][:, n * 512:(n + 1) * 512], start=(k == 0), stop=(k == 7)),
                     reads=[d_cv, d_wadab[bs_][part]], writes=[PB[n]])
        for n in range(6):
            S.op("dve", lambda n=n: dve.tensor_tensor(out=Rrow[0:2, n * 512:(n + 1) * 512], in0=bank(n)[0:2, :], in1=adab[0:2, n * 512:(n + 1) * 512], op=ALU.add),
                 reads=[PB[n], d_adab], writes=[d_R])
        pt = bank(6)[:, 0:32].rearrange("p (j r) -> p j r", r=2)
        for j in range(16):
            S.op("pe", lambda j=j: pe.transpose(out=pt[:, j, :], in_=Rrow[0:2, j * 128:(j + 1) * 128], identity=identF[0:2, 0:2]),
                 reads=[d_R, d_const], writes=[PB[6]], inc=(j == 15))
        S.op("dve", lambda l=l: dve.tensor_copy(out=modsc[:, l, 0, :, :], in_=pt[:, 0:8, :]), reads=[PB[6]], writes=[d_mods])
        S.op("dve", lambda l=l: dve.tensor_copy(out=modsc[:, l, 2, :, :], in_=pt[:, 8:16, :]), reads=[PB[6]], writes=[d_mods])
        for r in range(2):
            S.op("dve", lambda l=l, r=r: dve.scalar_tensor_tensor(out=modsc[:, l, 1, :, r], in0=modsc[:, l, 2, :, r], scalar=1.0, in1=gT[:, l, :], op0=ALU.add, op1=ALU.mult),
                 reads=[d_mods, d_gT], writes=[d_mods])
        for n in range(2):
            S.op("pe", lambda n=n: pe.matmul(bank(2 + n), lhsT=onesr[0:1, :], rhs=Rrow[0:1, 2048 + n * 512:2048 + (n + 1) * 512], start=True, stop=True),
                 reads=[d_R, d_ones], writes=[PB[2 + n]])
            S.op("act", lambda n=n, l=l: act.copy(out=gate_bc[l][:, n * 512:(n + 1) * 512], in_=bank(2 + n)), reads=[PB[2 + n]], writes=[d_gate[l]])
    if debug:
        S.dma("sp", lambda: sp.dma_start(out=dbg["mods"], in_=modsc), reads=[d_mods], writes=[], semdep=d_mods)
    S.barrier()

    def norm_phase(l, ntiles, src_fn, top):
        AR.top = top
        nxt = 4 if l == 0 else 0
        xt = [AR.f32(D) for _ in range(nxt)]
        xn = [AR.f32(D), AR.f32(D)]
        junk = AR.bf16(D)
        stat = AR.f32(3, 18)
        d_xt = [Dep("xt%d" % i_) for i_ in range(nxt)]
        d_xn = [Dep("xn0"), Dep("xn1")]
        d_junk, d_stat = Dep("junk"), [Dep("stat%d" % i) for i in range(18)]

        def item(i):
            s = i % 2
            src, is_ctx, sdep = src_fn(i)
            b0 = 4 + 2 * (i % 2)
            pt = psum[:, b0:b0 + 2, :].rearrange("p b (j c) -> p (b j) c", c=128)
            r = 1 if is_ctx else 0
            if sdep is None:
                xin, xdep = xt[i % 4], d_xt[i % 4]
            else:
                xin, xdep = src, sdep

            def s0():
                if sdep is None:
                    S.dma("sp", lambda: sp.dma_start(out=xt[i % 4], in_=src), writes=[d_xt[i % 4]], semdep=d_xt[i % 4])

            def s1():
                S.op("act", lambda: act.activation(out=junk, in_=xin, func=AF.Square, accum_out=stat[:, 0, i:i + 1]), reads=[xdep], writes=[d_junk, d_stat[i]])
                S.op("act", lambda: act.activation(out=stat[:, 1, i:i + 1], in_=stat[:, 0, i:i + 1], func=AF.Sqrt, scale=1.0 / D, bias=EPS), reads=[d_stat[i]], writes=[d_stat[i]])
                S.op("dve", lambda: dve.reciprocal(out=stat[:, 2, i:i + 1], in_=stat[:, 1, i:i + 1]), reads=[d_stat[i]], writes=[d_stat[i]])

            def s2():
                S.op("dve", lambda: dve.tensor_scalar(out=xn[s], in0=xin, scalar1=stat[:, 2, i:i + 1], scalar2=None, op0=ALU.mult), reads=[xdep, d_stat[i]], writes=[d_xn[s]])
                for k in range(8):
                    S.op("pe", lambda k=k: pe.transpose(out=pt[:, k, :], in_=xn[s][:, k * 128:(k + 1) * 128], identity=identF),
                         reads=[d_xn[s], d_const], writes=[PB[b0 + k // 4]], inc=(k == 7))
                for _ in range(6):
                    S.op("pe", lambda: pe.matmul(bank(3), lhsT=zerosB[:, 0:128], rhs=zerosB, start=True, stop=True, skip_group_check=True), reads=[d_const], writes=[PB[3]], inc=False)

            def s3():
                for k in range(8):
                    if k < 4:
                        S.op("act", lambda k=k: act.activation(out=hT[:, k, i * 128:(i + 1) * 128], in_=pt[:, k, :], func=AF.Identity,
                                                               scale=modsc[:, l, 1, k, r:r + 1], bias=modsc[:, l, 0, k, r:r + 1]),
                             reads=[PB[b0 + k // 4], d_mods], writes=[d_hT[i]])
                    else:
                        S.op("dve", lambda k=k: dve.tensor_scalar(out=hT[:, k, i * 128:(i + 1) * 128], in0=pt[:, k, :],
                                                                  scalar1=modsc[:, l, 1, k, r:r + 1], scalar2=modsc[:, l, 0, k, r:r + 1], op0=ALU.mult, op1=ALU.add),
                             reads=[PB[b0 + k // 4], d_mods], writes=[d_hT[i]])
            return [s0, s1, s2, s3]

        skew([item(i) for i in range(ntiles)], newest_first=True)

    def src0(i):
        if i < 2:
            return ctx_d[i * 128:(i + 1) * 128, :], True, None
        return x_d[(i - 2) * 128:(i - 1) * 128, :], False, None


    AR.top = P_TOP
    cosT = AR.f32(SEQ)
    sinT = AR.f32(SEQ)
    permS = AR.f32(128)
    bonesS = AR.f32(128)
    gqk = AR.f32(2)
    lamb = AR.f32(256)
    lamt = AR.f32(8)
    gS = AR.f32(128)
    epsT = AR.f32(1)
    mhA = AR.f32(1)
    wA = [AR.bf16(4, 8, 128), AR.bf16(4, 8, 128)]
    qT = [AR.bf16(SEQ), AR.bf16(SEQ)]
    kT0 = [AR.bf16(NT), AR.bf16(NT)]
    kT1 = [AR.bf16(NT), AR.bf16(NT)]
    vaug = [AR.bf16(18, 130), AR.bf16(18, 130)]
    gateA = [AR.bf16(16, 128), AR.bf16(16, 128)]
    t_sq = [AR.f32(512), AR.f32(512)]
    t_qg = [AR.f32(512), AR.f32(512)]
    t_rs = [AR.f32(512), AR.f32(512)]
    t_a = [AR.f32(512), AR.f32(512)]
    t_b = [AR.f32(512), AR.f32(512)]
    t_gs = AR.f32(512)
    Et = [AR.bf16(512) for _ in range(4)]
    ep_f = AR.f32(16, 128)
    ep_s = AR.f32(8, 8)
    ep_y = AR.bf16(8, 128)
    uc = [0]
    d_A = Dep("Aconst")
    d_Ar = Dep("Arope")
    d_wA = [Dep("wA0"), Dep("wA1")]
    d_qT = [[Dep("qT%d_%d" % (p_, i)) for i in range(4)] for p_ in range(2)]
    d_kT = [[Dep("kT%d_%d" % (p_, i)) for i in range(5)] for p_ in range(2)]
    d_v = [[Dep("vaug%d_%d" % (p_, i)) for i in range(5)] for p_ in range(2)]
    d_gA = [[Dep("gateA%d_%d" % (p_, i)) for i in range(4)] for p_ in range(2)]
    d_tsq = [Dep("tsq0"), Dep("tsq1")]
    d_tqg = [Dep("tqg0"), Dep("tqg1")]
    d_trs = [Dep("trs0"), Dep("trs1")]
    d_ta = [Dep("ta0"), Dep("ta1")]
    d_tb = [Dep("tb0"), Dep("tb1")]
    d_tgs = Dep("tgs")
    d_E = [Dep("E%d" % i) for i in range(4)]
    d_ep = [Dep("ep%d" % i) for i in range(8)]

    S.dma("sp", lambda: sp.dma_start(out=cosT, in_=cos_d), writes=[d_Ar], semdep=d_Ar)
    S.dma("sp", lambda: sp.dma_start(out=sinT, in_=sin_d), writes=[d_Ar], semdep=d_Ar)
    S.dma("sp", lambda: sp.dma_start(out=permS, in_=perm_d), writes=[d_Ar], semdep=d_Ar)
    S.dma("sp", lambda: sp.dma_start(out=bonesS, in_=bones_d), writes=[d_Ar], semdep=d_Ar)
    for m in range(2):
        S.dma("sp", lambda m=m: sp.dma_start(out=gqk[m * 64:(m + 1) * 64, :], in_=qkg_d.rearrange("r d -> d r"), allow_slow_non_contiguous=True), writes=[d_A], semdep=d_A)
    S.dma("sp", lambda: sp.dma_start(out=lamb, in_=lam_d.partition_broadcast(128)), writes=[d_A], semdep=d_A)
    S.dma("sp", lambda: sp.dma_start(out=gS, in_=subln_d.partition_broadcast(128)), writes=[d_A], semdep=d_A)
    S.op("dve", lambda: dve.tensor_scalar(out=gqk[:, 0:1], in0=gqk[:, 0:1], scalar1=0.125, scalar2=None, op0=ALU.mult), reads=[d_A], writes=[d_A])
    S.op("dve", lambda: dve.tensor_scalar(out=gS, in0=gS, scalar1=1.0 - LAM_INIT, scalar2=None, op0=ALU.mult), reads=[d_A], writes=[d_A])
    S.op("dve", lambda: dve.tensor_tensor(out=lamb[:, 0:64], in0=lamb[:, 0:64], in1=lamb[:, 64:128], op=ALU.mult), reads=[d_A], writes=[d_A])
    S.op("dve", lambda: dve.tensor_tensor(out=lamb[:, 128:192], in0=lamb[:, 128:192], in1=lamb[:, 192:256], op=ALU.mult), reads=[d_A], writes=[d_A])
    S.op("dve", lambda: dve.tensor_reduce(out=lamt[:, 0:1], in_=lamb[:, 0:64], op=ALU.add, axis=mybir.AxisListType.X), reads=[d_A], writes=[d_A])
    S.op("dve", lambda: dve.tensor_reduce(out=lamt[:, 1:2], in_=lamb[:, 128:192], op=ALU.add, axis=mybir.AxisListType.X), reads=[d_A], writes=[d_A])
    S.op("act", lambda: act.activation(out=lamt[:, 2:4], in_=lamt[:, 0:2], func=AF.Exp), reads=[d_A], writes=[d_A])
    S.op("dve", lambda: dve.scalar_tensor_tensor(out=lamt[:, 4:5], in0=lamt[:, 3:4], scalar=-LAM_INIT, in1=lamt[:, 2:3], op0=ALU.add, op1=ALU.subtract), reads=[d_A], writes=[d_A])
    S.op("pool", lambda: pool.memset(epsT, EPS), writes=[d_A])
    S.op("pool", lambda: pool.memset(mhA, -0.5), writes=[d_A])
    for p_ in range(2):
        S.op("pool", lambda p_=p_: pool.memset(kT0[p_], 0.0), writes=d_kT[p_])
        S.op("pool", lambda p_=p_: pool.memset(kT1[p_], 0.0), writes=d_kT[p_])
        S.op("pool", lambda p_=p_: pool.memset(vaug[p_][:, :, 128:130], 1.0), writes=d_v[p_])

    PROJ_B = [3, 4, 5, 6, 7]
    pr = [0]

    def nextbank():
        b = PROJ_B[pr[0] % len(PROJ_B)]
        pr[0] += 1
        return b

    blk = [0]

    def qk_item(h, which, tok0, ntok, rope, dst_fn, ddst):
        s = blk[0] % 2
        blk[0] += 1
        slot = h % 2
        st = {}

        def m1():
            bp = nextbank()
            st["bp"] = bp
            for k in range(8):
                S.op("pe", lambda k=k: pe.matmul(bank(bp)[:, 0:ntok], lhsT=wA[slot][:, which, k, :], rhs=hT[:, k, tok0:tok0 + ntok], start=(k == 0), stop=(k == 7)),
                     reads=[d_wA[slot]] + d_hT[tok0 // 128:(tok0 + ntok + 127) // 128], writes=[PB[bp]], inc=(k == 7))

        def m2():
            bp = st["bp"]
            S.op("dve", lambda: dve.tensor_copy(out=t_b[s][:, 0:ntok], in_=bank(bp)[:, 0:ntok]), reads=[PB[bp]], writes=[d_tb[s]])
            S.op("dve", lambda: dve.tensor_tensor(out=t_sq[s][:, 0:ntok], in0=t_b[s][:, 0:ntok], in1=t_b[s][:, 0:ntok], op=ALU.mult), reads=[d_tb[s]], writes=[d_tsq[s]])
            S.op("dve", lambda: dve.tensor_scalar(out=t_qg[s][:, 0:ntok], in0=t_b[s][:, 0:ntok], scalar1=gqk[:, which:which + 1], scalar2=None, op0=ALU.mult),
                 reads=[d_tb[s], d_A], writes=[d_tqg[s]])

        def m3():
            bs = nextbank()
            st["bs"] = bs
            S.op("pe", lambda: pe.matmul(bank(bs)[:, 0:ntok], lhsT=bonesS, rhs=t_sq[s][:, 0:ntok], start=True, stop=True), reads=[d_tsq[s], d_Ar], writes=[PB[bs]])
            if rope:
                br = nextbank()
                st["br"] = br
                S.op("pe", lambda: pe.matmul(bank(br)[:, 0:ntok], lhsT=permS, rhs=t_qg[s][:, 0:ntok], start=True, stop=True), reads=[d_tqg[s], d_Ar], writes=[PB[br]])

        def m4():
            bs = st["bs"]
            S.op("act", lambda: act.activation(out=t_rs[s][:, 0:ntok], in_=bank(bs)[:, 0:ntok], func=AF.Ln, scale=1.0 / 64, bias=epsT[:, 0:1]), reads=[PB[bs], d_A], writes=[d_trs[s]])
            S.op("act", lambda: act.activation(out=t_rs[s][:, 0:ntok], in_=t_rs[s][:, 0:ntok], func=AF.Exp, scale=-0.5), reads=[d_trs[s]], writes=[d_trs[s]])
            if rope:
                br = st["br"]
                p0 = tok0 - CTX
                S.op("pool", lambda: pool.tensor_tensor(out=t_a[s][:, 0:ntok], in0=t_qg[s][:, 0:ntok], in1=cosT[:, p0:p0 + ntok], op=ALU.mult), reads=[d_tqg[s], d_Ar], writes=[d_ta[s]])
                S.op("dve", lambda: dve.tensor_tensor(out=t_b[s][:, 0:ntok], in0=bank(br)[:, 0:ntok], in1=sinT[:, p0:p0 + ntok], op=ALU.mult), reads=[PB[br], d_Ar, d_tsq[s], d_tqg[s]], writes=[d_tb[s]])

        def m5():
            if rope:
                S.op("dve", lambda: dve.tensor_tensor(out=t_a[s][:, 0:ntok], in0=t_a[s][:, 0:ntok], in1=t_b[s][:, 0:ntok], op=ALU.add), reads=[d_ta[s], d_tb[s]], writes=[d_ta[s]])
                srcv, sdep = t_a[s], d_ta[s]
            else:
                srcv, sdep = t_qg[s], d_tqg[s]
            for (dst, p_lo, p_hi) in dst_fn():
                eng_, ee_ = ("dve", dve)
                S.op(eng_, lambda dst=dst, p_lo=p_lo, p_hi=p_hi: ee_.tensor_tensor(out=dst, in0=srcv[p_lo:p_hi, 0:ntok], in1=t_rs[s][p_lo:p_hi, 0:ntok], op=ALU.mult),
                     reads=[sdep, d_trs[s]], writes=[ddst])
        return [m1, m2, m3, m4, m5]

    def proj_tasks(h, interleave=False):
        hp = h % 2
        slot = h % 2
        items = []
        for i in range(4):
            items.append(qk_item(h, 0, CTX + i * 512, 512, True, lambda i=i: [(qT[hp][:, i * 512:(i + 1) * 512], 0, 128)], d_qT[hp][i]))
        items.append(qk_item(h, 1, 0, 256, False, lambda: [(kT0[hp][0:64, 0:256], 0, 64), (kT1[hp][64:128, 0:256], 64, 128)], d_kT[hp][0]))
        for i in range(4):
            t0 = CTX + i * 512
            items.append(qk_item(h, 1, t0, 512, True, lambda t0=t0: [(kT0[hp][0:64, t0:t0 + 512], 0, 64), (kT1[hp][64:128, t0:t0 + 512], 64, 128)], d_kT[hp][1 + i]))
        tasks = []
        if interleave:
            tasks.extend(items[0][0:3])
            for n_ in range(1, len(items)):
                a_, b_ = items[n_], items[n_ - 1]
                tasks.extend([a_[0], b_[3], a_[1], b_[4], a_[2]])
            tasks.extend(items[-1][3:5])
        else:
            for it in items:
                tasks.extend(it)

        def v_task(grp):
            st = {}

            def a():
                tiles = list(range(grp * 4, min(grp * 4 + 4, 18)))
                bp = nextbank()
                st["bp"] = bp
                pv = bank(bp).rearrange("p (j c) -> p j c", c=128)
                for jj, ti in enumerate(tiles):
                    for k in range(8):
                        S.op("pe", lambda k=k, jj=jj, ti=ti: pe.matmul(pv[:, jj, :], lhsT=hT[:, k, ti * 128:(ti + 1) * 128], rhs=wA[slot][:, 2, k, :], start=(k == 0), stop=(k == 7)),
                             reads=[d_wA[slot], d_hT[ti]], writes=[PB[bp]], inc=(k == 7 and jj == len(tiles) - 1))

            def b():
                bp = st["bp"]
                pv = bank(bp).rearrange("p (j c) -> p j c", c=128)
                n = len(list(range(grp * 4, min(grp * 4 + 4, 18))))
                S.op("dve", lambda: dve.tensor_copy(out=vaug[hp][:, grp * 4:grp * 4 + n, 0:128], in_=pv[:, 0:n, :]), reads=[PB[bp]], writes=[d_v[hp][grp]])
            return [a, b]

        def g_task(grp):
            st = {}

            def a():
                bp = nextbank()
                st["bp"] = bp
                pv = bank(bp).rearrange("p (j c) -> p j c", c=128)
                for jj in range(4):
                    ti = 2 + grp * 4 + jj
                    for k in range(8):
                        S.op("pe", lambda k=k, jj=jj, ti=ti: pe.matmul(pv[:, jj, :], lhsT=hT[:, k, ti * 128:(ti + 1) * 128], rhs=wA[slot][:, 3, k, :], start=(k == 0), stop=(k == 7)),
                             reads=[d_wA[slot], d_hT[ti]], writes=[PB[bp]], inc=(k == 7 and jj == 3))

            def b():
                bp = st["bp"]
                S.op("act", lambda: act.activation(out=t_gs, in_=bank(bp), func=AF.Exp, scale=-1.0), reads=[PB[bp]], writes=[d_tgs])
                S.op("dve", lambda: dve.tensor_scalar(out=t_gs, in0=t_gs, scalar1=1.0, scalar2=1e30, op0=ALU.add, op1=ALU.min), reads=[], writes=[d_tgs])

            def c():
                bp = st["bp"]
                pv = bank(bp).rearrange("p (j c) -> p j c", c=128)
                S.op("act", lambda: act.activation(out=t_gs, in_=t_gs, func=AF.Ln), reads=[], writes=[d_tgs])
                S.op("act", lambda: act.activation(out=t_gs, in_=t_gs, func=AF.Exp, scale=-1.0), reads=[], writes=[d_tgs])
                S.op("dve", lambda: dve.tensor_tensor(out=gateA[hp][:, grp * 4:grp * 4 + 4, :], in0=pv, in1=t_gs.rearrange("p (j c) -> p j c", c=128), op=ALU.mult), reads=[PB[bp], d_tgs], writes=[d_gA[hp][grp]])
            return [a, b, c]

        for grp in range(5):
            tasks.extend(v_task(grp))
        for grp in range(4):
            tasks.extend(g_task(grp))
        return tasks

    def load_wA(h):
        slot = h % 2
        for g in range(4):
            S.dma("pool", lambda g=g: pool.dma_start(
                out=wA[slot][:, g, :, :], in_=ewin_d.rearrange("(k p) n -> p k n", p=128)[:, :, g * 512 + h * 128:g * 512 + (h + 1) * 128]),
                writes=[d_wA[slot]], semdep=d_wA[slot])

    _save_top = AR.top
    AR.top = P_TOP
    lbl = AR.f32(2, 2, 4)
    lbv = AR.f32(3, 2, 4)
    tri = AR.f32(2, 64)
    smask = AR.f32(512)
    hgT = AR.f32(1)
    epsB = AR.f32(1)
    oneB = AR.f32(1)
    mhB = AR.f32(1)
    wB = AR.bf16(5, 8, 128)
    assert AR.top <= P_TOP + 2 * SEQ
    AR.top = _save_top
    d_B = Dep("Bconst")
    d_wB = Dep("wB")

    def load_wB(h, extra=()):
        for g in range(5):
            S.dma("pool", lambda g=g: pool.dma_start(
                out=wB[:, g, :, :], in_=ewin_d.rearrange("(k p) n -> p k n", p=128)[:, :, 2048 + g * 512 + h * 128:2048 + g * 512 + (h + 1) * 128]),
                writes=[d_wB] + list(extra), semdep=d_wB)

    def b_prefetch():
        ex = [d_Ar]
        for dd_ in range(2):
            for ee_ in range(2):
                S.dma("sp", lambda dd_=dd_, ee_=ee_: sp.dma_start(out=lbl[:, dd_, ee_, :], in_=lbl_d[dd_, ee_].rearrange("(h p) -> p h", p=128), allow_slow_non_contiguous=True), writes=[d_B] + ex, semdep=d_B)
        S.dma("sp", lambda: sp.dma_start(out=tri[0:64, :, :], in_=tri_d), writes=[d_B] + ex, semdep=d_B)
        S.dma("sp", lambda: sp.dma_start(out=smask, in_=smask_d), writes=[d_B] + ex, semdep=d_B)
        S.dma("sp", lambda: sp.dma_start(out=hgT, in_=hg_d.rearrange("(p o) -> p o", o=1), allow_slow_non_contiguous=True), writes=[d_B] + ex, semdep=d_B)
        load_wB(0, extra=ex)

    load_wA(0)
    norm_phase(0, 18, src0, NW - 6900)
    if debug:
        S.dma("sp", lambda: sp.dma_start(out=dbg["hT0"], in_=hT), reads=d_hT, writes=[], semdep=d_hT[0])
    S.barrier()
    for f in proj_tasks(0, interleave=True):
        f()
    deferred = []
    PROJ_B[:] = [6, 7]
    for h in range(4):
        hp = h % 2
        if h + 1 < 4:
            load_wA(h + 1)
            tasks = proj_tasks(h + 1)
        else:
            tasks = []

        def acc(m, t):
            sidx = m * 4 + t
            return psum[:, sidx // 3, (sidx % 3) * 130:(sidx % 3) * 130 + 130]

        def emit_score(i, u):
            j, m = u // 2, u % 2
            sb = 3 + (uc[0] + u) % 3
            kTm = kT0[hp] if m == 0 else kT1[hp]
            kd = d_kT[hp][0] if j < 2 else d_kT[hp][1 + (j - 2) // 4]
            S.op("pe", lambda: pe.matmul(bank(sb), lhsT=kTm[:, j * 128:(j + 1) * 128], rhs=qT[hp][:, i * 512:(i + 1) * 512], start=True, stop=True),
                 reads=[kd, d_qT[hp][i]], writes=[PB[sb]])

        def emit_exp_pv(i, u):
            j, m = u // 2, u % 2
            sb = 3 + (uc[0] + u) % 3
            es = (uc[0] + u) % 4
            S.op("act", lambda: act.activation(out=Et[es], in_=bank(sb), func=AF.Exp), reads=[PB[sb]], writes=[d_E[es]])
            for t in range(4):
                S.op("pe", lambda t=t: pe.matmul(acc(m, t)[:, 0:129], lhsT=Et[es][:, t * 128:(t + 1) * 128], rhs=vaug[hp][:, j, 0:129], start=False, stop=(j == 17), skip_group_check=True),
                     reads=[d_E[es], d_v[hp][j // 4]], writes=[PB[(m * 4 + t) // 3]], inc=(t == 3))

        def epilogue1(i):
            sl = i % 2
            for t in range(4):
                a0, a1 = acc(0, t), acc(1, t)
                b0_, b1_ = PB[t // 3], PB[(4 + t) // 3]
                es_ = ep_s[:, sl * 4 + t, :]
                f0, f1 = ep_f[:, sl * 8 + 2 * t, :], ep_f[:, sl * 8 + 2 * t + 1, :]
                dd = d_ep[sl * 4 + t]
                S.op("dve", lambda: dve.reciprocal(out=es_[:, 0:1], in_=a0[:, 128:129]), reads=[b0_], writes=[dd])
                S.op("dve", lambda: dve.reciprocal(out=es_[:, 1:2], in_=a1[:, 128:129]), reads=[b1_], writes=[dd])
                S.op("dve", lambda: dve.tensor_tensor(out=es_[:, 1:2], in0=es_[:, 1:2], in1=lamt[:, 4:5], op=ALU.mult), reads=[d_A], writes=[dd])
                S.op("dve", lambda: dve.tensor_scalar(out=f0, in0=a1[:, 0:128], scalar1=es_[:, 1:2], scalar2=None, op0=ALU.mult), reads=[b1_], writes=[dd])
                S.op("dve", lambda: dve.scalar_tensor_tensor(out=f1, in0=a0[:, 0:128], scalar=es_[:, 0:1], in1=f0, op0=ALU.mult, op1=ALU.add), reads=[b0_], writes=[dd])

        def epilogue1b(i, hp=hp):
            sl = i % 2
            for t in range(4):
                qt = i * 4 + t
                es_ = ep_s[:, sl * 4 + t, :]
                f0, f1 = ep_f[:, sl * 8 + 2 * t, :], ep_f[:, sl * 8 + 2 * t + 1, :]
                dd = d_ep[sl * 4 + t]
                S.op("dve", lambda: dve.scalar_tensor_tensor(out=f0, in0=f1, scalar=1.0, in1=f1, op0=ALU.mult, op1=ALU.mult, accum_out=es_[:, 2:3]), reads=[dd], writes=[dd])
                S.op("dve", lambda: dve.tensor_scalar(out=es_[:, 3:4], in0=es_[:, 2:3], scalar1=1.0 / 128, scalar2=EPS, op0=ALU.mult, op1=ALU.add), reads=[dd], writes=[dd])
                S.op("pool", lambda: pool.tensor_tensor(out=es_[:, 4:5], in0=es_[:, 3:4], in1=mhA, op=ALU.pow), reads=[dd, d_A], writes=[dd])
                S.op("dve", lambda: dve.scalar_tensor_tensor(out=f0, in0=f1, scalar=es_[:, 4:5], in1=gS, op0=ALU.mult, op1=ALU.mult), reads=[dd, d_A], writes=[dd])
                S.op("pool", lambda: pool.tensor_tensor(out=ep_y[:, sl * 4 + t, :], in0=f0, in1=gateA[hp][:, qt, :], op=ALU.mult), reads=[dd, d_gA[hp][qt // 4]], writes=[dd])

        def epilogue2(i, h=h):
            sl = i % 2
            bt = nextbank()
            pyt = bank_bf(bt)[:, 0:512].rearrange("p (t c) -> p t c", c=128)
            for t in range(4):
                S.op("pe", lambda t=t: pe.transpose(out=pyt[:, t, :], in_=ep_y[:, sl * 4 + t, :], identity=identB), reads=[d_ep[sl * 4 + t], d_const], writes=[PB[bt]], inc=(t == 3))
            S.op("dve", lambda: dve.tensor_copy(out=yT[:, h, i * 512:(i + 1) * 512], in_=bank_bf(bt)[:, 0:512]), reads=[PB[bt]], writes=d_yT[h][i * 4:i * 4 + 4])

        def zero_acc():
            for b in range(3):
                S.op("pe", lambda b=b: pe.matmul(bank(b), lhsT=zerosB[:, 0:128], rhs=zerosB, start=True, stop=True, skip_group_check=True), reads=[d_const], writes=[PB[b]])

        for i in range(4):
            if i == 0:
                zero_acc()
                for u0 in range(3):
                    emit_score(0, u0)
            for u in range(36):
                emit_exp_pv(i, u)
                if u + 3 < 36:
                    emit_score(i, u + 3)
                for dq in list(deferred):
                    dq[0] -= 1
                    if dq[0] <= 0:
                        deferred.remove(dq)
                        dq[1]()
                if tasks and u % 2 == 1:
                    tasks.pop(0)()
            uc[0] += 36
            if i + 1 < 4:
                for u0 in range(3):
                    emit_score(i + 1, u0)
            epilogue1(i)
            deferred.append([4, (lambda e=epilogue1b, i=i: e(i))])
            deferred.append([12, (lambda e=epilogue2, i=i: e(i))])
            if i + 1 < 4:
                zero_acc()
        while tasks:
            tasks.pop(0)()
        if h == 2:
            b_prefetch()
    for dq in deferred:
        dq[1]()
    PROJ_B[:] = [0, 1, 2, 3, 4, 5, 6, 7]
    S.barrier()

    AR.top = P_TOP
    lbl = AR.f32(2, 2, 4)
    lbv = AR.f32(3, 2, 4)
    tri = AR.f32(2, 64)
    smask = AR.f32(512)
    hgT = AR.f32(1)
    epsB = AR.f32(1)
    oneB = AR.f32(1)
    mhB = AR.f32(1)
    wB = AR.bf16(5, 8, 128)
    qg = [AR.bf16(NT), AR.bf16(NT)]
    kg = [AR.bf16(NT), AR.bf16(NT)]
    _off_kgt0 = AR.top
    kgtok = [AR.bf16(36, 128), AR.bf16(36, 128)]
    vtok = AR.bf16(36, 128)
    gtok = AR.bf16(32, 128)
    e1 = [AR.f32(36), AR.f32(36)]
    e2 = [AR.f32(36), AR.f32(36)]
    eL = [AR.f32(36), AR.f32(36)]
    Sring = [AR.f32(8, 128), AR.f32(8, 128)]
    Smid = [AR.bf16(32, 128), AR.bf16(32, 128)]
    Asb = AR.bf16(4, 2, 64)
    _off_t = AR.top
    bt_sig = [AR.f32(512), AR.f32(512)]
    bt_g = [AR.f32(512), AR.f32(512)]
    bt_kk = [AR.f32(512), AR.f32(512)]
    bt_G = [AR.f32(512), AR.f32(512)]
    kg2 = [bt_kk[0][:, 0:256].bitcast(BF16), bt_kk[1][:, 0:256].bitcast(BF16)]
    ot = [bt_sig[0].rearrange("p (j c) -> p j c", c=128), bt_sig[1].rearrange("p (j c) -> p j c", c=128)]
    bt_s8 = [AR.f32(2, 8), AR.f32(2, 8)]
    stage = [AR.bf16(512), AR.bf16(512)]
    ojunk = AR.f32(128)
    hb_ss = AR.f32(3, 32)
    hb_y = AR._view(AR.ap[:, _off_kgt0:_off_kgt0 + 2048].bitcast(BF16), (32, 128))
    d_qg = [[Dep("qg%d_%d" % (d, b)) for b in range(5)] for d in range(2)]
    d_kg = [[Dep("kg%d_%d" % (d, b)) for b in range(5)] for d in range(2)]
    d_kgt = [[Dep("kgt%d_%d" % (d, b)) for b in range(5)] for d in range(2)]
    d_vt = [Dep("vt%d" % i) for i in range(5)]
    d_gt = [Dep("gt%d" % i) for i in range(5)]
    d_e = [[Dep("e%d_%d" % (d, b)) for b in range(5)] for d in range(2)]
    d_Sr = [[Dep("Sr%d_%d" % (d, i)) for i in range(2)] for d in range(2)]
    d_Sm = [[Dep("Sm%d_%d" % (d, i)) for i in range(8)] for d in range(2)]
    d_As = Dep("As")
    d_bt = [{k: Dep("bt%d_%s" % (p_, k)) for k in ("sig", "g", "kk", "G", "s8")} for p_ in range(2)]
    d_stage = [Dep("stage0"), Dep("stage1")]
    d_ot = [Dep("ot0"), Dep("ot1")]
    d_hb = Dep("hb")
    d_hby = [Dep("hby%d" % i) for i in range(4)]

    S.op("pool", lambda: pool.memset(epsB, EPS), writes=[d_B])
    S.op("pool", lambda: pool.memset(oneB, 1.0), writes=[d_B])
    S.op("pool", lambda: pool.memset(mhB, -0.5), writes=[d_B])
    S.op("dve", lambda: dve.tensor_tensor(out=lbv[:, 1, :, :], in0=lbl[:, :, 1, :], in1=lbl[:, :, 0, :], op=ALU.subtract), reads=[d_B], writes=[d_B])
    S.op("act", lambda: act.activation(out=lbv[:, 0, :, :], in_=lbv[:, 1, :, :], func=AF.Exp), reads=[d_B], writes=[d_B])
    S.op("dve", lambda: dve.tensor_scalar(out=lbv[:, 0, :, :], in0=lbv[:, 0, :, :], scalar1=1.0, scalar2=None, op0=ALU.add), reads=[d_B], writes=[d_B])
    S.op("dve", lambda: dve.reciprocal(out=lbv[:, 0, :, :], in_=lbv[:, 0, :, :]), reads=[d_B], writes=[d_B])
    S.op("dve", lambda: dve.tensor_scalar(out=lbv[:, 1, :, :], in0=lbv[:, 0, :, :], scalar1=-1.0, scalar2=1.0, op0=ALU.mult, op1=ALU.add), reads=[d_B], writes=[d_B])
    S.op("dve", lambda: dve.tensor_scalar(out=lbv[:, 2, :, :], in0=lbv[:, 0, :, :], scalar1=-1.0, scalar2=None, op0=ALU.add), reads=[d_B], writes=[d_B])

    BLKS = [(0, 512), (512, 512), (1024, 512), (1536, 512), (2048, 256)]
    nb8 = [0]

    def nextbank8():
        b = nb8[0] % 7
        nb8[0] += 1
        return b

    d_junkps = Dep("junkps")

    def pe_keepwarm(n):
        for _ in range(n):
            S.op("pe", lambda: pe.matmul(bank(7), lhsT=zerosB[:, 0:128], rhs=zerosB, start=True, stop=True, skip_group_check=True), reads=[d_const], writes=[PB[7]], inc=False)

    bd = [0]
    for h in range(4):
        def vg_item(which, grp, bi, t0, nt, sp_):
            nch = nt // 64
            c0 = t0 // 64
            hdeps = d_hT[t0 // 128:(t0 + nt) // 128]
            st = {}

            def s1():
                bv = nextbank8()
                for k in range(8):
                    S.op("pe", lambda k=k: pe.matmul(bank(bv)[:, 0:nt], lhsT=wB[:, grp, k, :], rhs=hT[:, k, t0:t0 + nt], start=(k == 0), stop=(k == 7)),
                         reads=[d_wB] + hdeps, writes=[PB[bv]], inc=(k == 7))
                if which == 0:
                    S.op("act", lambda: act.copy(out=stage[sp_][:, 0:nt], in_=bank(bv)[:, 0:nt]), reads=[PB[bv]], writes=[d_stage[sp_]])
                else:
                    S.op("act", lambda: act.activation(out=stage[sp_][:, 0:nt], in_=bank(bv)[:, 0:nt], func=AF.Silu), reads=[PB[bv]], writes=[d_stage[sp_]])

            def s2():
                bt_ = nextbank8()
                pk = bank_bf(bt_).rearrange("p (j c) -> p j c", c=128)
                for cc in range(nch):
                    S.op("pe", lambda cc=cc: pe.transpose(out=pk[0:64, cc, :], in_=stage[sp_][:, cc * 64:(cc + 1) * 64], identity=identB),
                         reads=[d_stage[sp_], d_const], writes=[PB[bt_]], inc=(cc == nch - 1))
                if which == 0:
                    S.op("dve", lambda: dve.tensor_copy(out=vtok[0:64, c0:c0 + nch, :], in_=pk[0:64, 0:nch, :]), reads=[PB[bt_]], writes=[d_vt[bi]])
                else:
                    lo = 4 if bi == 0 else 0
                    S.op("dve", lambda: dve.tensor_copy(out=gtok[0:64, c0 + lo - 4:c0 + nch - 4, :], in_=pk[0:64, lo:nch, :]), reads=[PB[bt_]], writes=[d_gt[bi]])
            return [s1, s2]

        items = []
        for which, grp in ((0, 1), (1, 4)):
            for bi, (t0, nt) in enumerate(BLKS):
                items.append(vg_item(which, grp, bi, t0, nt, bd[0] % 2))
                bd[0] += 1
        skew(items, newest_first=True)

        qbank = {}

        def gd_item(bi, t0, nt, d, p_):
            nch = nt // 64
            c0 = t0 // 64
            hdeps = d_hT[t0 // 128:(t0 + nt) // 128]
            T_sig, T_g, T_kk, T_G, T_s8, T_kg2 = bt_sig[p_], bt_g[p_], bt_kk[p_], bt_G[p_], bt_s8[p_], kg2[p_]
            D_ = d_bt[p_]
            G3 = T_G[:, 0:nt].rearrange("p (c l) -> p c l", l=64)
            eR = e1[d] if d == 0 else e2[d]
            eD = e2[d] if d == 0 else e1[d]
            sa = 1.0 if d == 0 else -1.0

            def s1():
                if d == 0:
                    bq = nextbank8()
                    qbank[bi] = bq
                    for k in range(8):
                        S.op("pe", lambda k=k: pe.matmul(bank(bq)[:, 0:nt], lhsT=wB[:, 0, k, :], rhs=hT[:, k, t0:t0 + nt], start=(k == 0), stop=(k == 7)),
                             reads=[d_wB] + hdeps, writes=[PB[bq]], inc=(k == 7))
                bf = nextbank8()
                for k in range(8):
                    S.op("pe", lambda k=k: pe.matmul(bank(bf)[:, 0:nt], lhsT=wB[:, 2 + d, k, :], rhs=hT[:, k, t0:t0 + nt], start=(k == 0), stop=(k == 7)),
                         reads=[d_wB] + hdeps, writes=[PB[bf]], inc=(k == 7))
                pe_keepwarm(8)
                S.op("act", lambda: act.activation(out=T_sig[:, 0:nt], in_=bank(bf)[:, 0:nt], func=AF.Exp, scale=-1.0), reads=[PB[bf]], writes=[D_["sig"]])
                S.op("act", lambda: act.activation(out=T_sig[:, 0:nt], in_=T_sig[:, 0:nt], func=AF.Ln, bias=oneB[:, 0:1]), reads=[d_B], writes=[D_["sig"]])
                S.op("act", lambda: act.activation(out=T_sig[:, 0:nt], in_=T_sig[:, 0:nt], func=AF.Exp, scale=-1.0), reads=[], writes=[D_["sig"]])
                S.op("act", lambda: act.activation(out=T_g[:, 0:nt], in_=T_sig[:, 0:nt], func=AF.Ln, scale=lbv[:, 1, d, h:h + 1], bias=lbv[:, 0, d, h:h + 1]),
                     reads=[D_["sig"], d_B], writes=[D_["g"]])
                S.op("dve", lambda: dve.tensor_scalar(out=T_kk[:, 0:nt], in0=T_sig[:, 0:nt], scalar1=lbv[:, 2, d, h:h + 1], scalar2=lbv[:, 1, d, h:h + 1], op0=ALU.mult, op1=ALU.add),
                     reads=[D_["sig"], d_B], writes=[D_["kk"]])
                S.op("dve", lambda: dve.tensor_tensor_scan(out=T_G[:, 0:nt], data0=smask[:, 0:nt], data1=T_g[:, 0:nt], initial=0.0, op0=ALU.mult, op1=ALU.add),
                     reads=[D_["g"], d_B], writes=[D_["G"]])

            def s2():
                bq = qbank[bi]
                S.op("dve", lambda: dve.tensor_copy(out=T_s8[:, 0, 0:nch].unsqueeze(2), in_=G3[:, :, 31:32]), reads=[D_["G"]], writes=[D_["s8"]])
                S.op("dve", lambda: dve.tensor_tensor(out=T_s8[:, 1, 0:nch].unsqueeze(2), in0=G3[:, :, 63:64], in1=G3[:, :, 31:32], op=ALU.subtract), reads=[D_["G"]], writes=[D_["s8"]])
                S.op("act", lambda: act.activation(out=eR[:, c0:c0 + nch], in_=T_s8[:, 0, 0:nch], func=AF.Exp), reads=[D_["s8"]], writes=[d_e[d][bi]])
                S.op("act", lambda: act.activation(out=eL[d][:, c0:c0 + nch].unsqueeze(2), in_=G3[:, :, 63:64], func=AF.Exp), reads=[D_["G"]], writes=[d_e[d][bi]])
                S.op("act", lambda: act.activation(out=eD[:, c0:c0 + nch], in_=T_s8[:, 1, 0:nch], func=AF.Exp), reads=[D_["s8"]], writes=[d_e[d][bi]])
                S.op("dve", lambda: dve.tensor_tensor(out=G3, in0=G3, in1=T_s8[:, 0, 0:nch].unsqueeze(2).to_broadcast([128, nch, 64]), op=ALU.subtract), reads=[D_["s8"]], writes=[D_["G"]])
                if d == 1:
                    S.op("dve", lambda: dve.tensor_tensor(out=T_G[:, 0:nt], in0=T_G[:, 0:nt], in1=T_g[:, 0:nt], op=ALU.subtract), reads=[D_["g"]], writes=[D_["G"]])
                S.op("act", lambda: act.activation(out=T_sig[:, 0:nt], in_=T_G[:, 0:nt], func=AF.Exp, scale=sa), reads=[D_["G"]], writes=[D_["sig"]])
                S.op("act", lambda: act.activation(out=T_g[:, 0:nt], in_=T_G[:, 0:nt], func=AF.Exp, scale=-sa), reads=[D_["G"]], writes=[D_["g"]])
                S.op("dve", lambda: dve.tensor_tensor(out=qg[d][:, t0:t0 + nt], in0=bank(bq)[:, 0:nt], in1=T_sig[:, 0:nt], op=ALU.mult), reads=[PB[bq], D_["sig"]], writes=[d_qg[d][bi]])
                S.op("pool", lambda: pool.tensor_tensor(out=T_g[:, 0:nt], in0=T_kk[:, 0:nt], in1=T_g[:, 0:nt], op=ALU.mult), reads=[D_["kk"]], writes=[D_["g"]])
                S.op("act", lambda: act.copy(out=kg[d][:, t0:t0 + nt], in_=T_g[:, 0:nt]), reads=[D_["g"]], writes=[d_kg[d][bi]])
                S.op("dve", lambda: dve.tensor_tensor(out=T_kg2[:, 0:nt].rearrange("p (c l) -> p c l", l=64), in0=T_g[:, 0:nt].rearrange("p (c l) -> p c l", l=64),
                                                      in1=e2[d][:, c0:c0 + nch].unsqueeze(2).to_broadcast([128, nch, 64]), op=ALU.mult),
                     reads=[D_["g"], d_e[d][bi]], writes=[D_["kk"]])

            def s3():
                bt_ = nextbank8()
                pk = bank_bf(bt_).rearrange("p (j c) -> p j c", c=128)
                for cc in range(nch):
                    S.op("pe", lambda cc=cc: pe.transpose(out=pk[0:64, cc, :], in_=T_kg2[:, cc * 64:(cc + 1) * 64], identity=identB),
                         reads=[D_["kk"], d_const], writes=[PB[bt_]], inc=(cc == nch - 1))
                S.op("act", lambda: act.copy(out=kgtok[d][0:64, c0:c0 + nch, :], in_=pk[0:64, 0:nch, :]), reads=[PB[bt_]], writes=[d_kgt[d][bi]])
            return [s1, s2, s3]

        items = []
        for bi, (t0, nt) in enumerate(BLKS):
            for d in range(2):
                items.append(gd_item(bi, t0, nt, d, bd[0] % 2))
                bd[0] += 1
        skew(items)
        def slot_of(c):
            return c % 8

        for d in range(2):
            first = 0 if d == 0 else 3
            S.op("pool", lambda d=d, first=first: pool.memset(Sring[d][:, slot_of(first), :], 0.0), writes=[d_Sr[d][slot_of(first) // 4]])
        order = [list(range(36)), [3, 2, 1, 0] + list(range(35, 3, -1))]
        for g in range(9):
            for d in range(2):
                bu = (0, 1)[g % 2] if d == 0 else (2, 3)[g % 2]
                pu = bank(bu).rearrange("p (j c) -> p j c", c=128)
                cs = order[d][g * 4:g * 4 + 4]
                for jj, c in enumerate(cs):
                    S.op("pe", lambda jj=jj, c=c: pe.matmul(pu[:, jj, :], lhsT=kgtok[d][0:64, c, :], rhs=vtok[0:64, c, :], start=True, stop=True),
                         reads=[d_kgt[d][c // 8], d_vt[c // 8]], writes=[PB[bu]], inc=(jj == 3))
                for jj, c in enumerate(cs):
                    bi = c // 8
                    if d == 0:
                        cn = c + 1
                    else:
                        cn = 35 if c == 0 else c - 1
                    if (d == 0 and c == 35) or (d == 1 and c == 4):
                        continue
                    sp_, sn_ = slot_of(c), slot_of(cn)
                    S.op("dve", lambda c=c, sp_=sp_, sn_=sn_, jj=jj: dve.scalar_tensor_tensor(out=Sring[d][:, sn_, :], in0=Sring[d][:, sp_, :], scalar=eL[d][:, c:c + 1], in1=pu[:, jj, :], op0=ALU.mult, op1=ALU.add),
                         reads=[d_Sr[d][sp_ // 4], d_e[d][bi], PB[bu]], writes=[d_Sr[d][sn_ // 4]])
                if d == 0:
                    c0 = g * 4
                else:
                    c0 = (36 - g * 4) if g >= 1 else None
                    if c0 is not None and c0 > 32:
                        c0 = None
                if c0 is not None and c0 >= 4 and c0 + 3 <= 35:
                    sl0 = slot_of(c0)
                    S.op("pool", lambda c0=c0, sl0=sl0: pool.tensor_tensor(out=Smid[d][:, c0 - 4:c0, :], in0=Sring[d][:, sl0:sl0 + 4, :],
                                                                         in1=e1[d][:, c0:c0 + 4].unsqueeze(2).to_broadcast([128, 4, 128]), op=ALU.mult),
                         reads=[d_Sr[d][sl0 // 4], d_e[d][c0 // 8]], writes=[d_Sm[d][(c0 - 4) // 4]])
        S.op("pool", lambda: pool.tensor_tensor(out=Smid[1][:, 0:4, :], in0=Sring[1][:, 4:8, :], in1=e1[1][:, 4:8].unsqueeze(2).to_broadcast([128, 4, 128]), op=ALU.mult),
             reads=[d_Sr[1][1], d_e[1][0]], writes=[d_Sm[1][0]])
        if h < 3:
            load_wB(h + 1)
        tri4 = tri[0:64, :, :].unsqueeze(1).to_broadcast([64, 4, 2, 64])

        def o_s1(g):
            ba, bo = 4 + g % 2, 6 + g % 2
            pa = bank(ba)[0:64, :].rearrange("p (j d c) -> p j d c", d=2, c=64)
            po = bank(bo)[0:64, :].rearrange("p (j c) -> p j c", c=128)
            os_ = g % 2
            for jj in range(4):
                c = 4 + g * 4 + jj
                bi = c // 8
                for d in range(2):
                    S.op("pe", lambda jj=jj, c=c, d=d: pe.matmul(pa[:, jj, d, :], lhsT=kg[d][:, c * 64:(c + 1) * 64], rhs=qg[d][:, c * 64:(c + 1) * 64], start=True, stop=True, skip_group_check=True),
                         reads=[d_kg[d][bi], d_qg[d][bi]], writes=[PB[ba]], inc=(jj == 3 and d == 1))
            S.op("dve", lambda: dve.tensor_tensor(out=Asb[0:64, :, :, :], in0=pa, in1=tri4, op=ALU.mult), reads=[PB[ba], d_B], writes=[d_As])

        def o_s1b(g):
            ba, bo = 4 + g % 2, 6 + g % 2
            po = bank(bo)[0:64, :].rearrange("p (j c) -> p j c", c=128)
            os_ = g % 2
            first = True
            for jj in range(4):
                c = 4 + g * 4 + jj
                for d in range(2):
                    S.op("pe", lambda jj=jj, c=c, d=d, first=first: pe.matmul(po[:, jj, :], lhsT=Asb[0:64, jj, d, :], rhs=vtok[0:64, c, :], start=first, stop=False, skip_group_check=True),
                         reads=[d_As, d_vt[c // 8]], writes=[PB[bo]], inc=False)
                    first = False
            for jj in range(4):
                c = 4 + g * 4 + jj
                bi = c // 8
                for d in range(2):
                    last = (jj == 3 and d == 1)
                    S.op("pe", lambda jj=jj, c=c, d=d, last=last: pe.matmul(po[:, jj, :], lhsT=qg[d][:, c * 64:(c + 1) * 64], rhs=Smid[d][:, c - 4, :], start=False, stop=last, skip_group_check=True),
                         reads=[d_qg[d][bi], d_Sm[d][(c - 4) // 4]], writes=[PB[bo]], inc=last)
            S.op("act", lambda: act.copy(out=ot[os_][0:64, :, :], in_=po), reads=[PB[bo]], writes=[d_ot[os_], d_bt[os_]["sig"]])

        def o_s2(g):
            os_ = g % 2
            od_ = [d_ot[os_], d_bt[os_]["sig"]]
            for jj in range(4):
                cl = g * 4 + jj
                S.op("dve", lambda jj=jj, cl=cl: dve.scalar_tensor_tensor(out=ojunk[0:64, :], in0=ot[os_][0:64, jj, :], scalar=1.0, in1=ot[os_][0:64, jj, :], op0=ALU.mult, op1=ALU.mult, accum_out=hb_ss[0:64, 0, cl:cl + 1]),
                     reads=od_, writes=[d_hb])
            S.op("dve", lambda: dve.tensor_scalar(out=hb_ss[0:64, 1, g * 4:g * 4 + 4], in0=hb_ss[0:64, 0, g * 4:g * 4 + 4], scalar1=1.0 / 128, scalar2=EPS, op0=ALU.mult, op1=ALU.add), reads=[d_hb], writes=[d_hb])
            S.op("pool", lambda: pool.tensor_tensor(out=hb_ss[0:64, 2, g * 4:g * 4 + 4], in0=hb_ss[0:64, 1, g * 4:g * 4 + 4], in1=mhB[0:64, 0:1].to_broadcast([64, 4]), op=ALU.pow), reads=[d_hb, d_B], writes=[d_hb])
            for jj in range(4):
                cl = g * 4 + jj
                S.op("dve", lambda jj=jj, cl=cl: dve.scalar_tensor_tensor(out=hb_y[0:64, cl, :], in0=ot[os_][0:64, jj, :], scalar=hb_ss[0:64, 2, cl:cl + 1], in1=gtok[0:64, cl, :], op0=ALU.mult, op1=ALU.mult),
                     reads=od_ + [d_hb, d_gt[(cl + 4) // 8]], writes=[d_hby[g // 2]] + d_kgt[0])

        def o_s2b(g):
            if g % 2 == 1:
                grp = g // 2
                bt_ = grp % 4
                py = bank_bf(bt_)[:, 0:512].rearrange("p (j c) -> p j c", c=64)
                for jj in range(8):
                    c = grp * 8 + jj
                    S.op("pe", lambda jj=jj, c=c: pe.transpose(out=py[:, jj, :], in_=hb_y[0:64, c, :], identity=identB[0:64, 0:64]), reads=[d_hby[grp], d_const] + d_kgt[0], writes=[PB[bt_]], inc=(jj == 7))
                S.op("act", lambda: act.activation(out=yT[:, 4 + h, grp * 512:(grp + 1) * 512], in_=bank_bf(bt_)[:, 0:512], func=AF.Copy, scale=hgT[:, 0:1]), reads=[PB[bt_], d_B], writes=d_yT[4 + h][grp * 4:grp * 4 + 4])

        for g in range(10):
            if g < 8:
                o_s1(g)
            if 1 <= g <= 8:
                o_s2(g - 1)
            if g < 8:
                o_s1b(g)
            if 2 <= g <= 9:
                o_s2b(g - 2)
    if debug:
        S.dma("sp", lambda: sp.dma_start(out=dbg["yT0"], in_=yT), reads=[x for r in d_yT for x in r], writes=[], semdep=d_yT[0][0])
    S.barrier()

    X_OFF = P_TOP
    AR.top = X_OFF
    xnew = AR.f32(16, D)
    T2 = AR.top
    d_xnew = [Dep("xnew%d" % i) for i in range(16)]

    def out_phase(l, w_d, pre=None):
        AR.top = T2
        if pre is None:
            wo = AR.bf16(8, D)
            stg = [AR.f32(D) for _ in range(4)]
        xt = [AR.f32(D), AR.f32(D)]
        gtmp = [AR.f32(512), AR.f32(512)]
        d_gtmp = [Dep("gtmp0"), Dep("gtmp1")]
        d_wo, d_stg, d_xt = [Dep("wo%d" % k_) for k_ in range(8)], [Dep("stg%d" % k_) for k_ in range(4)], [Dep("oxt0"), Dep("oxt1")]
        def ld(k):
            S.dma("sp", lambda: sp.dma_start(out=stg[k % 4], in_=w_d[k * 128:(k + 1) * 128, :]), writes=[d_stg[k % 4]], semdep=d_stg[k % 4])

        if pre is None:
            for k in range(4):
                ld(k)
            for k in range(8):
                s = k % 4
                S.op("dve", lambda k=k, s=s: dve.tensor_tensor(out=wo[:, k, :], in0=stg[s], in1=gate_bc[l], op=ALU.mult), reads=[d_stg[s], d_gate[l]], writes=[d_wo[k]])
                if k + 4 < 8:
                    ld(k + 4)
        else:
            wo, d_wo = pre
        for i in range(16):
            s = i % 2
            if l == 0:
                S.dma("sp", lambda i=i, s=s: sp.dma_start(out=xt[s], in_=x_d[i * 128:(i + 1) * 128, :]), writes=[d_xt[s]], semdep=d_xt[s])
            for n in range(2):
                bp = nextbank()
                for k in range(8):
                    S.op("pe", lambda k=k, n=n, i=i: pe.matmul(bank(bp), lhsT=yT[:, k, i * 128:(i + 1) * 128], rhs=wo[:, k, n * 512:(n + 1) * 512], start=(k == 0), stop=(k == 7)),
                         reads=[d_wo[k], d_yT[k][i]], writes=[PB[bp]], inc=(k == 7))
                if l == 0:
                    S.op("dve", lambda n=n, i=i, s=s: dve.tensor_tensor(out=xnew[:, i, n * 512:(n + 1) * 512], in0=bank(bp), in1=xt[s][:, n * 512:(n + 1) * 512], op=ALU.add),
                         reads=[PB[bp], d_xt[s]], writes=[d_xnew[i]])
                elif pre is None:
                    S.op("dve", lambda n=n, i=i, s=s: dve.tensor_tensor(out=xt[s][:, n * 512:(n + 1) * 512], in0=bank(bp), in1=xnew[:, i, n * 512:(n + 1) * 512], op=ALU.add),
                         reads=[PB[bp], d_xnew[i]], writes=[d_xt[s]])
                else:
                    S.op("dve", lambda n=n: dve.tensor_tensor(out=gtmp[n], in0=bank(bp), in1=gate_bc[l][:, n * 512:(n + 1) * 512], op=ALU.mult),
                         reads=[PB[bp], d_gate[l]], writes=[d_gtmp[n]])
                    S.op("pool", lambda n=n, i=i, s=s: pool.tensor_tensor(out=xt[s][:, n * 512:(n + 1) * 512], in0=gtmp[n], in1=xnew[:, i, n * 512:(n + 1) * 512], op=ALU.add),
                         reads=[d_gtmp[n], d_xnew[i]], writes=[d_xt[s]])
            if l == 1:
                qn_, qe_ = (("sp", sp), ("act", act))[i % 2]
                S.dma(qn_, lambda i=i, s=s, qe_=qe_: qe_.dma_start(out=out_d[i * 128:(i + 1) * 128, :], in_=xt[s]), reads=[d_xt[s]], writes=[], semdep=d_xt[s])
        return d_xt

    out_phase(0, ewout_d)
    if debug:
        S.dma("sp", lambda: sp.dma_start(out=dbg["xnew"], in_=xnew), reads=d_xnew, writes=[], semdep=d_xnew[0])
    S.barrier()

    def src1(i):
        return xnew[:, i, :], False, d_xnew[i]

    AR.top = T2
    wC = AR.bf16(3, 8, 512)
    L1_TMP = AR.top
    AR.top = L1_TMP + 6000
    wD = [AR.bf16(4, 8, 128), AR.bf16(4, 8, 128)]
    L1_END = AR.top
    d_wC = Dep("wC")
    dd_w = [Dep("wD0"), Dep("wD1")]

    def load_wD(j):
        for g in range(4):
            S.dma("pool", lambda g=g: pool.dma_start(
                out=wD[j % 2][:, g, :, :], in_=owin_d.rearrange("(k p) n -> p k n", p=128)[:, :, 1536 + g * 512 + j * 128:1536 + g * 512 + (j + 1) * 128]),
                writes=[dd_w[j % 2]], semdep=dd_w[j % 2])

    for g in range(3):
        for kk in range(2):
            S.dma("pool", lambda g=g, kk=kk: pool.dma_start(out=wC[:, g, kk * 4:(kk + 1) * 4, :], in_=owin_d.rearrange("(k p) n -> p k n", p=128)[:, kk * 4:(kk + 1) * 4, g * 512:(g + 1) * 512]),
                  writes=[d_wC], semdep=d_wC)
    load_wD(0)
    load_wD(1)
    norm_phase(1, 16, src1, L1_TMP)
    S.barrier()

    AR.top = L1_TMP
    wsF = AR.f32(4, 128)
    wsT = AR.bf16(4, 128)
    bsT = AR.f32(4)
    vgS = AR.f32(512)
    mhalf = AR.f32(1)
    c_gu = [AR.f32(512), AR.f32(512)]
    c_sg = [AR.f32(512), AR.f32(512)]
    c_gv = [AR.f32(512), AR.f32(512)]
    c_vn = [AR.bf16(512), AR.bf16(512)]
    c_y = [AR.bf16(512), AR.bf16(512)]
    c_junk = AR.f32(512)
    c_st = AR.f32(16, 4)
    assert AR.top <= L1_TMP + 6000, AR.top - L1_TMP
    d_C = Dep("Cconst")
    d_c = [{k: Dep("c%d_%s" % (p_, k)) for k in ("gu", "sg", "gv", "vn", "y")} for p_ in range(2)]
    d_cj, d_cst = Dep("c_junk"), [Dep("c_st%d" % i) for i in range(16)]
    S.dma("sp", lambda: sp.dma_start(out=wsF, in_=ws_d.rearrange("g t s -> t g s")), writes=[d_C], semdep=d_C)
    S.dma("sp", lambda: sp.dma_start(out=bsT, in_=bs_d.rearrange("g t -> t g"), allow_slow_non_contiguous=True), writes=[d_C], semdep=d_C)
    S.dma("sp", lambda: sp.dma_start(out=vgS, in_=vg_d.partition_broadcast(128)), writes=[d_C], semdep=d_C)
    S.op("pool", lambda: pool.memset(mhalf, -0.5), writes=[d_C])
    bw = nextbank8()
    pw = bank(bw).rearrange("p (g c) -> p g c", c=128)
    for g in range(4):
        S.op("pe", lambda g=g: pe.transpose(out=pw[:, g, :], in_=wsF[:, g, :], identity=identF), reads=[d_C, d_const], writes=[PB[bw]], inc=(g == 3))
    S.op("dve", lambda: dve.tensor_copy(out=wsT, in_=pw), reads=[PB[bw]], writes=[d_C])

    def c_item(i):
        p_ = i % 2
        Dc = d_c[p_]
        gu, sg, gv, vn, yy = c_gu[p_], c_sg[p_], c_gv[p_], c_vn[p_], c_y[p_]
        st = {}

        def s1():
            bu, bv, bg = nextbank8(), nextbank8(), nextbank8()
            st["bg"] = bg
            for g, bb in ((0, bu), (1, bv), (2, bg)):
                for k in range(8):
                    S.op("pe", lambda k=k, g=g, bb=bb: pe.matmul(bank(bb), lhsT=hT[:, k, i * 128:(i + 1) * 128], rhs=wC[:, g, k, :], start=(k == 0), stop=(k == 7)),
                         reads=[d_wC, d_hT[i]], writes=[PB[bb]], inc=(k == 7))
            S.op("act", lambda: act.activation(out=gu, in_=bank(bu), func=AF.Gelu), reads=[PB[bu]], writes=[Dc["gu"]])
            S.op("act", lambda: act.activation(out=gv, in_=bank(bv), func=AF.Gelu), reads=[PB[bv]], writes=[Dc["gv"]])
            S.op("act", lambda: act.activation(out=sg, in_=bank(bg), func=AF.Tanh, scale=0.5), reads=[PB[bg]], writes=[Dc["sg"]])
            S.op("dve", lambda: dve.tensor_scalar(out=sg, in0=sg, scalar1=0.5, scalar2=0.5, op0=ALU.mult, op1=ALU.add), reads=[], writes=[Dc["sg"]])
            S.op("dve", lambda: dve.tensor_tensor(out=sg, in0=bank(bg), in1=sg, op=ALU.mult), reads=[PB[bg]], writes=[Dc["sg"]])

        def s2():
            S.op("pool", lambda: pool.tensor_tensor(out=gu, in0=gu, in1=sg, op=ALU.mult), reads=[Dc["sg"]], writes=[Dc["gu"]])
            S.op("dve", lambda: dve.scalar_tensor_tensor(out=c_junk, in0=gv, scalar=1.0, in1=gv, op0=ALU.mult, op1=ALU.mult, accum_out=c_st[:, i, 0:1]), reads=[Dc["gv"]], writes=[d_cj, d_cst[i]])
            S.op("dve", lambda: dve.tensor_scalar(out=c_st[:, i, 1:2], in0=c_st[:, i, 0:1], scalar1=1.0 / 512, scalar2=EPS, op0=ALU.mult, op1=ALU.add), reads=[], writes=[d_cst[i]])
            S.op("pool", lambda: pool.tensor_tensor(out=c_st[:, i, 2:3], in0=c_st[:, i, 1:2], in1=mhalf, op=ALU.pow), reads=[d_C], writes=[d_cst[i]])
            S.op("dve", lambda: dve.scalar_tensor_tensor(out=vn, in0=gv, scalar=c_st[:, i, 2:3], in1=vgS, op0=ALU.mult, op1=ALU.mult), reads=[Dc["gv"], d_cst[i], d_C], writes=[Dc["vn"]])

        def s3():
            bs_ = nextbank8()
            ps = bank(bs_).rearrange("p (g c) -> p g c", c=128)
            for g in range(4):
                S.op("pe", lambda g=g: pe.matmul(ps[:, g, :], lhsT=wsT[:, g, :], rhs=vn[:, g * 128:(g + 1) * 128], start=True, stop=True), reads=[d_C, Dc["vn"]], writes=[PB[bs_]], inc=(g == 3))
            for g in range(4):
                S.op("dve", lambda g=g: dve.scalar_tensor_tensor(out=yy[:, g * 128:(g + 1) * 128], in0=ps[:, g, :], scalar=bsT[:, g:g + 1], in1=gu[:, g * 128:(g + 1) * 128], op0=ALU.add, op1=ALU.mult),
                     reads=[PB[bs_], d_C, Dc["gu"]], writes=[Dc["y"]])

        def s3b():
            bt_ = nextbank8()
            py = bank_bf(bt_)[:, 0:512].rearrange("p (g c) -> p g c", c=128)
            for g in range(4):
                S.op("pe", lambda g=g: pe.transpose(out=py[:, g, :], in_=yy[:, g * 128:(g + 1) * 128], identity=identB), reads=[Dc["y"], d_const], writes=[PB[bt_]], inc=(g == 3))
            S.op("act", lambda: act.copy(out=yT[:, 0:4, i * 128:(i + 1) * 128], in_=py), reads=[PB[bt_]], writes=[d_yT[g][i] for g in range(4)])
        return [s1, s2, s3, s3b]

    c_items = [c_item(i) for i in range(16)]
    for t in range(16 + 2):
        if 0 <= t - 2 < 16:
            c_items[t - 2][2]()
        if 0 <= t - 1 < 16:
            c_items[t - 1][1]()
        if t < 16:
            c_items[t][0]()
        if 0 <= t - 2 < 16:
            c_items[t - 2][3]()
    S.barrier()

    AR.top = T2
    cwT = AR.f32(4, 3)
    zb = AR.f32(SEQ + 2)
    bsg = AR.f32(SEQ)
    cvt = AR.f32(SEQ)
    d_csb = [AR.f32(512), AR.f32(512)]
    _sgd = AR.f32(512)
    d_sgd = [_sgd, _sgd]
    wo1 = AR.bf16(8, D)
    assert AR.top <= L1_TMP + 6000, (AR.top, L1_TMP)
    dd_c = Dep("cw")
    dd_z = [Dep("z%d" % b_) for b_ in range(4)]
    dd_bsg = [Dep("bsg%d" % b_) for b_ in range(4)]
    dd_cv = [Dep("cvt%d" % b_) for b_ in range(4)]
    _dsgd = Dep("sgd")
    dd_csb, dd_sgd = [Dep("csb0"), Dep("csb1")], [_dsgd, _dsgd]
    d_wo1 = [Dep("wo1_%d" % k_) for k_ in range(8)]
    for kk_ in range(2):
        S.dma("pool", lambda kk_=kk_: pool.dma_start(out=wo1[:, kk_ * 4:(kk_ + 1) * 4, :], in_=owout_d.rearrange("(k p) n -> p k n", p=128)[:, kk_ * 4:(kk_ + 1) * 4, :]),
              writes=d_wo1[kk_ * 4:(kk_ + 1) * 4], semdep=d_wo1[kk_ * 4])
    for w_ in range(3):
        S.dma("sp", lambda w_=w_: sp.dma_start(out=cwT[:, :, w_], in_=cw_d[w_].rearrange("(j p) -> p j", p=128), allow_slow_non_contiguous=True), writes=[dd_c], semdep=dd_c)
    S.op("pool", lambda: pool.memset(zb, 0.0), writes=dd_z)
    for j in range(4):
        slot = j % 2

        def d_item(b, p_):
            def s1():
                b1, b2 = nextbank8(), nextbank8()
                for g, bk in ((1, b1), (2, b2)):
                    for k in range(8):
                        S.op("pe", lambda k=k, g=g, bk=bk: pe.matmul(bank(bk), lhsT=wD[slot][:, g, k, :], rhs=hT[:, k, b * 512:(b + 1) * 512], start=(k == 0), stop=(k == 7)),
                             reads=[dd_w[slot]] + d_hT[b * 4:b * 4 + 4], writes=[PB[bk]], inc=(k == 7))
                S.op("act", lambda: act.copy(out=d_csb[p_], in_=bank(b1)), reads=[PB[b1]], writes=[dd_csb[p_]])
                S.op("dve", lambda: dve.tensor_tensor(out=zb[:, 1 + b * 512:1 + (b + 1) * 512], in0=bank(b2), in1=d_csb[p_], op=ALU.mult), reads=[PB[b2], dd_csb[p_]], writes=[dd_z[b]])

            def s2():
                b3, b4 = nextbank8(), nextbank8()
                for g, bk in ((3, b3), (0, b4)):
                    for k in range(8):
                        S.op("pe", lambda k=k, g=g, bk=bk: pe.matmul(bank(bk), lhsT=wD[slot][:, g, k, :], rhs=hT[:, k, b * 512:(b + 1) * 512], start=(k == 0), stop=(k == 7)),
                             reads=[dd_w[slot]] + d_hT[b * 4:b * 4 + 4], writes=[PB[bk]], inc=(k == 7))
                S.op("act", lambda: act.activation(out=d_sgd[p_], in_=bank(b3), func=AF.Silu), reads=[PB[b3]], writes=[dd_sgd[p_]])
                S.op("dve", lambda: dve.tensor_tensor(out=bsg[:, b * 512:(b + 1) * 512], in0=bank(b4), in1=d_sgd[p_], op=ALU.mult), reads=[PB[b4], dd_sgd[p_]], writes=[dd_bsg[b]])
            return [s1, s2]

        d_items = [d_item(b, b % 2) for b in range(4)]

        def conv_blk(b, j=j):
            lo, hi = b * 512, (b + 1) * 512
            zdeps = dd_z[max(b - 1, 0):min(b + 2, 4)]
            S.op("act", lambda: act.activation(out=cvt[:, lo:hi], in_=zb[:, lo:hi], func=AF.Copy, scale=cwT[:, j, 0:1]), reads=zdeps + [dd_c], writes=[dd_cv[b]])
            S.op("dve", lambda: dve.scalar_tensor_tensor(out=cvt[:, lo:hi], in0=zb[:, lo + 1:hi + 1], scalar=cwT[:, j, 1:2], in1=cvt[:, lo:hi], op0=ALU.mult, op1=ALU.add), reads=zdeps + [dd_c], writes=[dd_cv[b]])
            S.op("dve", lambda: dve.scalar_tensor_tensor(out=cvt[:, lo:hi], in0=zb[:, lo + 2:hi + 2], scalar=cwT[:, j, 2:3], in1=cvt[:, lo:hi], op0=ALU.mult, op1=ALU.add), reads=zdeps + [dd_c], writes=[dd_cv[b]])
            S.op("pool", lambda: pool.tensor_tensor(out=yT[:, 4 + j, lo:hi], in0=cvt[:, lo:hi], in1=bsg[:, lo:hi], op=ALU.mult), reads=[dd_cv[b], dd_bsg[b]], writes=d_yT[4 + j][b * 4:b * 4 + 4])

        for t in range(5):
            if t >= 1:
                d_items[t - 1][1]()
            if t < 4:
                d_items[t][0]()
            if t == 4 and j + 2 < 4:
                load_wD(j + 2)
            if t >= 1:
                conv_blk(t - 1)
    if debug:
        S.dma("sp", lambda: sp.dma_start(out=dbg["yT1"], in_=yT), reads=[x for r in d_yT for x in r], writes=[], semdep=d_yT[0][0])
    S.barrier()

    d_fin = out_phase(1, owout_d, pre=(wo1, d_wo1))
    S.barrier()
    return nc


_CONST = {}


def _consts():
    if _CONST:
        return _CONST
    f32 = np.float32
    ident = np.eye(128, dtype=f32)
    perm = np.zeros((128, 128), f32)
    for m in range(128):
        partner = m + 32 if (m % 64) < 32 else m - 32
        perm[partner, m] = 1.0
    bones = np.zeros((128, 128), f32)
    bones[0:64, 0:64] = 1.0
    bones[64:128, 64:128] = 1.0
    rows = SEQ // 64
    row = np.repeat(np.arange(rows, dtype=f32), 64)
    col = np.tile(np.arange(64, dtype=f32), rows)
    n_freq = 16
    inv = (f32(10000.0) ** (-np.arange(n_freq, dtype=f32) / f32(n_freq))).astype(f32)
    ang = np.concatenate([row[:, None] * inv, col[:, None] * inv], axis=-1).astype(f32)
    cos = np.cos(ang).astype(f32).T
    sin = np.sin(ang).astype(f32).T
    cosT = np.concatenate([cos, cos, cos, cos], axis=0)
    sinT = np.concatenate([-sin, sin, -sin, sin], axis=0)
    tri = np.zeros((64, 2, 64), f32)
    s_idx = np.arange(64)[:, None]
    t_idx = np.arange(64)[None, :]
    tri[:, 0, :] = (s_idx <= t_idx)
    tri[:, 1, :] = (s_idx >= t_idx)
    smask = np.ones((128, 512), f32)
    smask[:, ::64] = 0.0
    _CONST.update(identF=ident, perm=perm, bones=bones, cosT=np.ascontiguousarray(cosT), sinT=np.ascontiguousarray(sinT), tri=tri, smask=smask)
    return _CONST


def make_in_maps(x, c, ctx, c_ctx, norm_gain, ada_w, ada_b, even_w_in, even_w_out, attn_qk_gain,
                 attn_lambda, attn_subln_gain, hgrn_lb_logits, hgrn_norm_gain, odd_w_in, odd_w_out,
                 gmlp_v_gain, gmlp_w_s, gmlp_b_s, conv_w):
    f = lambda a: np.ascontiguousarray(np.asarray(a, dtype=np.float32))
    shared = dict(
        norm_gain=f(norm_gain), ada_w=f(ada_w), ada_b=f(ada_b), even_w_in=f(even_w_in)[0], even_w_out=f(even_w_out)[0],
        qk_gain=f(attn_qk_gain)[0], attn_lambda=f(attn_lambda)[0].reshape(256), subln=f(attn_subln_gain)[0],
        lb_logits=f(hgrn_lb_logits), hgrn_g=f(hgrn_norm_gain)[0], odd_w_in=f(odd_w_in)[0], odd_w_out=f(odd_w_out)[0],
        v_gain=f(gmlp_v_gain)[0], w_s=f(gmlp_w_s)[0], b_s=f(gmlp_b_s)[0], conv_w=f(conv_w)[0])
    shared.update(_consts())
    x = f(x)
    c = f(c)
    ctx = f(ctx)
    c_ctx = f(c_ctx)
    maps = []
    for b in range(8):
        m = dict(shared)
        m["x"] = x[b]
        m["ctx"] = ctx[b]
        m["cvec"] = np.ascontiguousarray(np.stack([c[b], c_ctx], axis=0))
        maps.append(m)
    return maps


def kernel(**inputs):
    maps = make_in_maps(**inputs)
    nc = build(debug=False)
    res = run_bass_kernel_spmd(nc, maps, core_ids=list(range(8)))
    return np.stack([np.asarray(r["out"], dtype=np.float32) for r in res.results], axis=0)
```

```python
import numpy as np
import concourse.bass as bass
import concourse.mybir as mybir
from concourse.bass_utils import run_bass_kernel_spmd
from concourse.alu_op_type import AluOpType as ALU

F32 = mybir.dt.float32
BF16 = mybir.dt.bfloat16
AF = mybir.ActivationFunctionType

D = 1024
SEQ = 2048
CTX = 256
NT = SEQ + CTX
EPS = 1e-6
LAM_INIT = 0.8 - 0.6 * 1.0


class Dep:
    __slots__ = ("name", "w", "r", "dsem", "dcnt", "excl")

    def __init__(self, name, excl=False):
        self.name = name
        self.excl = excl
        self.w = None
        self.r = []
        self.dsem = None
        self.dcnt = 0


class Sched:
    ENG = ("pe", "act", "dve", "pool", "sp")

    def __init__(self, nc):
        self.nc = nc
        self.engs = {"pe": nc.tensor, "act": nc.scalar, "dve": nc.vector, "pool": nc.gpsimd, "sp": nc.sync}
        self.cnt = {e: 0 for e in self.ENG}
        self.sem = {}
        self.waited = {e: {} for e in self.ENG}
        self.dsems = []
        self.nops = 0
        for e in self.ENG:
            self.sem[e] = nc.alloc_semaphore("s_" + e)

    def _waits(self, eng, reads, writes):
        need = {}

        def add(p):
            if p is None:
                return
            s, v = p
            k = id(s)
            if k not in need or need[k][1] < v:
                need[k] = (s, v)

        for d in reads:
            add(d.w)
        for d in writes:
            add(d.w)
            for p in d.r:
                add(p)
        wd = self.waited[eng]
        own = self.sem[eng]
        engine = self.engs[eng]
        for k, (s, v) in need.items():
            if s is own and (eng == "pe" or v > self.cnt[eng]):
                continue
            if wd.get(k, 0) >= v:
                continue
            wd[k] = v
            engine.wait_ge(s, v)

    def op(self, eng, fn, reads=(), writes=(), inc=True):
        ex = [d for d in reads if d.excl]
        if ex:
            reads = [d for d in reads if not d.excl]
            writes = list(writes) + ex
        self._waits(eng, reads, writes)
        val = self.cnt[eng] + 1
        ins = fn()
        self.nops += 1
        if inc:
            self.cnt[eng] = val
            ins.then_inc(self.sem[eng], 1)
        tok = (self.sem[eng], val)
        for d in reads:
            d.r.append(tok)
            if len(d.r) > 64:
                d.r = self._compact(d.r)
        for d in writes:
            d.w = tok
            d.r = []

    @staticmethod
    def _compact(lst):
        best = {}
        for s, v in lst:
            k = id(s)
            if k not in best or best[k][1] < v:
                best[k] = (s, v)
        return list(best.values())

    def dma(self, eng, fn, reads=(), writes=(), semdep=None):
        self._waits(eng, reads, writes)
        d0 = semdep
        if d0.dsem is None:
            d0.dsem = self.nc.alloc_semaphore("d%d_%s" % (len(self.dsems), d0.name))
            self.dsems.append(d0)
        d0.dcnt += 16
        ins = fn()
        ins.then_inc(d0.dsem, 16)
        tok = (d0.dsem, d0.dcnt)
        for d in reads:
            d.r.append(tok)
        for d in writes:
            d.w = tok
            d.r = []

    def wait_all(self, eng, deps):
        self._waits(eng, (), deps)

    def barrier(self):
        for e in self.ENG:
            engine = self.engs[e]
            wd = self.waited[e]
            for f in self.ENG:
                if f == e or self.cnt[f] == 0:
                    continue
                s = self.sem[f]
                if wd.get(id(s), 0) >= self.cnt[f]:
                    continue
                wd[id(s)] = self.cnt[f]
                engine.wait_ge(s, self.cnt[f])
            for d0 in self.dsems:
                if wd.get(id(d0.dsem), 0) >= d0.dcnt:
                    continue
                wd[id(d0.dsem)] = d0.dcnt
                engine.wait_ge(d0.dsem, d0.dcnt)


def skew(items, newest_first=False):
    nst = max(len(it) for it in items)
    for t in range(len(items) + nst - 1):
        for s_ in (range(nst) if newest_first else reversed(range(nst))):
            i = t - s_
            if 0 <= i < len(items) and s_ < len(items[i]):
                items[i][s_]()


class Arena:
    def __init__(self, ap_f32):
        self.ap = ap_f32
        self.top = 0
        self.n = ap_f32.shape[1]

    @staticmethod
    def _view(v, shape):
        if len(shape) == 1:
            return v
        names = ["d%d" % i for i in range(len(shape))]
        pat = "p (" + " ".join(names) + ") -> p " + " ".join(names)
        kw = {names[i]: int(shape[i]) for i in range(1, len(shape))}
        return v.rearrange(pat, **kw)

    def f32(self, *shape):
        n = int(np.prod(shape))
        off = self.top
        self.top += n
        assert self.top <= self.n, ("arena overflow", self.top, self.n)
        return self._view(self.ap[:, off:off + n], shape)

    def bf16(self, *shape):
        n = int(np.prod(shape))
        nw = (n + 1) // 2
        off = self.top
        self.top += nw
        assert self.top <= self.n, ("arena overflow", self.top, self.n)
        return self._view(self.ap[:, off:off + nw].bitcast(BF16)[:, 0:n], shape)


def build(debug=False):
    nc = bass.Bass("TRN2", target_bir_lowering=False)

    def din(name, shape, dt=F32):
        return nc.dram_tensor(name, list(shape), dt, kind="ExternalInput").ap()

    x_d = din("x", [SEQ, D])
    ctx_d = din("ctx", [CTX, D])
    cvec_d = din("cvec", [2, D])
    ng_d = din("norm_gain", [2, D])
    adaw_d = din("ada_w", [2, D, 3 * D])
    adab_d = din("ada_b", [2, 3 * D])
    ewin_d = din("even_w_in", [D, 4608])
    ewout_d = din("even_w_out", [D, D])
    qkg_d = din("qk_gain", [2, 64])
    lam_d = din("attn_lambda", [256])
    subln_d = din("subln", [128])
    lbl_d = din("lb_logits", [2, 2, 512])
    hg_d = din("hgrn_g", [128])
    owin_d = din("odd_w_in", [D, 3584])
    owout_d = din("odd_w_out", [D, D])
    vg_d = din("v_gain", [512])
    ws_d = din("w_s", [4, 128, 128])
    bs_d = din("b_s", [4, 128])
    cw_d = din("conv_w", [3, 512])
    identF_d = din("identF", [128, 128])
    perm_d = din("perm", [128, 128])
    bones_d = din("bones", [128, 128])
    cos_d = din("cosT", [128, SEQ])
    sin_d = din("sinT", [128, SEQ])
    tri_d = din("tri", [64, 2, 64])
    smask_d = din("smask", [128, 512])
    out_d = nc.dram_tensor("out", [SEQ, D], F32, kind="ExternalOutput").ap()
    dbg = {}
    if debug:
        dbg["hT0"] = nc.dram_tensor("dbg_hT0", [128, 8, NT], BF16, kind="ExternalOutput").ap()
        dbg["yT0"] = nc.dram_tensor("dbg_yT0", [128, 8, SEQ], BF16, kind="ExternalOutput").ap()
        dbg["xnew"] = nc.dram_tensor("dbg_xnew", [128, 16, D], F32, kind="ExternalOutput").ap()
        dbg["yT1"] = nc.dram_tensor("dbg_yT1", [128, 8, SEQ], BF16, kind="ExternalOutput").ap()
        dbg["mods"] = nc.dram_tensor("dbg_mods", [128, 2, 3, 8, 2], F32, kind="ExternalOutput").ap()

    S = Sched(nc)
    E = nc
    NW = (nc.sbuf_bytes_remaining - 2048) // 4
    arena_t = nc.alloc_sbuf_tensor("arena", [128, NW], F32).ap()
    AR = Arena(arena_t)
    psum = nc.alloc_psum_tensor("psum", [128, 8, 512], F32).ap()
    PB = [Dep("pb%d" % i, excl=True) for i in range(8)]

    def bank(i):
        return psum[:, i, :]

    def bank_bf(i):
        return psum[:, i, :].bitcast(BF16)

    hT = AR.bf16(8, NT)
    yT = AR.bf16(8, SEQ)
    identF = AR.f32(128)
    identB = AR.bf16(128)
    zerosB = AR.bf16(512)
    gate_bc = [AR.f32(D), AR.f32(D)]
    modsc = AR.f32(2, 3, 8, 2)
    small = AR.f32(64)
    P_TOP = AR.top
    d_hT = [Dep("hT%d" % i) for i in range(18)]
    d_yT = [[Dep("yT%d_%d" % (k, i)) for i in range(16)] for k in range(8)]
    d_const = Dep("const")
    d_mods = Dep("mods")
    d_gate = [Dep("gate0"), Dep("gate1")]
    d_small = Dep("small")

    sp, act, dve, pool, pe = nc.sync, nc.scalar, nc.vector, nc.gpsimd, nc.tensor

    S.dma("sp", lambda: sp.dma_start(out=identF, in_=identF_d), writes=[d_const], semdep=d_const)
    S.op("dve", lambda: dve.tensor_copy(out=identB, in_=identF), reads=[d_const], writes=[d_const])
    S.op("pool", lambda: pool.memset(zerosB, 0.0), writes=[d_const])

    AR.top = P_TOP
    cvT = AR.f32(8, 2)
    csT = AR.bf16(8, 2)
    Rrow = AR.f32(3 * D)
    adab = AR.f32(3 * D)
    gT = AR.f32(2, 8)
    onesr = AR.f32(128)
    wada = [AR.f32(3 * D) for _ in range(3)]
    wadab = [AR.bf16(3 * D), AR.bf16(3 * D)]
    d_cv, d_R, d_adab, d_gT, d_ones = Dep("cv"), Dep("R"), Dep("adab"), Dep("gT"), Dep("ones")
    d_wada = [[Dep("wada%d_%d" % (i, j)) for j in range(3)] for i in range(3)]
    d_wadab = [[Dep("wadab%d_%d" % (i, j)) for j in range(4)] for i in range(2)]

    for r in range(2):
        S.dma("sp", lambda r=r: sp.dma_start(out=cvT[:, :, r], in_=cvec_d[r].rearrange("(k p) -> p k", p=128), allow_slow_non_contiguous=True), writes=[d_cv], semdep=d_cv)
        S.dma("sp", lambda r=r: sp.dma_start(out=gT[:, r, :], in_=ng_d[r].rearrange("(k p) -> p k", p=128), allow_slow_non_contiguous=True), writes=[d_gT], semdep=d_gT)
    S.op("act", lambda: act.activation(out=csT, in_=cvT, func=AF.Silu), reads=[d_cv], writes=[d_cv])
    S.op("pool", lambda: pool.memset(onesr, 1.0), writes=[d_ones])
    wi = 0
    for l in range(2):
        S.dma("sp", lambda l=l: sp.dma_start(out=adab[0:2, :], in_=adab_d[l].partition_broadcast(2)), writes=[d_adab], semdep=d_adab)
        for k in range(8):
            slot = wi % 3
            bs_ = wi % 2
            wi += 1
            for hh in range(3):
                qn_ = ("sp", "pool", "act")[hh]
                qe_ = (sp, pool, act)[hh]
                S.dma(qn_, lambda l=l, k=k, slot=slot, hh=hh, qe_=qe_: qe_.dma_start(
                    out=wada[slot][:, hh * 1024:(hh + 1) * 1024], in_=adaw_d[l][k * 128:(k + 1) * 128, hh * 1024:(hh + 1) * 1024]),
                    writes=[d_wada[slot][hh]], semdep=d_wada[slot][hh])
            S.op("dve", lambda slot=slot, bs_=bs_: dve.tensor_copy(out=wadab[bs_][:, 0:1024], in_=wada[slot][:, 0:1024]), reads=[d_wada[slot][0]], writes=[d_wadab[bs_][0]])
            S.op("act", lambda slot=slot, bs_=bs_: act.copy(out=wadab[bs_][:, 1024:1536], in_=wada[slot][:, 1024:1536]), reads=[d_wada[slot][1]], writes=[d_wadab[bs_][1]])
            S.op("act", lambda slot=slot, bs_=bs_: act.copy(out=wadab[bs_][:, 1536:2560], in_=wada[slot][:, 1536:2560]), reads=[d_wada[slot][1], d_wada[slot][2]], writes=[d_wadab[bs_][2]])
            S.op("pool", lambda slot=slot, bs_=bs_: pool.tensor_copy(out=wadab[bs_][:, 2560:3072], in_=wada[slot][:, 2560:3072]), reads=[d_wada[slot][2]], writes=[d_wadab[bs_][3]])
            for n in range(6):
                part = (0, 0, 1, 2, 2, 3)[n]
                S.op("pe", lambda k=k, n=n, bs_=bs_: pe.matmul(bank(n)[0:2, :], lhsT=csT[:, k, :], rhs=wadab[bs_][:, n * 512:(n + 1) * 512], start=(k == 0), stop=(k == 7)),
                     reads=[d_cv, d_wadab[bs_][part]], writes=[PB[n]])
        for n in range(6):
            S.op("dve", lambda n=n: dve.tensor_tensor(out=Rrow[0:2, n * 512:(n + 1) * 512], in0=bank(n)[0:2, :], in1=adab[0:2, n * 512:(n + 1) * 512], op=ALU.add),
                 reads=[PB[n], d_adab], writes=[d_R])
        pt = bank(6)[:, 0:32].rearrange("p (j r) -> p j r", r=2)
        for j in range(16):
            S.op("pe", lambda j=j: pe.transpose(out=pt[:, j, :], in_=Rrow[0:2, j * 128:(j + 1) * 128], identity=identF[0:2, 0:2]),
                 reads=[d_R, d_const], writes=[PB[6]], inc=(j == 15))
        S.op("dve", lambda l=l: dve.tensor_copy(out=modsc[:, l, 0, :, :], in_=pt[:, 0:8, :]), reads=[PB[6]], writes=[d_mods])
        S.op("dve", lambda l=l: dve.tensor_copy(out=modsc[:, l, 2, :, :], in_=pt[:, 8:16, :]), reads=[PB[6]], writes=[d_mods])
        for r in range(2):
            S.op("dve", lambda l=l, r=r: dve.scalar_tensor_tensor(out=modsc[:, l, 1, :, r], in0=modsc[:, l, 2, :, r], scalar=1.0, in1=gT[:, l, :], op0=ALU.add, op1=ALU.mult),
                 reads=[d_mods, d_gT], writes=[d_mods])
        for n in range(2):
            S.op("pe", lambda n=n: pe.matmul(bank(2 + n), lhsT=onesr[0:1, :], rhs=Rrow[0:1, 2048 + n * 512:2048 + (n + 1) * 512], start=True, stop=True),
                 reads=[d_R, d_ones], writes=[PB[2 + n]])
            S.op("act", lambda n=n, l=l: act.copy(out=gate_bc[l][:, n * 512:(n + 1) * 512], in_=bank(2 + n)), reads=[PB[2 + n]], writes=[d_gate[l]])
    if debug:
        S.dma("sp", lambda: sp.dma_start(out=dbg["mods"], in_=modsc), reads=[d_mods], writes=[], semdep=d_mods)
    S.barrier()

    def norm_phase(l, ntiles, src_fn, top):
        AR.top = top
        nxt = 4 if l == 0 else 0
        xt = [AR.f32(D) for _ in range(nxt)]
        xn = [AR.f32(D), AR.f32(D)]
        junk = AR.bf16(D)
        stat = AR.f32(3, 18)
        d_xt = [Dep("xt%d" % i_) for i_ in range(nxt)]
        d_xn = [Dep("xn0"), Dep("xn1")]
        d_junk, d_stat = Dep("junk"), [Dep("stat%d" % i) for i in range(18)]

        def item(i):
            s = i % 2
            src, is_ctx, sdep = src_fn(i)
            b0 = 4 + 2 * (i % 2)
            pt = psum[:, b0:b0 + 2, :].rearrange("p b (j c) -> p (b j) c", c=128)
            r = 1 if is_ctx else 0
            if sdep is None:
                xin, xdep = xt[i % 4], d_xt[i % 4]
            else:
                xin, xdep = src, sdep

            def s0():
                if sdep is None:
                    S.dma("sp", lambda: sp.dma_start(out=xt[i % 4], in_=src), writes=[d_xt[i % 4]], semdep=d_xt[i % 4])

            def s1():
                S.op("act", lambda: act.activation(out=junk, in_=xin, func=AF.Square, accum_out=stat[:, 0, i:i + 1]), reads=[xdep], writes=[d_junk, d_stat[i]])
                S.op("act", lambda: act.activation(out=stat[:, 1, i:i + 1], in_=stat[:, 0, i:i + 1], func=AF.Sqrt, scale=1.0 / D, bias=EPS), reads=[d_stat[i]], writes=[d_stat[i]])
                S.op("dve", lambda: dve.reciprocal(out=stat[:, 2, i:i + 1], in_=stat[:, 1, i:i + 1]), reads=[d_stat[i]], writes=[d_stat[i]])

            def s2():
                S.op("dve", lambda: dve.tensor_scalar(out=xn[s], in0=xin, scalar1=stat[:, 2, i:i + 1], scalar2=None, op0=ALU.mult), reads=[xdep, d_stat[i]], writes=[d_xn[s]])
                for k in range(8):
                    S.op("pe", lambda k=k: pe.transpose(out=pt[:, k, :], in_=xn[s][:, k * 128:(k + 1) * 128], identity=identF),
                         reads=[d_xn[s], d_const], writes=[PB[b0 + k // 4]], inc=(k == 7))
                for _ in range(6):
                    S.op("pe", lambda: pe.matmul(bank(3), lhsT=zerosB[:, 0:128], rhs=zerosB, start=True, stop=True, skip_group_check=True), reads=[d_const], writes=[PB[3]], inc=False)

            def s3():
                for k in range(8):
                    if k < 4:
                        S.op("act", lambda k=k: act.activation(out=hT[:, k, i * 128:(i + 1) * 128], in_=pt[:, k, :], func=AF.Identity,
                                                               scale=modsc[:, l, 1, k, r:r + 1], bias=modsc[:, l, 0, k, r:r + 1]),
                             reads=[PB[b0 + k // 4], d_mods], writes=[d_hT[i]])
                    else:
                        S.op("dve", lambda k=k: dve.tensor_scalar(out=hT[:, k, i * 128:(i + 1) * 128], in0=pt[:, k, :],
                                                                  scalar1=modsc[:, l, 1, k, r:r + 1], scalar2=modsc[:, l, 0, k, r:r + 1], op0=ALU.mult, op1=ALU.add),
                             reads=[PB[b0 + k // 4], d_mods], writes=[d_hT[i]])
            return [s0, s1, s2, s3]

        skew([item(i) for i in range(ntiles)], newest_first=True)

    def src0(i):
        if i < 2:
            return ctx_d[i * 128:(i + 1) * 128, :], True, None
        return x_d[(i - 2) * 128:(i - 1) * 128, :], False, None


    AR.top = P_TOP
    cosT = AR.f32(SEQ)
    sinT = AR.f32(SEQ)
    permS = AR.f32(128)
    bonesS = AR.f32(128)
    gqk = AR.f32(2)
    lamb = AR.f32(256)
    lamt = AR.f32(8)
    gS = AR.f32(128)
    epsT = AR.f32(1)
    mhA = AR.f32(1)
    wA = [AR.bf16(4, 8, 128), AR.bf16(4, 8, 128)]
    qT = [AR.bf16(SEQ), AR.bf16(SEQ)]
    kT0 = [AR.bf16(NT), AR.bf16(NT)]
    kT1 = [AR.bf16(NT), AR.bf16(NT)]
    vaug = [AR.bf16(18, 130), AR.bf16(18, 130)]
    gateA = [AR.bf16(16, 128), AR.bf16(16, 128)]
    t_sq = [AR.f32(512), AR.f32(512)]
    t_qg = [AR.f32(512), AR.f32(512)]
    t_rs = [AR.f32(512), AR.f32(512)]
    t_a = [AR.f32(512), AR.f32(512)]
    t_b = [AR.f32(512), AR.f32(512)]
    t_gs = AR.f32(512)
    Et = [AR.bf16(512) for _ in range(4)]
    ep_f = AR.f32(16, 128)
    ep_s = AR.f32(8, 8)
    ep_y = AR.bf16(8, 128)
    uc = [0]
    d_A = Dep("Aconst")
    d_Ar = Dep("Arope")
    d_wA = [Dep("wA0"), Dep("wA1")]
    d_qT = [[Dep("qT%d_%d" % (p_, i)) for i in range(4)] for p_ in range(2)]
    d_kT = [[Dep("kT%d_%d" % (p_, i)) for i in range(5)] for p_ in range(2)]
    d_v = [[Dep("vaug%d_%d" % (p_, i)) for i in range(5)] for p_ in range(2)]
    d_gA = [[Dep("gateA%d_%d" % (p_, i)) for i in range(4)] for p_ in range(2)]
    d_tsq = [Dep("tsq0"), Dep("tsq1")]
    d_tqg = [Dep("tqg0"), Dep("tqg1")]
    d_trs = [Dep("trs0"), Dep("trs1")]
    d_ta = [Dep("ta0"), Dep("ta1")]
    d_tb = [Dep("tb0"), Dep("tb1")]
    d_tgs = Dep("tgs")
    d_E = [Dep("E%d" % i) for i in range(4)]
    d_ep = [Dep("ep%d" % i) for i in range(8)]

    S.dma("sp", lambda: sp.dma_start(out=cosT, in_=cos_d), writes=[d_Ar], semdep=d_Ar)
    S.dma("sp", lambda: sp.dma_start(out=sinT, in_=sin_d), writes=[d_Ar], semdep=d_Ar)
    S.dma("sp", lambda: sp.dma_start(out=permS, in_=perm_d), writes=[d_Ar], semdep=d_Ar)
    S.dma("sp", lambda: sp.dma_start(out=bonesS, in_=bones_d), writes=[d_Ar], semdep=d_Ar)
    for m in range(2):
        S.dma("sp", lambda m=m: sp.dma_start(out=gqk[m * 64:(m + 1) * 64, :], in_=qkg_d.rearrange("r d -> d r"), allow_slow_non_contiguous=True), writes=[d_A], semdep=d_A)
    S.dma("sp", lambda: sp.dma_start(out=lamb, in_=lam_d.partition_broadcast(128)), writes=[d_A], semdep=d_A)
    S.dma("sp", lambda: sp.dma_start(out=gS, in_=subln_d.partition_broadcast(128)), writes=[d_A], semdep=d_A)
    S.op("dve", lambda: dve.tensor_scalar(out=gqk[:, 0:1], in0=gqk[:, 0:1], scalar1=0.125, scalar2=None, op0=ALU.mult), reads=[d_A], writes=[d_A])
    S.op("dve", lambda: dve.tensor_scalar(out=gS, in0=gS, scalar1=1.0 - LAM_INIT, scalar2=None, op0=ALU.mult), reads=[d_A], writes=[d_A])
    S.op("dve", lambda: dve.tensor_tensor(out=lamb[:, 0:64], in0=lamb[:, 0:64], in1=lamb[:, 64:128], op=ALU.mult), reads=[d_A], writes=[d_A])
    S.op("dve", lambda: dve.tensor_tensor(out=lamb[:, 128:192], in0=lamb[:, 128:192], in1=lamb[:, 192:256], op=ALU.mult), reads=[d_A], writes=[d_A])
    S.op("dve", lambda: dve.tensor_reduce(out=lamt[:, 0:1], in_=lamb[:, 0:64], op=ALU.add, axis=mybir.AxisListType.X), reads=[d_A], writes=[d_A])
    S.op("dve", lambda: dve.tensor_reduce(out=lamt[:, 1:2], in_=lamb[:, 128:192], op=ALU.add, axis=mybir.AxisListType.X), reads=[d_A], writes=[d_A])
    S.op("act", lambda: act.activation(out=lamt[:, 2:4], in_=lamt[:, 0:2], func=AF.Exp), reads=[d_A], writes=[d_A])
    S.op("dve", lambda: dve.scalar_tensor_tensor(out=lamt[:, 4:5], in0=lamt[:, 3:4], scalar=-LAM_INIT, in1=lamt[:, 2:3], op0=ALU.add, op1=ALU.subtract), reads=[d_A], writes=[d_A])
    S.op("pool", lambda: pool.memset(epsT, EPS), writes=[d_A])
    S.op("pool", lambda: pool.memset(mhA, -0.5), writes=[d_A])
    for p_ in range(2):
        S.op("pool", lambda p_=p_: pool.memset(kT0[p_], 0.0), writes=d_kT[p_])
        S.op("pool", lambda p_=p_: pool.memset(kT1[p_], 0.0), writes=d_kT[p_])
        S.op("pool", lambda p_=p_: pool.memset(vaug[p_][:, :, 128:130], 1.0), writes=d_v[p_])

    PROJ_B = [3, 4, 5, 6, 7]
    pr = [0]

    def nextbank():
        b = PROJ_B[pr[0] % len(PROJ_B)]
        pr[0] += 1
        return b

    blk = [0]

    def qk_item(h, which, tok0, ntok, rope, dst_fn, ddst):
        s = blk[0] % 2
        blk[0] += 1
        slot = h % 2
        st = {}

        def m1():
            bp = nextbank()
            st["bp"] = bp
            for k in range(8):
                S.op("pe", lambda k=k: pe.matmul(bank(bp)[:, 0:ntok], lhsT=wA[slot][:, which, k, :], rhs=hT[:, k, tok0:tok0 + ntok], start=(k == 0), stop=(k == 7)),
                     reads=[d_wA[slot]] + d_hT[tok0 // 128:(tok0 + ntok + 127) // 128], writes=[PB[bp]], inc=(k == 7))

        def m2():
            bp = st["bp"]
            S.op("dve", lambda: dve.tensor_copy(out=t_b[s][:, 0:ntok], in_=bank(bp)[:, 0:ntok]), reads=[PB[bp]], writes=[d_tb[s]])
            S.op("dve", lambda: dve.tensor_tensor(out=t_sq[s][:, 0:ntok], in0=t_b[s][:, 0:ntok], in1=t_b[s][:, 0:ntok], op=ALU.mult), reads=[d_tb[s]], writes=[d_tsq[s]])
            S.op("dve", lambda: dve.tensor_scalar(out=t_qg[s][:, 0:ntok], in0=t_b[s][:, 0:ntok], scalar1=gqk[:, which:which + 1], scalar2=None, op0=ALU.mult),
                 reads=[d_tb[s], d_A], writes=[d_tqg[s]])

        def m3():
            bs = nextbank()
            st["bs"] = bs
            S.op("pe", lambda: pe.matmul(bank(bs)[:, 0:ntok], lhsT=bonesS, rhs=t_sq[s][:, 0:ntok], start=True, stop=True), reads=[d_tsq[s], d_Ar], writes=[PB[bs]])
            if rope:
                br = nextbank()
                st["br"] = br
                S.op("pe", lambda: pe.matmul(bank(br)[:, 0:ntok], lhsT=permS, rhs=t_qg[s][:, 0:ntok], start=True, stop=True), reads=[d_tqg[s], d_Ar], writes=[PB[br]])

        def m4():
            bs = st["bs"]
            S.op("act", lambda: act.activation(out=t_rs[s][:, 0:ntok], in_=bank(bs)[:, 0:ntok], func=AF.Ln, scale=1.0 / 64, bias=epsT[:, 0:1]), reads=[PB[bs], d_A], writes=[d_trs[s]])
            S.op("act", lambda: act.activation(out=t_rs[s][:, 0:ntok], in_=t_rs[s][:, 0:ntok], func=AF.Exp, scale=-0.5), reads=[d_trs[s]], writes=[d_trs[s]])
            if rope:
                br = st["br"]
                p0 = tok0 - CTX
                S.op("pool", lambda: pool.tensor_tensor(out=t_a[s][:, 0:ntok], in0=t_qg[s][:, 0:ntok], in1=cosT[:, p0:p0 + ntok], op=ALU.mult), reads=[d_tqg[s], d_Ar], writes=[d_ta[s]])
                S.op("dve", lambda: dve.tensor_tensor(out=t_b[s][:, 0:ntok], in0=bank(br)[:, 0:ntok], in1=sinT[:, p0:p0 + ntok], op=ALU.mult), reads=[PB[br], d_Ar, d_tsq[s], d_tqg[s]], writes=[d_tb[s]])

        def m5():
            if rope:
                S.op("dve", lambda: dve.tensor_tensor(out=t_a[s][:, 0:ntok], in0=t_a[s][:, 0:ntok], in1=t_b[s][:, 0:ntok], op=ALU.add), reads=[d_ta[s], d_tb[s]], writes=[d_ta[s]])
                srcv, sdep = t_a[s], d_ta[s]
            else:
                srcv, sdep = t_qg[s], d_tqg[s]
            for (dst, p_lo, p_hi) in dst_fn():
                eng_, ee_ = ("dve", dve)
                S.op(eng_, lambda dst=dst, p_lo=p_lo, p_hi=p_hi: ee_.tensor_tensor(out=dst, in0=srcv[p_lo:p_hi, 0:ntok], in1=t_rs[s][p_lo:p_hi, 0:ntok], op=ALU.mult),
                     reads=[sdep, d_trs[s]], writes=[ddst])
        return [m1, m2, m3, m4, m5]

    def proj_tasks(h, interleave=False):
        hp = h % 2
        slot = h % 2
        items = []
        for i in range(4):
            items.append(qk_item(h, 0, CTX + i * 512, 512, True, lambda i=i: [(qT[hp][:, i * 512:(i + 1) * 512], 0, 128)], d_qT[hp][i]))
        items.append(qk_item(h, 1, 0, 256, False, lambda: [(kT0[hp][0:64, 0:256], 0, 64), (kT1[hp][64:128, 0:256], 64, 128)], d_kT[hp][0]))
        for i in range(4):
            t0 = CTX + i * 512
            items.append(qk_item(h, 1, t0, 512, True, lambda t0=t0: [(kT0[hp][0:64, t0:t0 + 512], 0, 64), (kT1[hp][64:128, t0:t0 + 512], 64, 128)], d_kT[hp][1 + i]))
        tasks = []
        if interleave:
            tasks.extend(items[0][0:3])
            for n_ in range(1, len(items)):
                a_, b_ = items[n_], items[n_ - 1]
                tasks.extend([a_[0], b_[3], a_[1], b_[4], a_[2]])
            tasks.extend(items[-1][3:5])
        else:
            for it in items:
                tasks.extend(it)

        def v_task(grp):
            st = {}

            def a():
                tiles = list(range(grp * 4, min(grp * 4 + 4, 18)))
                bp = nextbank()
                st["bp"] = bp
                pv = bank(bp).rearrange("p (j c) -> p j c", c=128)
                for jj, ti in enumerate(tiles):
                    for k in range(8):
                        S.op("pe", lambda k=k, jj=jj, ti=ti: pe.matmul(pv[:, jj, :], lhsT=hT[:, k, ti * 128:(ti + 1) * 128], rhs=wA[slot][:, 2, k, :], start=(k == 0), stop=(k == 7)),
                             reads=[d_wA[slot], d_hT[ti]], writes=[PB[bp]], inc=(k == 7 and jj == len(tiles) - 1))

            def b():
                bp = st["bp"]
                pv = bank(bp).rearrange("p (j c) -> p j c", c=128)
                n = len(list(range(grp * 4, min(grp * 4 + 4, 18))))
                S.op("dve", lambda: dve.tensor_copy(out=vaug[hp][:, grp * 4:grp * 4 + n, 0:128], in_=pv[:, 0:n, :]), reads=[PB[bp]], writes=[d_v[hp][grp]])
            return [a, b]

        def g_task(grp):
            st = {}

            def a():
                bp = nextbank()
                st["bp"] = bp
                pv = bank(bp).rearrange("p (j c) -> p j c", c=128)
                for jj in range(4):
                    ti = 2 + grp * 4 + jj
                    for k in range(8):
                        S.op("pe", lambda k=k, jj=jj, ti=ti: pe.matmul(pv[:, jj, :], lhsT=hT[:, k, ti * 128:(ti + 1) * 128], rhs=wA[slot][:, 3, k, :], start=(k == 0), stop=(k == 7)),
                             reads=[d_wA[slot], d_hT[ti]], writes=[PB[bp]], inc=(k == 7 and jj == 3))

            def b():
                bp = st["bp"]
                S.op("act", lambda: act.activation(out=t_gs, in_=bank(bp), func=AF.Exp, scale=-1.0), reads=[PB[bp]], writes=[d_tgs])
                S.op("dve", lambda: dve.tensor_scalar(out=t_gs, in0=t_gs, scalar1=1.0, scalar2=1e30, op0=ALU.add, op1=ALU.min), reads=[], writes=[d_tgs])

            def c():
                bp = st["bp"]
                pv = bank(bp).rearrange("p (j c) -> p j c", c=128)
                S.op("act", lambda: act.activation(out=t_gs, in_=t_gs, func=AF.Ln), reads=[], writes=[d_tgs])
                S.op("act", lambda: act.activation(out=t_gs, in_=t_gs, func=AF.Exp, scale=-1.0), reads=[], writes=[d_tgs])
                S.op("dve", lambda: dve.tensor_tensor(out=gateA[hp][:, grp * 4:grp * 4 + 4, :], in0=pv, in1=t_gs.rearrange("p (j c) -> p j c", c=128), op=ALU.mult), reads=[PB[bp], d_tgs], writes=[d_gA[hp][grp]])
            return [a, b, c]

        for grp in range(5):
            tasks.extend(v_task(grp))
        for grp in range(4):
            tasks.extend(g_task(grp))
        return tasks

    def load_wA(h):
        slot = h % 2
        for g in range(4):
            S.dma("pool", lambda g=g: pool.dma_start(
                out=wA[slot][:, g, :, :], in_=ewin_d.rearrange("(k p) n -> p k n", p=128)[:, :, g * 512 + h * 128:g * 512 + (h + 1) * 128]),
                writes=[d_wA[slot]], semdep=d_wA[slot])

    _save_top = AR.top
    AR.top = P_TOP
    lbl = AR.f32(2, 2, 4)
    lbv = AR.f32(3, 2, 4)
    tri = AR.f32(2, 64)
    smask = AR.f32(512)
    hgT = AR.f32(1)
    epsB = AR.f32(1)
    oneB = AR.f32(1)
    mhB = AR.f32(1)
    wB = AR.bf16(5, 8, 128)
    assert AR.top <= P_TOP + 2 * SEQ
    AR.top = _save_top
    d_B = Dep("Bconst")
    d_wB = Dep("wB")

    def load_wB(h, extra=()):
        for g in range(5):
            S.dma("pool", lambda g=g: pool.dma_start(
                out=wB[:, g, :, :], in_=ewin_d.rearrange("(k p) n -> p k n", p=128)[:, :, 2048 + g * 512 + h * 128:2048 + g * 512 + (h + 1) * 128]),
                writes=[d_wB] + list(extra), semdep=d_wB)

    def b_prefetch():
        ex = [d_Ar]
        for dd_ in range(2):
            for ee_ in range(2):
                S.dma("sp", lambda dd_=dd_, ee_=ee_: sp.dma_start(out=lbl[:, dd_, ee_, :], in_=lbl_d[dd_, ee_].rearrange("(h p) -> p h", p=128), allow_slow_non_contiguous=True), writes=[d_B] + ex, semdep=d_B)
        S.dma("sp", lambda: sp.dma_start(out=tri[0:64, :, :], in_=tri_d), writes=[d_B] + ex, semdep=d_B)
        S.dma("sp", lambda: sp.dma_start(out=smask, in_=smask_d), writes=[d_B] + ex, semdep=d_B)
        S.dma("sp", lambda: sp.dma_start(out=hgT, in_=hg_d.rearrange("(p o) -> p o", o=1), allow_slow_non_contiguous=True), writes=[d_B] + ex, semdep=d_B)
        load_wB(0, extra=ex)

    load_wA(0)
    norm_phase(0, 18, src0, NW - 6900)
    if debug:
        S.dma("sp", lambda: sp.dma_start(out=dbg["hT0"], in_=hT), reads=d_hT, writes=[], semdep=d_hT[0])
    S.barrier()
    for f in proj_tasks(0, interleave=True):
        f()
    deferred = []
    PROJ_B[:] = [6, 7]
    for h in range(4):
        hp = h % 2
        if h + 1 < 4:
            load_wA(h + 1)
            tasks = proj_tasks(h + 1)
        else:
            tasks = []

        def acc(m, t):
            sidx = m * 4 + t
            return psum[:, sidx // 3, (sidx % 3) * 130:(sidx % 3) * 130 + 130]

        def emit_score(i, u):
            j, m = u // 2, u % 2
            sb = 3 + (uc[0] + u) % 3
            kTm = kT0[hp] if m == 0 else kT1[hp]
            kd = d_kT[hp][0] if j < 2 else d_kT[hp][1 + (j - 2) // 4]
            S.op("pe", lambda: pe.matmul(bank(sb), lhsT=kTm[:, j * 128:(j + 1) * 128], rhs=qT[hp][:, i * 512:(i + 1) * 512], start=True, stop=True),
                 reads=[kd, d_qT[hp][i]], writes=[PB[sb]])

        def emit_exp_pv(i, u):
            j, m = u // 2, u % 2
            sb = 3 + (uc[0] + u) % 3
            es = (uc[0] + u) % 4
            S.op("act", lambda: act.activation(out=Et[es], in_=bank(sb), func=AF.Exp), reads=[PB[sb]], writes=[d_E[es]])
            for t in range(4):
                S.op("pe", lambda t=t: pe.matmul(acc(m, t)[:, 0:129], lhsT=Et[es][:, t * 128:(t + 1) * 128], rhs=vaug[hp][:, j, 0:129], start=False, stop=(j == 17), skip_group_check=True),
                     reads=[d_E[es], d_v[hp][j // 4]], writes=[PB[(m * 4 + t) // 3]], inc=(t == 3))

        def epilogue1(i):
            sl = i % 2
            for t in range(4):
                a0, a1 = acc(0, t), acc(1, t)
                b0_, b1_ = PB[t // 3], PB[(4 + t) // 3]
                es_ = ep_s[:, sl * 4 + t, :]
                f0, f1 = ep_f[:, sl * 8 + 2 * t, :], ep_f[:, sl * 8 + 2 * t + 1, :]
                dd = d_ep[sl * 4 + t]
                S.op("dve", lambda: dve.reciprocal(out=es_[:, 0:1], in_=a0[:, 128:129]), reads=[b0_], writes=[dd])
                S.op("dve", lambda: dve.reciprocal(out=es_[:, 1:2], in_=a1[:, 128:129]), reads=[b1_], writes=[dd])
                S.op("dve", lambda: dve.tensor_tensor(out=es_[:, 1:2], in0=es_[:, 1:2], in1=lamt[:, 4:5], op=ALU.mult), reads=[d_A], writes=[dd])
                S.op("dve", lambda: dve.tensor_scalar(out=f0, in0=a1[:, 0:128], scalar1=es_[:, 1:2], scalar2=None, op0=ALU.mult), reads=[b1_], writes=[dd])
                S.op("dve", lambda: dve.scalar_tensor_tensor(out=f1, in0=a0[:, 0:128], scalar=es_[:, 0:1], in1=f0, op0=ALU.mult, op1=ALU.add), reads=[b0_], writes=[dd])

        def epilogue1b(i, hp=hp):
            sl = i % 2
            for t in range(4):
                qt = i * 4 + t
                es_ = ep_s[:, sl * 4 + t, :]
                f0, f1 = ep_f[:, sl * 8 + 2 * t, :], ep_f[:, sl * 8 + 2 * t + 1, :]
                dd = d_ep[sl * 4 + t]
                S.op("dve", lambda: dve.scalar_tensor_tensor(out=f0, in0=f1, scalar=1.0, in1=f1, op0=ALU.mult, op1=ALU.mult, accum_out=es_[:, 2:3]), reads=[dd], writes=[dd])
                S.op("dve", lambda: dve.tensor_scalar(out=es_[:, 3:4], in0=es_[:, 2:3], scalar1=1.0 / 128, scalar2=EPS, op0=ALU.mult, op1=ALU.add), reads=[dd], writes=[dd])
                S.op("pool", lambda: pool.tensor_tensor(out=es_[:, 4:5], in0=es_[:, 3:4], in1=mhA, op=ALU.pow), reads=[dd, d_A], writes=[dd])
                S.op("dve", lambda: dve.scalar_tensor_tensor(out=f0, in0=f1, scalar=es_[:, 4:5], in1=gS, op0=ALU.mult, op1=ALU.mult), reads=[dd, d_A], writes=[dd])
                S.op("pool", lambda: pool.tensor_tensor(out=ep_y[:, sl * 4 + t, :], in0=f0, in1=gateA[hp][:, qt, :], op=ALU.mult), reads=[dd, d_gA[hp][qt // 4]], writes=[dd])

        def epilogue2(i, h=h):
            sl = i % 2
            bt = nextbank()
            pyt = bank_bf(bt)[:, 0:512].rearrange("p (t c) -> p t c", c=128)
            for t in range(4):
                S.op("pe", lambda t=t: pe.transpose(out=pyt[:, t, :], in_=ep_y[:, sl * 4 + t, :], identity=identB), reads=[d_ep[sl * 4 + t], d_const], writes=[PB[bt]], inc=(t == 3))
            S.op("dve", lambda: dve.tensor_copy(out=yT[:, h, i * 512:(i + 1) * 512], in_=bank_bf(bt)[:, 0:512]), reads=[PB[bt]], writes=d_yT[h][i * 4:i * 4 + 4])

        def zero_acc():
            for b in range(3):
                S.op("pe", lambda b=b: pe.matmul(bank(b), lhsT=zerosB[:, 0:128], rhs=zerosB, start=True, stop=True, skip_group_check=True), reads=[d_const], writes=[PB[b]])

        for i in range(4):
            if i == 0:
                zero_acc()
                for u0 in range(3):
                    emit_score(0, u0)
            for u in range(36):
                emit_exp_pv(i, u)
                if u + 3 < 36:
                    emit_score(i, u + 3)
                for dq in list(deferred):
                    dq[0] -= 1
                    if dq[0] <= 0:
                        deferred.remove(dq)
                        dq[1]()
                if tasks and u % 2 == 1:
                    tasks.pop(0)()
            uc[0] += 36
            if i + 1 < 4:
                for u0 in range(3):
                    emit_score(i + 1, u0)
            epilogue1(i)
            deferred.append([4, (lambda e=epilogue1b, i=i: e(i))])
            deferred.append([12, (lambda e=epilogue2, i=i: e(i))])
            if i + 1 < 4:
                zero_acc()
        while tasks:
            tasks.pop(0)()
        if h == 2:
            b_prefetch()
    for dq in deferred:
        dq[1]()
    PROJ_B[:] = [0, 1, 2, 3, 4, 5, 6, 7]
    S.barrier()

    AR.top = P_TOP
    lbl = AR.f32(2, 2, 4)
    lbv = AR.f32(3, 2, 4)
    tri = AR.f32(2, 64)
    smask = AR.f32(512)
    hgT = AR.f32(1)
    epsB = AR.f32(1)
    oneB = AR.f32(1)
    mhB = AR.f32(1)
    wB = AR.bf16(5, 8, 128)
    qg = [AR.bf16(NT), AR.bf16(NT)]
    kg = [AR.bf16(NT), AR.bf16(NT)]
    _off_kgt0 = AR.top
    kgtok = [AR.bf16(36, 128), AR.bf16(36, 128)]
    vtok = AR.bf16(36, 128)
    gtok = AR.bf16(32, 128)
    e1 = [AR.f32(36), AR.f32(36)]
    e2 = [AR.f32(36), AR.f32(36)]
    eL = [AR.f32(36), AR.f32(36)]
    Sring = [AR.f32(8, 128), AR.f32(8, 128)]
    Smid = [AR.bf16(32, 128), AR.bf16(32, 128)]
    Asb = AR.bf16(4, 2, 64)
    _off_t = AR.top
    bt_sig = [AR.f32(512), AR.f32(512)]
    bt_g = [AR.f32(512), AR.f32(512)]
    bt_kk = [AR.f32(512), AR.f32(512)]
    bt_G = [AR.f32(512), AR.f32(512)]
    kg2 = [bt_kk[0][:, 0:256].bitcast(BF16), bt_kk[1][:, 0:256].bitcast(BF16)]
    ot = [bt_sig[0].rearrange("p (j c) -> p j c", c=128), bt_sig[1].rearrange("p (j c) -> p j c", c=128)]
    bt_s8 = [AR.f32(2, 8), AR.f32(2, 8)]
    stage = [AR.bf16(512), AR.bf16(512)]
    ojunk = AR.f32(128)
    hb_ss = AR.f32(3, 32)
    hb_y = AR._view(AR.ap[:, _off_kgt0:_off_kgt0 + 2048].bitcast(BF16), (32, 128))
    d_qg = [[Dep("qg%d_%d" % (d, b)) for b in range(5)] for d in range(2)]
    d_kg = [[Dep("kg%d_%d" % (d, b)) for b in range(5)] for d in range(2)]
    d_kgt = [[Dep("kgt%d_%d" % (d, b)) for b in range(5)] for d in range(2)]
    d_vt = [Dep("vt%d" % i) for i in range(5)]
    d_gt = [Dep("gt%d" % i) for i in range(5)]
    d_e = [[Dep("e%d_%d" % (d, b)) for b in range(5)] for d in range(2)]
    d_Sr = [[Dep("Sr%d_%d" % (d, i)) for i in range(2)] for d in range(2)]
    d_Sm = [[Dep("Sm%d_%d" % (d, i)) for i in range(8)] for d in range(2)]
    d_As = Dep("As")
    d_bt = [{k: Dep("bt%d_%s" % (p_, k)) for k in ("sig", "g", "kk", "G", "s8")} for p_ in range(2)]
    d_stage = [Dep("stage0"), Dep("stage1")]
    d_ot = [Dep("ot0"), Dep("ot1")]
    d_hb = Dep("hb")
    d_hby = [Dep("hby%d" % i) for i in range(4)]

    S.op("pool", lambda: pool.memset(epsB, EPS), writes=[d_B])
    S.op("pool", lambda: pool.memset(oneB, 1.0), writes=[d_B])
    S.op("pool", lambda: pool.memset(mhB, -0.5), writes=[d_B])
    S.op("dve", lambda: dve.tensor_tensor(out=lbv[:, 1, :, :], in0=lbl[:, :, 1, :], in1=lbl[:, :, 0, :], op=ALU.subtract), reads=[d_B], writes=[d_B])
    S.op("act", lambda: act.activation(out=lbv[:, 0, :, :], in_=lbv[:, 1, :, :], func=AF.Exp), reads=[d_B], writes=[d_B])
    S.op("dve", lambda: dve.tensor_scalar(out=lbv[:, 0, :, :], in0=lbv[:, 0, :, :], scalar1=1.0, scalar2=None, op0=ALU.add), reads=[d_B], writes=[d_B])
    S.op("dve", lambda: dve.reciprocal(out=lbv[:, 0, :, :], in_=lbv[:, 0, :, :]), reads=[d_B], writes=[d_B])
    S.op("dve", lambda: dve.tensor_scalar(out=lbv[:, 1, :, :], in0=lbv[:, 0, :, :], scalar1=-1.0, scalar2=1.0, op0=ALU.mult, op1=ALU.add), reads=[d_B], writes=[d_B])
    S.op("dve", lambda: dve.tensor_scalar(out=lbv[:, 2, :, :], in0=lbv[:, 0, :, :], scalar1=-1.0, scalar2=None, op0=ALU.add), reads=[d_B], writes=[d_B])

    BLKS = [(0, 512), (512, 512), (1024, 512), (1536, 512), (2048, 256)]
    nb8 = [0]

    def nextbank8():
        b = nb8[0] % 7
        nb8[0] += 1
        return b

    d_junkps = Dep("junkps")

    def pe_keepwarm(n):
        for _ in range(n):
            S.op("pe", lambda: pe.matmul(bank(7), lhsT=zerosB[:, 0:128], rhs=zerosB, start=True, stop=True, skip_group_check=True), reads=[d_const], writes=[PB[7]], inc=False)

    bd = [0]
    for h in range(4):
        def vg_item(which, grp, bi, t0, nt, sp_):
            nch = nt // 64
            c0 = t0 // 64
            hdeps = d_hT[t0 // 128:(t0 + nt) // 128]
            st = {}

            def s1():
                bv = nextbank8()
                for k in range(8):
                    S.op("pe", lambda k=k: pe.matmul(bank(bv)[:, 0:nt], lhsT=wB[:, grp, k, :], rhs=hT[:, k, t0:t0 + nt], start=(k == 0), stop=(k == 7)),
                         reads=[d_wB] + hdeps, writes=[PB[bv]], inc=(k == 7))
                if which == 0:
                    S.op("act", lambda: act.copy(out=stage[sp_][:, 0:nt], in_=bank(bv)[:, 0:nt]), reads=[PB[bv]], writes=[d_stage[sp_]])
                else:
                    S.op("act", lambda: act.activation(out=stage[sp_][:, 0:nt], in_=bank(bv)[:, 0:nt], func=AF.Silu), reads=[PB[bv]], writes=[d_stage[sp_]])

            def s2():
                bt_ = nextbank8()
                pk = bank_bf(bt_).rearrange("p (j c) -> p j c", c=128)
                for cc in range(nch):
                    S.op("pe", lambda cc=cc: pe.transpose(out=pk[0:64, cc, :], in_=stage[sp_][:, cc * 64:(cc + 1) * 64], identity=identB),
                         reads=[d_stage[sp_], d_const], writes=[PB[bt_]], inc=(cc == nch - 1))
                if which == 0:
                    S.op("dve", lambda: dve.tensor_copy(out=vtok[0:64, c0:c0 + nch, :], in_=pk[0:64, 0:nch, :]), reads=[PB[bt_]], writes=[d_vt[bi]])
                else:
                    lo = 4 if bi == 0 else 0
                    S.op("dve", lambda: dve.tensor_copy(out=gtok[0:64, c0 + lo - 4:c0 + nch - 4, :], in_=pk[0:64, lo:nch, :]), reads=[PB[bt_]], writes=[d_gt[bi]])
            return [s1, s2]

        items = []
        for which, grp in ((0, 1), (1, 4)):
            for bi, (t0, nt) in enumerate(BLKS):
                items.append(vg_item(which, grp, bi, t0, nt, bd[0] % 2))
                bd[0] += 1
        skew(items, newest_first=True)

        qbank = {}

        def gd_item(bi, t0, nt, d, p_):
            nch = nt // 64
            c0 = t0 // 64
            hdeps = d_hT[t0 // 128:(t0 + nt) // 128]
            T_sig, T_g, T_kk, T_G, T_s8, T_kg2 = bt_sig[p_], bt_g[p_], bt_kk[p_], bt_G[p_], bt_s8[p_], kg2[p_]
            D_ = d_bt[p_]
            G3 = T_G[:, 0:nt].rearrange("p (c l) -> p c l", l=64)
            eR = e1[d] if d == 0 else e2[d]
            eD = e2[d] if d == 0 else e1[d]
            sa = 1.0 if d == 0 else -1.0

            def s1():
                if d == 0:
                    bq = nextbank8()
                    qbank[bi] = bq
                    for k in range(8):
                        S.op("pe", lambda k=k: pe.matmul(bank(bq)[:, 0:nt], lhsT=wB[:, 0, k, :], rhs=hT[:, k, t0:t0 + nt], start=(k == 0), stop=(k == 7)),
                             reads=[d_wB] + hdeps, writes=[PB[bq]], inc=(k == 7))
                bf = nextbank8()
                for k in range(8):
                    S.op("pe", lambda k=k: pe.matmul(bank(bf)[:, 0:nt], lhsT=wB[:, 2 + d, k, :], rhs=hT[:, k, t0:t0 + nt], start=(k == 0), stop=(k == 7)),
                         reads=[d_wB] + hdeps, writes=[PB[bf]], inc=(k == 7))
                pe_keepwarm(8)
                S.op("act", lambda: act.activation(out=T_sig[:, 0:nt], in_=bank(bf)[:, 0:nt], func=AF.Exp, scale=-1.0), reads=[PB[bf]], writes=[D_["sig"]])
                S.op("act", lambda: act.activation(out=T_sig[:, 0:nt], in_=T_sig[:, 0:nt], func=AF.Ln, bias=oneB[:, 0:1]), reads=[d_B], writes=[D_["sig"]])
                S.op("act", lambda: act.activation(out=T_sig[:, 0:nt], in_=T_sig[:, 0:nt], func=AF.Exp, scale=-1.0), reads=[], writes=[D_["sig"]])
                S.op("act", lambda: act.activation(out=T_g[:, 0:nt], in_=T_sig[:, 0:nt], func=AF.Ln, scale=lbv[:, 1, d, h:h + 1], bias=lbv[:, 0, d, h:h + 1]),
                     reads=[D_["sig"], d_B], writes=[D_["g"]])
                S.op("dve", lambda: dve.tensor_scalar(out=T_kk[:, 0:nt], in0=T_sig[:, 0:nt], scalar1=lbv[:, 2, d, h:h + 1], scalar2=lbv[:, 1, d, h:h + 1], op0=ALU.mult, op1=ALU.add),
                     reads=[D_["sig"], d_B], writes=[D_["kk"]])
                S.op("dve", lambda: dve.tensor_tensor_scan(out=T_G[:, 0:nt], data0=smask[:, 0:nt], data1=T_g[:, 0:nt], initial=0.0, op0=ALU.mult, op1=ALU.add),
                     reads=[D_["g"], d_B], writes=[D_["G"]])

            def s2():
                bq = qbank[bi]
                S.op("dve", lambda: dve.tensor_copy(out=T_s8[:, 0, 0:nch].unsqueeze(2), in_=G3[:, :, 31:32]), reads=[D_["G"]], writes=[D_["s8"]])
                S.op("dve", lambda: dve.tensor_tensor(out=T_s8[:, 1, 0:nch].unsqueeze(2), in0=G3[:, :, 63:64], in1=G3[:, :, 31:32], op=ALU.subtract), reads=[D_["G"]], writes=[D_["s8"]])
                S.op("act", lambda: act.activation(out=eR[:, c0:c0 + nch], in_=T_s8[:, 0, 0:nch], func=AF.Exp), reads=[D_["s8"]], writes=[d_e[d][bi]])
                S.op("act", lambda: act.activation(out=eL[d][:, c0:c0 + nch].unsqueeze(2), in_=G3[:, :, 63:64], func=AF.Exp), reads=[D_["G"]], writes=[d_e[d][bi]])
                S.op("act", lambda: act.activation(out=eD[:, c0:c0 + nch], in_=T_s8[:, 1, 0:nch], func=AF.Exp), reads=[D_["s8"]], writes=[d_e[d][bi]])
                S.op("dve", lambda: dve.tensor_tensor(out=G3, in0=G3, in1=T_s8[:, 0, 0:nch].unsqueeze(2).to_broadcast([128, nch, 64]), op=ALU.subtract), reads=[D_["s8"]], writes=[D_["G"]])
                if d == 1:
                    S.op("dve", lambda: dve.tensor_tensor(out=T_G[:, 0:nt], in0=T_G[:, 0:nt], in1=T_g[:, 0:nt], op=ALU.subtract), reads=[D_["g"]], writes=[D_["G"]])
                S.op("act", lambda: act.activation(out=T_sig[:, 0:nt], in_=T_G[:, 0:nt], func=AF.Exp, scale=sa), reads=[D_["G"]], writes=[D_["sig"]])
                S.op("act", lambda: act.activation(out=T_g[:, 0:nt], in_=T_G[:, 0:nt], func=AF.Exp, scale=-sa), reads=[D_["G"]], writes=[D_["g"]])
                S.op("dve", lambda: dve.tensor_tensor(out=qg[d][:, t0:t0 + nt], in0=bank(bq)[:, 0:nt], in1=T_sig[:, 0:nt], op=ALU.mult), reads=[PB[bq], D_["sig"]], writes=[d_qg[d][bi]])
                S.op("pool", lambda: pool.tensor_tensor(out=T_g[:, 0:nt], in0=T_kk[:, 0:nt], in1=T_g[:, 0:nt], op=ALU.mult), reads=[D_["kk"]], writes=[D_["g"]])
                S.op("act", lambda: act.copy(out=kg[d][:, t0:t0 + nt], in_=T_g[:, 0:nt]), reads=[D_["g"]], writes=[d_kg[d][bi]])
                S.op("dve", lambda: dve.tensor_tensor(out=T_kg2[:, 0:nt].rearrange("p (c l) -> p c l", l=64), in0=T_g[:, 0:nt].rearrange("p (c l) -> p c l", l=64),
                                                      in1=e2[d][:, c0:c0 + nch].unsqueeze(2).to_broadcast([128, nch, 64]), op=ALU.mult),
                     reads=[D_["g"], d_e[d][bi]], writes=[D_["kk"]])

            def s3():
                bt_ = nextbank8()
                pk = bank_bf(bt_).rearrange("p (j c) -> p j c", c=128)
                for cc in range(nch):
                    S.op("pe", lambda cc=cc: pe.transpose(out=pk[0:64, cc, :], in_=T_kg2[:, cc * 64:(cc + 1) * 64], identity=identB),
                         reads=[D_["kk"], d_const], writes=[PB[bt_]], inc=(cc == nch - 1))
                S.op("act", lambda: act.copy(out=kgtok[d][0:64, c0:c0 + nch, :], in_=pk[0:64, 0:nch, :]), reads=[PB[bt_]], writes=[d_kgt[d][bi]])
            return [s1, s2, s3]

        items = []
        for bi, (t0, nt) in enumerate(BLKS):
            for d in range(2):
                items.append(gd_item(bi, t0, nt, d, bd[0] % 2))
                bd[0] += 1
        skew(items)
        def slot_of(c):
            return c % 8

        for d in range(2):
            first = 0 if d == 0 else 3
            S.op("pool", lambda d=d, first=first: pool.memset(Sring[d][:, slot_of(first), :], 0.0), writes=[d_Sr[d][slot_of(first) // 4]])
        order = [list(range(36)), [3, 2, 1, 0] + list(range(35, 3, -1))]
        for g in range(9):
            for d in range(2):
                bu = (0, 1)[g % 2] if d == 0 else (2, 3)[g % 2]
                pu = bank(bu).rearrange("p (j c) -> p j c", c=128)
                cs = order[d][g * 4:g * 4 + 4]
                for jj, c in enumerate(cs):
                    S.op("pe", lambda jj=jj, c=c: pe.matmul(pu[:, jj, :], lhsT=kgtok[d][0:64, c, :], rhs=vtok[0:64, c, :], start=True, stop=True),
                         reads=[d_kgt[d][c // 8], d_vt[c // 8]], writes=[PB[bu]], inc=(jj == 3))
                for jj, c in enumerate(cs):
                    bi = c // 8
                    if d == 0:
                        cn = c + 1
                    else:
                        cn = 35 if c == 0 else c - 1
                    if (d == 0 and c == 35) or (d == 1 and c == 4):
                        continue
                    sp_, sn_ = slot_of(c), slot_of(cn)
                    S.op("dve", lambda c=c, sp_=sp_, sn_=sn_, jj=jj: dve.scalar_tensor_tensor(out=Sring[d][:, sn_, :], in0=Sring[d][:, sp_, :], scalar=eL[d][:, c:c + 1], in1=pu[:, jj, :], op0=ALU.mult, op1=ALU.add),
                         reads=[d_Sr[d][sp_ // 4], d_e[d][bi], PB[bu]], writes=[d_Sr[d][sn_ // 4]])
                if d == 0:
                    c0 = g * 4
                else:
                    c0 = (36 - g * 4) if g >= 1 else None
                    if c0 is not None and c0 > 32:
                        c0 = None
                if c0 is not None and c0 >= 4 and c0 + 3 <= 35:
                    sl0 = slot_of(c0)
                    S.op("pool", lambda c0=c0, sl0=sl0: pool.tensor_tensor(out=Smid[d][:, c0 - 4:c0, :], in0=Sring[d][:, sl0:sl0 + 4, :],
                                                                         in1=e1[d][:, c0:c0 + 4].unsqueeze(2).to_broadcast([128, 4, 128]), op=ALU.mult),
                         reads=[d_Sr[d][sl0 // 4], d_e[d][c0 // 8]], writes=[d_Sm[d][(c0 - 4) // 4]])
        S.op("pool", lambda: pool.tensor_tensor(out=Smid[1][:, 0:4, :], in0=Sring[1][:, 4:8, :], in1=e1[1][:, 4:8].unsqueeze(2).to_broadcast([128, 4, 128]), op=ALU.mult),
             reads=[d_Sr[1][1], d_e[1][0]], writes=[d_Sm[1][0]])
        if h < 3:
            load_wB(h + 1)
        tri4 = tri[0:64, :, :].unsqueeze(1).to_broadcast([64, 4, 2, 64])

        def o_s1(g):
            ba, bo = 4 + g % 2, 6 + g % 2
            pa = bank(ba)[0:64, :].rearrange("p (j d c) -> p j d c", d=2, c=64)
            po = bank(bo)[0:64, :].rearrange("p (j c) -> p j c", c=128)
            os_ = g % 2
            for jj in range(4):
                c = 4 + g * 4 + jj
                bi = c // 8
                for d in range(2):
                    S.op("pe", lambda jj=jj, c=c, d=d: pe.matmul(pa[:, jj, d, :], lhsT=kg[d][:, c * 64:(c + 1) * 64], rhs=qg[d][:, c * 64:(c + 1) * 64], start=True, stop=True, skip_group_check=True),
                         reads=[d_kg[d][bi], d_qg[d][bi]], writes=[PB[ba]], inc=(jj == 3 and d == 1))
            S.op("dve", lambda: dve.tensor_tensor(out=Asb[0:64, :, :, :], in0=pa, in1=tri4, op=ALU.mult), reads=[PB[ba], d_B], writes=[d_As])

        def o_s1b(g):
            ba, bo = 4 + g % 2, 6 + g % 2
            po = bank(bo)[0:64, :].rearrange("p (j c) -> p j c", c=128)
            os_ = g % 2
            first = True
            for jj in range(4):
                c = 4 + g * 4 + jj
                for d in range(2):
                    S.op("pe", lambda jj=jj, c=c, d=d, first=first: pe.matmul(po[:, jj, :], lhsT=Asb[0:64, jj, d, :], rhs=vtok[0:64, c, :], start=first, stop=False, skip_group_check=True),
                         reads=[d_As, d_vt[c // 8]], writes=[PB[bo]], inc=False)
                    first = False
            for jj in range(4):
                c = 4 + g * 4 + jj
                bi = c // 8
                for d in range(2):
                    last = (jj == 3 and d == 1)
                    S.op("pe", lambda jj=jj, c=c, d=d, last=last: pe.matmul(po[:, jj, :], lhsT=qg[d][:, c * 64:(c + 1) * 64], rhs=Smid[d][:, c - 4, :], start=False, stop=last, skip_group_check=True),
                         reads=[d_qg[d][bi], d_Sm[d][(c - 4) // 4]], writes=[PB[bo]], inc=last)
            S.op("act", lambda: act.copy(out=ot[os_][0:64, :, :], in_=po), reads=[PB[bo]], writes=[d_ot[os_], d_bt[os_]["sig"]])

        def o_s2(g):
            os_ = g % 2
            od_ = [d_ot[os_], d_bt[os_]["sig"]]
            for jj in range(4):
                cl = g * 4 + jj
                S.op("dve", lambda jj=jj, cl=cl: dve.scalar_tensor_tensor(out=ojunk[0:64, :], in0=ot[os_][0:64, jj, :], scalar=1.0, in1=ot[os_][0:64, jj, :], op0=ALU.mult, op1=ALU.mult, accum_out=hb_ss[0:64, 0, cl:cl + 1]),
                     reads=od_, writes=[d_hb])
            S.op("dve", lambda: dve.tensor_scalar(out=hb_ss[0:64, 1, g * 4:g * 4 + 4], in0=hb_ss[0:64, 0, g * 4:g * 4 + 4], scalar1=1.0 / 128, scalar2=EPS, op0=ALU.mult, op1=ALU.add), reads=[d_hb], writes=[d_hb])
            S.op("pool", lambda: pool.tensor_tensor(out=hb_ss[0:64, 2, g * 4:g * 4 + 4], in0=hb_ss[0:64, 1, g * 4:g * 4 + 4], in1=mhB[0:64, 0:1].to_broadcast([64, 4]), op=ALU.pow), reads=[d_hb, d_B], writes=[d_hb])
            for jj in range(4):
                cl = g * 4 + jj
                S.op("dve", lambda jj=jj, cl=cl: dve.scalar_tensor_tensor(out=hb_y[0:64, cl, :], in0=ot[os_][0:64, jj, :], scalar=hb_ss[0:64, 2, cl:cl + 1], in1=gtok[0:64, cl, :], op0=ALU.mult, op1=ALU.mult),
                     reads=od_ + [d_hb, d_gt[(cl + 4) // 8]], writes=[d_hby[g // 2]] + d_kgt[0])

        def o_s2b(g):
            if g % 2 == 1:
                grp = g // 2
                bt_ = grp % 4
                py = bank_bf(bt_)[:, 0:512].rearrange("p (j c) -> p j c", c=64)
                for jj in range(8):
                    c = grp * 8 + jj
                    S.op("pe", lambda jj=jj, c=c: pe.transpose(out=py[:, jj, :], in_=hb_y[0:64, c, :], identity=identB[0:64, 0:64]), reads=[d_hby[grp], d_const] + d_kgt[0], writes=[PB[bt_]], inc=(jj == 7))
                S.op("act", lambda: act.activation(out=yT[:, 4 + h, grp * 512:(grp + 1) * 512], in_=bank_bf(bt_)[:, 0:512], func=AF.Copy, scale=hgT[:, 0:1]), reads=[PB[bt_], d_B], writes=d_yT[4 + h][grp * 4:grp * 4 + 4])

        for g in range(10):
            if g < 8:
                o_s1(g)
            if 1 <= g <= 8:
                o_s2(g - 1)
            if g < 8:
                o_s1b(g)
            if 2 <= g <= 9:
                o_s2b(g - 2)
    if debug:
        S.dma("sp", lambda: sp.dma_start(out=dbg["yT0"], in_=yT), reads=[x for r in d_yT for x in r], writes=[], semdep=d_yT[0][0])
    S.barrier()

    X_OFF = P_TOP
    AR.top = X_OFF
    xnew = AR.f32(16, D)
    T2 = AR.top
    d_xnew = [Dep("xnew%d" % i) for i in range(16)]

    def out_phase(l, w_d, pre=None):
        AR.top = T2
        if pre is None:
            wo = AR.bf16(8, D)
            stg = [AR.f32(D) for _ in range(4)]
        xt = [AR.f32(D), AR.f32(D)]
        gtmp = [AR.f32(512), AR.f32(512)]
        d_gtmp = [Dep("gtmp0"), Dep("gtmp1")]
        d_wo, d_stg, d_xt = [Dep("wo%d" % k_) for k_ in range(8)], [Dep("stg%d" % k_) for k_ in range(4)], [Dep("oxt0"), Dep("oxt1")]
        def ld(k):
            S.dma("sp", lambda: sp.dma_start(out=stg[k % 4], in_=w_d[k * 128:(k + 1) * 128, :]), writes=[d_stg[k % 4]], semdep=d_stg[k % 4])

        if pre is None:
            for k in range(4):
                ld(k)
            for k in range(8):
                s = k % 4
                S.op("dve", lambda k=k, s=s: dve.tensor_tensor(out=wo[:, k, :], in0=stg[s], in1=gate_bc[l], op=ALU.mult), reads=[d_stg[s], d_gate[l]], writes=[d_wo[k]])
                if k + 4 < 8:
                    ld(k + 4)
        else:
            wo, d_wo = pre
        for i in range(16):
            s = i % 2
            if l == 0:
                S.dma("sp", lambda i=i, s=s: sp.dma_start(out=xt[s], in_=x_d[i * 128:(i + 1) * 128, :]), writes=[d_xt[s]], semdep=d_xt[s])
            for n in range(2):
                bp = nextbank()
                for k in range(8):
                    S.op("pe", lambda k=k, n=n, i=i: pe.matmul(bank(bp), lhsT=yT[:, k, i * 128:(i + 1) * 128], rhs=wo[:, k, n * 512:(n + 1) * 512], start=(k == 0), stop=(k == 7)),
                         reads=[d_wo[k], d_yT[k][i]], writes=[PB[bp]], inc=(k == 7))
                if l == 0:
                    S.op("dve", lambda n=n, i=i, s=s: dve.tensor_tensor(out=xnew[:, i, n * 512:(n + 1) * 512], in0=bank(bp), in1=xt[s][:, n * 512:(n + 1) * 512], op=ALU.add),
                         reads=[PB[bp], d_xt[s]], writes=[d_xnew[i]])
                elif pre is None:
                    S.op("dve", lambda n=n, i=i, s=s: dve.tensor_tensor(out=xt[s][:, n * 512:(n + 1) * 512], in0=bank(bp), in1=xnew[:, i, n * 512:(n + 1) * 512], op=ALU.add),
                         reads=[PB[bp], d_xnew[i]], writes=[d_xt[s]])
                else:
                    S.op("dve", lambda n=n: dve.tensor_tensor(out=gtmp[n], in0=bank(bp), in1=gate_bc[l][:, n * 512:(n + 1) * 512], op=ALU.mult),
                         reads=[PB[bp], d_gate[l]], writes=[d_gtmp[n]])
                    S.op("pool", lambda n=n, i=i, s=s: pool.tensor_tensor(out=xt[s][:, n * 512:(n + 1) * 512], in0=gtmp[n], in1=xnew[:, i, n * 512:(n + 1) * 512], op=ALU.add),
                         reads=[d_gtmp[n], d_xnew[i]], writes=[d_xt[s]])
            if l == 1:
                S.dma("sp", lambda i=i, s=s: sp.dma_start(out=out_d[i * 128:(i + 1) * 128, :], in_=xt[s]), reads=[d_xt[s]], writes=[], semdep=d_xt[s])
        return d_xt

    out_phase(0, ewout_d)
    if debug:
        S.dma("sp", lambda: sp.dma_start(out=dbg["xnew"], in_=xnew), reads=d_xnew, writes=[], semdep=d_xnew[0])
    S.barrier()

    def src1(i):
        return xnew[:, i, :], False, d_xnew[i]

    AR.top = T2
    wC = AR.bf16(3, 8, 512)
    L1_TMP = AR.top
    AR.top = L1_TMP + 6000
    wD = [AR.bf16(4, 8, 128), AR.bf16(4, 8, 128)]
    L1_END = AR.top
    d_wC = Dep("wC")
    dd_w = [Dep("wD0"), Dep("wD1")]

    def load_wD(j):
        for g in range(4):
            S.dma("pool", lambda g=g: pool.dma_start(
                out=wD[j % 2][:, g, :, :], in_=owin_d.rearrange("(k p) n -> p k n", p=128)[:, :, 1536 + g * 512 + j * 128:1536 + g * 512 + (j + 1) * 128]),
                writes=[dd_w[j % 2]], semdep=dd_w[j % 2])

    for g in range(3):
        for kk in range(2):
            S.dma("pool", lambda g=g, kk=kk: pool.dma_start(out=wC[:, g, kk * 4:(kk + 1) * 4, :], in_=owin_d.rearrange("(k p) n -> p k n", p=128)[:, kk * 4:(kk + 1) * 4, g * 512:(g + 1) * 512]),
                  writes=[d_wC], semdep=d_wC)
    load_wD(0)
    load_wD(1)
    norm_phase(1, 16, src1, L1_TMP)
    S.barrier()

    AR.top = L1_TMP
    wsF = AR.f32(4, 128)
    wsT = AR.bf16(4, 128)
    bsT = AR.f32(4)
    vgS = AR.f32(512)
    mhalf = AR.f32(1)
    c_gu = [AR.f32(512), AR.f32(512)]
    c_sg = [AR.f32(512), AR.f32(512)]
    c_gv = [AR.f32(512), AR.f32(512)]
    c_vn = [AR.bf16(512), AR.bf16(512)]
    c_y = [AR.bf16(512), AR.bf16(512)]
    c_junk = AR.f32(512)
    c_st = AR.f32(16, 4)
    assert AR.top <= L1_TMP + 6000, AR.top - L1_TMP
    d_C = Dep("Cconst")
    d_c = [{k: Dep("c%d_%s" % (p_, k)) for k in ("gu", "sg", "gv", "vn", "y")} for p_ in range(2)]
    d_cj, d_cst = Dep("c_junk"), [Dep("c_st%d" % i) for i in range(16)]
    S.dma("sp", lambda: sp.dma_start(out=wsF, in_=ws_d.rearrange("g t s -> t g s")), writes=[d_C], semdep=d_C)
    S.dma("sp", lambda: sp.dma_start(out=bsT, in_=bs_d.rearrange("g t -> t g"), allow_slow_non_contiguous=True), writes=[d_C], semdep=d_C)
    S.dma("sp", lambda: sp.dma_start(out=vgS, in_=vg_d.partition_broadcast(128)), writes=[d_C], semdep=d_C)
    S.op("pool", lambda: pool.memset(mhalf, -0.5), writes=[d_C])
    bw = nextbank8()
    pw = bank(bw).rearrange("p (g c) -> p g c", c=128)
    for g in range(4):
        S.op("pe", lambda g=g: pe.transpose(out=pw[:, g, :], in_=wsF[:, g, :], identity=identF), reads=[d_C, d_const], writes=[PB[bw]], inc=(g == 3))
    S.op("dve", lambda: dve.tensor_copy(out=wsT, in_=pw), reads=[PB[bw]], writes=[d_C])

    def c_item(i):
        p_ = i % 2
        Dc = d_c[p_]
        gu, sg, gv, vn, yy = c_gu[p_], c_sg[p_], c_gv[p_], c_vn[p_], c_y[p_]
        st = {}

        def s1():
            bu, bv, bg = nextbank8(), nextbank8(), nextbank8()
            st["bg"] = bg
            for g, bb in ((0, bu), (1, bv), (2, bg)):
                for k in range(8):
                    S.op("pe", lambda k=k, g=g, bb=bb: pe.matmul(bank(bb), lhsT=hT[:, k, i * 128:(i + 1) * 128], rhs=wC[:, g, k, :], start=(k == 0), stop=(k == 7)),
                         reads=[d_wC, d_hT[i]], writes=[PB[bb]], inc=(k == 7))
            S.op("act", lambda: act.activation(out=gu, in_=bank(bu), func=AF.Gelu), reads=[PB[bu]], writes=[Dc["gu"]])
            S.op("act", lambda: act.activation(out=gv, in_=bank(bv), func=AF.Gelu), reads=[PB[bv]], writes=[Dc["gv"]])
            S.op("act", lambda: act.activation(out=sg, in_=bank(bg), func=AF.Tanh, scale=0.5), reads=[PB[bg]], writes=[Dc["sg"]])
            S.op("dve", lambda: dve.tensor_scalar(out=sg, in0=sg, scalar1=0.5, scalar2=0.5, op0=ALU.mult, op1=ALU.add), reads=[], writes=[Dc["sg"]])
            S.op("dve", lambda: dve.tensor_tensor(out=sg, in0=bank(bg), in1=sg, op=ALU.mult), reads=[PB[bg]], writes=[Dc["sg"]])

        def s2():
            S.op("pool", lambda: pool.tensor_tensor(out=gu, in0=gu, in1=sg, op=ALU.mult), reads=[Dc["sg"]], writes=[Dc["gu"]])
            S.op("dve", lambda: dve.scalar_tensor_tensor(out=c_junk, in0=gv, scalar=1.0, in1=gv, op0=ALU.mult, op1=ALU.mult, accum_out=c_st[:, i, 0:1]), reads=[Dc["gv"]], writes=[d_cj, d_cst[i]])
            S.op("dve", lambda: dve.tensor_scalar(out=c_st[:, i, 1:2], in0=c_st[:, i, 0:1], scalar1=1.0 / 512, scalar2=EPS, op0=ALU.mult, op1=ALU.add), reads=[], writes=[d_cst[i]])
            S.op("pool", lambda: pool.tensor_tensor(out=c_st[:, i, 2:3], in0=c_st[:, i, 1:2], in1=mhalf, op=ALU.pow), reads=[d_C], writes=[d_cst[i]])
            S.op("dve", lambda: dve.scalar_tensor_tensor(out=vn, in0=gv, scalar=c_st[:, i, 2:3], in1=vgS, op0=ALU.mult, op1=ALU.mult), reads=[Dc["gv"], d_cst[i], d_C], writes=[Dc["vn"]])

        def s3():
            bs_ = nextbank8()
            ps = bank(bs_).rearrange("p (g c) -> p g c", c=128)
            for g in range(4):
                S.op("pe", lambda g=g: pe.matmul(ps[:, g, :], lhsT=wsT[:, g, :], rhs=vn[:, g * 128:(g + 1) * 128], start=True, stop=True), reads=[d_C, Dc["vn"]], writes=[PB[bs_]], inc=(g == 3))
            for g in range(4):
                S.op("dve", lambda g=g: dve.scalar_tensor_tensor(out=yy[:, g * 128:(g + 1) * 128], in0=ps[:, g, :], scalar=bsT[:, g:g + 1], in1=gu[:, g * 128:(g + 1) * 128], op0=ALU.add, op1=ALU.mult),
                     reads=[PB[bs_], d_C, Dc["gu"]], writes=[Dc["y"]])

        def s3b():
            bt_ = nextbank8()
            py = bank_bf(bt_)[:, 0:512].rearrange("p (g c) -> p g c", c=128)
            for g in range(4):
                S.op("pe", lambda g=g: pe.transpose(out=py[:, g, :], in_=yy[:, g * 128:(g + 1) * 128], identity=identB), reads=[Dc["y"], d_const], writes=[PB[bt_]], inc=(g == 3))
            S.op("act", lambda: act.copy(out=yT[:, 0:4, i * 128:(i + 1) * 128], in_=py), reads=[PB[bt_]], writes=[d_yT[g][i] for g in range(4)])
        return [s1, s2, s3, s3b]

    c_items = [c_item(i) for i in range(16)]
    for t in range(16 + 2):
        if 0 <= t - 2 < 16:
            c_items[t - 2][2]()
        if 0 <= t - 1 < 16:
            c_items[t - 1][1]()
        if t < 16:
            c_items[t][0]()
        if 0 <= t - 2 < 16:
            c_items[t - 2][3]()
    S.barrier()

    AR.top = T2
    cwT = AR.f32(4, 3)
    zb = AR.f32(SEQ + 2)
    bsg = AR.f32(SEQ)
    cvt = AR.f32(SEQ)
    d_csb = [AR.f32(512), AR.f32(512)]
    _sgd = AR.f32(512)
    d_sgd = [_sgd, _sgd]
    wo1 = AR.bf16(8, D)
    assert AR.top <= L1_TMP + 6000, (AR.top, L1_TMP)
    dd_c = Dep("cw")
    dd_z = [Dep("z%d" % b_) for b_ in range(4)]
    dd_bsg = [Dep("bsg%d" % b_) for b_ in range(4)]
    dd_cv = [Dep("cvt%d" % b_) for b_ in range(4)]
    _dsgd = Dep("sgd")
    dd_csb, dd_sgd = [Dep("csb0"), Dep("csb1")], [_dsgd, _dsgd]
    d_wo1 = [Dep("wo1_%d" % k_) for k_ in range(8)]
    for kk_ in range(2):
        S.dma("pool", lambda kk_=kk_: pool.dma_start(out=wo1[:, kk_ * 4:(kk_ + 1) * 4, :], in_=owout_d.rearrange("(k p) n -> p k n", p=128)[:, kk_ * 4:(kk_ + 1) * 4, :]),
              writes=d_wo1[kk_ * 4:(kk_ + 1) * 4], semdep=d_wo1[kk_ * 4])
    for w_ in range(3):
        S.dma("sp", lambda w_=w_: sp.dma_start(out=cwT[:, :, w_], in_=cw_d[w_].rearrange("(j p) -> p j", p=128), allow_slow_non_contiguous=True), writes=[dd_c], semdep=dd_c)
    S.op("pool", lambda: pool.memset(zb, 0.0), writes=dd_z)
    for j in range(4):
        slot = j % 2

        def d_item(b, p_):
            def s1():
                b1, b2 = nextbank8(), nextbank8()
                for g, bk in ((1, b1), (2, b2)):
                    for k in range(8):
                        S.op("pe", lambda k=k, g=g, bk=bk: pe.matmul(bank(bk), lhsT=wD[slot][:, g, k, :], rhs=hT[:, k, b * 512:(b + 1) * 512], start=(k == 0), stop=(k == 7)),
                             reads=[dd_w[slot]] + d_hT[b * 4:b * 4 + 4], writes=[PB[bk]], inc=(k == 7))
                S.op("act", lambda: act.copy(out=d_csb[p_], in_=bank(b1)), reads=[PB[b1]], writes=[dd_csb[p_]])
                S.op("dve", lambda: dve.tensor_tensor(out=zb[:, 1 + b * 512:1 + (b + 1) * 512], in0=bank(b2), in1=d_csb[p_], op=ALU.mult), reads=[PB[b2], dd_csb[p_]], writes=[dd_z[b]])

            def s2():
                b3, b4 = nextbank8(), nextbank8()
                for g, bk in ((3, b3), (0, b4)):
                    for k in range(8):
                        S.op("pe", lambda k=k, g=g, bk=bk: pe.matmul(bank(bk), lhsT=wD[slot][:, g, k, :], rhs=hT[:, k, b * 512:(b + 1) * 512], start=(k == 0), stop=(k == 7)),
                             reads=[dd_w[slot]] + d_hT[b * 4:b * 4 + 4], writes=[PB[bk]], inc=(k == 7))
                S.op("act", lambda: act.activation(out=d_sgd[p_], in_=bank(b3), func=AF.Silu), reads=[PB[b3]], writes=[dd_sgd[p_]])
                S.op("dve", lambda: dve.tensor_tensor(out=bsg[:, b * 512:(b + 1) * 512], in0=bank(b4), in1=d_sgd[p_], op=ALU.mult), reads=[PB[b4], dd_sgd[p_]], writes=[dd_bsg[b]])
            return [s1, s2]

        d_items = [d_item(b, b % 2) for b in range(4)]

        def conv_blk(b, j=j):
            lo, hi = b * 512, (b + 1) * 512
            zdeps = dd_z[max(b - 1, 0):min(b + 2, 4)]
            S.op("act", lambda: act.activation(out=cvt[:, lo:hi], in_=zb[:, lo:hi], func=AF.Copy, scale=cwT[:, j, 0:1]), reads=zdeps + [dd_c], writes=[dd_cv[b]])
            S.op("dve", lambda: dve.scalar_tensor_tensor(out=cvt[:, lo:hi], in0=zb[:, lo + 1:hi + 1], scalar=cwT[:, j, 1:2], in1=cvt[:, lo:hi], op0=ALU.mult, op1=ALU.add), reads=zdeps + [dd_c], writes=[dd_cv[b]])
            S.op("dve", lambda: dve.scalar_tensor_tensor(out=cvt[:, lo:hi], in0=zb[:, lo + 2:hi + 2], scalar=cwT[:, j, 2:3], in1=cvt[:, lo:hi], op0=ALU.mult, op1=ALU.add), reads=zdeps + [dd_c], writes=[dd_cv[b]])
            S.op("dve", lambda: dve.tensor_tensor(out=yT[:, 4 + j, lo:hi], in0=cvt[:, lo:hi], in1=bsg[:, lo:hi], op=ALU.mult), reads=[dd_cv[b], dd_bsg[b]], writes=d_yT[4 + j][b * 4:b * 4 + 4])

        for t in range(5):
            if t >= 1:
                d_items[t - 1][1]()
            if t < 4:
                d_items[t][0]()
            if t == 4 and j + 2 < 4:
                load_wD(j + 2)
            if t >= 1:
                conv_blk(t - 1)
    if debug:
        S.dma("sp", lambda: sp.dma_start(out=dbg["yT1"], in_=yT), reads=[x for r in d_yT for x in r], writes=[], semdep=d_yT[0][0])
    S.barrier()

    d_fin = out_phase(1, owout_d, pre=(wo1, d_wo1))
    S.barrier()
    return nc


_CONST = {}


def _consts():
    if _CONST:
        return _CONST
    f32 = np.float32
    ident = np.eye(128, dtype=f32)
    perm = np.zeros((128, 128), f32)
    for m in range(128):
        partner = m + 32 if (m % 64) < 32 else m - 32
        perm[partner, m] = 1.0
    bones = np.zeros((128, 128), f32)
    bones[0:64, 0:64] = 1.0
    bones[64:128, 64:128] = 1.0
    rows = SEQ // 64
    row = np.repeat(np.arange(rows, dtype=f32), 64)
    col = np.tile(np.arange(64, dtype=f32), rows)
    n_freq = 16
    inv = (f32(10000.0) ** (-np.arange(n_freq, dtype=f32) / f32(n_freq))).astype(f32)
    ang = np.concatenate([row[:, None] * inv, col[:, None] * inv], axis=-1).astype(f32)
    cos = np.cos(ang).astype(f32).T
    sin = np.sin(ang).astype(f32).T
    cosT = np.concatenate([cos, cos, cos, cos], axis=0)
    sinT = np.concatenate([-sin, sin, -sin, sin], axis=0)
    tri = np.zeros((64, 2, 64), f32)
    s_idx = np.arange(64)[:, None]
    t_idx = np.arange(64)[None, :]
    tri[:, 0, :] = (s_idx <= t_idx)
    tri[:, 1, :] = (s_idx >= t_idx)
    smask = np.ones((128, 512), f32)
    smask[:, ::64] = 0.0
    _CONST.update(identF=ident, perm=perm, bones=bones, cosT=np.ascontiguousarray(cosT), sinT=np.ascontiguousarray(sinT), tri=tri, smask=smask)
    return _CONST


def make_in_maps(x, c, ctx, c_ctx, norm_gain, ada_w, ada_b, even_w_in, even_w_out, attn_qk_gain,
                 attn_lambda, attn_subln_gain, hgrn_lb_logits, hgrn_norm_gain, odd_w_in, odd_w_out,
                 gmlp_v_gain, gmlp_w_s, gmlp_b_s, conv_w):
    f = lambda a: np.ascontiguousarray(np.asarray(a, dtype=np.float32))
    shared = dict(
        norm_gain=f(norm_gain), ada_w=f(ada_w), ada_b=f(ada_b), even_w_in=f(even_w_in)[0], even_w_out=f(even_w_out)[0],
        qk_gain=f(attn_qk_gain)[0], attn_lambda=f(attn_lambda)[0].reshape(256), subln=f(attn_subln_gain)[0],
        lb_logits=f(hgrn_lb_logits), hgrn_g=f(hgrn_norm_gain)[0], odd_w_in=f(odd_w_in)[0], odd_w_out=f(odd_w_out)[0],
        v_gain=f(gmlp_v_gain)[0], w_s=f(gmlp_w_s)[0], b_s=f(gmlp_b_s)[0], conv_w=f(conv_w)[0])
    shared.update(_consts())
    x = f(x)
    c = f(c)
    ctx = f(ctx)
    c_ctx = f(c_ctx)
    maps = []
    for b in range(8):
        m = dict(shared)
        m["x"] = x[b]
        m["ctx"] = ctx[b]
        m["cvec"] = np.ascontiguousarray(np.stack([c[b], c_ctx], axis=0))
        maps.append(m)
    return maps


def kernel(**inputs):
    maps = make_in_maps(**inputs)
    nc = build(debug=False)
    res = run_bass_kernel_spmd(nc, maps, core_ids=list(range(8)))
    return np.stack([np.asarray(r["out"], dtype=np.float32) for r in res.results], axis=0)
```
